# Optimizing a Trainium2 kernel written in Bass

```python
import math
import jax, jax.numpy as jnp
from jax import lax
import numpy as np

D_MODEL = 1024
BATCH = 8
SEQ = 2048
DEPTH = 2
DEC_BATCH = 128
DEC_SEQ = 8
PAST_LEN = 16384
PAGE_SIZE = 128

N_AB_LAYERS = (DEPTH + 1) // 2
N_CD_LAYERS = DEPTH // 2
PLE_DIM = 256
D_FF = 4 * D_MODEL
CHUNK = 64
EPS = 1e-6
NEG = -1e30
A_HEADS = 4
A_DH = D_MODEL // (2 * A_HEADS)
A_W = A_HEADS * A_DH
CONV_W = 4
B_HEADS = 4
B_DH = D_MODEL // (2 * B_HEADS)
B_W = B_HEADS * B_DH
ROPE_BASE = 10000.0
C_HEADS = 4
C_DK = 128
C_DV = D_MODEL // (2 * C_HEADS)
C_KW = C_HEADS * C_DK
C_W = C_HEADS * C_DV
D_W = D_MODEL - C_W
D_GROUP = 16
D_GROUPS = D_W // D_GROUP
D_STATE = 64
AB_COLS = 4 * A_W + 2 * A_HEADS + 4 * B_W
CD_COLS = 2 * C_KW + 2 * C_W + D_W

kernel_name = 'hybrid_mlstm_retnet_hgrn2_s5_step'


def rmsnorm(x, g):
    xf = x.astype(jnp.float32)
    y = xf * lax.rsqrt(jnp.mean(xf * xf, axis=-1, keepdims=True) + EPS)
    return (y * g.astype(jnp.float32)).astype(x.dtype)


def head_rmsnorm(x):
    return x * lax.rsqrt(jnp.mean(x * x, axis=-1, keepdims=True) + EPS)


def split_cols(z, sizes):
    out, start = [], 0
    for s in sizes:
        out.append(z[..., start:start + s])
        start += s
    return out


def to_heads(x, h):
    b, l, _ = x.shape
    return x.reshape(b, l, h, -1).transpose(0, 2, 1, 3)


def from_heads(x):
    b, h, l, d = x.shape
    return x.transpose(0, 2, 1, 3).reshape(b, l, h * d)


def chunk_len(length):
    return CHUNK if length % CHUNK == 0 else length


def causal_mask(c):
    return jnp.tril(jnp.ones((c, c), dtype=bool))


def to_chunks(a, c):
    b, h, l = a.shape[:3]
    return jnp.moveaxis(a.reshape((b, h, l // c, c) + a.shape[3:]), 2, 0)


def from_chunks(a):
    a = jnp.moveaxis(a, 0, 2)
    b, h, nc, c = a.shape[:4]
    return a.reshape((b, h, nc * c) + a.shape[4:])


def rope(x, pos):
    half = x.shape[-1] // 2
    inv = ROPE_BASE ** (-jnp.arange(half, dtype=jnp.float32) / half)
    ang = pos.astype(jnp.float32)[:, None] * inv[None, :]
    cos, sin = jnp.cos(ang), jnp.sin(ang)
    x1, x2 = x[..., :half], x[..., half:]
    return jnp.concatenate([x1 * cos - x2 * sin, x1 * sin + x2 * cos], axis=-1)


def mlstm_chunkwise(q, k, v, itil, logf, C0, n0, m0):
    c = chunk_len(q.shape[2])
    mask = causal_mask(c)

    def step(carry, xs):
        C, n, m = carry
        qc, kc, vc, ic, fc = xs
        b = jnp.cumsum(fc, axis=-1)
        dmat = jnp.where(mask, b[..., :, None] - b[..., None, :] + ic[..., None, :], NEG)
        inter = b + m[..., None]
        m_t = jnp.maximum(inter, jnp.max(dmat, axis=-1))
        w_intra = jnp.exp(dmat - m_t[..., None])
        w_inter = jnp.exp(inter - m_t)
        s = jnp.einsum('bhtd,bhsd->bhts', qc, kc) * w_intra
        num = (w_inter[..., None] * jnp.einsum('bhtd,bhde->bhte', qc, C)
               + jnp.einsum('bhts,bhse->bhte', s, vc))
        den = w_inter * jnp.einsum('bhtd,bhd->bht', qc, n) + jnp.sum(s, axis=-1)
        h = num / jnp.maximum(jnp.abs(den), jnp.exp(-m_t))[..., None]
        m_new = m_t[..., -1]
        w_last = w_intra[..., -1, :]
        decay = jnp.exp(inter[..., -1] - m_new)
        C_new = decay[..., None, None] * C + jnp.einsum('bhs,bhsd,bhse->bhde', w_last, kc, vc)
        n_new = decay[..., None] * n + jnp.einsum('bhs,bhsd->bhd', w_last, kc)
        return (C_new, n_new, m_new), h

    xs = (to_chunks(q, c), to_chunks(k, c), to_chunks(v, c), to_chunks(itil, c), to_chunks(logf, c))
    (C, n, m), h = lax.scan(step, (C0, n0, m0), xs)
    return from_chunks(h), C, n, m


def retention_chunkwise(q, k, v, log_gamma, S0):
    c = chunk_len(q.shape[2])
    j = jnp.arange(c, dtype=jnp.float32)
    rel = jnp.maximum(j[:, None] - j[None, :], 0.0)
    decay = jnp.where(causal_mask(c), jnp.exp(rel[None] * log_gamma[:, None, None]), 0.0)
    inter = jnp.exp((j + 1.0)[None, :] * log_gamma[:, None])[:, :, None]
    kdecay = jnp.exp((c - 1.0 - j)[None, :] * log_gamma[:, None])[:, :, None]
    cdecay = jnp.exp(c * log_gamma)[:, None, None]

    def step(S, xs):
        qc, kc, vc = xs
        o = (jnp.einsum('bhtd,bhde->bhte', qc, S) * inter
             + jnp.einsum('bhts,bhse->bhte', jnp.einsum('bhtd,bhsd->bhts', qc, kc) * decay, vc))
        S = cdecay * S + jnp.einsum('bhsd,bhse->bhde', kc * kdecay, vc)
        return S, o

    S, o = lax.scan(step, S0, (to_chunks(q, c), to_chunks(k, c), to_chunks(v, c)))
    return from_chunks(o), S


def gla_chunkwise(q, k, v, logf, S0):
    c = chunk_len(q.shape[2])
    mask = causal_mask(c)[:, :, None]

    def step(S, xs):
        qc, kc, vc, fc = xs
        b = jnp.cumsum(fc, axis=2)
        diff = jnp.where(mask, b[:, :, :, None, :] - b[:, :, None, :, :], NEG)
        attn = jnp.einsum('bhtk,bhtsk,bhsk->bhts', qc, jnp.exp(diff), kc)
        o = (jnp.einsum('bhtk,bhkv->bhtv', qc * jnp.exp(b), S)
             + jnp.einsum('bhts,bhsv->bhtv', attn, vc))
        b_last = b[:, :, -1]
        S = (jnp.exp(b_last)[..., None] * S
             + jnp.einsum('bhsk,bhsv->bhkv', kc * jnp.exp(b_last[:, :, None] - b), vc))
        return S, o

    xs = (to_chunks(q, c), to_chunks(k, c), to_chunks(v, c), to_chunks(logf, c))
    S, o = lax.scan(step, S0, xs)
    return from_chunks(o), S


def s5_scan(u, A_re, A_im, log_dt, B_re, B_im, C_re, C_im, x0_re, x0_im):
    bsz, length = u.shape[:2]
    dt = jnp.exp(log_dt)[:, None]
    mag = jnp.exp(dt * A_re)
    ar, ai = mag * jnp.cos(dt * A_im), mag * jnp.sin(dt * A_im)
    den = A_re * A_re + A_im * A_im
    nr, ni = ar - 1.0, ai
    zr = (nr * A_re + ni * A_im) / den
    zi = (ni * A_re - nr * A_im) / den
    bbr = zr[..., None] * B_re - zi[..., None] * B_im
    bbi = zr[..., None] * B_im + zi[..., None] * B_re
    bu_r = jnp.einsum('gph,blgh->blgp', bbr, u)
    bu_i = jnp.einsum('gph,blgh->blgp', bbi, u)
    a_r = jnp.broadcast_to(ar[None, None], (1, length) + ar.shape)
    a_i = jnp.broadcast_to(ai[None, None], (1, length) + ai.shape)

    def combine(e1, e2):
        a1r, a1i, b1r, b1i = e1
        a2r, a2i, b2r, b2i = e2
        return (a1r * a2r - a1i * a2i, a1r * a2i + a1i * a2r,
                a2r * b1r - a2i * b1i + b2r, a2r * b1i + a2i * b1r + b2i)

    pr, pi, sr, si = lax.associative_scan(combine, (a_r, a_i, bu_r, bu_i), axis=1)
    xr = sr + pr * x0_re[:, None] - pi * x0_im[:, None]
    xi = si + pr * x0_im[:, None] + pi * x0_re[:, None]
    y = jnp.einsum('ghp,blgp->blgh', C_re, xr) - jnp.einsum('ghp,blgp->blgh', C_im, xi)
    return y.reshape(bsz, length, -1), xr[:, -1], xi[:, -1]


def ab_mixer(h, pos, conv_buf, C0, n0, m0, S0, w_in, b_gate, conv_w, conv_b, gn_a, w_out):
    f32 = jnp.float32
    length = h.shape[1]
    z = jnp.matmul(h, w_in).astype(f32)
    mq, mk, mv, mo, mi, mf, rq, rk, rv, rg = split_cols(
        z, (A_W, A_W, A_W, A_W, A_HEADS, A_HEADS, B_W, B_W, B_W, B_W))
    qk_ext = jnp.concatenate([conv_buf.astype(f32), jnp.concatenate([mq, mk], axis=-1)], axis=1)
    cw = conv_w.astype(f32)
    conv = conv_b.astype(f32)
    for j in range(CONV_W):
        conv = conv + cw[j] * qk_ext[:, j:j + length]
    qk = jax.nn.silu(conv)
    q = to_heads(qk[..., :A_W], A_HEADS)
    k = to_heads(qk[..., A_W:], A_HEADS) * (A_DH ** -0.5)
    v = to_heads(mv, A_HEADS)
    bg = b_gate.astype(f32)
    itil = jnp.transpose(mi + bg[:A_HEADS], (0, 2, 1))
    logf = jnp.transpose(jax.nn.log_sigmoid(mf + bg[A_HEADS:]), (0, 2, 1))
    hm, C, n, m = mlstm_chunkwise(q, k, v, itil, logf, C0.astype(f32), n0.astype(f32), m0.astype(f32))
    hm = jax.nn.sigmoid(to_heads(mo, A_HEADS)) * hm
    hm = from_heads(head_rmsnorm(hm) * gn_a.astype(f32).reshape(A_HEADS, 1, A_DH))
    log_gamma = jnp.log1p(-jnp.exp2(-5.0 - jnp.arange(B_HEADS, dtype=f32)))
    qr = rope(to_heads(rq, B_HEADS), pos)
    kr = rope(to_heads(rk, B_HEADS), pos) * (B_DH ** -0.5)
    hr, S = retention_chunkwise(qr, kr, to_heads(rv, B_HEADS), log_gamma, S0.astype(f32))
    hr = from_heads(head_rmsnorm(hr)) * jax.nn.silu(rg)
    out = jnp.matmul(jnp.concatenate([hm, hr], axis=-1).astype(h.dtype), w_out)
    return out, qk_ext[:, -(CONV_W - 1):], C, n, m, S


def cd_mixer(h, lb, S0, x0_re, x0_im, w_in, gn_c, A_re, A_im, log_dt, B_re, B_im, C_re, C_im,
             D_skip, w_glu, b_glu, w_out):
    f32 = jnp.float32
    bsz, length, _ = h.shape
    z = jnp.matmul(h, w_in).astype(f32)
    hq, hf, hi, hg, su = split_cols(z, (C_KW, C_KW, C_W, C_W, D_W))
    logf = jnp.logaddexp(jnp.log(lb), jnp.log1p(-lb) + jax.nn.log_sigmoid(hf))
    q = to_heads(hq, C_HEADS) * (C_DK ** -0.5)
    k = to_heads(-jnp.expm1(logf), C_HEADS)
    o, S = gla_chunkwise(q, k, to_heads(hi, C_HEADS), to_heads(logf, C_HEADS), S0.astype(f32))
    o = from_heads(head_rmsnorm(o) * gn_c.astype(f32).reshape(C_HEADS, 1, C_DV)) * jax.nn.silu(hg)
    u = su.reshape(bsz, length, D_GROUPS, D_GROUP)
    y, xr, xi = s5_scan(u, A_re.astype(f32), A_im.astype(f32), log_dt.astype(f32),
                        B_re.astype(f32), B_im.astype(f32), C_re.astype(f32), C_im.astype(f32),
                        x0_re.astype(f32), x0_im.astype(f32))
    y = y + D_skip.astype(f32) * su
    a = jax.nn.gelu(y)
    s = a * jax.nn.sigmoid(jnp.matmul(a, w_glu.astype(f32)) + b_glu.astype(f32))
    out = jnp.matmul(jnp.concatenate([o, s], axis=-1).astype(h.dtype), w_out)
    return out, S, xr, xi


def trunk(x, p, pos, ab_state, cd_state, prm):
    conv_s, C_s, n_s, m_s, ret_s = ab_state
    hgrn_s, s5r_s, s5i_s = cd_state
    lb_all = jnp.cumsum(jax.nn.softmax(prm['lb_logits'].astype(jnp.float32), axis=0), axis=0)
    lb_all = lb_all - lb_all[0]
    new_ab = ([], [], [], [], [])
    new_cd = ([], [], [])
    h = x
    for i in range(DEPTH):
        j = i // 2
        hn = rmsnorm(h, prm['norm_mix'][i])
        if i % 2 == 0:
            out, *st = ab_mixer(hn, pos, conv_s[j], C_s[j], n_s[j], m_s[j], ret_s[j],
                                prm['w_in_ab'][j], prm['b_gate_ab'][j], prm['conv_w_ab'][j],
                                prm['conv_b_ab'][j], prm['gn_a'][j], prm['w_out_ab'][j])
            for lst, s in zip(new_ab, st):
                lst.append(s)
        else:
            out, *st = cd_mixer(hn, lb_all[i], hgrn_s[j], s5r_s[j], s5i_s[j],
                                prm['w_in_cd'][j], prm['gn_c'][j], prm['s5_A_re'][j], prm['s5_A_im'][j],
                                prm['s5_log_dt'][j], prm['s5_B_re'][j], prm['s5_B_im'][j],
                                prm['s5_C_re'][j], prm['s5_C_im'][j], prm['s5_D'][j],
                                prm['w_glu'][j], prm['b_glu'][j], prm['w_out_cd'][j])
            for lst, s in zip(new_cd, st):
                lst.append(s)
        h = h + out
        hn = rmsnorm(h, prm['norm_ff'][i])
        h = h + jnp.matmul(jnp.square(jax.nn.relu(jnp.matmul(hn, prm['w_ff1'][i]))), prm['w_ff2'][i])
        gate = jax.nn.sigmoid(jnp.matmul(rmsnorm(h, prm['norm_ple'][i]), prm['w_ple_gate'][i]))
        h = h + gate * jnp.matmul(p[i], prm['w_ple_proj'][i])
    y = rmsnorm(h, prm['norm_final'])
    return y, [jnp.stack(l) for l in new_ab], [jnp.stack(l) for l in new_cd]


def setup_inputs(seed: int = 0) -> dict:
    key = jax.random.key(seed)
    keys = jax.random.split(key, 48)
    counter = [0]

    def nk():
        kk = keys[counter[0]]
        counter[0] += 1
        return kk

    def nrm(shape, scale):
        return scale * jax.random.normal(nk(), shape, jnp.float32)

    f32 = jnp.float32
    d = {}
    d['x_prompt'] = nrm((BATCH, SEQ, D_MODEL), 1.0)
    d['x_sample'] = nrm((DEC_BATCH, DEC_SEQ, D_MODEL), 1.0)
    d['state_mlstm_conv'] = nrm((N_AB_LAYERS, DEC_BATCH, CONV_W - 1, 2 * A_W), 1.0)
    d['state_mlstm_C'] = nrm((N_AB_LAYERS, DEC_BATCH, A_HEADS, A_DH, A_DH), 0.1)
    d['state_mlstm_n'] = nrm((N_AB_LAYERS, DEC_BATCH, A_HEADS, A_DH), 0.1)
    d['state_mlstm_m'] = nrm((N_AB_LAYERS, DEC_BATCH, A_HEADS), 1.0)
    d['state_ret'] = nrm((N_AB_LAYERS, DEC_BATCH, B_HEADS, B_DH, B_DH), 0.3)
    d['state_hgrn'] = nrm((N_CD_LAYERS, DEC_BATCH, C_HEADS, C_DK, C_DV), 0.3)
    d['state_s5_re'] = nrm((N_CD_LAYERS, DEC_BATCH, D_GROUPS, D_STATE), 0.1)
    d['state_s5_im'] = nrm((N_CD_LAYERS, DEC_BATCH, D_GROUPS, D_STATE), 0.1)
    d['p_prompt'] = nrm((DEPTH, BATCH, SEQ, PLE_DIM), 1.0)
    d['p_sample'] = nrm((DEPTH, DEC_BATCH, DEC_SEQ, PLE_DIM), 1.0)
    d['norm_mix'] = 1.0 + nrm((DEPTH, D_MODEL), 0.02)
    d['norm_ff'] = 1.0 + nrm((DEPTH, D_MODEL), 0.02)
    d['norm_ple'] = 1.0 + nrm((DEPTH, D_MODEL), 0.02)
    d['norm_final'] = 1.0 + nrm((D_MODEL,), 0.02)
    d['w_in_ab'] = nrm((N_AB_LAYERS, D_MODEL, AB_COLS), D_MODEL ** -0.5)
    f_bias = jnp.linspace(3.0, 6.0, A_HEADS, dtype=f32)
    d['b_gate_ab'] = jnp.concatenate([nrm((N_AB_LAYERS, A_HEADS), 0.1),
                                      f_bias + nrm((N_AB_LAYERS, A_HEADS), 0.1)], axis=-1)
    d['conv_w_ab'] = nrm((N_AB_LAYERS, CONV_W, 2 * A_W), CONV_W ** -0.5)
    d['conv_b_ab'] = nrm((N_AB_LAYERS, 2 * A_W), 0.02)
    d['gn_a'] = 1.0 + nrm((N_AB_LAYERS, A_W), 0.02)
    d['w_out_ab'] = nrm((N_AB_LAYERS, D_MODEL, D_MODEL), 0.5 * D_MODEL ** -0.5)
    d['w_in_cd'] = nrm((N_CD_LAYERS, D_MODEL, CD_COLS), D_MODEL ** -0.5)
    d['lb_logits'] = nrm((DEPTH, C_KW), 0.1)
    d['gn_c'] = 1.0 + nrm((N_CD_LAYERS, C_W), 0.02)
    d['s5_A_re'] = -0.5 + nrm((N_CD_LAYERS, D_GROUPS, D_STATE), 0.01)
    d['s5_A_im'] = math.pi * jnp.arange(D_STATE, dtype=f32) + nrm((N_CD_LAYERS, D_GROUPS, D_STATE), 0.01)
    d['s5_log_dt'] = jax.random.uniform(nk(), (N_CD_LAYERS, D_GROUPS), f32, math.log(1e-3), math.log(1e-1))
    d['s5_B_re'] = nrm((N_CD_LAYERS, D_GROUPS, D_STATE, D_GROUP), (2 * D_GROUP) ** -0.5)
    d['s5_B_im'] = nrm((N_CD_LAYERS, D_GROUPS, D_STATE, D_GROUP), (2 * D_GROUP) ** -0.5)
    d['s5_C_re'] = nrm((N_CD_LAYERS, D_GROUPS, D_GROUP, D_STATE), (2 * D_STATE) ** -0.5)
    d['s5_C_im'] = nrm((N_CD_LAYERS, D_GROUPS, D_GROUP, D_STATE), (2 * D_STATE) ** -0.5)
    d['s5_D'] = nrm((N_CD_LAYERS, D_W), 1.0)
    d['w_glu'] = nrm((N_CD_LAYERS, D_W, D_W), D_W ** -0.5)
    d['b_glu'] = nrm((N_CD_LAYERS, D_W), 0.02)
    d['w_out_cd'] = nrm((N_CD_LAYERS, D_MODEL, D_MODEL), 0.5 * D_MODEL ** -0.5)
    d['w_ff1'] = nrm((DEPTH, D_MODEL, D_FF), D_MODEL ** -0.5)
    d['w_ff2'] = nrm((DEPTH, D_FF, D_MODEL), 0.5 * D_FF ** -0.5)
    d['w_ple_proj'] = nrm((DEPTH, PLE_DIM, D_MODEL), PLE_DIM ** -0.5)
    d['w_ple_gate'] = nrm((DEPTH, D_MODEL, D_MODEL), D_MODEL ** -0.5)
    return d


def reference(x_prompt, x_sample, state_mlstm_conv, state_mlstm_C, state_mlstm_n, state_mlstm_m,
              state_ret, state_hgrn, state_s5_re, state_s5_im, p_prompt, p_sample,
              norm_mix, norm_ff, norm_ple, norm_final, w_in_ab, b_gate_ab, conv_w_ab, conv_b_ab, gn_a,
              w_out_ab, w_in_cd, lb_logits, gn_c, s5_A_re, s5_A_im, s5_log_dt, s5_B_re, s5_B_im,
              s5_C_re, s5_C_im, s5_D, w_glu, b_glu, w_out_cd, w_ff1, w_ff2, w_ple_proj, w_ple_gate):
    f32 = jnp.float32
    prm = dict(norm_mix=norm_mix, norm_ff=norm_ff, norm_ple=norm_ple, norm_final=norm_final,
               w_in_ab=w_in_ab, b_gate_ab=b_gate_ab, conv_w_ab=conv_w_ab, conv_b_ab=conv_b_ab,
               gn_a=gn_a, w_out_ab=w_out_ab, w_in_cd=w_in_cd, lb_logits=lb_logits, gn_c=gn_c,
               s5_A_re=s5_A_re, s5_A_im=s5_A_im, s5_log_dt=s5_log_dt, s5_B_re=s5_B_re,
               s5_B_im=s5_B_im, s5_C_re=s5_C_re, s5_C_im=s5_C_im, s5_D=s5_D, w_glu=w_glu,
               b_glu=b_glu, w_out_cd=w_out_cd, w_ff1=w_ff1, w_ff2=w_ff2,
               w_ple_proj=w_ple_proj, w_ple_gate=w_ple_gate)
    bp, lp, _ = x_prompt.shape
    zero_ab = (jnp.zeros((N_AB_LAYERS, bp, CONV_W - 1, 2 * A_W), f32),
               jnp.zeros((N_AB_LAYERS, bp, A_HEADS, A_DH, A_DH), f32),
               jnp.zeros((N_AB_LAYERS, bp, A_HEADS, A_DH), f32),
               jnp.zeros((N_AB_LAYERS, bp, A_HEADS), f32),
               jnp.zeros((N_AB_LAYERS, bp, B_HEADS, B_DH, B_DH), f32))
    zero_cd = (jnp.zeros((N_CD_LAYERS, bp, C_HEADS, C_DK, C_DV), f32),
               jnp.zeros((N_CD_LAYERS, bp, D_GROUPS, D_STATE), f32),
               jnp.zeros((N_CD_LAYERS, bp, D_GROUPS, D_STATE), f32))
    y_prompt, ab_p, cd_p = trunk(x_prompt, p_prompt, jnp.arange(lp), zero_ab, zero_cd, prm)
    ls = x_sample.shape[1]
    y_sample, ab_s, cd_s = trunk(
        x_sample, p_sample, PAST_LEN + jnp.arange(ls),
        (state_mlstm_conv, state_mlstm_C, state_mlstm_n, state_mlstm_m, state_ret),
        (state_hgrn, state_s5_re, state_s5_im), prm)
    return (y_prompt, y_sample,
            ab_p[0], ab_s[0], ab_p[1], ab_s[1], ab_p[2], ab_s[2], ab_p[3], ab_s[3], ab_p[4], ab_s[4],
            cd_p[0], cd_s[0], cd_p[1], cd_s[1], cd_p[2], cd_s[2])
```

```python
import math, contextlib, os
import numpy as np
import concourse.bass as bass
import concourse.mybir as mybir
from concourse.bass_utils import run_bass_kernel_spmd

F32 = mybir.dt.float32
BF16 = mybir.dt.bfloat16
AF = mybir.ActivationFunctionType
ALU = mybir.AluOpType

NTM = 768
FW = 776
SBS = [(0, 768, False), (768, 768, False), (1536, 512, True)]
NTOK = 2176
EPS = 1e-6
PI = math.pi
LG = [math.log1p(-2.0 ** (-5.0 - h)) for h in range(4)]
LNK = -0.5 * math.log(128.0)


class Buf:
    __slots__ = ("w", "r")

    def __init__(self):
        self.w = None
        self.r = []


class _Rec:
    def __init__(self):
        self.call = None

    def __getattr__(self, name):
        def f(*a, **k):
            self.call = (name, a, k)
            return self
        return f


def _record(fn):
    r = _Rec()
    fn(r)
    assert r.call is not None
    return r.call


class Sched:
    ENGS = ("pe", "act", "dve", "pool", "sp")

    def __init__(self, nc):
        self.nc = nc
        self.ops = {e: [] for e in self.ENGS}
        self.cnt = {e: 0 for e in self.ENGS}
        self.seen = {e: {} for e in self.ENGS}
        self.sems = {}
        self.dma_cnt = {}

    def new_dma_sem(self):
        k = "dma%d" % len(self.dma_cnt)
        self.dma_cnt[k] = 0
        return k

    def _deps(self, eng, reads, writes, is_dma):
        waits = {}

        def add(ev, kind):
            key, val, src_eng, src_dma = ev
            if (not src_dma) and (not is_dma) and src_eng == eng and eng == "pe":
                return
            if self.seen[eng].get(key, 0) >= val:
                return
            if waits.get(key, 0) < val:
                waits[key] = val
        for b in reads:
            if b.w is not None:
                add(b.w, "raw")
        for b in writes:
            if b.w is not None:
                add(b.w, "waw")
            for r in b.r:
                add(r, "war")
        for k, v in waits.items():
            self.seen[eng][k] = v
        return list(waits.items())

    def _post(self, ev, reads, writes):
        for b in writes:
            b.w = ev
            b.r = []
        for b in reads:
            if b.w is not ev:
                b.r.append(ev)
                if len(b.r) > 24:
                    b.r = b.r[-24:] if False else b.r

    def op(self, eng, fn, reads=(), writes=()):
        waits = self._deps(eng, reads, writes, False)
        self.cnt[eng] += 1
        ev = ("e_" + eng, self.cnt[eng], eng, False)
        self.ops[eng].append((waits, _record(fn), ("e_" + eng, 1)))
        self._post(ev, reads, writes)

    def dma(self, eng, fn, sem, reads=(), writes=()):
        waits = self._deps(eng, reads, writes, True)
        prev = self.dma_cnt[sem]
        if prev > 0 and self.seen[eng].get(sem, 0) < prev:
            waits = [w_ for w_ in waits if w_[0] != sem] + [(sem, prev)]
            self.seen[eng][sem] = prev
        self.dma_cnt[sem] += 16
        ev = (sem, self.dma_cnt[sem], eng, True)
        self.ops[eng].append((waits, _record(fn), (sem, 16)))
        self._post(ev, reads, writes)

    def final_wait(self, eng, bufs):
        waits = self._deps(eng, bufs, bufs, True)
        have = dict(waits)
        for k, v in self.dma_cnt.items():
            if v > 0 and self.seen[eng].get(k, 0) < v and have.get(k, 0) < v:
                have[k] = v
        for e2 in self.ENGS:
            if e2 != eng and self.cnt[e2] > 0:
                have["e_" + e2] = self.cnt[e2]
        self.ops[eng].append((list(have.items()), None, None))

    def emit(self, stack):
        nc = self.nc
        keys = ["e_" + e for e in self.ENGS] + list(self.dma_cnt.keys())
        for k in keys:
            self.sems[k] = stack.enter_context(nc.semaphore(k))
        block = stack.enter_context(nc.Block())
        engobj = {"pe": "tensor", "act": "scalar", "dve": "vector", "pool": "gpsimd", "sp": "sync"}

        def mk(e):
            def body(engine):
                for (waits, fn, inc) in self.ops[e]:
                    for (k, v) in waits:
                        engine.wait_ge(self.sems[k], v)
                    if fn is not None:
                        name, a, k = fn
                        getattr(engine, name)(*a, **k).then_inc(self.sems[inc[0]], inc[1])
            return body
        for e in self.ENGS:
            if self.ops[e]:
                getattr(block, engobj[e])(mk(e))


PV = {}
_o = 0
for _n, _w in [("nmix0", 8), ("nmix1", 8), ("nff0", 8), ("nff1", 8), ("nple0", 8), ("nple1", 8), ("nfin", 8),
               ("cw0", 8), ("cw1", 8), ("cw2", 8), ("cw3", 8), ("cb", 8), ("gna", 4), ("gnc", 4), ("s5d", 4),
               ("bglu", 4), ("lb0", 4), ("lb1", 4), ("are", 16), ("aim", 16), ("ldt", 16), ("invf", 1),
               ("sgn", 1), ("pidx", 1), ("pidxs", 1), ("rowm", 16), ("s0", 1), ("s8", 1)]:
    PV[_n] = (_o, _w)
    _o += _w
NPV = _o


def _cols(v):
    return np.ascontiguousarray(np.asarray(v, np.float32).reshape(-1, 128).T)


def build_program():
    nc = bass.Bass("TRN2", target_bir_lowering=False)
    D = {}

    def din(name, shape):
        D[name] = nc.dram_tensor(name, list(shape), F32, kind="ExternalInput").ap()
        return D[name]

    def dout(name, shape):
        D[name] = nc.dram_tensor(name, list(shape), F32, kind="ExternalOutput").ap()
        return D[name]
    din("xT", [1024, NTOK]); din("pT", [2, 256, NTOK])
    din("convs", [1024, 16, 3]); din("Us", [16, 4, 128, 129]); din("ms", [4, 16])
    din("rets", [16, 4, 128, 128]); din("hgrns", [16, 4, 128, 128])
    din("x0re", [128, 16, 16]); din("x0im", [128, 16, 16])
    din("w_ab_h", [1024, 4096]); din("w_cd_h", [1024, 2048]); din("w_in_ab", [1024, 4104]); din("wg", [1024, 8]); din("w_out_ab", [1024, 1024])
    din("w_in_cd", [1024, 2560]); din("w_glu", [512, 512]); din("w_out_cd", [1024, 1024])
    din("w_ff1", [2, 1024, 4096]); din("w_ff2", [2, 4096, 1024])
    din("w_ple_proj", [2, 256, 1024]); din("w_ple_gate", [2, 1024, 1024])
    din("pvec", [128, NPV]); din("bg", [4, 2])
    din("BT", [2, 16, 128, 128]); din("CT", [2, 16, 128, 128])
    din("ident", [128, 128]); din("maskc", [128, 128]); din("maskb", [128, 128])
    din("blk3", [128, 16, 128]); din("segm", [3, 128, NTM]); din("negm", [4, 128]); din("sel", [4, 4, 128])
    din("posrow", [3, 128, NTM]); din("rowp", [3, 128, 2048]); din("jrow", [128, 4, 128])
    dout("yT", [1024, NTOK]); dout("convp", [1024, 3]); dout("convs_o", [1024, 16, 3])
    dout("Up", [4, 128, 129]); dout("Us_o", [16, 4, 128, 129]); dout("mp", [4, 1]); dout("ms_o", [4, 16])
    dout("retp", [4, 128, 128]); dout("rets_o", [16, 4, 128, 128])
    dout("hgrnp", [4, 128, 128]); dout("hgrns_o", [16, 4, 128, 128])
    dout("s5rep", [128, 16]); dout("s5imp", [128, 16]); dout("s5res", [128, 16, 16]); dout("s5ims", [128, 16, 16])

    st = contextlib.ExitStack()
    with st:
        S = Sched(nc)
        cnt = [0]

        def sb(shape, dt=F32):
            cnt[0] += 1
            return st.enter_context(nc.sbuf_tensor("t%d" % cnt[0], list(shape), dt))

        def psum(shape, dt=F32):
            cnt[0] += 1
            return st.enter_context(nc.psum_tensor("p%d" % cnt[0], list(shape), dt))
        V = lambda fn, r=(), w=(): S.op("dve", fn, r, w)
        A = lambda fn, r=(), w=(): S.op("act", fn, r, w)
        G = lambda fn, r=(), w=(): S.op("pool", fn, r, w)
        P = lambda fn, r=(), w=(): S.op("pe", fn, r, w)
        msems = {"sp": [S.new_dma_sem() for _ in range(24)], "pool": [S.new_dma_sem() for _ in range(8)]}
        mi = {"sp": 0, "pool": 0}

        def LD(out, in_, w, r=(), eng="sp"):
            k = msems[eng][mi[eng] % len(msems[eng])]
            mi[eng] += 1
            S.dma(eng, lambda e: e.dma_start(out=out, in_=in_), k, r, w)
        OUTS = []

        def STO(out, in_, r):
            b_ = Buf()
            OUTS.append(b_)
            LD(out, in_, [b_], r)

        h = sb([128, 8, NTM]); hn = sb([128, 8, NTM], BF16); mix = sb([128, 8, NTM], BF16)
        Bh = [[Buf() for _ in range(2)] for _ in range(8)]
        Bhn = [Buf() for _ in range(2)]
        Bmix = [[Buf() for _ in range(2)] for _ in range(8)]
        NW = 2
        wr = [sb([128, 8, 512], BF16) for _ in range(NW)]
        Bwrp = [[Buf() for _ in range(4)] for _ in range(NW)]
        wsem = [[S.new_dma_sem() for _ in range(4)] for _ in range(NW)]
        wi = [0]

        def wload(parts, nk):
            i = wi[0] % NW
            wi[0] += 1
            for pi_, (ap, co) in enumerate(parts):
                ncol = ap.shape[1]
                S.dma("pool", lambda e, ap=ap, co=co, ncol=ncol, i=i: e.dma_start(
                    out=wr[i][:, 0:nk, co:co + ncol], in_=ap.rearrange("(k p) n -> p k n", p=128)),
                    wsem[i][pi_], (), (Bwrp[i] if pi_ == 0 else [Bwrp[i][pi_]]))
            return wr[i], Bwrp[i]
        Fs = [sb([128, FW]) for _ in range(9)]
        BF = [Buf() for _ in range(9)]
        Hs = [sb([128, NTM], BF16) for _ in range(5)]
        BH = [Buf() for _ in range(5)]
        vtm = sb([128, 6, 129], BF16); Bv = Buf()
        NT5 = [sb([128, 512]) for _ in range(5)]
        BN = [Buf() for _ in range(5)]
        sqb = [sb([128, 512], BF16) for _ in range(2)]
        Bsq = [Buf() for _ in range(2)]
        pb = [psum([128, 512]) for _ in range(7)]
        Bp = [Buf() for _ in range(7)]
        ptb = psum([128, 1024], BF16); Bpt = Buf()
        pbi = [0]

        pinned = set()

        def PS():
            while True:
                i = pbi[0] % 4
                pbi[0] += 1
                if i not in pinned:
                    return pb[i], Bp[i]
        psi = [0]

        def PSS():
            i = 4 + psi[0] % 3
            psi[0] += 1
            return pb[i], Bp[i]
        ident = sb([128, 128]); identb = sb([128, 128], BF16); maskc = sb([128, 128], BF16); maskb = sb([128, 128], BF16)
        onesb = sb([128, 128], BF16); blk3 = sb([128, 16, 128], BF16); segm = sb([128, NTM]); negm = sb([4, 128])
        sel = sb([4, 4, 128]); pvec = sb([128, NPV]); bg = sb([4, 2]); nbg = sb([4, 1]); jrow = sb([128, 4, 128])
        Bc = Buf()
        LD(ident[:], D["ident"], [Bc]); LD(identb[:], D["ident"], [Bc], eng="pool")
        LD(maskc[:], D["maskc"], [Bc], eng="pool"); LD(maskb[:], D["maskb"], [Bc], eng="pool")
        LD(blk3[:], D["blk3"], [Bc], eng="pool"); LD(negm[:], D["negm"], [Bc]); LD(sel[:], D["sel"], [Bc])
        LD(pvec[:], D["pvec"], [Bc]); LD(bg[:], D["bg"], [Bc]); LD(jrow[:], D["jrow"], [Bc])
        V(lambda e: e.memset(onesb[:], 1.0), (), [Bc])
        V(lambda e: e.tensor_scalar(nbg[:], bg[:, 1:2], -1.0, None, ALU.mult), [Bc], [Bc])

        cb_ = sb([128, 8])
        CBV = [EPS, LNK, 1.0, 0.0, 0.5 * PI, 0.0, 0.0, 0.0]
        for _i, _v in enumerate(CBV):
            V(lambda e, _i=_i, _v=_v: e.memset(cb_[:, _i:_i + 1], _v), (), [Bc])
        CEPS, CLNK, CONE, CZERO, CHPI = [cb_[:, i:i + 1] for i in range(5)]
        RC = 12582912.0
        I2P = 1.0 / (2 * PI)

        def sin_of(dst, src, shift, tmp, rd, wr_, btmp, npart=128):
            V(lambda e: e.tensor_scalar(tmp, src, shift, I2P, ALU.add, ALU.mult), rd, [btmp])
            V(lambda e: e.tensor_scalar(tmp, tmp, RC, None, ALU.add), [btmp], [btmp])
            V(lambda e: e.tensor_scalar(tmp, tmp, -RC, None, ALU.add), [btmp], [btmp])
            V(lambda e: e.scalar_tensor_tensor(tmp, tmp, -2 * PI, src, ALU.mult, ALU.add), [btmp] + list(rd), [btmp])
            V(lambda e: e.tensor_scalar(tmp, tmp, -PI - shift + 4e-6, PI - shift - 4e-6, ALU.max, ALU.min), [btmp], [btmp])
            A(lambda e: e.activation(dst, tmp, AF.Sin, bias=(CHPI[0:npart] if shift != 0.0 else CZERO[0:npart])), [btmp, Bc], wr_)

        def pv(name, j=0, n=1):
            o, w = PV[name]
            return pvec[:, o + j:o + j + n]
        Gq = sb([128, 2, 4, 128], BF16); gk = sb([128, 2, 4])
        for v2 in range(2):
            for hh in range(4):
                A(lambda e, v2=v2, hh=hh: e.activation(Gq[:, v2, hh, :], jrow[:, v2, :], AF.Exp, scale=LG[hh]), [Bc], [Bc])
                A(lambda e, v2=v2, hh=hh: e.activation(gk[:, v2, hh:hh + 1], pv("pidxs" if v2 else "pidx"), AF.Exp,
                                                       scale=-LG[hh], bias=CLNK), [Bc], [Bc])
        lb = sb([128, 4]); oml = sb([128, 4])
        V(lambda e: e.tensor_tensor(lb[:], pv("lb1", 0, 4), pv("lb0", 0, 4), ALU.subtract), [Bc], [Bc])
        A(lambda e: e.activation(lb[:], lb[:], AF.Sigmoid), [Bc], [Bc])
        V(lambda e: e.tensor_scalar(oml[:], lb[:], -1.0, 1.0, ALU.mult, ALU.add), [Bc], [Bc])
        s5p = sb([128, 16, 16])
        th, rr, zr, zi, rho = s5p[:, 0, :], s5p[:, 1, :], s5p[:, 2, :], s5p[:, 3, :], s5p[:, 7, :]
        t4, t5, t6 = s5p[:, 4, :], s5p[:, 5, :], s5p[:, 6, :]
        ar_, ai_, a128r, a128i, izr, izi, t7 = (s5p[:, 8, :], s5p[:, 9, :], s5p[:, 10, :], s5p[:, 11, :], s5p[:, 12, :],
                                               s5p[:, 13, :], s5p[:, 14, :])
        are, aim = pv("are", 0, 16), pv("aim", 0, 16)
        A(lambda e: e.activation(t4, pv("ldt", 0, 16), AF.Exp), [Bc], [Bc])
        V(lambda e: e.tensor_tensor(th, t4, aim, ALU.mult), [Bc], [Bc])
        V(lambda e: e.tensor_tensor(rho, t4, are, ALU.mult), [Bc], [Bc])
        A(lambda e: e.activation(rr, rho, AF.Exp), [Bc], [Bc])
        sin_of(t4, th, 0.5 * PI, t6, [Bc], [Bc], Bc)
        sin_of(t5, th, 0.0, t6, [Bc], [Bc], Bc)
        V(lambda e: e.tensor_tensor(ar_, t4, rr, ALU.mult), [Bc], [Bc])
        V(lambda e: e.tensor_tensor(ai_, t5, rr, ALU.mult), [Bc], [Bc])
        V(lambda e: e.tensor_scalar(t4, ar_, -1.0, None, ALU.add), [Bc], [Bc])
        V(lambda e: e.tensor_copy(t5, ai_), [Bc], [Bc])
        V(lambda e: e.tensor_tensor(t6, are, are, ALU.mult), [Bc], [Bc])
        V(lambda e: e.tensor_tensor(zr, aim, aim, ALU.mult), [Bc], [Bc])
        V(lambda e: e.tensor_tensor(t6, t6, zr, ALU.add), [Bc], [Bc])
        V(lambda e: e.reciprocal(t6, t6), [Bc], [Bc])
        V(lambda e: e.tensor_tensor(zr, t4, are, ALU.mult), [Bc], [Bc])
        V(lambda e: e.tensor_tensor(zi, t5, aim, ALU.mult), [Bc], [Bc])
        V(lambda e: e.tensor_tensor(zr, zr, zi, ALU.add), [Bc], [Bc])
        V(lambda e: e.tensor_tensor(zi, t5, are, ALU.mult), [Bc], [Bc])
        V(lambda e: e.tensor_tensor(t7, t4, aim, ALU.mult), [Bc], [Bc])
        V(lambda e: e.tensor_tensor(zi, zi, t7, ALU.subtract), [Bc], [Bc])
        V(lambda e: e.tensor_tensor(zr, zr, t6, ALU.mult), [Bc], [Bc])
        V(lambda e: e.tensor_tensor(zi, zi, t6, ALU.mult), [Bc], [Bc])
        V(lambda e: e.tensor_tensor(t4, zr, zr, ALU.mult), [Bc], [Bc])
        V(lambda e: e.tensor_tensor(t5, zi, zi, ALU.mult), [Bc], [Bc])
        V(lambda e: e.tensor_tensor(t4, t4, t5, ALU.add), [Bc], [Bc])
        V(lambda e: e.reciprocal(t4, t4), [Bc], [Bc])
        V(lambda e: e.tensor_tensor(izr, zr, t4, ALU.mult), [Bc], [Bc])
        V(lambda e: e.scalar_tensor_tensor(izi, zi, -1.0, t4, ALU.mult, ALU.mult), [Bc], [Bc])
        V(lambda e: e.tensor_scalar(t7, th, 128.0, None, ALU.mult), [Bc], [Bc])
        sin_of(t4, t7, 0.5 * PI, t6, [Bc], [Bc], Bc)
        sin_of(t5, t7, 0.0, t6, [Bc], [Bc], Bc)
        A(lambda e: e.activation(t6, rho, AF.Exp, scale=128.0), [Bc], [Bc])
        V(lambda e: e.tensor_tensor(a128r, t4, t6, ALU.mult), [Bc], [Bc])
        V(lambda e: e.tensor_tensor(a128i, t5, t6, ALU.mult), [Bc], [Bc])
        tabE = nc.dram_tensor("tabE", [128, 2, 2048], BF16).ap(); tabZ = nc.dram_tensor("tabZ", [128, 2, 16, 128], BF16).ap()
        tbe = sb([128, 2, 512], BF16); tbz = sb([128, 2, 4, 128], BF16); Btab = Buf(); Bscr = Buf()

        def build_tables(kc, scol, jr, outE, outZ, wE, wZ):
            f0, f1, f2, f3, f4, f5 = [Fs[k][:, 0:512] for k in range(6)]
            b0_, b1_, b2_, b3_, b4_, b5_ = BF[0:6]
            for k in range(3):
                LD(Fs[k][:, 0:512], D["rowp"][k][:, kc * 512:(kc + 1) * 512], [BF[k]])
            A(lambda e: e.activation(f2, f2, AF.Exp), [b2_], [b2_])
            V(lambda e: e.tensor_tensor(f1, f1, f2, ALU.mult), [b1_, b2_], [b1_])
            V(lambda e: e.tensor_tensor(f0, f0, f2, ALU.mult), [b0_, b2_], [b0_])
            V(lambda e: e.tensor_scalar(f1, f1, scol, None, ALU.mult), [b1_, Bc], [b1_])
            A(lambda e: e.activation(f0, f0, AF.Exp, scale=scol), [b0_, Bc], [b0_])
            V(lambda e: e.reciprocal(f0, f0), [b0_], [b0_])
            sin_of(f3, f1, 0.5 * PI, f2, [b1_], [b3_], b2_)
            sin_of(f4, f1, 0.0, f2, [b1_], [b4_], b2_)
            V(lambda e: e.tensor_tensor(outE(0), f3, f0, ALU.mult), [b3_, b0_], wE)
            V(lambda e: e.scalar_tensor_tensor(outE(1), f4, -1.0, f0, ALU.mult, ALU.mult), [b4_, b0_], wE)
            g0, g1, g2, g3, g4 = [Fs[k][:, 0:512].rearrange("p (a b) -> p a b", a=4) for k in range(5)]
            i4 = slice(4 * kc, 4 * kc + 4)
            jb = jr.unsqueeze(1).broadcast_to([128, 4, 128])
            bc4 = lambda v: v[:, i4].unsqueeze(2).broadcast_to([128, 4, 128])
            V(lambda e: e.tensor_tensor(g1, jb, bc4(th), ALU.mult), [Bc], [b1_])
            V(lambda e: e.tensor_tensor(g0, jb, bc4(rho), ALU.mult), [Bc], [b0_])
            A(lambda e: e.activation(Fs[0][:, 0:512], Fs[0][:, 0:512], AF.Exp), [b0_], [b0_])
            sin_of(f3, f1, 0.5 * PI, f2, [b1_], [b3_], b2_)
            sin_of(f4, f1, 0.0, f2, [b1_], [b4_], b2_)
            V(lambda e: e.tensor_tensor(f3, f3, f0, ALU.mult), [b3_, b0_], [b3_])
            V(lambda e: e.tensor_tensor(f4, f4, f0, ALU.mult), [b4_, b0_], [b4_])
            V(lambda e: e.tensor_tensor(g0, g3, bc4(zr), ALU.mult), [b3_, Bc], [b0_])
            V(lambda e: e.tensor_tensor(g1, g4, bc4(zi), ALU.mult), [b4_, Bc], [b1_])
            V(lambda e: e.tensor_tensor(outZ(0), g0, g1, ALU.subtract), [b0_, b1_], wZ)
            V(lambda e: e.tensor_tensor(g0, g3, bc4(zi), ALU.mult), [b3_, Bc], [b0_])
            V(lambda e: e.tensor_tensor(g1, g4, bc4(zr), ALU.mult), [b4_, Bc], [b1_])
            V(lambda e: e.tensor_tensor(outZ(1), g0, g1, ALU.add), [b0_, b1_], wZ)
        Up = sb([128, 12, 129]); Upb = sb([128, 12, 129], BF16); nbc = sb([128, 4, 128], BF16)
        BU = [Buf() for _ in range(12)]
        V(lambda e: e.memset(Up[:], 0.0), (), BU); V(lambda e: e.memset(Upb[:], 0.0), (), BU)
        V(lambda e: e.memset(nbc[:], 0.0), (), BU)
        tails = sb([128, 8, 3]); Btl = Buf()
        V(lambda e: e.memset(tails[:], 0.0), (), [Btl])
        carr = sb([4, 2]); Bcar = Buf()
        V(lambda e: e.memset(carr[:], 0.0), (), [Bcar])
        s5c = sb([128, 2, 16]); Bs5c = Buf()
        V(lambda e: e.memset(s5c[:], 0.0), (), [Bs5c])
        Usf = sb([128, 16, 129]); Usb = sb([128, 16, 129], BF16); BUs = Buf()
        qz = sb([128, 16, 128], BF16); kz = sb([128, 16, 128], BF16); nbs = kz
        Bqz = Buf(); Bkz = Buf(); Bnbs = Bkz
        stb = [sb([128, 128], BF16) for _ in range(2)]; Bst = [Buf() for _ in range(2)]
        khb = [sb([128, 128], BF16) for _ in range(2)]; Bkh = [Buf() for _ in range(2)]
        ektm = sb([128, 6, 4]); Bek = Buf()
        decbc = sb([128, 4, 24]); Bdec = Buf()
        decrow = sb([4, 24]); mxe = sb([4, 8]); ms0 = sb([4, 16]); msout = sb([4, 17]); Bsm = Buf()
        x0s = Fs[5][:, 0:512].rearrange("p (a b c) -> p a b c", a=2, b=16); Bx0 = BF[5]
        s5so = Usf[:].rearrange("p a b -> p (a b)")[:, 0:512].rearrange("p (a b c) -> p a b c", a=2, b=16); s5po = sb([128, 2, 16]); Bs5o = Buf()
        Bes = Buf()
        cbs = sb([128, 2, 4, 128], BF16); Bcbs = Buf()
        pt4all = sb([128, 2048], BF16)
        pt4 = [pt4all[:, q_ * 512:(q_ + 1) * 512] for q_ in range(4)]
        gT2 = pt4all[:, 0:NTM]; vtm2 = pt4all[:, NTM:NTM + 774].rearrange("p (a b) -> p a b", a=6); Bg2 = Buf(); Bv2 = Buf()
        hdec = sb([128, 2, 24]); Bhd = Buf()
        Bprb = [Buf() for _ in range(2)]; Bt4 = [Buf() for _ in range(4)]; Bcsb = [Buf() for _ in range(2)]; Bp4 = [Buf() for _ in range(4)]
        S5FINE = Bprb + Bt4 + Bcsb + Bp4
        qzf = qz[:].rearrange("p a b -> p (a b)"); kzf = kz[:].rearrange("p a b -> p (a b)"); usbf = Usb[:].rearrange("p a b -> p (a b)")
        wtb = [[sb([128, 512], BF16) for _ in range(2)] for _ in range(2)]; Bwt = [Buf() for _ in range(2)]
        xtb = [[sb([128, 512], BF16) for _ in range(2)] for _ in range(2)]; Bxt = [Buf() for _ in range(2)]
        cch = sb([128, 6, 4]); Bcch = Buf()
        bct = sb([128, 2, 2, 4, 128], BF16); Bbct = [Buf() for _ in range(4)]; bcsem = [S.new_dma_sem() for _ in range(4)]
        pTb = sb([128, 2, NTM], BF16); BpT = Buf(); pTsem = S.new_dma_sem()

        def blocks(ntot):
            out = []
            o = 0
            while o < ntot:
                n = min(512, ntot - o)
                out.append((o, n)); o += n
            return out

        def rmsnorm(gname, NT, final_out=None):
            for bi, (o, n) in enumerate(blocks(NT)):
                ps, bp = PS()
                for c in range(8):
                    q = sqb[c % 2]; bq = Bsq[c % 2]
                    A(lambda e, c=c, q=q: e.activation(q[:, 0:n], h[:, c, o:o + n], AF.Square), [Bh[c][bi]], [bq])
                    P(lambda e, c=c, q=q, ps=ps: e.matmul(ps[:, 0:n], onesb[:], q[:, 0:n], start=(c == 0), stop=(c == 7)),
                      [bq, Bc], [bp])
                rs = NT5[4]
                A(lambda e, ps=ps: e.activation(rs[:, 0:n], ps[:, 0:n], AF.Ln, scale=1.0 / 1024, bias=CEPS), [bp], [BN[4]])
                A(lambda e: e.activation(rs[:, 0:n], rs[:, 0:n], AF.Exp, scale=-0.5), [BN[4]], [BN[4]])
                for c in range(8):
                    if final_out is None:
                        V(lambda e, c=c: e.scalar_tensor_tensor(hn[:, c, o:o + n], h[:, c, o:o + n], pv(gname, c), rs[:, 0:n],
                                                                ALU.mult, ALU.mult), [Bh[c][bi], BN[4], Bc], [Bhn[bi]])
                    else:
                        t = NT5[c % 2]
                        V(lambda e, c=c, t=t: e.scalar_tensor_tensor(t[:, 0:n], h[:, c, o:o + n], pv(gname, c), rs[:, 0:n],
                                                                     ALU.mult, ALU.mult), [Bh[c][bi], BN[4], Bc], [BN[c % 2]])
                        STO(final_out[c * 128:(c + 1) * 128, o:o + n], t[:, 0:n], [BN[c % 2]])

        def proj_fm(slot, bs, col, NT, evac, rhs=None, nk=8, rbufs=None):
            for bi, (o, n) in enumerate(blocks(NT)):
                ps, bp = PS()
                for k in range(nk):
                    src = hn if rhs is None else rhs
                    P(lambda e, k=k, ps=ps, src=src: e.matmul(ps[:, 0:n], slot[:, k, col:col + 128], src[:, k, o:o + n],
                                                             start=(k == 0), stop=(k == nk - 1)),
                      bs + ([Bhn[bi]] if rbufs is None else rbufs(bi)), [bp])
                evac(ps, bp, bi, o, n)

        def resid_proj(w_ap, NT, src, srcb):
            for u in range(2):
                slot, bs = wload([(w_ap[:, u * 512:(u + 1) * 512], 0)], 8)
                for oc in range(4):
                    c = u * 4 + oc

                    def ev(ps, bp, bi, o, n, c=c):
                        V(lambda e: e.tensor_tensor(h[:, c, o:o + n], h[:, c, o:o + n], ps[:, 0:n], ALU.add),
                          [bp, Bh[c][bi]], [Bh[c][bi]])
                    proj_fm(slot, bs, oc * 128, NT, ev, rhs=src, rbufs=lambda bi: [srcb[k][bi] for k in range(8)])

        def att_tile(qT, kT, bq, bk, col, vt, E, si, ek, dec, PT, bPT, pcol, sample, den=None, mlstm_h=None, usbuf=None, bv=None):
            Bv = bv
            ps, bp = PSS()
            P(lambda e: e.matmul(ps[:, 0:128], kT[:, col:col + 128], qT[:, col:col + 128], start=True, stop=True),
              [bq, bk], [bp])
            i2 = att_tile.k % 2
            att_tile.k += 1
            sT = stb[i2]; bsT = Bst[i2]
            msk = maskb if sample else maskc
            if ek is not None:
                V(lambda e: e.scalar_tensor_tensor(sT[:], ps[:, 0:128], ek, msk[:], ALU.mult, ALU.mult), [bp, Bek, Bc], [bsT])
            else:
                V(lambda e: e.tensor_tensor(sT[:], ps[:, 0:128], msk[:], ALU.mult), [bp, Bc], [bsT])
            P(lambda e: e.matmul(PT[:, pcol:pcol + 128], vt[:, 0:128], sT[:], start=True, stop=False), [Bv, bsT], [bPT])
            if not sample:
                P(lambda e: e.matmul(PT[:, pcol:pcol + 128], Upb[:, si, 0:128], qT[:, col:col + 128], start=False, stop=True),
                  [BU[si], bq], [bPT])
            else:
                for j in range(16):
                    P(lambda e, j=j: e.matmul(PT[:, pcol:pcol + 128], Usb[:, j, 0:128], qz[:, j, :], start=False, stop=(j == 15)),
                      [BUs, Bqz], [bPT])
            if den is not None:
                dps, bd = den
                P(lambda e: e.matmul(dps[:, pcol:pcol + 128], onesb[:], sT[:], start=True, stop=False), [Bc, bsT], [bd])
                if not sample:
                    P(lambda e: e.matmul(dps[:, pcol:pcol + 128], nbc[:, mlstm_h, :], qT[:, col:col + 128], start=False, stop=True),
                      [BU[si], bq], [bd])
                else:
                    for j in range(16):
                        P(lambda e, j=j: e.matmul(dps[:, pcol:pcol + 128], nbs[:, j, :], qz[:, j, :], start=False, stop=(j == 15)),
                          [Bnbs, Bqz], [bd])
            P(lambda e: e.transpose(ptb[:, 0:128], kT[:, col:col + 128], identb[:]), [bk, Bc], [Bpt])
            kh = khb[i2]; bkh = Bkh[i2]
            if ek is not None:
                A(lambda e: e.activation(kh[:], ptb[:, 0:128], AF.Copy, scale=ek), [Bpt, Bek], [bkh])
            else:
                A(lambda e: e.copy(kh[:], ptb[:, 0:128]), [Bpt], [bkh])
            if not sample:
                ps2, bp2 = PSS()
                P(lambda e: e.matmul(ps2[:, 0:E], ident[:], Up[:, si, 0:E], start=True, stop=False), [Bc, BU[si]], [bp2])
                P(lambda e: e.matmul(ps2[:, 0:E], kh[:], vt[:, 0:E], start=False, stop=True), [bkh, Bv], [bp2])
                A(lambda e: e.activation(Up[:, si, 0:E], ps2[:, 0:E], AF.Copy, scale=dec), [bp2, Bdec, Bhd], [BU[si]])
                V(lambda e: e.tensor_copy(Upb[:, si, 0:E], Up[:, si, 0:E]), [BU[si]], [BU[si]])
                if mlstm_h is not None:
                    V(lambda e: e.tensor_copy(nbc[:, mlstm_h, :], Up[:, si, 128:129].broadcast_to([128, 128])), [BU[si]], [BU[si]])
            else:
                V(lambda e: e.tensor_tensor(kz[:], kh[:].unsqueeze(1).broadcast_to([128, 16, 128]),
                                            pv("rowm", 0, 16).unsqueeze(2).broadcast_to([128, 16, 128]), ALU.mult),
                  [bkh, Bc], [Bkz])
                for j in range(16):
                    ps2, bp2 = PSS()
                    P(lambda e, j=j, ps2=ps2: e.matmul(ps2[:, 0:E], ident[:], Usf[:, j, 0:E], start=True, stop=False), [Bc, BUs], [bp2])
                    P(lambda e, j=j, ps2=ps2: e.matmul(ps2[:, 0:E], kz[:, j, :], vt[:, 0:E], start=False, stop=True), [Bkz, Bv], [bp2])
                    A(lambda e, j=j, ps2=ps2: e.activation(usbuf[:, j, 0:E], ps2[:, 0:E], AF.Copy, scale=dec(j)),
                      [bp2, Bdec, Bhd], [BUs])
        att_tile.k = 0

        def load_sample_state(src, hh, E):
            LD(Usf[:, :, 0:E], src[:, hh, :, :].rearrange("j d e -> d j e"), [BUs])
            V(lambda e: e.tensor_copy(Usb[:, :, 0:E], Usf[:, :, 0:E]), [BUs], [BUs])

        def make_qz(qT, bq, col):
            V(lambda e: e.tensor_tensor(qz[:], qT[:, col:col + 128].unsqueeze(1).broadcast_to([128, 16, 128]), blk3[:], ALU.mult),
              [bq, Bc], [Bqz])

        def vproj(slot, bs, col, NT, E, vt_, bv_):
            nt = NT // 128
            for c in range(nt):
                ps, bp = PS()
                for k in range(8):
                    P(lambda e, k=k, ps=ps: e.matmul(ps[:, 0:128], hn[:, k, c * 128:(c + 1) * 128], slot[:, k, col:col + 128],
                                                     start=(k == 0), stop=(k == 7)), bs + [Bhn[(c * 128) // 512]], [bp])
                A(lambda e, ps=ps: e.copy(vt_[:, c, 0:128], ps[:, 0:128]), [bp], [bv_])

        def rstd_from(sq_src_fn, n, srcb):
            q = sqb[0]
            sq_src_fn(q)
            ps, bp = PSS()
            P(lambda e: e.matmul(ps[:, 0:n], onesb[:], q[:, 0:n], start=True, stop=True), [Bsq[0], Bc], [bp])
            rs = NT5[3]
            A(lambda e: e.activation(rs[:, 0:n], ps[:, 0:n], AF.Ln, scale=1.0 / 128, bias=CEPS), [bp], [BN[3]])
            A(lambda e: e.activation(rs[:, 0:n], rs[:, 0:n], AF.Exp, scale=-0.5), [BN[3]], [BN[3]])
            return rs

        def build_tab_kc(kc_):
            build_tables(kc_, pv("s0"), jrow[:, 2, :], lambda part: tbe[:, part, :], lambda part: tbz[:, part, :, :], [Btab], [Btab])
            LD(tabE[:, :, kc_ * 512:(kc_ + 1) * 512], tbe[:], [Bscr], r=[Btab])
            LD(tabZ[:, :, 4 * kc_:4 * kc_ + 4, :], tbz[:], [Bscr], r=[Btab])
        STEP = [None]

        def step():
            g = STEP[0]
            if g is not None:
                try:
                    next(g)
                except StopIteration:
                    STEP[0] = None
        CUT = int(os.environ.get("KCUT", "0"))

        class _Stop(Exception):
            pass

        def ck(k):
            if CUT == k:
                raise _Stop()
        try:
            for sbi, (tok0, NP, has_s) in enumerate(SBS):
                NT = NP + (128 if has_s else 0)
                ntp = NP // 128
                blks = blocks(NT)
                last = (sbi == len(SBS) - 1)
                LD(segm[:, 0:NTM], D["segm"][sbi], [Bc], r=[Bc])
                for c in range(8):
                    for bi, (o, n) in enumerate(blks):
                        LD(h[:, c, o:o + n], D["xT"][c * 128:(c + 1) * 128, tok0 + o:tok0 + o + n], [Bh[c][bi]])
                for layer in range(2):
                    rmsnorm("nmix%d" % layer, NT)
                    ck(1)
                    if layer == 0:
                        wgs, bwg = wload([(D["wg"], 0)], 8)
                        A1, A2, A3, A4 = Fs[2], Fs[3], Fs[5], Fs[4]
                        b1, b2, b3, b4 = BF[2], BF[3], BF[5], BF[4]
                        for bi, (o, n) in enumerate(blks):
                            ps, bp = PS()
                            for k in range(8):
                                P(lambda e, k=k, ps=ps: e.matmul(ps[0:4, 0:n], wgs[:, k, 0:4], hn[:, k, o:o + n], start=(k == 0), stop=(k == 7)),
                                  bwg + [Bhn[bi]], [bp])
                            A(lambda e, ps=ps: e.activation(A1[0:4, o:o + n], ps[0:4, 0:n], AF.Identity, bias=bg[:, 0:1]), [bp, Bc], [b1])
                            ps, bp = PS()
                            for k in range(8):
                                P(lambda e, k=k, ps=ps: e.matmul(ps[0:4, 0:n], wgs[:, k, 4:8], hn[:, k, o:o + n], start=(k == 0), stop=(k == 7)),
                                  bwg + [Bhn[bi]], [bp])
                            A(lambda e, ps=ps: e.activation(A2[0:4, o:o + n], ps[0:4, 0:n], AF.Exp, scale=-1.0, bias=nbg[:, 0:1]), [bp, Bc], [b2])
                        A(lambda e: e.activation(A2[0:4, 0:NT], A2[0:4, 0:NT], AF.Ln, bias=CONE[0:4]), [b2], [b2])
                        V(lambda e: e.memset(A4[0:4, 0:NT], 1.0), (), [b4])
                        V(lambda e: e.tensor_tensor_scan(A3[0:4, 0:NP], A4[0:4, 0:NP], A2[0:4, 0:NP], carr[:, 0:1], ALU.mult, ALU.add),
                          [b2, b4, Bcar], [b3])
                        if has_s:
                            V(lambda e: e.tensor_tensor_scan(A3[0:4, NP:NT], segm[0:4, NP:NT], A2[0:4, NP:NT], 0.0, ALU.mult, ALU.add),
                              [b2, Bc], [b3])
                        V(lambda e: e.tensor_tensor(A1[0:4, 0:NT], A1[0:4, 0:NT], A3[0:4, 0:NT], ALU.add), [b1, b3], [b1])
                        V(lambda e: e.memset(A4[0:4, 0:NT], 0.0), (), [b4])
                        V(lambda e: e.tensor_tensor_scan(A2[0:4, 0:NP], A4[0:4, 0:NP], A1[0:4, 0:NP], carr[:, 1:2], ALU.add, ALU.max),
                          [b1, b4, Bcar], [b2])
                        V(lambda e: e.tensor_copy(mxe[:, 0:1], carr[:, 1:2]), [Bcar], [Bsm])
                        V(lambda e: e.tensor_copy(mxe[:, 1:1 + ntp], A2[0:4, 0:NP].rearrange("p (c t) -> p c t", t=128)[:, :, 127]), [b2], [Bsm])
                        if has_s:
                            LD(ms0[:], D["ms"], [Bsm])
                            V(lambda e: e.tensor_copy(A4[0:4, NP:NT], A1[0:4, NP:NT]), [b1], [b4])
                            g3 = A4[0:4, NP:NT].rearrange("p (j l) -> p j l", l=8)
                            V(lambda e: e.tensor_tensor(g3[:, :, 0], g3[:, :, 0], ms0[:], ALU.max), [b4, Bsm], [b4])
                            V(lambda e: e.tensor_tensor_scan(A2[0:4, NP:NT], negm[:], A4[0:4, NP:NT], 0.0, ALU.add, ALU.max), [b4, Bc], [b2])
                        V(lambda e: e.tensor_copy(A4[0:4, 0:NP].rearrange("p (c t) -> p c t", t=128),
                                                  mxe[:, 0:ntp].unsqueeze(2).broadcast_to([4, ntp, 128])), [Bsm], [b4])
                        if has_s:
                            V(lambda e: e.tensor_copy(A4[0:4, NP:NT].rearrange("p (j l) -> p j l", l=8),
                                                      ms0[:].unsqueeze(2).broadcast_to([4, 16, 8])), [Bsm], [b4])
                        V(lambda e: e.tensor_tensor(decrow[:, 0:ntp], mxe[:, 0:ntp], mxe[:, 1:1 + ntp], ALU.subtract), [Bsm], [Bsm])
                        if has_s:
                            V(lambda e: e.tensor_tensor(decrow[:, 8:24], ms0[:], A2[0:4, NP:NT].rearrange("p (j l) -> p j l", l=8)[:, :, 7],
                                                        ALU.subtract), [Bsm, b2], [Bsm])
                        else:
                            V(lambda e: e.memset(decrow[:, 8:24], 0.0), (), [Bsm])
                        if ntp < 8:
                            V(lambda e: e.memset(decrow[:, ntp:8], 0.0), (), [Bsm])
                        A(lambda e: e.activation(decrow[:], decrow[:], AF.Exp), [Bsm], [Bsm])
                        ps, bp = PSS()
                        for hh in range(4):
                            P(lambda e, hh=hh, ps=ps: e.matmul(ps[:, hh * 24:(hh + 1) * 24], sel[:, hh, :], decrow[:], start=True, stop=True),
                              [Bc, Bsm], [bp])
                        V(lambda e, ps=ps: e.tensor_copy(decbc[:].rearrange("p a b -> p (a b)"), ps[:, 0:96]), [bp], [Bdec])
                        if last:
                            V(lambda e: e.tensor_tensor(msout[:, 16:17], A2[0:4, NP - 1:NP], A3[0:4, NP - 1:NP], ALU.subtract), [b2, b3], [Bsm])
                            V(lambda e: e.tensor_tensor(msout[:, 0:16], A2[0:4, NP:NT].rearrange("p (j l) -> p j l", l=8)[:, :, 7],
                                                        A3[0:4, NP:NT].rearrange("p (j l) -> p j l", l=8)[:, :, 7], ALU.subtract), [b2, b3], [Bsm])
                            STO(D["mp"], msout[:, 16:17], [Bsm]); STO(D["ms_o"], msout[:, 0:16], [Bsm])
                        V(lambda e: e.tensor_copy(carr[:, 0:1], A3[0:4, NP - 1:NP]), [b3], [Bcar])
                        V(lambda e: e.tensor_copy(carr[:, 1:2], A2[0:4, NP - 1:NP]), [b2], [Bcar])
                        V(lambda e: e.tensor_tensor(A3[0:4, 0:NT], A4[0:4, 0:NT], A3[0:4, 0:NT], ALU.subtract), [b3, b4], [b3])
                        V(lambda e: e.tensor_tensor(A1[0:4, 0:NT], A1[0:4, 0:NT], A4[0:4, 0:NT], ALU.subtract), [b1, b4], [b1])
                        A(lambda e: e.activation(A1[0:4, 0:NT], A1[0:4, 0:NT], AF.Exp, bias=CLNK[0:4]), [b1], [b1])
                        ps, bp = PSS()
                        for c in range(NT // 128):
                            P(lambda e, c=c, ps=ps: e.matmul(ps[:, c * 4:(c + 1) * 4], A1[0:4, c * 128:(c + 1) * 128], ident[0:4, 0:4],
                                                             start=True, stop=True), [b1, Bc], [bp])
                        V(lambda e, ps=ps: e.tensor_copy(ektm[:, 0:NT // 128, :].rearrange("p a b -> p (a b)"), ps[:, 0:4 * (NT // 128)]),
                          [bp], [Bek])
                        ck(2)
                        C0, S0 = Fs[6], Fs[7]
                        LD(Fs[8][:, 0:NT], D["posrow"][sbi][:, 0:NT], [BF[8]])
                        V(lambda e: e.tensor_scalar(Fs[8][:, 0:NT], Fs[8][:, 0:NT], pv("invf"), None, ALU.mult), [BF[8], Bc], [BF[8]])
                        sin_of(C0[:, 0:NT], Fs[8][:, 0:NT], 0.5 * PI, Fs[2][:, 0:NT], [BF[8]], [BF[6]], BF[2])
                        sin_of(S0[:, 0:NT], Fs[8][:, 0:NT], 0.0, Fs[2][:, 0:NT], [BF[8]], [BF[7]], BF[2])
                        V(lambda e: e.tensor_scalar(S0[:, 0:NT], S0[:, 0:NT], pv("sgn"), None, ALU.mult), [BF[7], Bc], [BF[7]])
                        V(lambda e: e.memset(vtm[:, :, 128:129], 1.0), (), [Bv])
                        V(lambda e: e.memset(vtm2[:, :, 128:129], 1.0), (), [Bv2])
                        SETS = [(Hs[0], Hs[1], Hs[2], vtm, BH[0], BH[1], BH[2], Bv), (Hs[3], Hs[4], gT2, vtm2, BH[3], BH[4], Bg2, Bv2)]
                        xq, xk, qT, kT, gT = Fs[0], Fs[1], Hs[0], Hs[1], Hs[2]
                        bxq, bxk, bqT, bkT, bgT = BF[0], BF[1], BH[0], BH[1], BH[2]
                        W = D["w_in_ab"]
                        def front_m(hh):
                            qT, kT, gT, vt_, bqT, bkT, bgT, bv_ = SETS[hh % 2]
                            slot, bs = wload([(D["w_ab_h"][:, hh * 512:(hh + 1) * 512], 0)], 8)
                            for (xx, bx, cc) in ((xq, bxq, 0), (xk, bxk, 128)):
                                def ev(ps, bp, bi, o, n, xx=xx, bx=bx):
                                    if o < NP:
                                        A(lambda e: e.copy(xx[:, 3 + o:3 + o + n], ps[:, 0:n]), [bp], [bx])
                                    else:
                                        A(lambda e: e.copy(xx[:, NP + 3:NP + 3 + 176].rearrange("p (j l) -> p j l", l=11)[:, :, 3:11],
                                                           ps[:, 0:128].rearrange("p (j l) -> p j l", l=8)), [bp], [bx])
                                proj_fm(slot, bs, cc, NT, ev)
                                yield

                            def evg(ps, bp, bi, o, n):
                                A(lambda e: e.activation(gT[:, o:o + n], ps[:, 0:n], AF.Sigmoid), [bp], [bgT])
                            proj_fm(slot, bs, 384, NT, evg)
                            yield
                            vproj(slot, bs, 256, NT, 129, vt_, bv_)
                            yield
                            for (xx, bx, ch, oT, boT) in ((xq, bxq, hh, qT, bqT), (xk, bxk, 4 + hh, kT, bkT)):
                                V(lambda e, xx=xx, ch=ch: e.tensor_copy(xx[:, 0:3], tails[:, ch, :]), [Btl], [bx])
                                acc = Fs[2]
                                V(lambda e, xx=xx, ch=ch: e.tensor_scalar(acc[:, 0:NP], xx[:, 0:NP], pv("cw0", ch), pv("cb", ch), ALU.mult, ALU.add),
                                  [bx, Bc], [BF[2]])
                                for j in range(1, 4):
                                    V(lambda e, xx=xx, ch=ch, j=j: e.scalar_tensor_tensor(acc[:, 0:NP], xx[:, j:j + NP], pv("cw%d" % j, ch), acc[:, 0:NP],
                                                                                         ALU.mult, ALU.add), [bx, Bc, BF[2]], [BF[2]])
                                if has_s:
                                    LD(xx[:, NP + 3:NP + 3 + 176].rearrange("p (j l) -> p j l", l=11)[:, :, 0:3],
                                       D["convs"][ch * 128:(ch + 1) * 128], [bx])
                                    xs3 = xx[:, NP + 3:NP + 3 + 176].rearrange("p (j l) -> p j l", l=11)
                                    a3 = acc[:, NP:NT].rearrange("p (j l) -> p j l", l=8)
                                    V(lambda e, xs3=xs3, a3=a3, ch=ch: e.tensor_scalar(a3, xs3[:, :, 0:8], pv("cw0", ch), pv("cb", ch), ALU.mult, ALU.add),
                                      [bx, Bc], [BF[2]])
                                    for j in range(1, 4):
                                        V(lambda e, xs3=xs3, a3=a3, ch=ch, j=j: e.scalar_tensor_tensor(a3, xs3[:, :, j:j + 8], pv("cw%d" % j, ch), a3,
                                                                                                      ALU.mult, ALU.add), [bx, Bc, BF[2]], [BF[2]])
                                    STO(D["convs_o"][ch * 128:(ch + 1) * 128], xs3[:, :, 8:11], [bx])
                                    STO(D["convp"][ch * 128:(ch + 1) * 128], xx[:, NP:NP + 3], [bx])
                                A(lambda e, oT=oT: e.activation(oT[:, 0:NT], acc[:, 0:NT], AF.Silu), [BF[2]], [boT])
                                V(lambda e, xx=xx, ch=ch: e.tensor_copy(tails[:, ch, :], xx[:, NP:NP + 3]), [bx], [Btl])
                                yield

                        def back_m(hh):
                            qT, kT, gT, vt_, bqT, bkT, bgT, bv_ = SETS[hh % 2]
                            if has_s:
                                load_sample_state(D["Us"], hh, 129)
                                make_qz(qT, bqT, NP)
                                V(lambda e: e.tensor_copy(nbs[:], Usf[:, :, 128:129].broadcast_to([128, 16, 128])), [BUs], [Bnbs])
                            for bi, (o, n) in enumerate(blks):
                                PT, bPT = PS()
                                dps, bd = PS()
                                pinned.update((pb.index(PT), pb.index(dps)))
                                for c in range(n // 128):
                                    tcol = o + c * 128
                                    tix = tcol // 128
                                    smp = tcol >= NP
                                    att_tile(qT, kT, bqT, bkT, tcol, vt_[:, tix, :], 129, hh, ektm[:, tix, hh:hh + 1],
                                             (lambda j, hh=hh: decbc[:, hh, 8 + j:9 + j]) if smp else decbc[:, hh, tix:tix + 1],
                                             PT, bPT, c * 128, smp, den=(dps, bd), mlstm_h=hh, usbuf=Usf, bv=bv_)
                                    step()
                                ps, bp = PSS()
                                P(lambda e, ps=ps, hh=hh: e.matmul(ps[:, 0:n], sel[:, hh, :], A3[0:4, o:o + n], start=True, stop=True), [Bc, b3], [bp])
                                dn = NT5[0]
                                A(lambda e, ps=ps: e.activation(dn[:, 0:n], ps[:, 0:n], AF.Exp, scale=-1.0), [bp], [BN[0]])
                                ab = NT5[1]
                                A(lambda e, dps=dps: e.activation(ab[:, 0:n], dps[:, 0:n], AF.Abs), [bd], [BN[1]])
                                V(lambda e: e.tensor_tensor(ab[:, 0:n], ab[:, 0:n], dn[:, 0:n], ALU.max), [BN[0], BN[1]], [BN[1]])
                                V(lambda e: e.reciprocal(ab[:, 0:n], ab[:, 0:n]), [BN[1]], [BN[1]])
                                hv = NT5[2]
                                V(lambda e, PT=PT: e.tensor_tensor(hv[:, 0:n], PT[:, 0:n], ab[:, 0:n], ALU.mult), [bPT, BN[1]], [BN[2]])
                                V(lambda e: e.tensor_tensor(hv[:, 0:n], hv[:, 0:n], gT[:, o:o + n], ALU.mult), [BN[2], bgT], [BN[2]])
                                rs = rstd_from(lambda q: A(lambda e: e.activation(q[:, 0:n], hv[:, 0:n], AF.Square), [BN[2]], [Bsq[0]]), n, None)
                                V(lambda e, hh=hh: e.scalar_tensor_tensor(mix[:, hh, o:o + n], hv[:, 0:n], pv("gna", hh), rs[:, 0:n], ALU.mult, ALU.mult),
                                  [BN[2], BN[3], Bc], [Bmix[hh][bi]])
                                pinned.clear()
                                step()
                            if has_s:
                                STO(D["Us_o"][:, hh, :, :].rearrange("j d e -> d j e"), Usf[:, :, :], [BUs])
                            if last:
                                STO(D["Up"][hh], Up[:, hh, :], [BU[hh]])
                        for _ in front_m(0):
                            pass
                        for hh in range(4):
                            STEP[0] = front_m(hh + 1) if hh + 1 < 4 else None
                            back_m(hh)
                            while STEP[0] is not None:
                                step()
                        ck(3)
                        def front_r(hh):
                            qT, kT, gT, vt_, bqT, bkT, bgT, bv_ = SETS[hh % 2]
                            b0 = 2056
                            slot, bs = wload([(D["w_ab_h"][:, (4 + hh) * 512:(5 + hh) * 512], 0)], 8)
                            for (xx, bx, cc, oT, boT) in ((xq, bxq, 0, qT, bqT), (xk, bxk, 128, kT, bkT)):
                                def ev(ps, bp, bi, o, n, xx=xx, bx=bx):
                                    A(lambda e: e.copy(xx[:, o:o + n], ps[:, 0:n]), [bp], [bx])
                                proj_fm(slot, bs, cc, NT, ev)
                                yield
                                xsw, t1, t2 = Fs[2], Fs[3], Fs[4]
                                A(lambda e, xx=xx: e.copy(xsw[0:64, 0:NT], xx[64:128, 0:NT]), [bx], [BF[2]])
                                A(lambda e, xx=xx: e.copy(xsw[64:128, 0:NT], xx[0:64, 0:NT]), [bx], [BF[2]])
                                V(lambda e, xx=xx: e.tensor_tensor(t1[:, 0:NT], xx[:, 0:NT], C0[:, 0:NT], ALU.mult), [bx, BF[6]], [BF[3]])
                                V(lambda e: e.tensor_tensor(t2[:, 0:NT], xsw[:, 0:NT], S0[:, 0:NT], ALU.mult), [BF[2], BF[7]], [BF[4]])
                                V(lambda e, oT=oT: e.tensor_tensor(oT[:, 0:NT], t1[:, 0:NT], t2[:, 0:NT], ALU.add), [BF[3], BF[4]], [boT])

                            def evg(ps, bp, bi, o, n):
                                A(lambda e: e.activation(gT[:, o:o + n], ps[:, 0:n], AF.Silu), [bp], [bgT])
                            proj_fm(slot, bs, 384, NT, evg)
                            yield
                            vproj(slot, bs, 256, NT, 128, vt_, bv_)
                            yield

                        def back_r(hh):
                            qT, kT, gT, vt_, bqT, bkT, bgT, bv_ = SETS[hh % 2]
                            if has_s:
                                load_sample_state(D["rets"], hh, 128)
                                make_qz(qT, bqT, NP)
                            g128 = math.exp(128 * LG[hh]); g8 = math.exp(8 * LG[hh])
                            for bi, (o, n) in enumerate(blks):
                                PT, bPT = PS()
                                pinned.add(pb.index(PT))
                                for c in range(n // 128):
                                    tcol = o + c * 128
                                    tix = tcol // 128
                                    smp = tcol >= NP
                                    att_tile(qT, kT, bqT, bkT, tcol, vt_[:, tix, :], 128, 4 + hh, gk[:, 1 if smp else 0, hh:hh + 1],
                                             (lambda j, g8=g8: g8) if smp else g128, PT, bPT, c * 128, smp, usbuf=Usf, bv=bv_)
                                    step()
                                hv = NT5[2]
                                if o < NP:
                                    V(lambda e, PT=PT, hh=hh: e.tensor_tensor(hv[:, 0:n].rearrange("p (c t) -> p c t", t=128),
                                                                             PT[:, 0:n].rearrange("p (c t) -> p c t", t=128),
                                                                             Gq[:, 0, hh, :].unsqueeze(1).broadcast_to([128, n // 128, 128]), ALU.mult),
                                      [bPT, Bc], [BN[2]])
                                else:
                                    V(lambda e, PT=PT, hh=hh: e.tensor_tensor(hv[:, 0:n], PT[:, 0:n], Gq[:, 1, hh, :], ALU.mult), [bPT, Bc], [BN[2]])
                                rs = rstd_from(lambda q: A(lambda e: e.activation(q[:, 0:n], hv[:, 0:n], AF.Square), [BN[2]], [Bsq[0]]), n, None)
                                V(lambda e: e.tensor_tensor(hv[:, 0:n], hv[:, 0:n], rs[:, 0:n], ALU.mult), [BN[2], BN[3]], [BN[2]])
                                V(lambda e, hh=hh: e.tensor_tensor(mix[:, 4 + hh, o:o + n], hv[:, 0:n], gT[:, o:o + n], ALU.mult),
                                  [BN[2], bgT], [Bmix[4 + hh][bi]])
                                pinned.clear()
                                step()
                            if has_s:
                                STO(D["rets_o"][:, hh, :, :].rearrange("j d e -> d j e"), Usf[:, :, 0:128], [BUs])
                            if last:
                                STO(D["retp"][hh], Up[:, 4 + hh, 0:128], [BU[4 + hh]])
                        for _ in front_r(0):
                            pass
                        for hh in range(4):
                            STEP[0] = front_r(hh + 1) if hh + 1 < 4 else None
                            back_r(hh)
                            while STEP[0] is not None:
                                step()
                        resid_proj(D["w_out_ab"][0] if False else D["w_out_ab"], NT, mix, Bmix)
                        ck(4)
                    else:
                        W = D["w_in_cd"]
                        qf, ff, eb, enb, tmp = Fs[0], Fs[1], Fs[2], Fs[3], Fs[4]
                        qT, kT, gT = Hs[0], Hs[1], Hs[2]
                        bqT, bkT, bgT = BH[0], BH[1], BH[2]
                        def front_h(hh):
                            qT, kT, gT, vt_, bqT, bkT, bgT, bv_ = SETS[hh % 2]
                            slot, bs = wload([(D["w_cd_h"][:, hh * 512:(hh + 1) * 512], 0)], 8)

                            def evq(ps, bp, bi, o, n):
                                A(lambda e: e.activation(qf[:, o:o + n], ps[:, 0:n], AF.Copy, scale=128.0 ** -0.5), [bp], [BF[0]])
                            proj_fm(slot, bs, 0, NT, evq)
                            yield

                            def evf(ps, bp, bi, o, n):
                                A(lambda e: e.activation(ff[:, o:o + n], ps[:, 0:n], AF.Sigmoid), [bp], [BF[1]])
                            proj_fm(slot, bs, 128, NT, evf)
                            yield

                            def evg(ps, bp, bi, o, n):
                                A(lambda e: e.activation(gT[:, o:o + n], ps[:, 0:n], AF.Silu), [bp], [bgT])
                            proj_fm(slot, bs, 384, NT, evg)
                            yield
                            vproj(slot, bs, 256, NT, 128, vt_, bv_)
                            yield
                            V(lambda e, hh=hh: e.tensor_scalar(ff[:, 0:NT], ff[:, 0:NT], oml[:, hh:hh + 1], lb[:, hh:hh + 1], ALU.mult, ALU.add),
                              [BF[1], Bc], [BF[1]])
                            A(lambda e: e.activation(tmp[:, 0:NT], ff[:, 0:NT], AF.Ln), [BF[1]], [BF[4]])
                            V(lambda e: e.tensor_tensor_scan(eb[:, 0:NT], segm[:, 0:NT], tmp[:, 0:NT], 0.0, ALU.mult, ALU.add), [BF[4], Bc], [BF[2]])
                            A(lambda e: e.activation(enb[:, 0:NT], eb[:, 0:NT], AF.Exp, scale=-1.0), [BF[2]], [BF[3]])
                            A(lambda e: e.activation(eb[:, 0:NT], eb[:, 0:NT], AF.Exp), [BF[2], BF[3]], [BF[2]])
                            V(lambda e, hh=hh: e.tensor_copy(hdec[:, hh % 2, 0:ntp], eb[:, 0:NP].rearrange("p (c t) -> p c t", t=128)[:, :, 127]), [BF[2]], [Bhd])
                            if has_s:
                                V(lambda e, hh=hh: e.tensor_copy(hdec[:, hh % 2, 8:24], eb[:, NP:NT].rearrange("p (j l) -> p j l", l=8)[:, :, 7]), [BF[2]], [Bhd])
                            V(lambda e: e.tensor_scalar(ff[:, 0:NT], ff[:, 0:NT], -1.0, 1.0, ALU.mult, ALU.add), [BF[1], BF[4]], [BF[1]])
                            V(lambda e: e.tensor_tensor(qT[:, 0:NT], qf[:, 0:NT], eb[:, 0:NT], ALU.mult), [BF[0], BF[2]], [bqT])
                            V(lambda e: e.tensor_tensor(kT[:, 0:NT], ff[:, 0:NT], enb[:, 0:NT], ALU.mult), [BF[1], BF[3]], [bkT])

                        def back_h(hh):
                            qT, kT, gT, vt_, bqT, bkT, bgT, bv_ = SETS[hh % 2]
                            if has_s:
                                load_sample_state(D["hgrns"], hh, 128)
                                make_qz(qT, bqT, NP)
                            for bi, (o, n) in enumerate(blks):
                                PT, bPT = PS()
                                pinned.add(pb.index(PT))
                                for c in range(n // 128):
                                    tcol = o + c * 128
                                    tix = tcol // 128
                                    smp = tcol >= NP
                                    att_tile(qT, kT, bqT, bkT, tcol, vt_[:, tix, :], 128, 8 + hh, None,
                                             (lambda j, hh=hh: hdec[:, hh % 2, 8 + j:9 + j]) if smp else hdec[:, hh % 2, tix:tix + 1],
                                             PT, bPT, c * 128, smp, usbuf=Usf, bv=bv_)
                                    step()
                                rs = rstd_from(lambda q, PT=PT, bPT=bPT: A(lambda e: e.activation(q[:, 0:n], PT[:, 0:n], AF.Square), [bPT], [Bsq[0]]), n, None)
                                hv = NT5[2]
                                V(lambda e, PT=PT: e.tensor_tensor(hv[:, 0:n], PT[:, 0:n], rs[:, 0:n], ALU.mult), [bPT, BN[3]], [BN[2]])
                                V(lambda e, hh=hh: e.scalar_tensor_tensor(mix[:, hh, o:o + n], hv[:, 0:n], pv("gnc", hh), gT[:, o:o + n], ALU.mult, ALU.mult),
                                  [BN[2], bgT, Bc], [Bmix[hh][bi]])
                                pinned.clear()
                                step()
                            if has_s:
                                STO(D["hgrns_o"][:, hh, :, :].rearrange("j d e -> d j e"), Usf[:, :, 0:128], [BUs])
                            if last:
                                STO(D["hgrnp"][hh], Up[:, 8 + hh, 0:128], [BU[8 + hh]])
                        for _ in front_h(0):
                            pass
                        for hh in range(4):
                            STEP[0] = front_h(hh + 1) if hh + 1 < 4 else None
                            back_h(hh)
                            while STEP[0] is not None:
                                step()
                        ck(7)
                        suf = Fs[8]; bsuf = BF[8]
                        V(lambda e: e.memset(cch[:, 5, 0:1], 0.0), (), S5FINE + [BUs, Bkz, Bqz, Bcch, Bg2, Bv2])
                        sub = Hs[0]; bsub = BH[0]
                        if has_s:
                            LD(x0s[:, 0], D["x0re"], [Bx0]); LD(x0s[:, 1], D["x0im"], [Bx0])
                            bz = lambda v: v.unsqueeze(2).broadcast_to([128, 16, 16])
                            u1 = Fs[6][:, 0:256].rearrange("p (a b) -> p a b", a=16); u2 = Fs[7][:, 0:256].rearrange("p (a b) -> p a b", a=16)
                            u3 = Fs[6][:, 256:512].rearrange("p (a b) -> p a b", a=16); u4 = Fs[7][:, 256:512].rearrange("p (a b) -> p a b", a=16)
                            for (cr, ci) in ((izr, izi), (ar_, ai_)):
                                V(lambda e, cr=cr: e.tensor_tensor(u1, x0s[:, 0], bz(cr), ALU.mult), [Bx0, Bc], [BF[6]])
                                V(lambda e, ci=ci: e.tensor_tensor(u2, x0s[:, 1], bz(ci), ALU.mult), [Bx0, Bc], [BF[7]])
                                V(lambda e, ci=ci: e.tensor_tensor(u3, x0s[:, 0], bz(ci), ALU.mult), [Bx0, Bc], [BF[6]])
                                V(lambda e, cr=cr: e.tensor_tensor(u4, x0s[:, 1], bz(cr), ALU.mult), [Bx0, Bc], [BF[7]])
                                V(lambda e: e.tensor_tensor(x0s[:, 0], u1, u2, ALU.subtract), [BF[6], BF[7]], [Bx0])
                                V(lambda e: e.tensor_tensor(x0s[:, 1], u3, u4, ALU.add), [BF[6], BF[7]], [Bx0])
                        ntl = NT // 128
                        slot_su, bs_su = wload([(W[:, 2048:2560], 0)], 8)
                        for oc in range(4):
                            slot, bs = slot_su, bs_su

                            def evs(ps, bp, bi, o, n):
                                A(lambda e: e.copy(suf[:, o:o + n], ps[:, 0:n]), [bp], [bsuf])
                                A(lambda e: e.copy(sub[:, o:o + n], ps[:, 0:n]), [bp], [bsub])
                            proj_fm(slot, bs, oc * 128, NT, evs)
                            for part in range(2):
                                S.dma("pool", lambda e, part=part, oc=oc: e.dma_start(out=bct[:, 0, part], in_=D["BT"][part, oc * 4:(oc + 1) * 4].rearrange("i k m -> k i m")),
                                      bcsem[part * 2], (), (Bbct if part == 0 else [Bbct[part * 2]]))
                                S.dma("pool", lambda e, part=part, oc=oc: e.dma_start(out=bct[:, 1, part], in_=D["CT"][part, oc * 4:(oc + 1) * 4].rearrange("i k m -> k i m")),
                                      bcsem[part * 2 + 1], (), [Bbct[part * 2 + 1]])
                            V(lambda e: e.tensor_scalar(bct[:, 1, 1], bct[:, 1, 1], -1.0, None, ALU.mult), Bbct, [Bbct[3]])
                            LD(tbe[:], tabE[:, :, oc * 512:(oc + 1) * 512], [Btab], r=[Bscr])
                            LD(tbz[:], tabZ[:, :, 4 * oc:4 * oc + 4, :], [Btab], r=[Bscr])
                            if has_s:
                                EsT = [Hs[1][:, 0:512], Hs[2][:, 0:512]]
                                ZsT = [Hs[3][:, 0:512], Hs[4][:, 0:512]]
                                build_tables(oc, pv("s8"), jrow[:, 3, :], lambda part: EsT[part],
                                             lambda part: ZsT[part].rearrange("p (a b) -> p a b", a=4), [Bes, BH[1], BH[2]], [Bes, BH[3], BH[4]])
                            i4 = slice(4 * oc, 4 * oc + 4)
                            ypsh = [None]

                            def stA(c):
                                smp = c * 128 >= NP; tc0 = c * 128; k2 = c % 2
                                wrb, wib = wtb[k2][0], wtb[k2][1]; xrb, xib = xtb[k2][0], xtb[k2][1]
                                bE = Bes if smp else Btab
                                smp = c * 128 >= NP
                                tc0 = c * 128
                                Er = EsT[0] if smp else tbe[:, 0, :]
                                Ei = EsT[1] if smp else tbe[:, 1, :]
                                bE = Bes if smp else Btab
                                k2 = c % 2
                                pr, bpr = PS(); pi_, bpi = PS()
                                P(lambda e, pr=pr: e.matmul(pr[:, 0:512], sub[:, tc0:tc0 + 128], bct[:, 0, 0].rearrange("p a b -> p (a b)"), start=True, stop=True),
                                  Bbct + [bsub], [bpr])
                                P(lambda e, pi_=pi_: e.matmul(pi_[:, 0:512], sub[:, tc0:tc0 + 128], bct[:, 0, 1].rearrange("p a b -> p (a b)"), start=True, stop=True),
                                  Bbct + [bsub], [bpi])
                                prb = qzf[:, (2 * k2) * 512:(2 * k2 + 1) * 512]; pib = qzf[:, (2 * k2 + 1) * 512:(2 * k2 + 2) * 512]
                                A(lambda e, pr=pr: e.copy(prb, pr[:, 0:512]), [bpr], [Bprb[k2]])
                                A(lambda e, pi_=pi_: e.copy(pib, pi_[:, 0:512]), [bpi], [Bprb[k2]])
                                wrb, wib = wtb[k2][0], wtb[k2][1]
                                ta, tb, tc_, td = [kzf[:, q_ * 512:(q_ + 1) * 512] for q_ in range(4)]
                                V(lambda e: e.tensor_tensor(ta, prb, Er, ALU.mult), [Bprb[k2], bE], [Bt4[0]])
                                V(lambda e: e.tensor_tensor(tb, pib, Ei, ALU.mult), [Bprb[k2], bE], [Bt4[1]])
                                V(lambda e: e.tensor_tensor(wrb[:], ta, tb, ALU.subtract), [Bt4[0], Bt4[1]], [Bwt[k2]])
                                V(lambda e: e.tensor_tensor(tc_, pib, Er, ALU.mult), [Bprb[k2], bE], [Bt4[2]])
                                V(lambda e: e.tensor_tensor(td, prb, Ei, ALU.mult), [Bprb[k2], bE], [Bt4[3]])
                                V(lambda e: e.tensor_tensor(wib[:], tc_, td, ALU.add), [Bt4[2], Bt4[3]], [Bwt[k2]])

                            def stB(c):
                                smp = c * 128 >= NP; tc0 = c * 128; k2 = c % 2
                                wrb, wib = wtb[k2][0], wtb[k2][1]; xrb, xib = xtb[k2][0], xtb[k2][1]
                                bE = Bes if smp else Btab
                                csr, bcsr = PSS(); csi, bcsi = PSS()
                                msk = maskb if smp else maskc
                                if smp:
                                    for part in range(2):
                                        V(lambda e, part=part: e.tensor_copy(cbs[:, part].rearrange("p a (j l) -> p a j l", l=8),
                                                                             x0s[:, part, 4 * oc:4 * oc + 4, :].unsqueeze(3).broadcast_to([128, 4, 16, 8])),
                                          [Bx0], [Bcbs])
                                else:
                                    for part in range(2):
                                        V(lambda e, part=part: e.tensor_copy(cbs[:, part], s5c[:, part, i4].unsqueeze(2).broadcast_to([128, 4, 128])),
                                          [Bs5c], [Bcbs])
                                for il in range(4):
                                    P(lambda e, il=il, csr=csr: e.matmul(csr[:, il * 128:(il + 1) * 128], wrb[:, il * 128:(il + 1) * 128], msk[:], start=True, stop=False),
                                      [Bwt[k2], Bc], [bcsr])
                                    P(lambda e, il=il, csi=csi: e.matmul(csi[:, il * 128:(il + 1) * 128], wib[:, il * 128:(il + 1) * 128], msk[:], start=True, stop=False),
                                      [Bwt[k2], Bc], [bcsi])
                                    if True:
                                        P(lambda e, il=il, csr=csr: e.matmul(csr[:, il * 128:(il + 1) * 128], identb[:], cbs[:, 0, il, :], start=False, stop=True), [Bc, Bcbs], [bcsr])
                                        P(lambda e, il=il, csi=csi: e.matmul(csi[:, il * 128:(il + 1) * 128], identb[:], cbs[:, 1, il, :], start=False, stop=True), [Bc, Bcbs], [bcsi])
                                cs3r = csr[:, 0:512].rearrange("p (a b) -> p a b", a=4); cs3i = csi[:, 0:512].rearrange("p (a b) -> p a b", a=4)
                                csbr = usbf[:, (2 * k2) * 512:(2 * k2 + 1) * 512]; csbi = usbf[:, (2 * k2 + 1) * 512:(2 * k2 + 2) * 512]
                                if not smp:
                                    A(lambda e, csr=csr: e.copy(csbr, csr[:, 0:512]), [bcsr], [Bcsb[k2]])
                                    A(lambda e, csi=csi: e.copy(csbi, csi[:, 0:512]), [bcsi], [Bcsb[k2]])
                                    V(lambda e: e.tensor_tensor(cch[:, 2, :], cs3r[:, :, 127], a128r[:, i4], ALU.mult), [bcsr, Bc, Bcsb[k2]], [Bcch])
                                    V(lambda e: e.tensor_tensor(cch[:, 3, :], cs3i[:, :, 127], a128i[:, i4], ALU.mult), [bcsi, Bc, Bcsb[k2]], [Bcch])
                                    V(lambda e: e.tensor_tensor(cch[:, 4, :], cs3r[:, :, 127], a128i[:, i4], ALU.mult), [bcsr, Bc, Bcsb[k2]], [Bcch])
                                    V(lambda e: e.tensor_tensor(cch[:, 5, :], cs3i[:, :, 127], a128r[:, i4], ALU.mult), [bcsi, Bc, Bcsb[k2]], [Bcch])
                                    V(lambda e: e.tensor_tensor(s5c[:, 0, i4], cch[:, 2, :], cch[:, 3, :], ALU.subtract), [Bcch], [Bs5c])
                                    V(lambda e: e.tensor_tensor(s5c[:, 1, i4], cch[:, 4, :], cch[:, 5, :], ALU.add), [Bcch], [Bs5c])
                                    Zr = tbz[:, 0].rearrange("p a b -> p (a b)"); Zi = tbz[:, 1].rearrange("p a b -> p (a b)"); bZ = Btab
                                else:
                                    A(lambda e, csr=csr: e.copy(csbr, csr[:, 0:512]), [bcsr], [Bcsb[k2]])
                                    A(lambda e, csi=csi: e.copy(csbi, csi[:, 0:512]), [bcsi], [Bcsb[k2]])
                                    Zr = ZsT[0]; Zi = ZsT[1]; bZ = Bes
                                p1, p2, p3, p4 = [pt4[q_] for q_ in range(4)]
                                xrb, xib = xtb[k2][0], xtb[k2][1]
                                V(lambda e: e.tensor_tensor(p1[:], csbr, Zr, ALU.mult), [Bcsb[k2], bZ], [Bp4[0]])
                                V(lambda e: e.tensor_tensor(p2[:], csbi, Zi, ALU.mult), [Bcsb[k2], bZ], [Bp4[1]])
                                V(lambda e: e.tensor_tensor(xrb[:], p1[:], p2[:], ALU.subtract), [Bp4[0], Bp4[1]], [Bxt[k2]])
                                V(lambda e: e.tensor_tensor(p3[:], csbr, Zi, ALU.mult), [Bcsb[k2], bZ], [Bp4[2]])
                                V(lambda e: e.tensor_tensor(p4[:], csbi, Zr, ALU.mult), [Bcsb[k2], bZ], [Bp4[3]])
                                V(lambda e: e.tensor_tensor(xib[:], p3[:], p4[:], ALU.add), [Bp4[2], Bp4[3]], [Bxt[k2]])
                                if last and (smp or c == NP // 128 - 1):
                                    p13 = [q_[:].rearrange("p (a b) -> p a b", a=4) for q_ in (p1, p2, p3, p4)]
                                    if not smp:
                                        V(lambda e: e.tensor_tensor(s5po[:, 0, i4], p13[0][:, :, 127], p13[1][:, :, 127], ALU.subtract), [Bp4[0], Bp4[1]], [Bs5o])
                                        V(lambda e: e.tensor_tensor(s5po[:, 1, i4], p13[2][:, :, 127], p13[3][:, :, 127], ALU.add), [Bp4[2], Bp4[3]], [Bs5o])
                                    else:
                                        l7 = lambda q3: q3.rearrange("p a (j l) -> p a j l", l=8)[:, :, :, 7]
                                        V(lambda e: e.tensor_tensor(s5so[:, 0, i4, :], l7(p13[0]), l7(p13[1]), ALU.subtract), [Bp4[0], Bp4[1]], [Bs5o])
                                        V(lambda e: e.tensor_tensor(s5so[:, 1, i4, :], l7(p13[2]), l7(p13[3]), ALU.add), [Bp4[2], Bp4[3]], [Bs5o])

                            def stC(c):
                                smp = c * 128 >= NP; tc0 = c * 128; k2 = c % 2
                                wrb, wib = wtb[k2][0], wtb[k2][1]; xrb, xib = xtb[k2][0], xtb[k2][1]
                                bE = Bes if smp else Btab
                                if c % 4 == 0:
                                    pinned.clear()
                                    ypsh[0] = PS()
                                    pinned.add(pb.index(ypsh[0][0]))
                                yp, byp = ypsh[0]
                                yc = (c % 4) * 128
                                for il in range(4):
                                    P(lambda e, il=il, yp=yp: e.matmul(yp[:, yc:yc + 128], bct[:, 1, 0, il, :], xrb[:, il * 128:(il + 1) * 128], start=(il == 0), stop=False),
                                      Bbct + [Bxt[k2]], [byp])
                                    P(lambda e, il=il, yp=yp: e.matmul(yp[:, yc:yc + 128], bct[:, 1, 1, il, :], xib[:, il * 128:(il + 1) * 128], start=False, stop=(il == 3)),
                                      Bbct + [Bxt[k2]], [byp])
                                if c % 4 == 3 or c == ntl - 1:
                                    o = (c // 4) * 512
                                    n = (c % 4 + 1) * 128
                                    bi = o // 512
                                    yv = NT5[4]
                                    V(lambda e, yp=yp: e.scalar_tensor_tensor(yv[:, 0:n], suf[:, o:o + n], pv("s5d", oc), yp[:, 0:n], ALU.mult, ALU.add),
                                      [bsuf, byp, Bc], [BN[4]])
                                    A(lambda e: e.activation(mix[:, 4 + oc, o:o + n], yv[:, 0:n], AF.Gelu), [BN[4]], [Bmix[4 + oc][bi]])
                            stA(0)
                            for c in range(ntl):
                                if c + 1 < ntl:
                                    stA(c + 1)
                                stB(c)
                                stC(c)
                        V(lambda e: e.memset(cch[:, 5, 0:1], 0.0), (), S5FINE + [BUs, Bkz, Bqz, Bcch, Bg2, Bv2])
                        pinned.clear()
                        if last:
                            STO(D["s5rep"], s5po[:, 0, :], [Bs5o]); STO(D["s5imp"], s5po[:, 1, :], [Bs5o])
                            STO(D["s5res"], s5so[:, 0], [Bs5o]); STO(D["s5ims"], s5so[:, 1], [Bs5o])
                        ck(8)
                        slot, bs = wload([(D["w_glu"], 0)], 4)
                        for oc in range(4):
                            for bi, (o, n) in enumerate(blks):
                                ps, bp = PS()
                                for k in range(4):
                                    P(lambda e, k=k, ps=ps, oc=oc: e.matmul(ps[:, 0:n], slot[:, k, oc * 128:(oc + 1) * 128], mix[:, 4 + k, o:o + n],
                                                                            start=(k == 0), stop=(k == 3)), bs + [Bmix[4 + k][bi] for k in range(4)], [bp])
                                A(lambda e, ps=ps, oc=oc: e.activation(hn[:, oc, o:o + n], ps[:, 0:n], AF.Sigmoid, bias=pv("bglu", oc)), [bp, Bc], [Bhn[bi]])
                        for oc in range(4):
                            for bi, (o, n) in enumerate(blks):
                                V(lambda e, oc=oc: e.tensor_tensor(mix[:, 4 + oc, o:o + n], mix[:, 4 + oc, o:o + n], hn[:, oc, o:o + n], ALU.mult),
                                  [Bhn[bi], Bmix[4 + oc][bi]], [Bmix[4 + oc][bi]])
                        resid_proj(D["w_out_cd"], NT, mix, Bmix)
                    rmsnorm("nff%d" % layer, NT)
                    for q in range(4):
                        if sbi == 0 and layer == 0:
                            build_tab_kc(q)
                        for u in range(2):
                            c0 = q * 1024 + u * 512
                            slot, bs = wload([(D["w_ff1"][layer][:, c0:c0 + 512], 0)], 8)
                            for hc in range(4):
                                c = u * 4 + hc

                                def ev(ps, bp, bi, o, n, c=c):
                                    t = sqb[1]
                                    A(lambda e: e.activation(t[:, 0:n], ps[:, 0:n], AF.Relu), [bp], [Bsq[1]])
                                    V(lambda e: e.tensor_tensor(mix[:, c, o:o + n], t[:, 0:n], t[:, 0:n], ALU.mult), [Bsq[1]], [Bmix[c][bi]])
                                proj_fm(slot, bs, hc * 128, NT, ev)
                        for u in range(2):
                            slot, bs = wload([(D["w_ff2"][layer][q * 1024:(q + 1) * 1024, u * 512:(u + 1) * 512], 0)], 8)
                            for oc in range(4):
                                c = u * 4 + oc

                                def ev(ps, bp, bi, o, n, c=c):
                                    V(lambda e: e.tensor_tensor(h[:, c, o:o + n], h[:, c, o:o + n], ps[:, 0:n], ALU.add), [bp, Bh[c][bi]], [Bh[c][bi]])
                                proj_fm(slot, bs, oc * 128, NT, ev, rhs=mix, rbufs=lambda bi: [Bmix[k][bi] for k in range(8)])
                    ck(5)
                    rmsnorm("nple%d" % layer, NT)
                    S.dma("pool", lambda e, layer=layer: e.dma_start(out=pTb[:, :, 0:NT], in_=D["pT"][layer][:, tok0:tok0 + NT].rearrange("(k p) n -> p k n", p=128)),
                          pTsem, (), [BpT])
                    for u in range(2):
                        slot, bs = wload([(D["w_ple_gate"][layer][:, u * 512:(u + 1) * 512], 0)], 8)
                        slot2, bs2 = wload([(D["w_ple_proj"][layer][:, u * 512:(u + 1) * 512], 0)], 2)
                        for oc in range(4):
                            c = u * 4 + oc
                            for bi, (o, n) in enumerate(blks):
                                ps, bp = PS()
                                for k in range(8):
                                    P(lambda e, k=k, ps=ps: e.matmul(ps[:, 0:n], slot[:, k, oc * 128:(oc + 1) * 128], hn[:, k, o:o + n], start=(k == 0), stop=(k == 7)),
                                      bs + [Bhn[bi]], [bp])
                                gt = NT5[0]
                                A(lambda e, ps=ps: e.activation(gt[:, 0:n], ps[:, 0:n], AF.Sigmoid), [bp], [BN[0]])
                                ps2, bp2 = PS()
                                for k in range(2):
                                    P(lambda e, k=k, ps2=ps2: e.matmul(ps2[:, 0:n], slot2[:, k, oc * 128:(oc + 1) * 128], pTb[:, k, o:o + n], start=(k == 0), stop=(k == 1)),
                                      bs2 + [BpT], [bp2])
                                V(lambda e, ps2=ps2: e.tensor_tensor(gt[:, 0:n], gt[:, 0:n], ps2[:, 0:n], ALU.mult), [bp2, BN[0]], [BN[0]])
                                V(lambda e, c=c: e.tensor_tensor(h[:, c, o:o + n], h[:, c, o:o + n], gt[:, 0:n], ALU.add), [BN[0], Bh[c][bi]], [Bh[c][bi]])
                ck(9)
                rmsnorm("nfin", NT, final_out=D["yT"][:, tok0:tok0 + NT] if True else None)
        except _Stop:
            pass
        S.final_wait("sp", OUTS)
        S.emit(st)
    return nc


_NC = [None]


def kernel(**I):
    if _NC[0] is None:
        _NC[0] = build_program()
    nc = _NC[0]
    in_maps = prep_inputs(I)
    res = run_bass_kernel_spmd(nc, in_maps, core_ids=list(range(8)))
    return assemble(res.results)


def prep_inputs(I):
    f = lambda a: np.ascontiguousarray(np.asarray(a, np.float32))
    ident = np.eye(128, dtype=np.float32)
    s_ = np.arange(128)
    maskc = (s_[:, None] <= s_[None, :]).astype(np.float32)
    maskb = maskc * (s_[:, None] // 8 == s_[None, :] // 8)
    blk3 = np.broadcast_to((np.arange(16)[:, None] == (s_[None, :] // 8)).astype(np.float32)[None], (128, 16, 128)).copy()
    rowm = (s_[:, None] // 8 == np.arange(16)[None, :]).astype(np.float32)
    segm = np.ones((3, 128, NTM), np.float32); posrow = np.zeros((3, 128, NTM), np.float32); tau = np.ones((3, 128, NTM), np.float32)
    for i, (t0, NP, hs) in enumerate(SBS):
        segm[i, :, 0:NP:128] = 0.0
        posrow[i, :, 0:NP] = np.arange(t0, t0 + NP)[None]
        tau[i, :, 0:NP] = np.arange(1, NP + 1)[None]
        if hs:
            segm[i, :, NP:NP + 128:8] = 0.0
            posrow[i, :, NP:NP + 128] = (16384 + (np.arange(128) % 8))[None]
            tau[i, :, NP:NP + 128] = (1 + (np.arange(128) % 8))[None]
    negm = np.zeros((4, 128), np.float32); negm[:, 0::8] = -1e30
    sel = np.zeros((4, 4, 128), np.float32)
    for k in range(4):
        sel[k, k, :] = 1.0
    jrow = np.zeros((128, 4, 128), np.float32); jrow[:, 0, :] = (s_ + 1)[None]; jrow[:, 1, :] = (s_ % 8 + 1)[None]; jrow[:, 2, :] = s_[None]; jrow[:, 3, :] = (s_ % 8)[None]
    pvec = np.zeros((128, NPV), np.float32)

    def put(name, arr):
        o, w = PV[name]
        pvec[:, o:o + w] = arr
    for l in range(2):
        put("nmix%d" % l, _cols(I["norm_mix"][l])); put("nff%d" % l, _cols(I["norm_ff"][l])); put("nple%d" % l, _cols(I["norm_ple"][l]))
        put("lb%d" % l, _cols(I["lb_logits"][l]))
    put("nfin", _cols(I["norm_final"]))
    for j in range(4):
        put("cw%d" % j, _cols(I["conv_w_ab"][0][j]))
    put("cb", _cols(I["conv_b_ab"][0])); put("gna", _cols(I["gn_a"][0])); put("gnc", _cols(I["gn_c"][0]))
    put("s5d", _cols(I["s5_D"][0])); put("bglu", _cols(I["b_glu"][0]))
    st = lambda a: np.ascontiguousarray(np.asarray(a, np.float32).reshape(16, 2, 64).reshape(16, 128).T)
    put("are", st(I["s5_A_re"][0])); put("aim", st(I["s5_A_im"][0]))
    put("ldt", st(np.repeat(np.asarray(I["s5_log_dt"][0], np.float32)[:, None], 64, axis=1)))
    put("invf", (10000.0 ** (-(np.arange(128) % 64) / 64.0)).astype(np.float32)[:, None])
    put("sgn", np.where(s_ < 64, -1.0, 1.0).astype(np.float32)[:, None])
    put("s0", s_.astype(np.float32)[:, None]); put("s8", (s_ % 8).astype(np.float32)[:, None]); put("pidx", (s_ + 1).astype(np.float32)[:, None]); put("pidxs", (s_ % 8 + 1).astype(np.float32)[:, None]); put("rowm", rowm)
    rw = lambda a: np.broadcast_to(np.asarray(a, np.float32).reshape(1, 2048), (128, 2048))
    rowp = np.ascontiguousarray(np.stack([rw(I["s5_A_re"][0]), rw(I["s5_A_im"][0]),
                                          rw(np.repeat(np.asarray(I["s5_log_dt"][0], np.float32)[:, None], 64, axis=1))]))
    bgv = np.asarray(I["b_gate_ab"][0], np.float32)
    bg = np.stack([bgv[:4], bgv[4:]], axis=1).copy()
    BT = np.zeros((2, 16, 128, 128), np.float32); CT = np.zeros((2, 16, 128, 128), np.float32)
    for part, (Bm, Cm) in enumerate(((I["s5_B_re"][0], I["s5_C_re"][0]), (I["s5_B_im"][0], I["s5_C_im"][0]))):
        Bm = np.asarray(Bm, np.float32); Cm = np.asarray(Cm, np.float32)
        for g in range(32):
            i, gl = g // 2, g % 2
            k0 = (g % 8) * 16
            BT[part, i, k0:k0 + 16, gl * 64:(gl + 1) * 64] = Bm[g].T
            CT[part, i, gl * 64:(gl + 1) * 64, k0:k0 + 16] = Cm[g].T
    wab = np.asarray(I["w_in_ab"][0], np.float32); wcd = np.asarray(I["w_in_cd"][0], np.float32)
    w_ab_h = np.concatenate([wab[:, b0 + sec * 512 + hh * 128:b0 + sec * 512 + (hh + 1) * 128]
                             for b0 in (0, 2056) for hh in range(4) for sec in range(4)], axis=1)
    w_cd_h = np.concatenate([wcd[:, sec * 512 + hh * 128:sec * 512 + (hh + 1) * 128] for hh in range(4) for sec in range(4)], axis=1)
    common = dict(w_ab_h=f(w_ab_h), w_cd_h=f(w_cd_h), w_in_ab=f(I["w_in_ab"][0]), wg=f(I["w_in_ab"][0][:, 2048:2056]), w_out_ab=f(I["w_out_ab"][0]),
                  w_in_cd=f(I["w_in_cd"][0]), w_glu=f(I["w_glu"][0]), w_out_cd=f(I["w_out_cd"][0]),
                  w_ff1=f(I["w_ff1"]), w_ff2=f(I["w_ff2"]), w_ple_proj=f(I["w_ple_proj"]), w_ple_gate=f(I["w_ple_gate"]),
                  pvec=pvec, bg=bg, BT=BT, CT=CT, ident=ident, maskc=maskc, maskb=maskb.astype(np.float32), blk3=blk3,
                  segm=segm, negm=negm, sel=sel, posrow=posrow, rowp=rowp, jrow=jrow)
    in_maps = []
    for c in range(8):
        sl = slice(16 * c, 16 * c + 16)
        xT = np.concatenate([np.asarray(I["x_prompt"][c]).T, np.asarray(I["x_sample"][sl]).reshape(128, 1024).T], axis=1)
        pT = np.concatenate([np.transpose(np.asarray(I["p_prompt"][:, c]), (0, 2, 1)),
                             np.transpose(np.asarray(I["p_sample"][:, sl]).reshape(2, 128, 256), (0, 2, 1))], axis=2)
        Us = np.concatenate([np.asarray(I["state_mlstm_C"][0][sl]), np.asarray(I["state_mlstm_n"][0][sl])[..., None]], axis=-1)
        x0 = lambda a: np.transpose(np.asarray(a, np.float32).reshape(16, 16, 128), (2, 1, 0))
        m = dict(common)
        m.update(xT=f(xT), pT=f(pT), convs=f(np.transpose(np.asarray(I["state_mlstm_conv"][0][sl]), (2, 0, 1))), Us=f(Us),
                 ms=f(np.asarray(I["state_mlstm_m"][0][sl]).T), rets=f(I["state_ret"][0][sl]), hgrns=f(I["state_hgrn"][0][sl]),
                 x0re=f(x0(I["state_s5_re"][0][sl])), x0im=f(x0(I["state_s5_im"][0][sl])))
        in_maps.append(m)
    return in_maps


def assemble(R):
    yp = np.zeros((8, 2048, 1024), np.float32); ys = np.zeros((128, 8, 1024), np.float32)
    convp = np.zeros((1, 8, 3, 1024), np.float32); convs = np.zeros((1, 128, 3, 1024), np.float32)
    Cp = np.zeros((1, 8, 4, 128, 128), np.float32); Cs = np.zeros((1, 128, 4, 128, 128), np.float32)
    np_ = np.zeros((1, 8, 4, 128), np.float32); ns = np.zeros((1, 128, 4, 128), np.float32)
    mp = np.zeros((1, 8, 4), np.float32); ms = np.zeros((1, 128, 4), np.float32)
    retp = np.zeros((1, 8, 4, 128, 128), np.float32); rets = np.zeros((1, 128, 4, 128, 128), np.float32)
    hgp = np.zeros((1, 8, 4, 128, 128), np.float32); hgs = np.zeros((1, 128, 4, 128, 128), np.float32)
    s5rp = np.zeros((1, 8, 32, 64), np.float32); s5ip = np.zeros((1, 8, 32, 64), np.float32)
    s5rs = np.zeros((1, 128, 32, 64), np.float32); s5is = np.zeros((1, 128, 32, 64), np.float32)
    for c in range(len(R)):
        r = R[c]
        sl = slice(16 * c, 16 * c + 16)
        yp[c] = r["yT"][:, :2048].T
        ys[sl] = r["yT"][:, 2048:].T.reshape(16, 8, 1024)
        convp[0, c] = r["convp"].T
        convs[0, sl] = np.transpose(r["convs_o"], (1, 2, 0))
        Cp[0, c] = r["Up"][:, :, :128]; np_[0, c] = r["Up"][:, :, 128]
        Cs[0, sl] = r["Us_o"][..., :128]; ns[0, sl] = r["Us_o"][..., 128]
        mp[0, c] = r["mp"][:, 0]; ms[0, sl] = r["ms_o"].T
        retp[0, c] = r["retp"]; rets[0, sl] = r["rets_o"]; hgp[0, c] = r["hgrnp"]; hgs[0, sl] = r["hgrns_o"]
        s5rp[0, c] = r["s5rep"].T.reshape(32, 64); s5ip[0, c] = r["s5imp"].T.reshape(32, 64)
        s5rs[0, sl] = np.transpose(r["s5res"], (2, 1, 0)).reshape(16, 32, 64)
        s5is[0, sl] = np.transpose(r["s5ims"], (2, 1, 0)).reshape(16, 32, 64)
    return (yp, ys, convp, convs, Cp, Cs, np_, ns, mp, ms, retp, rets, hgp, hgs, s5rp, s5rs, s5ip, s5is)
```

```python
import math, contextlib, os
import numpy as np
import concourse.bass as bass
import concourse.mybir as mybir
from concourse.bass_utils import run_bass_kernel_spmd

F32 = mybir.dt.float32
BF16 = mybir.dt.bfloat16
AF = mybir.ActivationFunctionType
ALU = mybir.AluOpType

NTM = 768
FW = 776
SBS = [(0, 768, False), (768, 768, False), (1536, 512, True)]
NTOK = 2176
EPS = 1e-6
PI = math.pi
LG = [math.log1p(-2.0 ** (-5.0 - h)) for h in range(4)]
LNK = -0.5 * math.log(128.0)


class Buf:
    __slots__ = ("w", "r")

    def __init__(self):
        self.w = None
        self.r = []


class _Rec:
    def __init__(self):
        self.call = None

    def __getattr__(self, name):
        def f(*a, **k):
            self.call = (name, a, k)
            return self
        return f


def _record(fn):
    r = _Rec()
    fn(r)
    assert r.call is not None
    return r.call


class Sched:
    ENGS = ("pe", "act", "dve", "pool", "sp")

    def __init__(self, nc):
        self.nc = nc
        self.ops = {e: [] for e in self.ENGS}
        self.cnt = {e: 0 for e in self.ENGS}
        self.seen = {e: {} for e in self.ENGS}
        self.sems = {}
        self.dma_cnt = {}

    def new_dma_sem(self):
        k = "dma%d" % len(self.dma_cnt)
        self.dma_cnt[k] = 0
        return k

    def _deps(self, eng, reads, writes, is_dma):
        waits = {}

        def add(ev, kind):
            key, val, src_eng, src_dma = ev
            if (not src_dma) and (not is_dma) and src_eng == eng and eng == "pe":
                return
            if self.seen[eng].get(key, 0) >= val:
                return
            if waits.get(key, 0) < val:
                waits[key] = val
        for b in reads:
            if b.w is not None:
                add(b.w, "raw")
        for b in writes:
            if b.w is not None:
                add(b.w, "waw")
            for r in b.r:
                add(r, "war")
        for k, v in waits.items():
            self.seen[eng][k] = v
        return list(waits.items())

    def _post(self, ev, reads, writes):
        for b in writes:
            b.w = ev
            b.r = []
        for b in reads:
            if b.w is not ev:
                b.r.append(ev)
                if len(b.r) > 24:
                    b.r = b.r[-24:] if False else b.r

    def op(self, eng, fn, reads=(), writes=()):
        waits = self._deps(eng, reads, writes, False)
        self.cnt[eng] += 1
        ev = ("e_" + eng, self.cnt[eng], eng, False)
        self.ops[eng].append((waits, _record(fn), ("e_" + eng, 1)))
        self._post(ev, reads, writes)

    def dma(self, eng, fn, sem, reads=(), writes=()):
        waits = self._deps(eng, reads, writes, True)
        prev = self.dma_cnt[sem]
        if prev > 0 and self.seen[eng].get(sem, 0) < prev:
            waits = [w_ for w_ in waits if w_[0] != sem] + [(sem, prev)]
            self.seen[eng][sem] = prev
        self.dma_cnt[sem] += 16
        ev = (sem, self.dma_cnt[sem], eng, True)
        self.ops[eng].append((waits, _record(fn), (sem, 16)))
        self._post(ev, reads, writes)

    def final_wait(self, eng, bufs):
        waits = self._deps(eng, bufs, bufs, True)
        have = dict(waits)
        for k, v in self.dma_cnt.items():
            if v > 0 and self.seen[eng].get(k, 0) < v and have.get(k, 0) < v:
                have[k] = v
        for e2 in self.ENGS:
            if e2 != eng and self.cnt[e2] > 0:
                have["e_" + e2] = self.cnt[e2]
        self.ops[eng].append((list(have.items()), None, None))

    def emit(self, stack):
        nc = self.nc
        keys = ["e_" + e for e in self.ENGS] + list(self.dma_cnt.keys())
        for k in keys:
            self.sems[k] = stack.enter_context(nc.semaphore(k))
        block = stack.enter_context(nc.Block())
        engobj = {"pe": "tensor", "act": "scalar", "dve": "vector", "pool": "gpsimd", "sp": "sync"}

        def mk(e):
            def body(engine):
                for (waits, fn, inc) in self.ops[e]:
                    for (k, v) in waits:
                        engine.wait_ge(self.sems[k], v)
                    if fn is not None:
                        name, a, k = fn
                        getattr(engine, name)(*a, **k).then_inc(self.sems[inc[0]], inc[1])
            return body
        for e in self.ENGS:
            if self.ops[e]:
                getattr(block, engobj[e])(mk(e))


PV = {}
_o = 0
for _n, _w in [("nmix0", 8), ("nmix1", 8), ("nff0", 8), ("nff1", 8), ("nple0", 8), ("nple1", 8), ("nfin", 8),
               ("cw0", 8), ("cw1", 8), ("cw2", 8), ("cw3", 8), ("cb", 8), ("gna", 4), ("gnc", 4), ("s5d", 4),
               ("bglu", 4), ("lb0", 4), ("lb1", 4), ("are", 16), ("aim", 16), ("ldt", 16), ("invf", 1),
               ("sgn", 1), ("pidx", 1), ("pidxs", 1), ("rowm", 16), ("s0", 1), ("s8", 1)]:
    PV[_n] = (_o, _w)
    _o += _w
NPV = _o


def _cols(v):
    return np.ascontiguousarray(np.asarray(v, np.float32).reshape(-1, 128).T)


def build_program():
    nc = bass.Bass("TRN2", target_bir_lowering=False)
    D = {}

    def din(name, shape):
        D[name] = nc.dram_tensor(name, list(shape), F32, kind="ExternalInput").ap()
        return D[name]

    def dout(name, shape):
        D[name] = nc.dram_tensor(name, list(shape), F32, kind="ExternalOutput").ap()
        return D[name]
    din("xT", [1024, NTOK]); din("pT", [2, 256, NTOK])
    din("convs", [1024, 16, 3]); din("Us", [16, 4, 128, 129]); din("ms", [4, 16])
    din("rets", [16, 4, 128, 128]); din("hgrns", [16, 4, 128, 128])
    din("x0re", [128, 16, 16]); din("x0im", [128, 16, 16])
    din("w_ab_h", [1024, 4096]); din("w_cd_h", [1024, 2048]); din("w_in_ab", [1024, 4104]); din("wg", [1024, 8]); din("w_out_ab", [1024, 1024])
    din("w_in_cd", [1024, 2560]); din("w_glu", [512, 512]); din("w_out_cd", [1024, 1024])
    din("w_ff1", [2, 1024, 4096]); din("w_ff2", [2, 4096, 1024])
    din("w_ple_proj", [2, 256, 1024]); din("w_ple_gate", [2, 1024, 1024])
    din("pvec", [128, NPV]); din("bg", [4, 2])
    din("BT", [2, 16, 128, 128]); din("CT", [2, 16, 128, 128])
    din("ident", [128, 128]); din("maskc", [128, 128]); din("maskb", [128, 128])
    din("blk3", [128, 16, 128]); din("segm", [3, 128, NTM]); din("negm", [4, 128]); din("sel", [4, 4, 128])
    din("posrow", [3, 128, NTM]); din("rowp", [3, 128, 2048]); din("jrow", [128, 4, 128])
    dout("yT", [1024, NTOK]); dout("convp", [1024, 3]); dout("convs_o", [1024, 16, 3])
    dout("Up", [4, 128, 129]); dout("Us_o", [16, 4, 128, 129]); dout("mp", [4, 1]); dout("ms_o", [4, 16])
    dout("retp", [4, 128, 128]); dout("rets_o", [16, 4, 128, 128])
    dout("hgrnp", [4, 128, 128]); dout("hgrns_o", [16, 4, 128, 128])
    dout("s5rep", [128, 16]); dout("s5imp", [128, 16]); dout("s5res", [128, 16, 16]); dout("s5ims", [128, 16, 16])

    st = contextlib.ExitStack()
    with st:
        S = Sched(nc)
        cnt = [0]

        def sb(shape, dt=F32):
            cnt[0] += 1
            return st.enter_context(nc.sbuf_tensor("t%d" % cnt[0], list(shape), dt))

        def psum(shape, dt=F32):
            cnt[0] += 1
            return st.enter_context(nc.psum_tensor("p%d" % cnt[0], list(shape), dt))
        V = lambda fn, r=(), w=(): S.op("dve", fn, r, w)
        A = lambda fn, r=(), w=(): S.op("act", fn, r, w)
        G = lambda fn, r=(), w=(): S.op("pool", fn, r, w)
        P = lambda fn, r=(), w=(): S.op("pe", fn, r, w)
        msems = {"sp": [S.new_dma_sem() for _ in range(24)], "pool": [S.new_dma_sem() for _ in range(8)]}
        mi = {"sp": 0, "pool": 0}

        def LD(out, in_, w, r=(), eng="sp"):
            k = msems[eng][mi[eng] % len(msems[eng])]
            mi[eng] += 1
            S.dma(eng, lambda e: e.dma_start(out=out, in_=in_), k, r, w)
        OUTS = []

        def STO(out, in_, r):
            b_ = Buf()
            OUTS.append(b_)
            LD(out, in_, [b_], r)

        h = sb([128, 8, NTM]); hn = sb([128, 8, NTM], BF16); mix = sb([128, 8, NTM], BF16)
        Bh = [[Buf() for _ in range(2)] for _ in range(8)]
        Bhn = [Buf() for _ in range(2)]
        Bmix = [[Buf() for _ in range(2)] for _ in range(8)]
        NW = 2
        wr = [sb([128, 8, 512], BF16) for _ in range(NW)]
        Bwrp = [[Buf() for _ in range(4)] for _ in range(NW)]
        wsem = [[S.new_dma_sem() for _ in range(4)] for _ in range(NW)]
        wi = [0]

        def wload(parts, nk):
            i = wi[0] % NW
            wi[0] += 1
            for pi_, (ap, co) in enumerate(parts):
                ncol = ap.shape[1]
                S.dma("pool", lambda e, ap=ap, co=co, ncol=ncol, i=i: e.dma_start(
                    out=wr[i][:, 0:nk, co:co + ncol], in_=ap.rearrange("(k p) n -> p k n", p=128)),
                    wsem[i][pi_], (), (Bwrp[i] if pi_ == 0 else [Bwrp[i][pi_]]))
            return wr[i], Bwrp[i]
        Fs = [sb([128, FW]) for _ in range(9)]
        BF = [Buf() for _ in range(9)]
        Hs = [sb([128, NTM], BF16) for _ in range(5)]
        BH = [Buf() for _ in range(5)]
        vtm = sb([128, 6, 129], BF16); Bv = Buf()
        NT5 = [sb([128, 512]) for _ in range(5)]
        BN = [Buf() for _ in range(5)]
        sqb = [sb([128, 512], BF16) for _ in range(2)]
        Bsq = [Buf() for _ in range(2)]
        pb = [psum([128, 512]) for _ in range(7)]
        Bp = [Buf() for _ in range(7)]
        ptb = psum([128, 1024], BF16); Bpt = Buf()
        pbi = [0]

        pinned = set()

        def PS():
            while True:
                i = pbi[0] % 4
                pbi[0] += 1
                if i not in pinned:
                    return pb[i], Bp[i]
        psi = [0]

        def PSS():
            i = 4 + psi[0] % 3
            psi[0] += 1
            return pb[i], Bp[i]
        ident = sb([128, 128]); identb = sb([128, 128], BF16); maskc = sb([128, 128], BF16); maskb = sb([128, 128], BF16)
        onesb = sb([128, 128], BF16); blk3 = sb([128, 16, 128], BF16); segm = sb([128, NTM]); negm = sb([4, 128])
        sel = sb([4, 4, 128]); pvec = sb([128, NPV]); bg = sb([4, 2]); nbg = sb([4, 1]); jrow = sb([128, 4, 128])
        Bc = Buf()
        LD(ident[:], D["ident"], [Bc]); LD(identb[:], D["ident"], [Bc], eng="pool")
        LD(maskc[:], D["maskc"], [Bc], eng="pool"); LD(maskb[:], D["maskb"], [Bc], eng="pool")
        LD(blk3[:], D["blk3"], [Bc], eng="pool"); LD(negm[:], D["negm"], [Bc]); LD(sel[:], D["sel"], [Bc])
        LD(pvec[:], D["pvec"], [Bc]); LD(bg[:], D["bg"], [Bc]); LD(jrow[:], D["jrow"], [Bc])
        V(lambda e: e.memset(onesb[:], 1.0), (), [Bc])
        V(lambda e: e.tensor_scalar(nbg[:], bg[:, 1:2], -1.0, None, ALU.mult), [Bc], [Bc])

        cb_ = sb([128, 8])
        CBV = [EPS, LNK, 1.0, 0.0, 0.5 * PI, 0.0, 0.0, 0.0]
        for _i, _v in enumerate(CBV):
            V(lambda e, _i=_i, _v=_v: e.memset(cb_[:, _i:_i + 1], _v), (), [Bc])
        CEPS, CLNK, CONE, CZERO, CHPI = [cb_[:, i:i + 1] for i in range(5)]
        RC = 12582912.0
        I2P = 1.0 / (2 * PI)

        def sin_of(dst, src, shift, tmp, rd, wr_, btmp, npart=128):
            V(lambda e: e.tensor_scalar(tmp, src, shift, I2P, ALU.add, ALU.mult), rd, [btmp])
            V(lambda e: e.tensor_scalar(tmp, tmp, RC, None, ALU.add), [btmp], [btmp])
            V(lambda e: e.tensor_scalar(tmp, tmp, -RC, None, ALU.add), [btmp], [btmp])
            V(lambda e: e.scalar_tensor_tensor(tmp, tmp, -2 * PI, src, ALU.mult, ALU.add), [btmp] + list(rd), [btmp])
            V(lambda e: e.tensor_scalar(tmp, tmp, -PI - shift + 4e-6, PI - shift - 4e-6, ALU.max, ALU.min), [btmp], [btmp])
            A(lambda e: e.activation(dst, tmp, AF.Sin, bias=(CHPI[0:npart] if shift != 0.0 else CZERO[0:npart])), [btmp, Bc], wr_)

        def pv(name, j=0, n=1):
            o, w = PV[name]
            return pvec[:, o + j:o + j + n]
        Gq = sb([128, 2, 4, 128], BF16); gk = sb([128, 2, 4])
        for v2 in range(2):
            for hh in range(4):
                A(lambda e, v2=v2, hh=hh: e.activation(Gq[:, v2, hh, :], jrow[:, v2, :], AF.Exp, scale=LG[hh]), [Bc], [Bc])
                A(lambda e, v2=v2, hh=hh: e.activation(gk[:, v2, hh:hh + 1], pv("pidxs" if v2 else "pidx"), AF.Exp,
                                                       scale=-LG[hh], bias=CLNK), [Bc], [Bc])
        lb = sb([128, 4]); oml = sb([128, 4])
        V(lambda e: e.tensor_tensor(lb[:], pv("lb1", 0, 4), pv("lb0", 0, 4), ALU.subtract), [Bc], [Bc])
        A(lambda e: e.activation(lb[:], lb[:], AF.Sigmoid), [Bc], [Bc])
        V(lambda e: e.tensor_scalar(oml[:], lb[:], -1.0, 1.0, ALU.mult, ALU.add), [Bc], [Bc])
        s5p = sb([128, 16, 16])
        th, rr, zr, zi, rho = s5p[:, 0, :], s5p[:, 1, :], s5p[:, 2, :], s5p[:, 3, :], s5p[:, 7, :]
        t4, t5, t6 = s5p[:, 4, :], s5p[:, 5, :], s5p[:, 6, :]
        ar_, ai_, a128r, a128i, izr, izi, t7 = (s5p[:, 8, :], s5p[:, 9, :], s5p[:, 10, :], s5p[:, 11, :], s5p[:, 12, :],
                                               s5p[:, 13, :], s5p[:, 14, :])
        are, aim = pv("are", 0, 16), pv("aim", 0, 16)
        A(lambda e: e.activation(t4, pv("ldt", 0, 16), AF.Exp), [Bc], [Bc])
        V(lambda e: e.tensor_tensor(th, t4, aim, ALU.mult), [Bc], [Bc])
        V(lambda e: e.tensor_tensor(rho, t4, are, ALU.mult), [Bc], [Bc])
        A(lambda e: e.activation(rr, rho, AF.Exp), [Bc], [Bc])
        sin_of(t4, th, 0.5 * PI, t6, [Bc], [Bc], Bc)
        sin_of(t5, th, 0.0, t6, [Bc], [Bc], Bc)
        V(lambda e: e.tensor_tensor(ar_, t4, rr, ALU.mult), [Bc], [Bc])
        V(lambda e: e.tensor_tensor(ai_, t5, rr, ALU.mult), [Bc], [Bc])
        V(lambda e: e.tensor_scalar(t4, ar_, -1.0, None, ALU.add), [Bc], [Bc])
        V(lambda e: e.tensor_copy(t5, ai_), [Bc], [Bc])
        V(lambda e: e.tensor_tensor(t6, are, are, ALU.mult), [Bc], [Bc])
        V(lambda e: e.tensor_tensor(zr, aim, aim, ALU.mult), [Bc], [Bc])
        V(lambda e: e.tensor_tensor(t6, t6, zr, ALU.add), [Bc], [Bc])
        V(lambda e: e.reciprocal(t6, t6), [Bc], [Bc])
        V(lambda e: e.tensor_tensor(zr, t4, are, ALU.mult), [Bc], [Bc])
        V(lambda e: e.tensor_tensor(zi, t5, aim, ALU.mult), [Bc], [Bc])
        V(lambda e: e.tensor_tensor(zr, zr, zi, ALU.add), [Bc], [Bc])
        V(lambda e: e.tensor_tensor(zi, t5, are, ALU.mult), [Bc], [Bc])
        V(lambda e: e.tensor_tensor(t7, t4, aim, ALU.mult), [Bc], [Bc])
        V(lambda e: e.tensor_tensor(zi, zi, t7, ALU.subtract), [Bc], [Bc])
        V(lambda e: e.tensor_tensor(zr, zr, t6, ALU.mult), [Bc], [Bc])
        V(lambda e: e.tensor_tensor(zi, zi, t6, ALU.mult), [Bc], [Bc])
        V(lambda e: e.tensor_tensor(t4, zr, zr, ALU.mult), [Bc], [Bc])
        V(lambda e: e.tensor_tensor(t5, zi, zi, ALU.mult), [Bc], [Bc])
        V(lambda e: e.tensor_tensor(t4, t4, t5, ALU.add), [Bc], [Bc])
        V(lambda e: e.reciprocal(t4, t4), [Bc], [Bc])
        V(lambda e: e.tensor_tensor(izr, zr, t4, ALU.mult), [Bc], [Bc])
        V(lambda e: e.scalar_tensor_tensor(izi, zi, -1.0, t4, ALU.mult, ALU.mult), [Bc], [Bc])
        V(lambda e: e.tensor_scalar(t7, th, 128.0, None, ALU.mult), [Bc], [Bc])
        sin_of(t4, t7, 0.5 * PI, t6, [Bc], [Bc], Bc)
        sin_of(t5, t7, 0.0, t6, [Bc], [Bc], Bc)
        A(lambda e: e.activation(t6, rho, AF.Exp, scale=128.0), [Bc], [Bc])
        V(lambda e: e.tensor_tensor(a128r, t4, t6, ALU.mult), [Bc], [Bc])
        V(lambda e: e.tensor_tensor(a128i, t5, t6, ALU.mult), [Bc], [Bc])
        tabE = nc.dram_tensor("tabE", [128, 2, 2048], BF16).ap(); tabZ = nc.dram_tensor("tabZ", [128, 2, 16, 128], BF16).ap()
        tbe = sb([128, 2, 512], BF16); tbz = sb([128, 2, 4, 128], BF16); Btab = Buf(); Bscr = Buf()

        def build_tables(kc, scol, jr, outE, outZ, wE, wZ):
            f0, f1, f2, f3, f4, f5 = [Fs[k][:, 0:512] for k in range(6)]
            b0_, b1_, b2_, b3_, b4_, b5_ = BF[0:6]
            for k in range(3):
                LD(Fs[k][:, 0:512], D["rowp"][k][:, kc * 512:(kc + 1) * 512], [BF[k]])
            A(lambda e: e.activation(f2, f2, AF.Exp), [b2_], [b2_])
            V(lambda e: e.tensor_tensor(f1, f1, f2, ALU.mult), [b1_, b2_], [b1_])
            V(lambda e: e.tensor_tensor(f0, f0, f2, ALU.mult), [b0_, b2_], [b0_])
            V(lambda e: e.tensor_scalar(f1, f1, scol, None, ALU.mult), [b1_, Bc], [b1_])
            A(lambda e: e.activation(f0, f0, AF.Exp, scale=scol), [b0_, Bc], [b0_])
            V(lambda e: e.reciprocal(f0, f0), [b0_], [b0_])
            sin_of(f3, f1, 0.5 * PI, f2, [b1_], [b3_], b2_)
            sin_of(f4, f1, 0.0, f2, [b1_], [b4_], b2_)
            V(lambda e: e.tensor_tensor(outE(0), f3, f0, ALU.mult), [b3_, b0_], wE)
            V(lambda e: e.scalar_tensor_tensor(outE(1), f4, -1.0, f0, ALU.mult, ALU.mult), [b4_, b0_], wE)
            g0, g1, g2, g3, g4 = [Fs[k][:, 0:512].rearrange("p (a b) -> p a b", a=4) for k in range(5)]
            i4 = slice(4 * kc, 4 * kc + 4)
            jb = jr.unsqueeze(1).broadcast_to([128, 4, 128])
            bc4 = lambda v: v[:, i4].unsqueeze(2).broadcast_to([128, 4, 128])
            V(lambda e: e.tensor_tensor(g1, jb, bc4(th), ALU.mult), [Bc], [b1_])
            V(lambda e: e.tensor_tensor(g0, jb, bc4(rho), ALU.mult), [Bc], [b0_])
            A(lambda e: e.activation(Fs[0][:, 0:512], Fs[0][:, 0:512], AF.Exp), [b0_], [b0_])
            sin_of(f3, f1, 0.5 * PI, f2, [b1_], [b3_], b2_)
            sin_of(f4, f1, 0.0, f2, [b1_], [b4_], b2_)
            V(lambda e: e.tensor_tensor(f3, f3, f0, ALU.mult), [b3_, b0_], [b3_])
            V(lambda e: e.tensor_tensor(f4, f4, f0, ALU.mult), [b4_, b0_], [b4_])
            V(lambda e: e.tensor_tensor(g0, g3, bc4(zr), ALU.mult), [b3_, Bc], [b0_])
            V(lambda e: e.tensor_tensor(g1, g4, bc4(zi), ALU.mult), [b4_, Bc], [b1_])
            V(lambda e: e.tensor_tensor(outZ(0), g0, g1, ALU.subtract), [b0_, b1_], wZ)
            V(lambda e: e.tensor_tensor(g0, g3, bc4(zi), ALU.mult), [b3_, Bc], [b0_])
            V(lambda e: e.tensor_tensor(g1, g4, bc4(zr), ALU.mult), [b4_, Bc], [b1_])
            V(lambda e: e.tensor_tensor(outZ(1), g0, g1, ALU.add), [b0_, b1_], wZ)
        Up = sb([128, 12, 129]); Upb = sb([128, 12, 129], BF16); nbc = sb([128, 4, 128], BF16)
        BU = [Buf() for _ in range(12)]
        V(lambda e: e.memset(Up[:], 0.0), (), BU); V(lambda e: e.memset(Upb[:], 0.0), (), BU)
        V(lambda e: e.memset(nbc[:], 0.0), (), BU)
        tails = sb([128, 8, 3]); Btl = Buf()
        V(lambda e: e.memset(tails[:], 0.0), (), [Btl])
        carr = sb([4, 2]); Bcar = Buf()
        V(lambda e: e.memset(carr[:], 0.0), (), [Bcar])
        s5c = sb([128, 2, 16]); Bs5c = Buf()
        s5d = sb([128, 2, 2, 16]); Bs5d = [Buf(), Buf()]; S5PAR = [0, 0, 0, 0]; Bcr = Buf()
        V(lambda e: e.memset(s5c[:], 0.0), (), [Bs5c])
        V(lambda e: e.memset(s5d[:], 0.0), (), Bs5d)
        Usf = sb([128, 16, 129]); Usb = sb([128, 16, 129], BF16); BUs = Buf()
        qz = sb([128, 16, 128], BF16); kz = sb([128, 16, 128], BF16); nbs = kz
        Bqz = Buf(); Bkz = Buf(); Bnbs = Bkz
        stb = [sb([128, 128], BF16) for _ in range(2)]; Bst = [Buf() for _ in range(2)]
        khb = [sb([128, 128], BF16) for _ in range(2)]; Bkh = [Buf() for _ in range(2)]
        ektm = sb([128, 6, 4]); Bek = Buf()
        decbc = sb([128, 4, 24]); Bdec = Buf()
        decrow = sb([4, 24]); mxe = sb([4, 8]); ms0 = sb([4, 16]); msout = sb([4, 17]); Bsm = Buf()
        x0s = Fs[5][:, 0:512].rearrange("p (a b c) -> p a b c", a=2, b=16); Bx0 = BF[5]
        s5so = Usf[:].rearrange("p a b -> p (a b)")[:, 0:512].rearrange("p (a b c) -> p a b c", a=2, b=16); s5po = sb([128, 2, 16]); Bs5o = Buf()
        Bes = Buf()
        cbs = sb([128, 2, 4, 128], BF16); Bcbs = Buf()
        pt4all = sb([128, 2048], BF16)
        pt4 = [pt4all[:, q_ * 512:(q_ + 1) * 512] for q_ in range(4)]
        gT2 = pt4all[:, 0:NTM]; vtm2 = pt4all[:, NTM:NTM + 774].rearrange("p (a b) -> p a b", a=6); Bg2 = Buf(); Bv2 = Buf()
        hdec = sb([128, 2, 24]); Bhd = Buf()
        Bprb = [Buf() for _ in range(2)]; Bt4 = [Buf() for _ in range(4)]; Bcsb = [Buf() for _ in range(2)]; Bp4 = [Buf() for _ in range(4)]
        S5FINE = Bprb + Bt4 + Bcsb + Bp4
        qzf = qz[:].rearrange("p a b -> p (a b)"); kzf = kz[:].rearrange("p a b -> p (a b)"); usbf = Usb[:].rearrange("p a b -> p (a b)")
        wtb = [[sb([128, 512], BF16) for _ in range(2)] for _ in range(2)]; Bwt = [Buf() for _ in range(2)]
        xtb = [[sb([128, 512], BF16) for _ in range(2)] for _ in range(2)]; Bxt = [Buf() for _ in range(2)]
        cch = sb([128, 6, 4]); Bcch = Buf()
        bct = sb([128, 2, 2, 4, 128], BF16); Bbct = [Buf() for _ in range(4)]; bcsem = [S.new_dma_sem() for _ in range(4)]
        pTb = sb([128, 2, NTM], BF16); BpT = Buf(); pTsem = S.new_dma_sem()

        def blocks(ntot):
            out = []
            o = 0
            while o < ntot:
                n = min(512, ntot - o)
                out.append((o, n)); o += n
            return out

        def rmsnorm(gname, NT, final_out=None):
            for bi, (o, n) in enumerate(blocks(NT)):
                ps, bp = PS()
                for c in range(8):
                    q = sqb[c % 2]; bq = Bsq[c % 2]
                    A(lambda e, c=c, q=q: e.activation(q[:, 0:n], h[:, c, o:o + n], AF.Square), [Bh[c][bi]], [bq])
                    P(lambda e, c=c, q=q, ps=ps: e.matmul(ps[:, 0:n], onesb[:], q[:, 0:n], start=(c == 0), stop=(c == 7)),
                      [bq, Bc], [bp])
                rs = NT5[4]
                A(lambda e, ps=ps: e.activation(rs[:, 0:n], ps[:, 0:n], AF.Ln, scale=1.0 / 1024, bias=CEPS), [bp], [BN[4]])
                A(lambda e: e.activation(rs[:, 0:n], rs[:, 0:n], AF.Exp, scale=-0.5), [BN[4]], [BN[4]])
                for c in range(8):
                    if final_out is None:
                        V(lambda e, c=c: e.scalar_tensor_tensor(hn[:, c, o:o + n], h[:, c, o:o + n], pv(gname, c), rs[:, 0:n],
                                                                ALU.mult, ALU.mult), [Bh[c][bi], BN[4], Bc], [Bhn[bi]])
                    else:
                        t = NT5[c % 2]
                        V(lambda e, c=c, t=t: e.scalar_tensor_tensor(t[:, 0:n], h[:, c, o:o + n], pv(gname, c), rs[:, 0:n],
                                                                     ALU.mult, ALU.mult), [Bh[c][bi], BN[4], Bc], [BN[c % 2]])
                        STO(final_out[c * 128:(c + 1) * 128, o:o + n], t[:, 0:n], [BN[c % 2]])

        def proj_fm(slot, bs, col, NT, evac, rhs=None, nk=8, rbufs=None):
            for bi, (o, n) in enumerate(blocks(NT)):
                ps, bp = PS()
                for k in range(nk):
                    src = hn if rhs is None else rhs
                    P(lambda e, k=k, ps=ps, src=src: e.matmul(ps[:, 0:n], slot[:, k, col:col + 128], src[:, k, o:o + n],
                                                             start=(k == 0), stop=(k == nk - 1)),
                      bs + ([Bhn[bi]] if rbufs is None else rbufs(bi)), [bp])
                evac(ps, bp, bi, o, n)

        def resid_proj(w_ap, NT, src, srcb):
            for u in range(2):
                slot, bs = wload([(w_ap[:, u * 512:(u + 1) * 512], 0)], 8)
                for oc in range(4):
                    c = u * 4 + oc

                    def ev(ps, bp, bi, o, n, c=c):
                        V(lambda e: e.tensor_tensor(h[:, c, o:o + n], h[:, c, o:o + n], ps[:, 0:n], ALU.add),
                          [bp, Bh[c][bi]], [Bh[c][bi]])
                    proj_fm(slot, bs, oc * 128, NT, ev, rhs=src, rbufs=lambda bi: [srcb[k][bi] for k in range(8)])

        def att_tile(qT, kT, bq, bk, col, vt, E, si, ek, dec, PT, bPT, pcol, sample, den=None, mlstm_h=None, usbuf=None, bv=None):
            Bv = bv
            ps, bp = PSS()
            P(lambda e: e.matmul(ps[:, 0:128], kT[:, col:col + 128], qT[:, col:col + 128], start=True, stop=True),
              [bq, bk], [bp])
            i2 = att_tile.k % 2
            att_tile.k += 1
            sT = stb[i2]; bsT = Bst[i2]
            msk = maskb if sample else maskc
            if ek is not None:
                V(lambda e: e.scalar_tensor_tensor(sT[:], ps[:, 0:128], ek, msk[:], ALU.mult, ALU.mult), [bp, Bek, Bc], [bsT])
            else:
                V(lambda e: e.tensor_tensor(sT[:], ps[:, 0:128], msk[:], ALU.mult), [bp, Bc], [bsT])
            P(lambda e: e.matmul(PT[:, pcol:pcol + 128], vt[:, 0:128], sT[:], start=True, stop=False), [Bv, bsT], [bPT])
            if not sample:
                P(lambda e: e.matmul(PT[:, pcol:pcol + 128], Upb[:, si, 0:128], qT[:, col:col + 128], start=False, stop=True),
                  [BU[si], bq], [bPT])
            else:
                for j in range(16):
                    P(lambda e, j=j: e.matmul(PT[:, pcol:pcol + 128], Usb[:, j, 0:128], qz[:, j, :], start=False, stop=(j == 15)),
                      [BUs, Bqz], [bPT])
            if den is not None:
                dps, bd = den
                P(lambda e: e.matmul(dps[:, pcol:pcol + 128], onesb[:], sT[:], start=True, stop=False), [Bc, bsT], [bd])
                if not sample:
                    P(lambda e: e.matmul(dps[:, pcol:pcol + 128], nbc[:, mlstm_h, :], qT[:, col:col + 128], start=False, stop=True),
                      [BU[si], bq], [bd])
                else:
                    for j in range(16):
                        P(lambda e, j=j: e.matmul(dps[:, pcol:pcol + 128], nbs[:, j, :], qz[:, j, :], start=False, stop=(j == 15)),
                          [Bnbs, Bqz], [bd])
            P(lambda e: e.transpose(ptb[:, 0:128], kT[:, col:col + 128], identb[:]), [bk, Bc], [Bpt])
            kh = khb[i2]; bkh = Bkh[i2]
            if ek is not None:
                A(lambda e: e.activation(kh[:], ptb[:, 0:128], AF.Copy, scale=ek), [Bpt, Bek], [bkh])
            else:
                A(lambda e: e.copy(kh[:], ptb[:, 0:128]), [Bpt], [bkh])
            if not sample:
                ps2, bp2 = PSS()
                P(lambda e: e.matmul(ps2[:, 0:E], ident[:], Up[:, si, 0:E], start=True, stop=False), [Bc, BU[si]], [bp2])
                P(lambda e: e.matmul(ps2[:, 0:E], kh[:], vt[:, 0:E], start=False, stop=True), [bkh, Bv], [bp2])
                A(lambda e: e.activation(Up[:, si, 0:E], ps2[:, 0:E], AF.Copy, scale=dec), [bp2, Bdec, Bhd], [BU[si]])
                V(lambda e: e.tensor_copy(Upb[:, si, 0:E], Up[:, si, 0:E]), [BU[si]], [BU[si]])
                if mlstm_h is not None:
                    V(lambda e: e.tensor_copy(nbc[:, mlstm_h, :], Up[:, si, 128:129].broadcast_to([128, 128])), [BU[si]], [BU[si]])
            else:
                V(lambda e: e.tensor_tensor(kz[:], kh[:].unsqueeze(1).broadcast_to([128, 16, 128]),
                                            pv("rowm", 0, 16).unsqueeze(2).broadcast_to([128, 16, 128]), ALU.mult),
                  [bkh, Bc], [Bkz])
                for j in range(16):
                    ps2, bp2 = PSS()
                    P(lambda e, j=j, ps2=ps2: e.matmul(ps2[:, 0:E], ident[:], Usf[:, j, 0:E], start=True, stop=False), [Bc, BUs], [bp2])
                    P(lambda e, j=j, ps2=ps2: e.matmul(ps2[:, 0:E], kz[:, j, :], vt[:, 0:E], start=False, stop=True), [Bkz, Bv], [bp2])
                    A(lambda e, j=j, ps2=ps2: e.activation(usbuf[:, j, 0:E], ps2[:, 0:E], AF.Copy, scale=dec(j)),
                      [bp2, Bdec, Bhd], [BUs])
        att_tile.k = 0

        def load_sample_state(src, hh, E):
            LD(Usf[:, :, 0:E], src[:, hh, :, :].rearrange("j d e -> d j e"), [BUs])
            V(lambda e: e.tensor_copy(Usb[:, :, 0:E], Usf[:, :, 0:E]), [BUs], [BUs])

        def make_qz(qT, bq, col):
            V(lambda e: e.tensor_tensor(qz[:], qT[:, col:col + 128].unsqueeze(1).broadcast_to([128, 16, 128]), blk3[:], ALU.mult),
              [bq, Bc], [Bqz])

        def vproj(slot, bs, col, NT, E, vt_, bv_):
            nt = NT // 128
            for c in range(nt):
                ps, bp = PS()
                for k in range(8):
                    P(lambda e, k=k, ps=ps: e.matmul(ps[:, 0:128], hn[:, k, c * 128:(c + 1) * 128], slot[:, k, col:col + 128],
                                                     start=(k == 0), stop=(k == 7)), bs + [Bhn[(c * 128) // 512]], [bp])
                A(lambda e, ps=ps: e.copy(vt_[:, c, 0:128], ps[:, 0:128]), [bp], [bv_])

        def rstd_from(sq_src_fn, n, srcb):
            q = sqb[0]
            sq_src_fn(q)
            ps, bp = PSS()
            P(lambda e: e.matmul(ps[:, 0:n], onesb[:], q[:, 0:n], start=True, stop=True), [Bsq[0], Bc], [bp])
            rs = NT5[3]
            A(lambda e: e.activation(rs[:, 0:n], ps[:, 0:n], AF.Ln, scale=1.0 / 128, bias=CEPS), [bp], [BN[3]])
            A(lambda e: e.activation(rs[:, 0:n], rs[:, 0:n], AF.Exp, scale=-0.5), [BN[3]], [BN[3]])
            return rs

        def build_tab_kc(kc_):
            build_tables(kc_, pv("s0"), jrow[:, 2, :], lambda part: tbe[:, part, :], lambda part: tbz[:, part, :, :], [Btab], [Btab])
            LD(tabE[:, :, kc_ * 512:(kc_ + 1) * 512], tbe[:], [Bscr], r=[Btab])
            LD(tabZ[:, :, 4 * kc_:4 * kc_ + 4, :], tbz[:], [Bscr], r=[Btab])
        STEP = [None]

        def step():
            g = STEP[0]
            if g is not None:
                try:
                    next(g)
                except StopIteration:
                    STEP[0] = None
        CUT = int(os.environ.get("KCUT", "0"))

        class _Stop(Exception):
            pass

        def ck(k):
            if CUT == k:
                raise _Stop()
        try:
            for sbi, (tok0, NP, has_s) in enumerate(SBS):
                NT = NP + (128 if has_s else 0)
                ntp = NP // 128
                blks = blocks(NT)
                last = (sbi == len(SBS) - 1)
                LD(segm[:, 0:NTM], D["segm"][sbi], [Bc], r=[Bc])
                for c in range(8):
                    for bi, (o, n) in enumerate(blks):
                        LD(h[:, c, o:o + n], D["xT"][c * 128:(c + 1) * 128, tok0 + o:tok0 + o + n], [Bh[c][bi]])
                for layer in range(2):
                    rmsnorm("nmix%d" % layer, NT)
                    ck(1)
                    if layer == 0:
                        wgs, bwg = wload([(D["wg"], 0)], 8)
                        A1, A2, A3, A4 = Fs[2], Fs[3], Fs[5], Fs[4]
                        b1, b2, b3, b4 = BF[2], BF[3], BF[5], BF[4]
                        for bi, (o, n) in enumerate(blks):
                            ps, bp = PS()
                            for k in range(8):
                                P(lambda e, k=k, ps=ps: e.matmul(ps[0:4, 0:n], wgs[:, k, 0:4], hn[:, k, o:o + n], start=(k == 0), stop=(k == 7)),
                                  bwg + [Bhn[bi]], [bp])
                            A(lambda e, ps=ps: e.activation(A1[0:4, o:o + n], ps[0:4, 0:n], AF.Identity, bias=bg[:, 0:1]), [bp, Bc], [b1])
                            ps, bp = PS()
                            for k in range(8):
                                P(lambda e, k=k, ps=ps: e.matmul(ps[0:4, 0:n], wgs[:, k, 4:8], hn[:, k, o:o + n], start=(k == 0), stop=(k == 7)),
                                  bwg + [Bhn[bi]], [bp])
                            A(lambda e, ps=ps: e.activation(A2[0:4, o:o + n], ps[0:4, 0:n], AF.Exp, scale=-1.0, bias=nbg[:, 0:1]), [bp, Bc], [b2])
                        A(lambda e: e.activation(A2[0:4, 0:NT], A2[0:4, 0:NT], AF.Ln, bias=CONE[0:4]), [b2], [b2])
                        V(lambda e: e.memset(A4[0:4, 0:NT], 1.0), (), [b4])
                        V(lambda e: e.tensor_tensor_scan(A3[0:4, 0:NP], A4[0:4, 0:NP], A2[0:4, 0:NP], carr[:, 0:1], ALU.mult, ALU.add),
                          [b2, b4, Bcar], [b3])
                        if has_s:
                            V(lambda e: e.tensor_tensor_scan(A3[0:4, NP:NT], segm[0:4, NP:NT], A2[0:4, NP:NT], 0.0, ALU.mult, ALU.add),
                              [b2, Bc], [b3])
                        V(lambda e: e.tensor_tensor(A1[0:4, 0:NT], A1[0:4, 0:NT], A3[0:4, 0:NT], ALU.add), [b1, b3], [b1])
                        V(lambda e: e.memset(A4[0:4, 0:NT], 0.0), (), [b4])
                        V(lambda e: e.tensor_tensor_scan(A2[0:4, 0:NP], A4[0:4, 0:NP], A1[0:4, 0:NP], carr[:, 1:2], ALU.add, ALU.max),
                          [b1, b4, Bcar], [b2])
                        V(lambda e: e.tensor_copy(mxe[:, 0:1], carr[:, 1:2]), [Bcar], [Bsm])
                        V(lambda e: e.tensor_copy(mxe[:, 1:1 + ntp], A2[0:4, 0:NP].rearrange("p (c t) -> p c t", t=128)[:, :, 127]), [b2], [Bsm])
                        if has_s:
                            LD(ms0[:], D["ms"], [Bsm])
                            V(lambda e: e.tensor_copy(A4[0:4, NP:NT], A1[0:4, NP:NT]), [b1], [b4])
                            g3 = A4[0:4, NP:NT].rearrange("p (j l) -> p j l", l=8)
                            V(lambda e: e.tensor_tensor(g3[:, :, 0], g3[:, :, 0], ms0[:], ALU.max), [b4, Bsm], [b4])
                            V(lambda e: e.tensor_tensor_scan(A2[0:4, NP:NT], negm[:], A4[0:4, NP:NT], 0.0, ALU.add, ALU.max), [b4, Bc], [b2])
                        V(lambda e: e.tensor_copy(A4[0:4, 0:NP].rearrange("p (c t) -> p c t", t=128),
                                                  mxe[:, 0:ntp].unsqueeze(2).broadcast_to([4, ntp, 128])), [Bsm], [b4])
                        if has_s:
                            V(lambda e: e.tensor_copy(A4[0:4, NP:NT].rearrange("p (j l) -> p j l", l=8),
                                                      ms0[:].unsqueeze(2).broadcast_to([4, 16, 8])), [Bsm], [b4])
                        V(lambda e: e.tensor_tensor(decrow[:, 0:ntp], mxe[:, 0:ntp], mxe[:, 1:1 + ntp], ALU.subtract), [Bsm], [Bsm])
                        if has_s:
                            V(lambda e: e.tensor_tensor(decrow[:, 8:24], ms0[:], A2[0:4, NP:NT].rearrange("p (j l) -> p j l", l=8)[:, :, 7],
                                                        ALU.subtract), [Bsm, b2], [Bsm])
                        else:
                            V(lambda e: e.memset(decrow[:, 8:24], 0.0), (), [Bsm])
                        if ntp < 8:
                            V(lambda e: e.memset(decrow[:, ntp:8], 0.0), (), [Bsm])
                        A(lambda e: e.activation(decrow[:], decrow[:], AF.Exp), [Bsm], [Bsm])
                        ps, bp = PSS()
                        for hh in range(4):
                            P(lambda e, hh=hh, ps=ps: e.matmul(ps[:, hh * 24:(hh + 1) * 24], sel[:, hh, :], decrow[:], start=True, stop=True),
                              [Bc, Bsm], [bp])
                        V(lambda e, ps=ps: e.tensor_copy(decbc[:].rearrange("p a b -> p (a b)"), ps[:, 0:96]), [bp], [Bdec])
                        if last:
                            V(lambda e: e.tensor_tensor(msout[:, 16:17], A2[0:4, NP - 1:NP], A3[0:4, NP - 1:NP], ALU.subtract), [b2, b3], [Bsm])
                            V(lambda e: e.tensor_tensor(msout[:, 0:16], A2[0:4, NP:NT].rearrange("p (j l) -> p j l", l=8)[:, :, 7],
                                                        A3[0:4, NP:NT].rearrange("p (j l) -> p j l", l=8)[:, :, 7], ALU.subtract), [b2, b3], [Bsm])
                            STO(D["mp"], msout[:, 16:17], [Bsm]); STO(D["ms_o"], msout[:, 0:16], [Bsm])
                        V(lambda e: e.tensor_copy(carr[:, 0:1], A3[0:4, NP - 1:NP]), [b3], [Bcar])
                        V(lambda e: e.tensor_copy(carr[:, 1:2], A2[0:4, NP - 1:NP]), [b2], [Bcar])
                        V(lambda e: e.tensor_tensor(A3[0:4, 0:NT], A4[0:4, 0:NT], A3[0:4, 0:NT], ALU.subtract), [b3, b4], [b3])
                        V(lambda e: e.tensor_tensor(A1[0:4, 0:NT], A1[0:4, 0:NT], A4[0:4, 0:NT], ALU.subtract), [b1, b4], [b1])
                        A(lambda e: e.activation(A1[0:4, 0:NT], A1[0:4, 0:NT], AF.Exp, bias=CLNK[0:4]), [b1], [b1])
                        ps, bp = PSS()
                        for c in range(NT // 128):
                            P(lambda e, c=c, ps=ps: e.matmul(ps[:, c * 4:(c + 1) * 4], A1[0:4, c * 128:(c + 1) * 128], ident[0:4, 0:4],
                                                             start=True, stop=True), [b1, Bc], [bp])
                        V(lambda e, ps=ps: e.tensor_copy(ektm[:, 0:NT // 128, :].rearrange("p a b -> p (a b)"), ps[:, 0:4 * (NT // 128)]),
                          [bp], [Bek])
                        ck(2)
                        C0, S0 = Fs[6], Fs[7]
                        LD(Fs[8][:, 0:NT], D["posrow"][sbi][:, 0:NT], [BF[8]])
                        V(lambda e: e.tensor_scalar(Fs[8][:, 0:NT], Fs[8][:, 0:NT], pv("invf"), None, ALU.mult), [BF[8], Bc], [BF[8]])
                        sin_of(C0[:, 0:NT], Fs[8][:, 0:NT], 0.5 * PI, Fs[2][:, 0:NT], [BF[8]], [BF[6]], BF[2])
                        sin_of(S0[:, 0:NT], Fs[8][:, 0:NT], 0.0, Fs[2][:, 0:NT], [BF[8]], [BF[7]], BF[2])
                        V(lambda e: e.tensor_scalar(S0[:, 0:NT], S0[:, 0:NT], pv("sgn"), None, ALU.mult), [BF[7], Bc], [BF[7]])
                        V(lambda e: e.memset(vtm[:, :, 128:129], 1.0), (), [Bv])
                        V(lambda e: e.memset(vtm2[:, :, 128:129], 1.0), (), [Bv2])
                        SETS = [(Hs[0], Hs[1], Hs[2], vtm, BH[0], BH[1], BH[2], Bv), (Hs[3], Hs[4], gT2, vtm2, BH[3], BH[4], Bg2, Bv2)]
                        xq, xk, qT, kT, gT = Fs[0], Fs[1], Hs[0], Hs[1], Hs[2]
                        bxq, bxk, bqT, bkT, bgT = BF[0], BF[1], BH[0], BH[1], BH[2]
                        W = D["w_in_ab"]
                        def front_m(hh):
                            qT, kT, gT, vt_, bqT, bkT, bgT, bv_ = SETS[hh % 2]
                            slot, bs = wload([(D["w_ab_h"][:, hh * 512:(hh + 1) * 512], 0)], 8)
                            for (xx, bx, cc) in ((xq, bxq, 0), (xk, bxk, 128)):
                                def ev(ps, bp, bi, o, n, xx=xx, bx=bx):
                                    if o < NP:
                                        A(lambda e: e.copy(xx[:, 3 + o:3 + o + n], ps[:, 0:n]), [bp], [bx])
                                    else:
                                        A(lambda e: e.copy(xx[:, NP + 3:NP + 3 + 176].rearrange("p (j l) -> p j l", l=11)[:, :, 3:11],
                                                           ps[:, 0:128].rearrange("p (j l) -> p j l", l=8)), [bp], [bx])
                                proj_fm(slot, bs, cc, NT, ev)
                                yield

                            def evg(ps, bp, bi, o, n):
                                A(lambda e: e.activation(gT[:, o:o + n], ps[:, 0:n], AF.Sigmoid), [bp], [bgT])
                            proj_fm(slot, bs, 384, NT, evg)
                            yield
                            vproj(slot, bs, 256, NT, 129, vt_, bv_)
                            yield
                            for (xx, bx, ch, oT, boT) in ((xq, bxq, hh, qT, bqT), (xk, bxk, 4 + hh, kT, bkT)):
                                V(lambda e, xx=xx, ch=ch: e.tensor_copy(xx[:, 0:3], tails[:, ch, :]), [Btl], [bx])
                                acc = Fs[2]
                                V(lambda e, xx=xx, ch=ch: e.tensor_scalar(acc[:, 0:NP], xx[:, 0:NP], pv("cw0", ch), pv("cb", ch), ALU.mult, ALU.add),
                                  [bx, Bc], [BF[2]])
                                for j in range(1, 4):
                                    V(lambda e, xx=xx, ch=ch, j=j: e.scalar_tensor_tensor(acc[:, 0:NP], xx[:, j:j + NP], pv("cw%d" % j, ch), acc[:, 0:NP],
                                                                                         ALU.mult, ALU.add), [bx, Bc, BF[2]], [BF[2]])
                                if has_s:
                                    LD(xx[:, NP + 3:NP + 3 + 176].rearrange("p (j l) -> p j l", l=11)[:, :, 0:3],
                                       D["convs"][ch * 128:(ch + 1) * 128], [bx])
                                    xs3 = xx[:, NP + 3:NP + 3 + 176].rearrange("p (j l) -> p j l", l=11)
                                    a3 = acc[:, NP:NT].rearrange("p (j l) -> p j l", l=8)
                                    V(lambda e, xs3=xs3, a3=a3, ch=ch: e.tensor_scalar(a3, xs3[:, :, 0:8], pv("cw0", ch), pv("cb", ch), ALU.mult, ALU.add),
                                      [bx, Bc], [BF[2]])
                                    for j in range(1, 4):
                                        V(lambda e, xs3=xs3, a3=a3, ch=ch, j=j: e.scalar_tensor_tensor(a3, xs3[:, :, j:j + 8], pv("cw%d" % j, ch), a3,
                                                                                                      ALU.mult, ALU.add), [bx, Bc, BF[2]], [BF[2]])
                                    STO(D["convs_o"][ch * 128:(ch + 1) * 128], xs3[:, :, 8:11], [bx])
                                    STO(D["convp"][ch * 128:(ch + 1) * 128], xx[:, NP:NP + 3], [bx])
                                A(lambda e, oT=oT: e.activation(oT[:, 0:NT], acc[:, 0:NT], AF.Silu), [BF[2]], [boT])
                                V(lambda e, xx=xx, ch=ch: e.tensor_copy(tails[:, ch, :], xx[:, NP:NP + 3]), [bx], [Btl])
                                yield

                        def back_m(hh):
                            qT, kT, gT, vt_, bqT, bkT, bgT, bv_ = SETS[hh % 2]
                            if has_s:
                                load_sample_state(D["Us"], hh, 129)
                                make_qz(qT, bqT, NP)
                                V(lambda e: e.tensor_copy(nbs[:], Usf[:, :, 128:129].broadcast_to([128, 16, 128])), [BUs], [Bnbs])
                            for bi, (o, n) in enumerate(blks):
                                PT, bPT = PS()
                                dps, bd = PS()
                                pinned.update((pb.index(PT), pb.index(dps)))
                                for c in range(n // 128):
                                    tcol = o + c * 128
                                    tix = tcol // 128
                                    smp = tcol >= NP
                                    att_tile(qT, kT, bqT, bkT, tcol, vt_[:, tix, :], 129, hh, ektm[:, tix, hh:hh + 1],
                                             (lambda j, hh=hh: decbc[:, hh, 8 + j:9 + j]) if smp else decbc[:, hh, tix:tix + 1],
                                             PT, bPT, c * 128, smp, den=(dps, bd), mlstm_h=hh, usbuf=Usf, bv=bv_)
                                    step()
                                ps, bp = PSS()
                                P(lambda e, ps=ps, hh=hh: e.matmul(ps[:, 0:n], sel[:, hh, :], A3[0:4, o:o + n], start=True, stop=True), [Bc, b3], [bp])
                                dn = NT5[0]
                                A(lambda e, ps=ps: e.activation(dn[:, 0:n], ps[:, 0:n], AF.Exp, scale=-1.0), [bp], [BN[0]])
                                ab = NT5[1]
                                A(lambda e, dps=dps: e.activation(ab[:, 0:n], dps[:, 0:n], AF.Abs), [bd], [BN[1]])
                                V(lambda e: e.tensor_tensor(ab[:, 0:n], ab[:, 0:n], dn[:, 0:n], ALU.max), [BN[0], BN[1]], [BN[1]])
                                V(lambda e: e.reciprocal(ab[:, 0:n], ab[:, 0:n]), [BN[1]], [BN[1]])
                                hv = NT5[2]
                                V(lambda e, PT=PT: e.tensor_tensor(hv[:, 0:n], PT[:, 0:n], ab[:, 0:n], ALU.mult), [bPT, BN[1]], [BN[2]])
                                V(lambda e: e.tensor_tensor(hv[:, 0:n], hv[:, 0:n], gT[:, o:o + n], ALU.mult), [BN[2], bgT], [BN[2]])
                                rs = rstd_from(lambda q: A(lambda e: e.activation(q[:, 0:n], hv[:, 0:n], AF.Square), [BN[2]], [Bsq[0]]), n, None)
                                V(lambda e, hh=hh: e.scalar_tensor_tensor(mix[:, hh, o:o + n], hv[:, 0:n], pv("gna", hh), rs[:, 0:n], ALU.mult, ALU.mult),
                                  [BN[2], BN[3], Bc], [Bmix[hh][bi]])
                                pinned.clear()
                                step()
                            if has_s:
                                STO(D["Us_o"][:, hh, :, :].rearrange("j d e -> d j e"), Usf[:, :, :], [BUs])
                            if last:
                                STO(D["Up"][hh], Up[:, hh, :], [BU[hh]])
                        for _ in front_m(0):
                            pass
                        for hh in range(4):
                            STEP[0] = front_m(hh + 1) if hh + 1 < 4 else None
                            back_m(hh)
                            while STEP[0] is not None:
                                step()
                        ck(3)
                        def front_r(hh):
                            qT, kT, gT, vt_, bqT, bkT, bgT, bv_ = SETS[hh % 2]
                            b0 = 2056
                            slot, bs = wload([(D["w_ab_h"][:, (4 + hh) * 512:(5 + hh) * 512], 0)], 8)
                            for (xx, bx, cc, oT, boT) in ((xq, bxq, 0, qT, bqT), (xk, bxk, 128, kT, bkT)):
                                def ev(ps, bp, bi, o, n, xx=xx, bx=bx):
                                    A(lambda e: e.copy(xx[:, o:o + n], ps[:, 0:n]), [bp], [bx])
                                proj_fm(slot, bs, cc, NT, ev)
                                yield
                                xsw, t1, t2 = Fs[2], Fs[3], Fs[4]
                                A(lambda e, xx=xx: e.copy(xsw[0:64, 0:NT], xx[64:128, 0:NT]), [bx], [BF[2]])
                                A(lambda e, xx=xx: e.copy(xsw[64:128, 0:NT], xx[0:64, 0:NT]), [bx], [BF[2]])
                                V(lambda e, xx=xx: e.tensor_tensor(t1[:, 0:NT], xx[:, 0:NT], C0[:, 0:NT], ALU.mult), [bx, BF[6]], [BF[3]])
                                V(lambda e: e.tensor_tensor(t2[:, 0:NT], xsw[:, 0:NT], S0[:, 0:NT], ALU.mult), [BF[2], BF[7]], [BF[4]])
                                V(lambda e, oT=oT: e.tensor_tensor(oT[:, 0:NT], t1[:, 0:NT], t2[:, 0:NT], ALU.add), [BF[3], BF[4]], [boT])

                            def evg(ps, bp, bi, o, n):
                                A(lambda e: e.activation(gT[:, o:o + n], ps[:, 0:n], AF.Silu), [bp], [bgT])
                            proj_fm(slot, bs, 384, NT, evg)
                            yield
                            vproj(slot, bs, 256, NT, 128, vt_, bv_)
                            yield

                        def back_r(hh):
                            qT, kT, gT, vt_, bqT, bkT, bgT, bv_ = SETS[hh % 2]
                            if has_s:
                                load_sample_state(D["rets"], hh, 128)
                                make_qz(qT, bqT, NP)
                            g128 = math.exp(128 * LG[hh]); g8 = math.exp(8 * LG[hh])
                            for bi, (o, n) in enumerate(blks):
                                PT, bPT = PS()
                                pinned.add(pb.index(PT))
                                for c in range(n // 128):
                                    tcol = o + c * 128
                                    tix = tcol // 128
                                    smp = tcol >= NP
                                    att_tile(qT, kT, bqT, bkT, tcol, vt_[:, tix, :], 128, 4 + hh, gk[:, 1 if smp else 0, hh:hh + 1],
                                             (lambda j, g8=g8: g8) if smp else g128, PT, bPT, c * 128, smp, usbuf=Usf, bv=bv_)
                                    step()
                                hv = NT5[2]
                                if o < NP:
                                    V(lambda e, PT=PT, hh=hh: e.tensor_tensor(hv[:, 0:n].rearrange("p (c t) -> p c t", t=128),
                                                                             PT[:, 0:n].rearrange("p (c t) -> p c t", t=128),
                                                                             Gq[:, 0, hh, :].unsqueeze(1).broadcast_to([128, n // 128, 128]), ALU.mult),
                                      [bPT, Bc], [BN[2]])
                                else:
                                    V(lambda e, PT=PT, hh=hh: e.tensor_tensor(hv[:, 0:n], PT[:, 0:n], Gq[:, 1, hh, :], ALU.mult), [bPT, Bc], [BN[2]])
                                rs = rstd_from(lambda q: A(lambda e: e.activation(q[:, 0:n], hv[:, 0:n], AF.Square), [BN[2]], [Bsq[0]]), n, None)
                                V(lambda e: e.tensor_tensor(hv[:, 0:n], hv[:, 0:n], rs[:, 0:n], ALU.mult), [BN[2], BN[3]], [BN[2]])
                                V(lambda e, hh=hh: e.tensor_tensor(mix[:, 4 + hh, o:o + n], hv[:, 0:n], gT[:, o:o + n], ALU.mult),
                                  [BN[2], bgT], [Bmix[4 + hh][bi]])
                                pinned.clear()
                                step()
                            if has_s:
                                STO(D["rets_o"][:, hh, :, :].rearrange("j d e -> d j e"), Usf[:, :, 0:128], [BUs])
                            if last:
                                STO(D["retp"][hh], Up[:, 4 + hh, 0:128], [BU[4 + hh]])
                        for _ in front_r(0):
                            pass
                        for hh in range(4):
                            STEP[0] = front_r(hh + 1) if hh + 1 < 4 else None
                            back_r(hh)
                            while STEP[0] is not None:
                                step()
                        resid_proj(D["w_out_ab"][0] if False else D["w_out_ab"], NT, mix, Bmix)
                        ck(4)
                    else:
                        W = D["w_in_cd"]
                        qf, ff, eb, enb, tmp = Fs[0], Fs[1], Fs[2], Fs[3], Fs[4]
                        qT, kT, gT = Hs[0], Hs[1], Hs[2]
                        bqT, bkT, bgT = BH[0], BH[1], BH[2]
                        def front_h(hh):
                            qT, kT, gT, vt_, bqT, bkT, bgT, bv_ = SETS[hh % 2]
                            slot, bs = wload([(D["w_cd_h"][:, hh * 512:(hh + 1) * 512], 0)], 8)

                            def evq(ps, bp, bi, o, n):
                                A(lambda e: e.activation(qf[:, o:o + n], ps[:, 0:n], AF.Copy, scale=128.0 ** -0.5), [bp], [BF[0]])
                            proj_fm(slot, bs, 0, NT, evq)
                            yield

                            def evf(ps, bp, bi, o, n):
                                A(lambda e: e.activation(ff[:, o:o + n], ps[:, 0:n], AF.Sigmoid), [bp], [BF[1]])
                            proj_fm(slot, bs, 128, NT, evf)
                            yield

                            def evg(ps, bp, bi, o, n):
                                A(lambda e: e.activation(gT[:, o:o + n], ps[:, 0:n], AF.Silu), [bp], [bgT])
                            proj_fm(slot, bs, 384, NT, evg)
                            yield
                            vproj(slot, bs, 256, NT, 128, vt_, bv_)
                            yield
                            V(lambda e, hh=hh: e.tensor_scalar(ff[:, 0:NT], ff[:, 0:NT], oml[:, hh:hh + 1], lb[:, hh:hh + 1], ALU.mult, ALU.add),
                              [BF[1], Bc], [BF[1]])
                            A(lambda e: e.activation(tmp[:, 0:NT], ff[:, 0:NT], AF.Ln), [BF[1]], [BF[4]])
                            V(lambda e: e.tensor_tensor_scan(eb[:, 0:NT], segm[:, 0:NT], tmp[:, 0:NT], 0.0, ALU.mult, ALU.add), [BF[4], Bc], [BF[2]])
                            A(lambda e: e.activation(enb[:, 0:NT], eb[:, 0:NT], AF.Exp, scale=-1.0), [BF[2]], [BF[3]])
                            A(lambda e: e.activation(eb[:, 0:NT], eb[:, 0:NT], AF.Exp), [BF[2], BF[3]], [BF[2]])
                            V(lambda e, hh=hh: e.tensor_copy(hdec[:, hh % 2, 0:ntp], eb[:, 0:NP].rearrange("p (c t) -> p c t", t=128)[:, :, 127]), [BF[2]], [Bhd])
                            if has_s:
                                V(lambda e, hh=hh: e.tensor_copy(hdec[:, hh % 2, 8:24], eb[:, NP:NT].rearrange("p (j l) -> p j l", l=8)[:, :, 7]), [BF[2]], [Bhd])
                            V(lambda e: e.tensor_scalar(ff[:, 0:NT], ff[:, 0:NT], -1.0, 1.0, ALU.mult, ALU.add), [BF[1], BF[4]], [BF[1]])
                            V(lambda e: e.tensor_tensor(qT[:, 0:NT], qf[:, 0:NT], eb[:, 0:NT], ALU.mult), [BF[0], BF[2]], [bqT])
                            V(lambda e: e.tensor_tensor(kT[:, 0:NT], ff[:, 0:NT], enb[:, 0:NT], ALU.mult), [BF[1], BF[3]], [bkT])

                        def back_h(hh):
                            qT, kT, gT, vt_, bqT, bkT, bgT, bv_ = SETS[hh % 2]
                            if has_s:
                                load_sample_state(D["hgrns"], hh, 128)
                                make_qz(qT, bqT, NP)
                            for bi, (o, n) in enumerate(blks):
                                PT, bPT = PS()
                                pinned.add(pb.index(PT))
                                for c in range(n // 128):
                                    tcol = o + c * 128
                                    tix = tcol // 128
                                    smp = tcol >= NP
                                    att_tile(qT, kT, bqT, bkT, tcol, vt_[:, tix, :], 128, 8 + hh, None,
                                             (lambda j, hh=hh: hdec[:, hh % 2, 8 + j:9 + j]) if smp else hdec[:, hh % 2, tix:tix + 1],
                                             PT, bPT, c * 128, smp, usbuf=Usf, bv=bv_)
                                    step()
                                rs = rstd_from(lambda q, PT=PT, bPT=bPT: A(lambda e: e.activation(q[:, 0:n], PT[:, 0:n], AF.Square), [bPT], [Bsq[0]]), n, None)
                                hv = NT5[2]
                                V(lambda e, PT=PT: e.tensor_tensor(hv[:, 0:n], PT[:, 0:n], rs[:, 0:n], ALU.mult), [bPT, BN[3]], [BN[2]])
                                V(lambda e, hh=hh: e.scalar_tensor_tensor(mix[:, hh, o:o + n], hv[:, 0:n], pv("gnc", hh), gT[:, o:o + n], ALU.mult, ALU.mult),
                                  [BN[2], bgT, Bc], [Bmix[hh][bi]])
                                pinned.clear()
                                step()
                            if has_s:
                                STO(D["hgrns_o"][:, hh, :, :].rearrange("j d e -> d j e"), Usf[:, :, 0:128], [BUs])
                            if last:
                                STO(D["hgrnp"][hh], Up[:, 8 + hh, 0:128], [BU[8 + hh]])
                        for _ in front_h(0):
                            pass
                        for hh in range(4):
                            STEP[0] = front_h(hh + 1) if hh + 1 < 4 else None
                            back_h(hh)
                            while STEP[0] is not None:
                                step()
                        ck(7)
                        suf = Fs[8]; bsuf = BF[8]
                        V(lambda e: e.memset(cch[:, 5, 0:1], 0.0), (), S5FINE + [BUs, Bkz, Bqz, Bcch, Bg2, Bv2])
                        sub = Hs[0]; bsub = BH[0]
                        if has_s:
                            LD(x0s[:, 0], D["x0re"], [Bx0]); LD(x0s[:, 1], D["x0im"], [Bx0])
                            bz = lambda v: v.unsqueeze(2).broadcast_to([128, 16, 16])
                            u1 = Fs[6][:, 0:256].rearrange("p (a b) -> p a b", a=16); u2 = Fs[7][:, 0:256].rearrange("p (a b) -> p a b", a=16)
                            u3 = Fs[6][:, 256:512].rearrange("p (a b) -> p a b", a=16); u4 = Fs[7][:, 256:512].rearrange("p (a b) -> p a b", a=16)
                            for (cr, ci) in ((izr, izi), (ar_, ai_)):
                                V(lambda e, cr=cr: e.tensor_tensor(u1, x0s[:, 0], bz(cr), ALU.mult), [Bx0, Bc], [BF[6]])
                                V(lambda e, ci=ci: e.tensor_tensor(u2, x0s[:, 1], bz(ci), ALU.mult), [Bx0, Bc], [BF[7]])
                                V(lambda e, ci=ci: e.tensor_tensor(u3, x0s[:, 0], bz(ci), ALU.mult), [Bx0, Bc], [BF[6]])
                                V(lambda e, cr=cr: e.tensor_tensor(u4, x0s[:, 1], bz(cr), ALU.mult), [Bx0, Bc], [BF[7]])
                                V(lambda e: e.tensor_tensor(x0s[:, 0], u1, u2, ALU.subtract), [BF[6], BF[7]], [Bx0])
                                V(lambda e: e.tensor_tensor(x0s[:, 1], u3, u4, ALU.add), [BF[6], BF[7]], [Bx0])
                        ntl = NT // 128
                        slot_su, bs_su = wload([(W[:, 2048:2560], 0)], 8)
                        for oc in range(4):
                            slot, bs = slot_su, bs_su

                            def evs(ps, bp, bi, o, n):
                                A(lambda e: e.copy(suf[:, o:o + n], ps[:, 0:n]), [bp], [bsuf])
                                A(lambda e: e.copy(sub[:, o:o + n], ps[:, 0:n]), [bp], [bsub])
                            proj_fm(slot, bs, oc * 128, NT, evs)
                            for part in range(2):
                                S.dma("pool", lambda e, part=part, oc=oc: e.dma_start(out=bct[:, 0, part], in_=D["BT"][part, oc * 4:(oc + 1) * 4].rearrange("i k m -> k i m")),
                                      bcsem[part * 2], (), (Bbct if part == 0 else [Bbct[part * 2]]))
                                S.dma("pool", lambda e, part=part, oc=oc: e.dma_start(out=bct[:, 1, part], in_=D["CT"][part, oc * 4:(oc + 1) * 4].rearrange("i k m -> k i m")),
                                      bcsem[part * 2 + 1], (), [Bbct[part * 2 + 1]])
                            V(lambda e: e.tensor_scalar(bct[:, 1, 1], bct[:, 1, 1], -1.0, None, ALU.mult), Bbct, [Bbct[3]])
                            LD(tbe[:], tabE[:, :, oc * 512:(oc + 1) * 512], [Btab], r=[Bscr])
                            LD(tbz[:], tabZ[:, :, 4 * oc:4 * oc + 4, :], [Btab], r=[Bscr])
                            if has_s:
                                EsT = [Hs[1][:, 0:512], Hs[2][:, 0:512]]
                                ZsT = [Hs[3][:, 0:512], Hs[4][:, 0:512]]
                                build_tables(oc, pv("s8"), jrow[:, 3, :], lambda part: EsT[part],
                                             lambda part: ZsT[part].rearrange("p (a b) -> p a b", a=4), [Bes, BH[1], BH[2]], [Bes, BH[3], BH[4]])
                                for part in range(2):
                                    V(lambda e, part=part, oc=oc: e.tensor_copy(cbs[:, part].rearrange("p a (j l) -> p a j l", l=8),
                                                                             x0s[:, part, 4 * oc:4 * oc + 4, :].unsqueeze(3).broadcast_to([128, 4, 16, 8])),
                                      [Bx0], [Bcbs])
                            i4 = slice(4 * oc, 4 * oc + 4)
                            ypsh = [None]

                            def stA(c):
                                smp = c * 128 >= NP; tc0 = c * 128; k2 = c % 2
                                wrb, wib = wtb[k2][0], wtb[k2][1]; xrb, xib = xtb[k2][0], xtb[k2][1]
                                bE = Bes if smp else Btab
                                smp = c * 128 >= NP
                                tc0 = c * 128
                                Er = EsT[0] if smp else tbe[:, 0, :]
                                Ei = EsT[1] if smp else tbe[:, 1, :]
                                bE = Bes if smp else Btab
                                k2 = c % 2
                                pr, bpr = PS(); pi_, bpi = PS()
                                P(lambda e, pr=pr: e.matmul(pr[:, 0:512], sub[:, tc0:tc0 + 128], bct[:, 0, 0].rearrange("p a b -> p (a b)"), start=True, stop=True),
                                  Bbct + [bsub], [bpr])
                                P(lambda e, pi_=pi_: e.matmul(pi_[:, 0:512], sub[:, tc0:tc0 + 128], bct[:, 0, 1].rearrange("p a b -> p (a b)"), start=True, stop=True),
                                  Bbct + [bsub], [bpi])
                                prb = qzf[:, (2 * k2) * 512:(2 * k2 + 1) * 512]; pib = qzf[:, (2 * k2 + 1) * 512:(2 * k2 + 2) * 512]
                                A(lambda e, pr=pr: e.copy(prb, pr[:, 0:512]), [bpr], [Bprb[k2]])
                                A(lambda e, pi_=pi_: e.copy(pib, pi_[:, 0:512]), [bpi], [Bprb[k2]])
                                wrb, wib = wtb[k2][0], wtb[k2][1]
                                ta, tb, tc_, td = [kzf[:, q_ * 512:(q_ + 1) * 512] for q_ in range(4)]
                                V(lambda e: e.tensor_tensor(ta, prb, Er, ALU.mult), [Bprb[k2], bE], [Bt4[0]])
                                V(lambda e: e.tensor_tensor(tb, pib, Ei, ALU.mult), [Bprb[k2], bE], [Bt4[1]])
                                V(lambda e: e.tensor_tensor(wrb[:], ta, tb, ALU.subtract), [Bt4[0], Bt4[1]], [Bwt[k2]])
                                V(lambda e: e.tensor_tensor(tc_, pib, Er, ALU.mult), [Bprb[k2], bE], [Bt4[2]])
                                V(lambda e: e.tensor_tensor(td, prb, Ei, ALU.mult), [Bprb[k2], bE], [Bt4[3]])
                                V(lambda e: e.tensor_tensor(wib[:], tc_, td, ALU.add), [Bt4[2], Bt4[3]], [Bwt[k2]])

                            TC = {}

                            def stB1(c):
                                smp = c * 128 >= NP; tc0 = c * 128; k2 = c % 2
                                wrb, wib = wtb[k2][0], wtb[k2][1]; xrb, xib = xtb[k2][0], xtb[k2][1]
                                bE = Bes if smp else Btab
                                csr, bcsr = PSS(); csi, bcsi = PSS()
                                cur = S5PAR[oc]
                                TC[c] = (csr, bcsr, csi, bcsi, cur)
                                msk = maskb if smp else maskc
                                for il in range(4):
                                    P(lambda e, il=il, csr=csr: e.matmul(csr[:, il * 128:(il + 1) * 128], wrb[:, il * 128:(il + 1) * 128], msk[:], start=True, stop=(not smp)),
                                      [Bwt[k2], Bc], [bcsr])
                                    P(lambda e, il=il, csi=csi: e.matmul(csi[:, il * 128:(il + 1) * 128], wib[:, il * 128:(il + 1) * 128], msk[:], start=True, stop=(not smp)),
                                      [Bwt[k2], Bc], [bcsi])
                                    if smp:
                                        P(lambda e, il=il, csr=csr: e.matmul(csr[:, il * 128:(il + 1) * 128], identb[:], cbs[:, 0, il, :], start=False, stop=True), [Bc, Bcbs], [bcsr])
                                        P(lambda e, il=il, csi=csi: e.matmul(csi[:, il * 128:(il + 1) * 128], identb[:], cbs[:, 1, il, :], start=False, stop=True), [Bc, Bcbs], [bcsi])
                                cs3r = csr[:, 0:512].rearrange("p (a b) -> p a b", a=4); cs3i = csi[:, 0:512].rearrange("p (a b) -> p a b", a=4)
                                if not smp:
                                    nxt = 1 - cur
                                    V(lambda e: e.tensor_tensor(cch[:, 0, :], cs3r[:, :, 127], s5d[:, cur, 0, i4], ALU.add), [bcsr, Bs5d[cur]], [Bcr])
                                    V(lambda e: e.tensor_tensor(cch[:, 1, :], cs3i[:, :, 127], s5d[:, cur, 1, i4], ALU.add), [bcsi, Bs5d[cur]], [Bcr])
                                    V(lambda e: e.tensor_tensor(cch[:, 2, :], cch[:, 0, :], a128r[:, i4], ALU.mult), [Bcr, Bc], [Bcch])
                                    V(lambda e: e.tensor_tensor(cch[:, 3, :], cch[:, 1, :], a128i[:, i4], ALU.mult), [Bcr, Bc], [Bcch])
                                    V(lambda e: e.tensor_tensor(cch[:, 4, :], cch[:, 0, :], a128i[:, i4], ALU.mult), [Bcr, Bc], [Bcch])
                                    V(lambda e: e.tensor_tensor(cch[:, 5, :], cch[:, 1, :], a128r[:, i4], ALU.mult), [Bcr, Bc], [Bcch])
                                    V(lambda e: e.tensor_tensor(s5d[:, nxt, 0, i4], cch[:, 2, :], cch[:, 3, :], ALU.subtract), [Bcch], [Bs5d[nxt]])
                                    V(lambda e: e.tensor_tensor(s5d[:, nxt, 1, i4], cch[:, 4, :], cch[:, 5, :], ALU.add), [Bcch], [Bs5d[nxt]])
                                    S5PAR[oc] = nxt

                            def stB2(c):
                                smp = c * 128 >= NP; tc0 = c * 128; k2 = c % 2
                                wrb, wib = wtb[k2][0], wtb[k2][1]; xrb, xib = xtb[k2][0], xtb[k2][1]
                                csr, bcsr, csi, bcsi, cur = TC.pop(c)
                                csbr = usbf[:, (2 * k2) * 512:(2 * k2 + 1) * 512]; csbi = usbf[:, (2 * k2 + 1) * 512:(2 * k2 + 2) * 512]
                                if not smp:
                                    for il in range(4):
                                        i = 4 * oc + il
                                        sl = slice(il * 128, (il + 1) * 128)
                                        A(lambda e, sl=sl, i=i, csr=csr: e.activation(csbr[:, sl], csr[:, sl], AF.Identity, bias=s5d[:, cur, 0, i:i + 1]), [bcsr, Bs5d[cur], Bcr], [Bcsb[k2]])
                                        A(lambda e, sl=sl, i=i, csi=csi: e.activation(csbi[:, sl], csi[:, sl], AF.Identity, bias=s5d[:, cur, 1, i:i + 1]), [bcsi, Bs5d[cur], Bcr], [Bcsb[k2]])
                                    Zr = tbz[:, 0].rearrange("p a b -> p (a b)"); Zi = tbz[:, 1].rearrange("p a b -> p (a b)"); bZ = Btab
                                else:
                                    A(lambda e, csr=csr: e.copy(csbr, csr[:, 0:512]), [bcsr], [Bcsb[k2]])
                                    A(lambda e, csi=csi: e.copy(csbi, csi[:, 0:512]), [bcsi], [Bcsb[k2]])
                                    Zr = ZsT[0]; Zi = ZsT[1]; bZ = Bes
                                p1, p2, p3, p4 = [pt4[q_] for q_ in range(4)]
                                xrb, xib = xtb[k2][0], xtb[k2][1]
                                V(lambda e: e.tensor_tensor(p1[:], csbr, Zr, ALU.mult), [Bcsb[k2], bZ], [Bp4[0]])
                                V(lambda e: e.tensor_tensor(p2[:], csbi, Zi, ALU.mult), [Bcsb[k2], bZ], [Bp4[1]])
                                V(lambda e: e.tensor_tensor(xrb[:], p1[:], p2[:], ALU.subtract), [Bp4[0], Bp4[1]], [Bxt[k2]])
                                V(lambda e: e.tensor_tensor(p3[:], csbr, Zi, ALU.mult), [Bcsb[k2], bZ], [Bp4[2]])
                                V(lambda e: e.tensor_tensor(p4[:], csbi, Zr, ALU.mult), [Bcsb[k2], bZ], [Bp4[3]])
                                V(lambda e: e.tensor_tensor(xib[:], p3[:], p4[:], ALU.add), [Bp4[2], Bp4[3]], [Bxt[k2]])
                                if last and (smp or c == NP // 128 - 1):
                                    p13 = [q_[:].rearrange("p (a b) -> p a b", a=4) for q_ in (p1, p2, p3, p4)]
                                    if not smp:
                                        V(lambda e: e.tensor_tensor(s5po[:, 0, i4], p13[0][:, :, 127], p13[1][:, :, 127], ALU.subtract), [Bp4[0], Bp4[1]], [Bs5o])
                                        V(lambda e: e.tensor_tensor(s5po[:, 1, i4], p13[2][:, :, 127], p13[3][:, :, 127], ALU.add), [Bp4[2], Bp4[3]], [Bs5o])
                                    else:
                                        l7 = lambda q3: q3.rearrange("p a (j l) -> p a j l", l=8)[:, :, :, 7]
                                        V(lambda e: e.tensor_tensor(s5so[:, 0, i4, :], l7(p13[0]), l7(p13[1]), ALU.subtract), [Bp4[0], Bp4[1]], [Bs5o])
                                        V(lambda e: e.tensor_tensor(s5so[:, 1, i4, :], l7(p13[2]), l7(p13[3]), ALU.add), [Bp4[2], Bp4[3]], [Bs5o])

                            def stC(c):
                                smp = c * 128 >= NP; tc0 = c * 128; k2 = c % 2
                                wrb, wib = wtb[k2][0], wtb[k2][1]; xrb, xib = xtb[k2][0], xtb[k2][1]
                                bE = Bes if smp else Btab
                                if c % 4 == 0:
                                    pinned.clear()
                                    ypsh[0] = PS()
                                    pinned.add(pb.index(ypsh[0][0]))
                                yp, byp = ypsh[0]
                                yc = (c % 4) * 128
                                for il in range(4):
                                    P(lambda e, il=il, yp=yp: e.matmul(yp[:, yc:yc + 128], bct[:, 1, 0, il, :], xrb[:, il * 128:(il + 1) * 128], start=(il == 0), stop=False),
                                      Bbct + [Bxt[k2]], [byp])
                                    P(lambda e, il=il, yp=yp: e.matmul(yp[:, yc:yc + 128], bct[:, 1, 1, il, :], xib[:, il * 128:(il + 1) * 128], start=False, stop=(il == 3)),
                                      Bbct + [Bxt[k2]], [byp])
                                if c % 4 == 3 or c == ntl - 1:
                                    o = (c // 4) * 512
                                    n = (c % 4 + 1) * 128
                                    bi = o // 512
                                    yv = NT5[4]
                                    V(lambda e, yp=yp: e.scalar_tensor_tensor(yv[:, 0:n], suf[:, o:o + n], pv("s5d", oc), yp[:, 0:n], ALU.mult, ALU.add),
                                      [bsuf, byp, Bc], [BN[4]])
                                    A(lambda e: e.activation(mix[:, 4 + oc, o:o + n], yv[:, 0:n], AF.Gelu), [BN[4]], [Bmix[4 + oc][bi]])
                            stA(0)
                            for c in range(ntl):
                                stB1(c)
                                if c + 1 < ntl:
                                    stA(c + 1)
                                stB2(c)
                                stC(c)
                        V(lambda e: e.memset(cch[:, 5, 0:1], 0.0), (), S5FINE + [BUs, Bkz, Bqz, Bcch, Bg2, Bv2])
                        pinned.clear()
                        if last:
                            STO(D["s5rep"], s5po[:, 0, :], [Bs5o]); STO(D["s5imp"], s5po[:, 1, :], [Bs5o])
                            STO(D["s5res"], s5so[:, 0], [Bs5o]); STO(D["s5ims"], s5so[:, 1], [Bs5o])
                        ck(8)
                        slot, bs = wload([(D["w_glu"], 0)], 4)
                        for oc in range(4):
                            for bi, (o, n) in enumerate(blks):
                                ps, bp = PS()
                                for k in range(4):
                                    P(lambda e, k=k, ps=ps, oc=oc: e.matmul(ps[:, 0:n], slot[:, k, oc * 128:(oc + 1) * 128], mix[:, 4 + k, o:o + n],
                                                                            start=(k == 0), stop=(k == 3)), bs + [Bmix[4 + k][bi] for k in range(4)], [bp])
                                A(lambda e, ps=ps, oc=oc: e.activation(hn[:, oc, o:o + n], ps[:, 0:n], AF.Sigmoid, bias=pv("bglu", oc)), [bp, Bc], [Bhn[bi]])
                        for oc in range(4):
                            for bi, (o, n) in enumerate(blks):
                                V(lambda e, oc=oc: e.tensor_tensor(mix[:, 4 + oc, o:o + n], mix[:, 4 + oc, o:o + n], hn[:, oc, o:o + n], ALU.mult),
                                  [Bhn[bi], Bmix[4 + oc][bi]], [Bmix[4 + oc][bi]])
                        resid_proj(D["w_out_cd"], NT, mix, Bmix)
                    rmsnorm("nff%d" % layer, NT)
                    for q in range(4):
                        if sbi == 0 and layer == 0:
                            build_tab_kc(q)
                        for u in range(2):
                            c0 = q * 1024 + u * 512
                            slot, bs = wload([(D["w_ff1"][layer][:, c0:c0 + 512], 0)], 8)
                            for hc in range(4):
                                c = u * 4 + hc

                                def ev(ps, bp, bi, o, n, c=c):
                                    t = sqb[1]
                                    A(lambda e: e.activation(t[:, 0:n], ps[:, 0:n], AF.Relu), [bp], [Bsq[1]])
                                    V(lambda e: e.tensor_tensor(mix[:, c, o:o + n], t[:, 0:n], t[:, 0:n], ALU.mult), [Bsq[1]], [Bmix[c][bi]])
                                proj_fm(slot, bs, hc * 128, NT, ev)
                        for u in range(2):
                            slot, bs = wload([(D["w_ff2"][layer][q * 1024:(q + 1) * 1024, u * 512:(u + 1) * 512], 0)], 8)
                            for oc in range(4):
                                c = u * 4 + oc

                                def ev(ps, bp, bi, o, n, c=c):
                                    V(lambda e: e.tensor_tensor(h[:, c, o:o + n], h[:, c, o:o + n], ps[:, 0:n], ALU.add), [bp, Bh[c][bi]], [Bh[c][bi]])
                                proj_fm(slot, bs, oc * 128, NT, ev, rhs=mix, rbufs=lambda bi: [Bmix[k][bi] for k in range(8)])
                    ck(5)
                    rmsnorm("nple%d" % layer, NT)
                    S.dma("pool", lambda e, layer=layer: e.dma_start(out=pTb[:, :, 0:NT], in_=D["pT"][layer][:, tok0:tok0 + NT].rearrange("(k p) n -> p k n", p=128)),
                          pTsem, (), [BpT])
                    for u in range(2):
                        slot, bs = wload([(D["w_ple_gate"][layer][:, u * 512:(u + 1) * 512], 0)], 8)
                        slot2, bs2 = wload([(D["w_ple_proj"][layer][:, u * 512:(u + 1) * 512], 0)], 2)
                        for oc in range(4):
                            c = u * 4 + oc
                            for bi, (o, n) in enumerate(blks):
                                ps, bp = PS()
                                for k in range(8):
                                    P(lambda e, k=k, ps=ps: e.matmul(ps[:, 0:n], slot[:, k, oc * 128:(oc + 1) * 128], hn[:, k, o:o + n], start=(k == 0), stop=(k == 7)),
                                      bs + [Bhn[bi]], [bp])
                                gt = NT5[0]
                                A(lambda e, ps=ps: e.activation(gt[:, 0:n], ps[:, 0:n], AF.Sigmoid), [bp], [BN[0]])
                                ps2, bp2 = PS()
                                for k in range(2):
                                    P(lambda e, k=k, ps2=ps2: e.matmul(ps2[:, 0:n], slot2[:, k, oc * 128:(oc + 1) * 128], pTb[:, k, o:o + n], start=(k == 0), stop=(k == 1)),
                                      bs2 + [BpT], [bp2])
                                V(lambda e, ps2=ps2: e.tensor_tensor(gt[:, 0:n], gt[:, 0:n], ps2[:, 0:n], ALU.mult), [bp2, BN[0]], [BN[0]])
                                V(lambda e, c=c: e.tensor_tensor(h[:, c, o:o + n], h[:, c, o:o + n], gt[:, 0:n], ALU.add), [BN[0], Bh[c][bi]], [Bh[c][bi]])
                ck(9)
                rmsnorm("nfin", NT, final_out=D["yT"][:, tok0:tok0 + NT] if True else None)
        except _Stop:
            pass
        S.final_wait("sp", OUTS)
        S.emit(st)
    return nc


_NC = [None]


def kernel(**I):
    if _NC[0] is None:
        _NC[0] = build_program()
    nc = _NC[0]
    in_maps = prep_inputs(I)
    res = run_bass_kernel_spmd(nc, in_maps, core_ids=list(range(8)))
    return assemble(res.results)


def prep_inputs(I):
    f = lambda a: np.ascontiguousarray(np.asarray(a, np.float32))
    ident = np.eye(128, dtype=np.float32)
    s_ = np.arange(128)
    maskc = (s_[:, None] <= s_[None, :]).astype(np.float32)
    maskb = maskc * (s_[:, None] // 8 == s_[None, :] // 8)
    blk3 = np.broadcast_to((np.arange(16)[:, None] == (s_[None, :] // 8)).astype(np.float32)[None], (128, 16, 128)).copy()
    rowm = (s_[:, None] // 8 == np.arange(16)[None, :]).astype(np.float32)
    segm = np.ones((3, 128, NTM), np.float32); posrow = np.zeros((3, 128, NTM), np.float32); tau = np.ones((3, 128, NTM), np.float32)
    for i, (t0, NP, hs) in enumerate(SBS):
        segm[i, :, 0:NP:128] = 0.0
        posrow[i, :, 0:NP] = np.arange(t0, t0 + NP)[None]
        tau[i, :, 0:NP] = np.arange(1, NP + 1)[None]
        if hs:
            segm[i, :, NP:NP + 128:8] = 0.0
            posrow[i, :, NP:NP + 128] = (16384 + (np.arange(128) % 8))[None]
            tau[i, :, NP:NP + 128] = (1 + (np.arange(128) % 8))[None]
    negm = np.zeros((4, 128), np.float32); negm[:, 0::8] = -1e30
    sel = np.zeros((4, 4, 128), np.float32)
    for k in range(4):
        sel[k, k, :] = 1.0
    jrow = np.zeros((128, 4, 128), np.float32); jrow[:, 0, :] = (s_ + 1)[None]; jrow[:, 1, :] = (s_ % 8 + 1)[None]; jrow[:, 2, :] = s_[None]; jrow[:, 3, :] = (s_ % 8)[None]
    pvec = np.zeros((128, NPV), np.float32)

    def put(name, arr):
        o, w = PV[name]
        pvec[:, o:o + w] = arr
    for l in range(2):
        put("nmix%d" % l, _cols(I["norm_mix"][l])); put("nff%d" % l, _cols(I["norm_ff"][l])); put("nple%d" % l, _cols(I["norm_ple"][l]))
        put("lb%d" % l, _cols(I["lb_logits"][l]))
    put("nfin", _cols(I["norm_final"]))
    for j in range(4):
        put("cw%d" % j, _cols(I["conv_w_ab"][0][j]))
    put("cb", _cols(I["conv_b_ab"][0])); put("gna", _cols(I["gn_a"][0])); put("gnc", _cols(I["gn_c"][0]))
    put("s5d", _cols(I["s5_D"][0])); put("bglu", _cols(I["b_glu"][0]))
    st = lambda a: np.ascontiguousarray(np.asarray(a, np.float32).reshape(16, 2, 64).reshape(16, 128).T)
    put("are", st(I["s5_A_re"][0])); put("aim", st(I["s5_A_im"][0]))
    put("ldt", st(np.repeat(np.asarray(I["s5_log_dt"][0], np.float32)[:, None], 64, axis=1)))
    put("invf", (10000.0 ** (-(np.arange(128) % 64) / 64.0)).astype(np.float32)[:, None])
    put("sgn", np.where(s_ < 64, -1.0, 1.0).astype(np.float32)[:, None])
    put("s0", s_.astype(np.float32)[:, None]); put("s8", (s_ % 8).astype(np.float32)[:, None]); put("pidx", (s_ + 1).astype(np.float32)[:, None]); put("pidxs", (s_ % 8 + 1).astype(np.float32)[:, None]); put("rowm", rowm)
    rw = lambda a: np.broadcast_to(np.asarray(a, np.float32).reshape(1, 2048), (128, 2048))
    rowp = np.ascontiguousarray(np.stack([rw(I["s5_A_re"][0]), rw(I["s5_A_im"][0]),
                                          rw(np.repeat(np.asarray(I["s5_log_dt"][0], np.float32)[:, None], 64, axis=1))]))
    bgv = np.asarray(I["b_gate_ab"][0], np.float32)
    bg = np.stack([bgv[:4], bgv[4:]], axis=1).copy()
    BT = np.zeros((2, 16, 128, 128), np.float32); CT = np.zeros((2, 16, 128, 128), np.float32)
    for part, (Bm, Cm) in enumerate(((I["s5_B_re"][0], I["s5_C_re"][0]), (I["s5_B_im"][0], I["s5_C_im"][0]))):
        Bm = np.asarray(Bm, np.float32); Cm = np.asarray(Cm, np.float32)
        for g in range(32):
            i, gl = g // 2, g % 2
            k0 = (g % 8) * 16
            BT[part, i, k0:k0 + 16, gl * 64:(gl + 1) * 64] = Bm[g].T
            CT[part, i, gl * 64:(gl + 1) * 64, k0:k0 + 16] = Cm[g].T
    wab = np.asarray(I["w_in_ab"][0], np.float32); wcd = np.asarray(I["w_in_cd"][0], np.float32)
    w_ab_h = np.concatenate([wab[:, b0 + sec * 512 + hh * 128:b0 + sec * 512 + (hh + 1) * 128]
                             for b0 in (0, 2056) for hh in range(4) for sec in range(4)], axis=1)
    w_cd_h = np.concatenate([wcd[:, sec * 512 + hh * 128:sec * 512 + (hh + 1) * 128] for hh in range(4) for sec in range(4)], axis=1)
    common = dict(w_ab_h=f(w_ab_h), w_cd_h=f(w_cd_h), w_in_ab=f(I["w_in_ab"][0]), wg=f(I["w_in_ab"][0][:, 2048:2056]), w_out_ab=f(I["w_out_ab"][0]),
                  w_in_cd=f(I["w_in_cd"][0]), w_glu=f(I["w_glu"][0]), w_out_cd=f(I["w_out_cd"][0]),
                  w_ff1=f(I["w_ff1"]), w_ff2=f(I["w_ff2"]), w_ple_proj=f(I["w_ple_proj"]), w_ple_gate=f(I["w_ple_gate"]),
                  pvec=pvec, bg=bg, BT=BT, CT=CT, ident=ident, maskc=maskc, maskb=maskb.astype(np.float32), blk3=blk3,
                  segm=segm, negm=negm, sel=sel, posrow=posrow, rowp=rowp, jrow=jrow)
    in_maps = []
    for c in range(8):
        sl = slice(16 * c, 16 * c + 16)
        xT = np.concatenate([np.asarray(I["x_prompt"][c]).T, np.asarray(I["x_sample"][sl]).reshape(128, 1024).T], axis=1)
        pT = np.concatenate([np.transpose(np.asarray(I["p_prompt"][:, c]), (0, 2, 1)),
                             np.transpose(np.asarray(I["p_sample"][:, sl]).reshape(2, 128, 256), (0, 2, 1))], axis=2)
        Us = np.concatenate([np.asarray(I["state_mlstm_C"][0][sl]), np.asarray(I["state_mlstm_n"][0][sl])[..., None]], axis=-1)
        x0 = lambda a: np.transpose(np.asarray(a, np.float32).reshape(16, 16, 128), (2, 1, 0))
        m = dict(common)
        m.update(xT=f(xT), pT=f(pT), convs=f(np.transpose(np.asarray(I["state_mlstm_conv"][0][sl]), (2, 0, 1))), Us=f(Us),
                 ms=f(np.asarray(I["state_mlstm_m"][0][sl]).T), rets=f(I["state_ret"][0][sl]), hgrns=f(I["state_hgrn"][0][sl]),
                 x0re=f(x0(I["state_s5_re"][0][sl])), x0im=f(x0(I["state_s5_im"][0][sl])))
        in_maps.append(m)
    return in_maps


def assemble(R):
    yp = np.zeros((8, 2048, 1024), np.float32); ys = np.zeros((128, 8, 1024), np.float32)
    convp = np.zeros((1, 8, 3, 1024), np.float32); convs = np.zeros((1, 128, 3, 1024), np.float32)
    Cp = np.zeros((1, 8, 4, 128, 128), np.float32); Cs = np.zeros((1, 128, 4, 128, 128), np.float32)
    np_ = np.zeros((1, 8, 4, 128), np.float32); ns = np.zeros((1, 128, 4, 128), np.float32)
    mp = np.zeros((1, 8, 4), np.float32); ms = np.zeros((1, 128, 4), np.float32)
    retp = np.zeros((1, 8, 4, 128, 128), np.float32); rets = np.zeros((1, 128, 4, 128, 128), np.float32)
    hgp = np.zeros((1, 8, 4, 128, 128), np.float32); hgs = np.zeros((1, 128, 4, 128, 128), np.float32)
    s5rp = np.zeros((1, 8, 32, 64), np.float32); s5ip = np.zeros((1, 8, 32, 64), np.float32)
    s5rs = np.zeros((1, 128, 32, 64), np.float32); s5is = np.zeros((1, 128, 32, 64), np.float32)
    for c in range(len(R)):
        r = R[c]
        sl = slice(16 * c, 16 * c + 16)
        yp[c] = r["yT"][:, :2048].T
        ys[sl] = r["yT"][:, 2048:].T.reshape(16, 8, 1024)
        convp[0, c] = r["convp"].T
        convs[0, sl] = np.transpose(r["convs_o"], (1, 2, 0))
        Cp[0, c] = r["Up"][:, :, :128]; np_[0, c] = r["Up"][:, :, 128]
        Cs[0, sl] = r["Us_o"][..., :128]; ns[0, sl] = r["Us_o"][..., 128]
        mp[0, c] = r["mp"][:, 0]; ms[0, sl] = r["ms_o"].T
        retp[0, c] = r["retp"]; rets[0, sl] = r["rets_o"]; hgp[0, c] = r["hgrnp"]; hgs[0, sl] = r["hgrns_o"]
        s5rp[0, c] = r["s5rep"].T.reshape(32, 64); s5ip[0, c] = r["s5imp"].T.reshape(32, 64)
        s5rs[0, sl] = np.transpose(r["s5res"], (2, 1, 0)).reshape(16, 32, 64)
        s5is[0, sl] = np.transpose(r["s5ims"], (2, 1, 0)).reshape(16, 32, 64)
    return (yp, ys, convp, convs, Cp, Cs, np_, ns, mp, ms, retp, rets, hgp, hgs, s5rp, s5rs, s5ip, s5is)
```

```python
import math, contextlib, os
import numpy as np
import concourse.bass as bass
import concourse.mybir as mybir
from concourse.bass_utils import run_bass_kernel_spmd

F32 = mybir.dt.float32
BF16 = mybir.dt.bfloat16
AF = mybir.ActivationFunctionType
ALU = mybir.AluOpType

NTM = 768
FW = 776
SBS = [(0, 768, False), (768, 768, False), (1536, 512, True)]
NTOK = 2176
EPS = 1e-6
PI = math.pi
LG = [math.log1p(-2.0 ** (-5.0 - h)) for h in range(4)]
LNK = -0.5 * math.log(128.0)


class Buf:
    __slots__ = ("w", "r")

    def __init__(self):
        self.w = None
        self.r = []


class _Rec:
    def __init__(self):
        self.call = None

    def __getattr__(self, name):
        def f(*a, **k):
            self.call = (name, a, k)
            return self
        return f


def _record(fn):
    r = _Rec()
    fn(r)
    assert r.call is not None
    return r.call


class Sched:
    ENGS = ("pe", "act", "dve", "pool", "sp")

    def __init__(self, nc):
        self.nc = nc
        self.ops = {e: [] for e in self.ENGS}
        self.cnt = {e: 0 for e in self.ENGS}
        self.seen = {e: {} for e in self.ENGS}
        self.sems = {}
        self.dma_cnt = {}

    def new_dma_sem(self):
        k = "dma%d" % len(self.dma_cnt)
        self.dma_cnt[k] = 0
        return k

    def _deps(self, eng, reads, writes, is_dma):
        waits = {}

        def add(ev, kind):
            key, val, src_eng, src_dma = ev
            if (not src_dma) and (not is_dma) and src_eng == eng and eng == "pe":
                return
            if self.seen[eng].get(key, 0) >= val:
                return
            if waits.get(key, 0) < val:
                waits[key] = val
        for b in reads:
            if b.w is not None:
                add(b.w, "raw")
        for b in writes:
            if b.w is not None:
                add(b.w, "waw")
            for r in b.r:
                add(r, "war")
        for k, v in waits.items():
            self.seen[eng][k] = v
        return list(waits.items())

    def _post(self, ev, reads, writes):
        for b in writes:
            b.w = ev
            b.r = []
        for b in reads:
            if b.w is not ev:
                b.r.append(ev)
                if len(b.r) > 24:
                    b.r = b.r[-24:] if False else b.r

    def op(self, eng, fn, reads=(), writes=()):
        waits = self._deps(eng, reads, writes, False)
        self.cnt[eng] += 1
        ev = ("e_" + eng, self.cnt[eng], eng, False)
        self.ops[eng].append((waits, _record(fn), ("e_" + eng, 1)))
        self._post(ev, reads, writes)

    def dma(self, eng, fn, sem, reads=(), writes=()):
        waits = self._deps(eng, reads, writes, True)
        prev = self.dma_cnt[sem]
        if prev > 0 and self.seen[eng].get(sem, 0) < prev:
            waits = [w_ for w_ in waits if w_[0] != sem] + [(sem, prev)]
            self.seen[eng][sem] = prev
        self.dma_cnt[sem] += 16
        ev = (sem, self.dma_cnt[sem], eng, True)
        self.ops[eng].append((waits, _record(fn), (sem, 16)))
        self._post(ev, reads, writes)

    def final_wait(self, eng, bufs):
        waits = self._deps(eng, bufs, bufs, True)
        have = dict(waits)
        for k, v in self.dma_cnt.items():
            if v > 0 and self.seen[eng].get(k, 0) < v and have.get(k, 0) < v:
                have[k] = v
        for e2 in self.ENGS:
            if e2 != eng and self.cnt[e2] > 0:
                have["e_" + e2] = self.cnt[e2]
        self.ops[eng].append((list(have.items()), None, None))

    def emit(self, stack):
        nc = self.nc
        keys = ["e_" + e for e in self.ENGS] + list(self.dma_cnt.keys())
        for k in keys:
            self.sems[k] = stack.enter_context(nc.semaphore(k))
        block = stack.enter_context(nc.Block())
        engobj = {"pe": "tensor", "act": "scalar", "dve": "vector", "pool": "gpsimd", "sp": "sync"}

        def mk(e):
            def body(engine):
                for (waits, fn, inc) in self.ops[e]:
                    for (k, v) in waits:
                        engine.wait_ge(self.sems[k], v)
                    if fn is not None:
                        name, a, k = fn
                        getattr(engine, name)(*a, **k).then_inc(self.sems[inc[0]], inc[1])
            return body
        for e in self.ENGS:
            if self.ops[e]:
                getattr(block, engobj[e])(mk(e))


PV = {}
_o = 0
for _n, _w in [("nmix0", 8), ("nmix1", 8), ("nff0", 8), ("nff1", 8), ("nple0", 8), ("nple1", 8), ("nfin", 8),
               ("cw0", 8), ("cw1", 8), ("cw2", 8), ("cw3", 8), ("cb", 8), ("gna", 4), ("gnc", 4), ("s5d", 4),
               ("bglu", 4), ("lb0", 4), ("lb1", 4), ("are", 16), ("aim", 16), ("ldt", 16), ("invf", 1),
               ("sgn", 1), ("pidx", 1), ("pidxs", 1), ("rowm", 16), ("s0", 1), ("s8", 1)]:
    PV[_n] = (_o, _w)
    _o += _w
NPV = _o


def _cols(v):
    return np.ascontiguousarray(np.asarray(v, np.float32).reshape(-1, 128).T)


def build_program():
    nc = bass.Bass("TRN2", target_bir_lowering=False)
    D = {}

    def din(name, shape):
        D[name] = nc.dram_tensor(name, list(shape), F32, kind="ExternalInput").ap()
        return D[name]

    def dout(name, shape):
        D[name] = nc.dram_tensor(name, list(shape), F32, kind="ExternalOutput").ap()
        return D[name]
    din("xT", [1024, NTOK]); din("pT", [2, 256, NTOK])
    din("convs", [1024, 16, 3]); din("Us", [16, 4, 128, 129]); din("ms", [4, 16])
    din("rets", [16, 4, 128, 128]); din("hgrns", [16, 4, 128, 128])
    din("x0re", [128, 16, 16]); din("x0im", [128, 16, 16])
    din("w_ab_h", [1024, 4096]); din("w_cd_h", [1024, 2048]); din("w_in_ab", [1024, 4104]); din("wg", [1024, 8]); din("w_out_ab", [1024, 1024])
    din("w_in_cd", [1024, 2560]); din("w_glu", [512, 512]); din("w_out_cd", [1024, 1024])
    din("w_ff1", [2, 1024, 4096]); din("w_ff2", [2, 4096, 1024])
    din("w_ple_proj", [2, 256, 1024]); din("w_ple_gate", [2, 1024, 1024])
    din("pvec", [128, NPV]); din("bg", [4, 2])
    din("BT", [2, 16, 128, 128]); din("CT", [2, 16, 128, 128])
    din("ident", [128, 128]); din("maskc", [128, 128]); din("maskb", [128, 128])
    din("blk3", [128, 16, 128]); din("segm", [3, 128, NTM]); din("negm", [4, 128]); din("sel", [4, 4, 128])
    din("posrow", [3, 128, NTM]); din("rowp", [3, 128, 2048]); din("jrow", [128, 4, 128])
    dout("yT", [1024, NTOK]); dout("convp", [1024, 3]); dout("convs_o", [1024, 16, 3])
    dout("Up", [4, 128, 129]); dout("Us_o", [16, 4, 128, 129]); dout("mp", [4, 1]); dout("ms_o", [4, 16])
    dout("retp", [4, 128, 128]); dout("rets_o", [16, 4, 128, 128])
    dout("hgrnp", [4, 128, 128]); dout("hgrns_o", [16, 4, 128, 128])
    dout("s5rep", [128, 16]); dout("s5imp", [128, 16]); dout("s5res", [128, 16, 16]); dout("s5ims", [128, 16, 16])

    st = contextlib.ExitStack()
    with st:
        S = Sched(nc)
        cnt = [0]

        def sb(shape, dt=F32):
            cnt[0] += 1
            return st.enter_context(nc.sbuf_tensor("t%d" % cnt[0], list(shape), dt))

        def psum(shape, dt=F32):
            cnt[0] += 1
            return st.enter_context(nc.psum_tensor("p%d" % cnt[0], list(shape), dt))
        V = lambda fn, r=(), w=(): S.op("dve", fn, r, w)
        A = lambda fn, r=(), w=(): S.op("act", fn, r, w)
        G = lambda fn, r=(), w=(): S.op("pool", fn, r, w)
        P = lambda fn, r=(), w=(): S.op("pe", fn, r, w)
        msems = {"sp": [S.new_dma_sem() for _ in range(24)], "pool": [S.new_dma_sem() for _ in range(8)]}
        mi = {"sp": 0, "pool": 0}

        def LD(out, in_, w, r=(), eng="sp"):
            k = msems[eng][mi[eng] % len(msems[eng])]
            mi[eng] += 1
            S.dma(eng, lambda e: e.dma_start(out=out, in_=in_), k, r, w)
        OUTS = []

        def STO(out, in_, r):
            b_ = Buf()
            OUTS.append(b_)
            LD(out, in_, [b_], r)

        h = sb([128, 8, NTM]); hn = sb([128, 8, NTM], BF16); mix = sb([128, 8, NTM], BF16)
        Bh = [[Buf() for _ in range(2)] for _ in range(8)]
        Bhn = [Buf() for _ in range(2)]
        Bmix = [[Buf() for _ in range(2)] for _ in range(8)]
        NW = 2
        wr = [sb([128, 8, 512], BF16) for _ in range(NW)]
        Bwrp = [[Buf() for _ in range(4)] for _ in range(NW)]
        wsem = [[S.new_dma_sem() for _ in range(4)] for _ in range(NW)]
        wi = [0]

        def wload(parts, nk):
            i = wi[0] % NW
            wi[0] += 1
            for pi_, (ap, co) in enumerate(parts):
                ncol = ap.shape[1]
                S.dma("pool", lambda e, ap=ap, co=co, ncol=ncol, i=i: e.dma_start(
                    out=wr[i][:, 0:nk, co:co + ncol], in_=ap.rearrange("(k p) n -> p k n", p=128)),
                    wsem[i][pi_], (), (Bwrp[i] if pi_ == 0 else [Bwrp[i][pi_]]))
            return wr[i], Bwrp[i]
        Fs = [sb([128, FW]) for _ in range(9)]
        BF = [Buf() for _ in range(9)]
        Hs = [sb([128, NTM], BF16) for _ in range(5)]
        BH = [Buf() for _ in range(5)]
        vtm = sb([128, 6, 129], BF16); Bv = Buf()
        NT5 = [sb([128, 512]) for _ in range(5)]
        BN = [Buf() for _ in range(5)]
        sqb = [sb([128, 512], BF16) for _ in range(2)]
        Bsq = [Buf() for _ in range(2)]
        pb = [psum([128, 512]) for _ in range(7)]
        Bp = [Buf() for _ in range(7)]
        ptb = psum([128, 1024], BF16); Bpt = Buf()
        pbi = [0]

        pinned = set()

        S5MODE = [False]

        def PS():
            while True:
                i = pbi[0] % (3 if S5MODE[0] else 4)
                pbi[0] += 1
                if i not in pinned:
                    return pb[i], Bp[i]
        psi = [0]

        def PSS():
            if S5MODE[0]:
                i = (4, 5, 6, 3)[psi[0] % 4]
            else:
                i = 4 + psi[0] % 3
            psi[0] += 1
            return pb[i], Bp[i]
        ident = sb([128, 128]); identb = sb([128, 128], BF16); maskc = sb([128, 128], BF16); maskb = sb([128, 128], BF16)
        onesb = sb([128, 128], BF16); blk3 = sb([128, 16, 128], BF16); segm = sb([128, NTM]); negm = sb([4, 128])
        sel = sb([4, 4, 128]); pvec = sb([128, NPV]); bg = sb([4, 2]); nbg = sb([4, 1]); jrow = sb([128, 4, 128])
        Bc = Buf()
        LD(ident[:], D["ident"], [Bc]); LD(identb[:], D["ident"], [Bc], eng="pool")
        LD(maskc[:], D["maskc"], [Bc], eng="pool"); LD(maskb[:], D["maskb"], [Bc], eng="pool")
        LD(blk3[:], D["blk3"], [Bc], eng="pool"); LD(negm[:], D["negm"], [Bc]); LD(sel[:], D["sel"], [Bc])
        LD(pvec[:], D["pvec"], [Bc]); LD(bg[:], D["bg"], [Bc]); LD(jrow[:], D["jrow"], [Bc])
        V(lambda e: e.memset(onesb[:], 1.0), (), [Bc])
        V(lambda e: e.tensor_scalar(nbg[:], bg[:, 1:2], -1.0, None, ALU.mult), [Bc], [Bc])

        cb_ = sb([128, 8])
        CBV = [EPS, LNK, 1.0, 0.0, 0.5 * PI, 0.0, 0.0, 0.0]
        for _i, _v in enumerate(CBV):
            V(lambda e, _i=_i, _v=_v: e.memset(cb_[:, _i:_i + 1], _v), (), [Bc])
        CEPS, CLNK, CONE, CZERO, CHPI = [cb_[:, i:i + 1] for i in range(5)]
        RC = 12582912.0
        I2P = 1.0 / (2 * PI)

        def sin_of(dst, src, shift, tmp, rd, wr_, btmp, npart=128):
            V(lambda e: e.tensor_scalar(tmp, src, shift, I2P, ALU.add, ALU.mult), rd, [btmp])
            V(lambda e: e.tensor_scalar(tmp, tmp, RC, None, ALU.add), [btmp], [btmp])
            V(lambda e: e.tensor_scalar(tmp, tmp, -RC, None, ALU.add), [btmp], [btmp])
            V(lambda e: e.scalar_tensor_tensor(tmp, tmp, -2 * PI, src, ALU.mult, ALU.add), [btmp] + list(rd), [btmp])
            V(lambda e: e.tensor_scalar(tmp, tmp, -PI - shift + 4e-6, PI - shift - 4e-6, ALU.max, ALU.min), [btmp], [btmp])
            A(lambda e: e.activation(dst, tmp, AF.Sin, bias=(CHPI[0:npart] if shift != 0.0 else CZERO[0:npart])), [btmp, Bc], wr_)

        def pv(name, j=0, n=1):
            o, w = PV[name]
            return pvec[:, o + j:o + j + n]
        Gq = sb([128, 2, 4, 128], BF16); gk = sb([128, 2, 4])
        for v2 in range(2):
            for hh in range(4):
                A(lambda e, v2=v2, hh=hh: e.activation(Gq[:, v2, hh, :], jrow[:, v2, :], AF.Exp, scale=LG[hh]), [Bc], [Bc])
                A(lambda e, v2=v2, hh=hh: e.activation(gk[:, v2, hh:hh + 1], pv("pidxs" if v2 else "pidx"), AF.Exp,
                                                       scale=-LG[hh], bias=CLNK), [Bc], [Bc])
        lb = sb([128, 4]); oml = sb([128, 4])
        V(lambda e: e.tensor_tensor(lb[:], pv("lb1", 0, 4), pv("lb0", 0, 4), ALU.subtract), [Bc], [Bc])
        A(lambda e: e.activation(lb[:], lb[:], AF.Sigmoid), [Bc], [Bc])
        V(lambda e: e.tensor_scalar(oml[:], lb[:], -1.0, 1.0, ALU.mult, ALU.add), [Bc], [Bc])
        s5p = sb([128, 16, 16])
        th, rr, zr, zi, rho = s5p[:, 0, :], s5p[:, 1, :], s5p[:, 2, :], s5p[:, 3, :], s5p[:, 7, :]
        t4, t5, t6 = s5p[:, 4, :], s5p[:, 5, :], s5p[:, 6, :]
        ar_, ai_, a128r, a128i, izr, izi, t7 = (s5p[:, 8, :], s5p[:, 9, :], s5p[:, 10, :], s5p[:, 11, :], s5p[:, 12, :],
                                               s5p[:, 13, :], s5p[:, 14, :])
        are, aim = pv("are", 0, 16), pv("aim", 0, 16)
        A(lambda e: e.activation(t4, pv("ldt", 0, 16), AF.Exp), [Bc], [Bc])
        V(lambda e: e.tensor_tensor(th, t4, aim, ALU.mult), [Bc], [Bc])
        V(lambda e: e.tensor_tensor(rho, t4, are, ALU.mult), [Bc], [Bc])
        A(lambda e: e.activation(rr, rho, AF.Exp), [Bc], [Bc])
        sin_of(t4, th, 0.5 * PI, t6, [Bc], [Bc], Bc)
        sin_of(t5, th, 0.0, t6, [Bc], [Bc], Bc)
        V(lambda e: e.tensor_tensor(ar_, t4, rr, ALU.mult), [Bc], [Bc])
        V(lambda e: e.tensor_tensor(ai_, t5, rr, ALU.mult), [Bc], [Bc])
        V(lambda e: e.tensor_scalar(t4, ar_, -1.0, None, ALU.add), [Bc], [Bc])
        V(lambda e: e.tensor_copy(t5, ai_), [Bc], [Bc])
        V(lambda e: e.tensor_tensor(t6, are, are, ALU.mult), [Bc], [Bc])
        V(lambda e: e.tensor_tensor(zr, aim, aim, ALU.mult), [Bc], [Bc])
        V(lambda e: e.tensor_tensor(t6, t6, zr, ALU.add), [Bc], [Bc])
        V(lambda e: e.reciprocal(t6, t6), [Bc], [Bc])
        V(lambda e: e.tensor_tensor(zr, t4, are, ALU.mult), [Bc], [Bc])
        V(lambda e: e.tensor_tensor(zi, t5, aim, ALU.mult), [Bc], [Bc])
        V(lambda e: e.tensor_tensor(zr, zr, zi, ALU.add), [Bc], [Bc])
        V(lambda e: e.tensor_tensor(zi, t5, are, ALU.mult), [Bc], [Bc])
        V(lambda e: e.tensor_tensor(t7, t4, aim, ALU.mult), [Bc], [Bc])
        V(lambda e: e.tensor_tensor(zi, zi, t7, ALU.subtract), [Bc], [Bc])
        V(lambda e: e.tensor_tensor(zr, zr, t6, ALU.mult), [Bc], [Bc])
        V(lambda e: e.tensor_tensor(zi, zi, t6, ALU.mult), [Bc], [Bc])
        V(lambda e: e.tensor_tensor(t4, zr, zr, ALU.mult), [Bc], [Bc])
        V(lambda e: e.tensor_tensor(t5, zi, zi, ALU.mult), [Bc], [Bc])
        V(lambda e: e.tensor_tensor(t4, t4, t5, ALU.add), [Bc], [Bc])
        V(lambda e: e.reciprocal(t4, t4), [Bc], [Bc])
        V(lambda e: e.tensor_tensor(izr, zr, t4, ALU.mult), [Bc], [Bc])
        V(lambda e: e.scalar_tensor_tensor(izi, zi, -1.0, t4, ALU.mult, ALU.mult), [Bc], [Bc])
        V(lambda e: e.tensor_scalar(t7, th, 128.0, None, ALU.mult), [Bc], [Bc])
        sin_of(t4, t7, 0.5 * PI, t6, [Bc], [Bc], Bc)
        sin_of(t5, t7, 0.0, t6, [Bc], [Bc], Bc)
        A(lambda e: e.activation(t6, rho, AF.Exp, scale=128.0), [Bc], [Bc])
        V(lambda e: e.tensor_tensor(a128r, t4, t6, ALU.mult), [Bc], [Bc])
        V(lambda e: e.tensor_tensor(a128i, t5, t6, ALU.mult), [Bc], [Bc])
        tabE = nc.dram_tensor("tabE", [128, 2, 2048], BF16).ap(); tabZ = nc.dram_tensor("tabZ", [128, 2, 16, 128], BF16).ap()
        tbe = sb([128, 2, 512], BF16); tbz = sb([128, 2, 4, 128], BF16); Btab = Buf(); Bscr = Buf()

        def build_tables(kc, scol, jr, outE, outZ, wE, wZ):
            f0, f1, f2, f3, f4, f5 = [Fs[k][:, 0:512] for k in range(6)]
            b0_, b1_, b2_, b3_, b4_, b5_ = BF[0:6]
            for k in range(3):
                LD(Fs[k][:, 0:512], D["rowp"][k][:, kc * 512:(kc + 1) * 512], [BF[k]])
            A(lambda e: e.activation(f2, f2, AF.Exp), [b2_], [b2_])
            V(lambda e: e.tensor_tensor(f1, f1, f2, ALU.mult), [b1_, b2_], [b1_])
            V(lambda e: e.tensor_tensor(f0, f0, f2, ALU.mult), [b0_, b2_], [b0_])
            V(lambda e: e.tensor_scalar(f1, f1, scol, None, ALU.mult), [b1_, Bc], [b1_])
            A(lambda e: e.activation(f0, f0, AF.Exp, scale=scol), [b0_, Bc], [b0_])
            V(lambda e: e.reciprocal(f0, f0), [b0_], [b0_])
            sin_of(f3, f1, 0.5 * PI, f2, [b1_], [b3_], b2_)
            sin_of(f4, f1, 0.0, f2, [b1_], [b4_], b2_)
            V(lambda e: e.tensor_tensor(outE(0), f3, f0, ALU.mult), [b3_, b0_], wE)
            V(lambda e: e.scalar_tensor_tensor(outE(1), f4, -1.0, f0, ALU.mult, ALU.mult), [b4_, b0_], wE)
            g0, g1, g2, g3, g4 = [Fs[k][:, 0:512].rearrange("p (a b) -> p a b", a=4) for k in range(5)]
            i4 = slice(4 * kc, 4 * kc + 4)
            jb = jr.unsqueeze(1).broadcast_to([128, 4, 128])
            bc4 = lambda v: v[:, i4].unsqueeze(2).broadcast_to([128, 4, 128])
            V(lambda e: e.tensor_tensor(g1, jb, bc4(th), ALU.mult), [Bc], [b1_])
            V(lambda e: e.tensor_tensor(g0, jb, bc4(rho), ALU.mult), [Bc], [b0_])
            A(lambda e: e.activation(Fs[0][:, 0:512], Fs[0][:, 0:512], AF.Exp), [b0_], [b0_])
            sin_of(f3, f1, 0.5 * PI, f2, [b1_], [b3_], b2_)
            sin_of(f4, f1, 0.0, f2, [b1_], [b4_], b2_)
            V(lambda e: e.tensor_tensor(f3, f3, f0, ALU.mult), [b3_, b0_], [b3_])
            V(lambda e: e.tensor_tensor(f4, f4, f0, ALU.mult), [b4_, b0_], [b4_])
            V(lambda e: e.tensor_tensor(g0, g3, bc4(zr), ALU.mult), [b3_, Bc], [b0_])
            V(lambda e: e.tensor_tensor(g1, g4, bc4(zi), ALU.mult), [b4_, Bc], [b1_])
            V(lambda e: e.tensor_tensor(outZ(0), g0, g1, ALU.subtract), [b0_, b1_], wZ)
            V(lambda e: e.tensor_tensor(g0, g3, bc4(zi), ALU.mult), [b3_, Bc], [b0_])
            V(lambda e: e.tensor_tensor(g1, g4, bc4(zr), ALU.mult), [b4_, Bc], [b1_])
            V(lambda e: e.tensor_tensor(outZ(1), g0, g1, ALU.add), [b0_, b1_], wZ)
        Up = sb([128, 12, 129]); Upb = sb([128, 12, 129], BF16); nbc = sb([128, 4, 128], BF16)
        BU = [Buf() for _ in range(12)]
        V(lambda e: e.memset(Up[:], 0.0), (), BU); V(lambda e: e.memset(Upb[:], 0.0), (), BU)
        V(lambda e: e.memset(nbc[:], 0.0), (), BU)
        tails = sb([128, 8, 3]); Btl = Buf()
        V(lambda e: e.memset(tails[:], 0.0), (), [Btl])
        carr = sb([4, 2]); Bcar = Buf()
        V(lambda e: e.memset(carr[:], 0.0), (), [Bcar])
        s5c = sb([128, 2, 16]); Bs5c = Buf()
        s5d = sb([128, 2, 2, 16]); Bs5d = [Buf(), Buf()]; S5PAR = [0, 0, 0, 0]; Bcr = Buf()
        V(lambda e: e.memset(s5c[:], 0.0), (), [Bs5c])
        V(lambda e: e.memset(s5d[:], 0.0), (), Bs5d)
        Usf = sb([128, 16, 129]); Usb = sb([128, 16, 129], BF16); BUs = Buf()
        qz = sb([128, 16, 128], BF16); kz = sb([128, 16, 128], BF16); nbs = kz
        Bqz = Buf(); Bkz = Buf(); Bnbs = Bkz
        stb = [sb([128, 128], BF16) for _ in range(2)]; Bst = [Buf() for _ in range(2)]
        khb = [sb([128, 128], BF16) for _ in range(2)]; Bkh = [Buf() for _ in range(2)]
        ektm = sb([128, 6, 4]); Bek = Buf()
        decbc = sb([128, 4, 24]); Bdec = Buf()
        decrow = sb([4, 24]); mxe = sb([4, 8]); ms0 = sb([4, 16]); msout = sb([4, 17]); Bsm = Buf()
        x0s = Fs[5][:, 0:512].rearrange("p (a b c) -> p a b c", a=2, b=16); Bx0 = BF[5]
        s5so = Usf[:].rearrange("p a b -> p (a b)")[:, 0:512].rearrange("p (a b c) -> p a b c", a=2, b=16); s5po = sb([128, 2, 16]); Bs5o = Buf()
        Bes = Buf()
        cbs = sb([128, 2, 4, 128], BF16); Bcbs = Buf()
        pt4all = sb([128, 2048], BF16)
        pt4 = [pt4all[:, q_ * 512:(q_ + 1) * 512] for q_ in range(4)]
        gT2 = pt4all[:, 0:NTM]; vtm2 = pt4all[:, NTM:NTM + 774].rearrange("p (a b) -> p a b", a=6); Bg2 = Buf(); Bv2 = Buf()
        hdec = sb([128, 2, 24]); Bhd = Buf()
        Bprb = [Buf() for _ in range(2)]; Bt4 = [Buf() for _ in range(4)]; Bcsb = [Buf() for _ in range(2)]; Bp4 = [Buf() for _ in range(4)]
        S5FINE = Bprb + Bt4 + Bcsb + Bp4
        qzf = qz[:].rearrange("p a b -> p (a b)"); kzf = kz[:].rearrange("p a b -> p (a b)"); usbf = Usb[:].rearrange("p a b -> p (a b)")
        wtb = [[sb([128, 512], BF16) for _ in range(2)] for _ in range(2)]; Bwt = [Buf() for _ in range(2)]
        xtb = [[sb([128, 512], BF16) for _ in range(2)] for _ in range(2)]; Bxt = [Buf() for _ in range(2)]
        cch = sb([128, 6, 4]); Bcch = Buf()
        bct = sb([128, 2, 2, 4, 128], BF16); Bbct = [Buf() for _ in range(4)]; bcsem = [S.new_dma_sem() for _ in range(4)]
        pTb = sb([128, 2, NTM], BF16); BpT = Buf(); pTsem = S.new_dma_sem()

        def blocks(ntot):
            out = []
            o = 0
            while o < ntot:
                n = min(512, ntot - o)
                out.append((o, n)); o += n
            return out

        def rmsnorm(gname, NT, final_out=None):
            for bi, (o, n) in enumerate(blocks(NT)):
                ps, bp = PS()
                for c in range(8):
                    q = sqb[c % 2]; bq = Bsq[c % 2]
                    A(lambda e, c=c, q=q: e.activation(q[:, 0:n], h[:, c, o:o + n], AF.Square), [Bh[c][bi]], [bq])
                    P(lambda e, c=c, q=q, ps=ps: e.matmul(ps[:, 0:n], onesb[:], q[:, 0:n], start=(c == 0), stop=(c == 7)),
                      [bq, Bc], [bp])
                rs = NT5[4]
                A(lambda e, ps=ps: e.activation(rs[:, 0:n], ps[:, 0:n], AF.Ln, scale=1.0 / 1024, bias=CEPS), [bp], [BN[4]])
                A(lambda e: e.activation(rs[:, 0:n], rs[:, 0:n], AF.Exp, scale=-0.5), [BN[4]], [BN[4]])
                for c in range(8):
                    if final_out is None:
                        V(lambda e, c=c: e.scalar_tensor_tensor(hn[:, c, o:o + n], h[:, c, o:o + n], pv(gname, c), rs[:, 0:n],
                                                                ALU.mult, ALU.mult), [Bh[c][bi], BN[4], Bc], [Bhn[bi]])
                    else:
                        t = NT5[c % 2]
                        V(lambda e, c=c, t=t: e.scalar_tensor_tensor(t[:, 0:n], h[:, c, o:o + n], pv(gname, c), rs[:, 0:n],
                                                                     ALU.mult, ALU.mult), [Bh[c][bi], BN[4], Bc], [BN[c % 2]])
                        STO(final_out[c * 128:(c + 1) * 128, o:o + n], t[:, 0:n], [BN[c % 2]])

        def proj_fm(slot, bs, col, NT, evac, rhs=None, nk=8, rbufs=None):
            for bi, (o, n) in enumerate(blocks(NT)):
                ps, bp = PS()
                for k in range(nk):
                    src = hn if rhs is None else rhs
                    P(lambda e, k=k, ps=ps, src=src: e.matmul(ps[:, 0:n], slot[:, k, col:col + 128], src[:, k, o:o + n],
                                                             start=(k == 0), stop=(k == nk - 1)),
                      bs + ([Bhn[bi]] if rbufs is None else rbufs(bi)), [bp])
                evac(ps, bp, bi, o, n)

        def resid_proj(w_ap, NT, src, srcb):
            for u in range(2):
                slot, bs = wload([(w_ap[:, u * 512:(u + 1) * 512], 0)], 8)
                for oc in range(4):
                    c = u * 4 + oc

                    def ev(ps, bp, bi, o, n, c=c):
                        V(lambda e: e.tensor_tensor(h[:, c, o:o + n], h[:, c, o:o + n], ps[:, 0:n], ALU.add),
                          [bp, Bh[c][bi]], [Bh[c][bi]])
                    proj_fm(slot, bs, oc * 128, NT, ev, rhs=src, rbufs=lambda bi: [srcb[k][bi] for k in range(8)])

        def att_pre(qT, kT, bq, bk, col, ek, sample):
            ps, bp = PSS()
            P(lambda e: e.matmul(ps[:, 0:128], kT[:, col:col + 128], qT[:, col:col + 128], start=True, stop=True),
              [bq, bk], [bp])
            i2 = att_tile.k % 2
            att_tile.k += 1
            sT = stb[i2]; bsT = Bst[i2]
            msk = maskb if sample else maskc
            if ek is not None:
                V(lambda e: e.scalar_tensor_tensor(sT[:], ps[:, 0:128], ek, msk[:], ALU.mult, ALU.mult), [bp, Bek, Bc], [bsT])
            else:
                V(lambda e: e.tensor_tensor(sT[:], ps[:, 0:128], msk[:], ALU.mult), [bp, Bc], [bsT])
            P(lambda e: e.transpose(ptb[:, 0:128], kT[:, col:col + 128], identb[:]), [bk, Bc], [Bpt])
            kh = khb[i2]; bkh = Bkh[i2]
            if ek is not None:
                A(lambda e: e.activation(kh[:], ptb[:, 0:128], AF.Copy, scale=ek), [Bpt, Bek], [bkh])
            else:
                A(lambda e: e.copy(kh[:], ptb[:, 0:128]), [Bpt], [bkh])
            return (sT, bsT, kh, bkh)

        def att_tile(qT, kT, bq, bk, col, vt, E, si, ek, dec, PT, bPT, pcol, sample, den=None, mlstm_h=None, usbuf=None, bv=None, ctx=None):
            Bv = bv
            if ctx is None:
                ctx = att_pre(qT, kT, bq, bk, col, ek, sample)
            sT, bsT, kh, bkh = ctx
            P(lambda e: e.matmul(PT[:, pcol:pcol + 128], vt[:, 0:128], sT[:], start=True, stop=False), [Bv, bsT], [bPT])
            if not sample:
                P(lambda e: e.matmul(PT[:, pcol:pcol + 128], Upb[:, si, 0:128], qT[:, col:col + 128], start=False, stop=True),
                  [BU[si], bq], [bPT])
            else:
                for j in range(16):
                    P(lambda e, j=j: e.matmul(PT[:, pcol:pcol + 128], Usb[:, j, 0:128], qz[:, j, :], start=False, stop=(j == 15)),
                      [BUs, Bqz], [bPT])
            if den is not None:
                dps, bd = den
                P(lambda e: e.matmul(dps[:, pcol:pcol + 128], onesb[:], sT[:], start=True, stop=False), [Bc, bsT], [bd])
                if not sample:
                    P(lambda e: e.matmul(dps[:, pcol:pcol + 128], nbc[:, mlstm_h, :], qT[:, col:col + 128], start=False, stop=True),
                      [BU[si], bq], [bd])
                else:
                    for j in range(16):
                        P(lambda e, j=j: e.matmul(dps[:, pcol:pcol + 128], nbs[:, j, :], qz[:, j, :], start=False, stop=(j == 15)),
                          [Bnbs, Bqz], [bd])
            if not sample:
                ps2, bp2 = PSS()
                P(lambda e: e.matmul(ps2[:, 0:E], ident[:], Up[:, si, 0:E], start=True, stop=False), [Bc, BU[si]], [bp2])
                P(lambda e: e.matmul(ps2[:, 0:E], kh[:], vt[:, 0:E], start=False, stop=True), [bkh, Bv], [bp2])
                A(lambda e: e.activation(Up[:, si, 0:E], ps2[:, 0:E], AF.Copy, scale=dec), [bp2, Bdec, Bhd], [BU[si]])
                V(lambda e: e.tensor_copy(Upb[:, si, 0:E], Up[:, si, 0:E]), [BU[si]], [BU[si]])
                if mlstm_h is not None:
                    V(lambda e: e.tensor_copy(nbc[:, mlstm_h, :], Up[:, si, 128:129].broadcast_to([128, 128])), [BU[si]], [BU[si]])
            else:
                V(lambda e: e.tensor_tensor(kz[:], kh[:].unsqueeze(1).broadcast_to([128, 16, 128]),
                                            pv("rowm", 0, 16).unsqueeze(2).broadcast_to([128, 16, 128]), ALU.mult),
                  [bkh, Bc], [Bkz])
                for j in range(16):
                    ps2, bp2 = PSS()
                    P(lambda e, j=j, ps2=ps2: e.matmul(ps2[:, 0:E], ident[:], Usf[:, j, 0:E], start=True, stop=False), [Bc, BUs], [bp2])
                    P(lambda e, j=j, ps2=ps2: e.matmul(ps2[:, 0:E], kz[:, j, :], vt[:, 0:E], start=False, stop=True), [Bkz, Bv], [bp2])
                    A(lambda e, j=j, ps2=ps2: e.activation(usbuf[:, j, 0:E], ps2[:, 0:E], AF.Copy, scale=dec(j)),
                      [bp2, Bdec, Bhd], [BUs])
        att_tile.k = 0
        CTX = {}

        def load_sample_state(src, hh, E):
            LD(Usf[:, :, 0:E], src[:, hh, :, :].rearrange("j d e -> d j e"), [BUs])
            V(lambda e: e.tensor_copy(Usb[:, :, 0:E], Usf[:, :, 0:E]), [BUs], [BUs])

        def make_qz(qT, bq, col):
            V(lambda e: e.tensor_tensor(qz[:], qT[:, col:col + 128].unsqueeze(1).broadcast_to([128, 16, 128]), blk3[:], ALU.mult),
              [bq, Bc], [Bqz])

        def vproj(slot, bs, col, NT, E, vt_, bv_):
            nt = NT // 128
            for c in range(nt):
                ps, bp = PS()
                for k in range(8):
                    P(lambda e, k=k, ps=ps: e.matmul(ps[:, 0:128], hn[:, k, c * 128:(c + 1) * 128], slot[:, k, col:col + 128],
                                                     start=(k == 0), stop=(k == 7)), bs + [Bhn[(c * 128) // 512]], [bp])
                A(lambda e, ps=ps: e.copy(vt_[:, c, 0:128], ps[:, 0:128]), [bp], [bv_])

        def rstd_from(sq_src_fn, n, srcb):
            q = sqb[0]
            sq_src_fn(q)
            ps, bp = PSS()
            P(lambda e: e.matmul(ps[:, 0:n], onesb[:], q[:, 0:n], start=True, stop=True), [Bsq[0], Bc], [bp])
            rs = NT5[3]
            A(lambda e: e.activation(rs[:, 0:n], ps[:, 0:n], AF.Ln, scale=1.0 / 128, bias=CEPS), [bp], [BN[3]])
            A(lambda e: e.activation(rs[:, 0:n], rs[:, 0:n], AF.Exp, scale=-0.5), [BN[3]], [BN[3]])
            return rs

        def build_tab_kc(kc_):
            build_tables(kc_, pv("s0"), jrow[:, 2, :], lambda part: tbe[:, part, :], lambda part: tbz[:, part, :, :], [Btab], [Btab])
            LD(tabE[:, :, kc_ * 512:(kc_ + 1) * 512], tbe[:], [Bscr], r=[Btab])
            LD(tabZ[:, :, 4 * kc_:4 * kc_ + 4, :], tbz[:], [Bscr], r=[Btab])
        STEP = [None]

        def step():
            g = STEP[0]
            if g is not None:
                try:
                    next(g)
                except StopIteration:
                    STEP[0] = None
        CUT = int(os.environ.get("KCUT", "0"))

        class _Stop(Exception):
            pass

        def ck(k):
            if CUT == k:
                raise _Stop()
        try:
            for sbi, (tok0, NP, has_s) in enumerate(SBS):
                NT = NP + (128 if has_s else 0)
                ntp = NP // 128
                blks = blocks(NT)
                last = (sbi == len(SBS) - 1)
                LD(segm[:, 0:NTM], D["segm"][sbi], [Bc], r=[Bc])
                for c in range(8):
                    for bi, (o, n) in enumerate(blks):
                        LD(h[:, c, o:o + n], D["xT"][c * 128:(c + 1) * 128, tok0 + o:tok0 + o + n], [Bh[c][bi]])
                for layer in range(2):
                    rmsnorm("nmix%d" % layer, NT)
                    ck(1)
                    if layer == 0:
                        wgs, bwg = wload([(D["wg"], 0)], 8)
                        A1, A2, A3, A4 = Fs[2], Fs[3], Fs[5], Fs[4]
                        b1, b2, b3, b4 = BF[2], BF[3], BF[5], BF[4]
                        for bi, (o, n) in enumerate(blks):
                            ps, bp = PS()
                            for k in range(8):
                                P(lambda e, k=k, ps=ps: e.matmul(ps[0:4, 0:n], wgs[:, k, 0:4], hn[:, k, o:o + n], start=(k == 0), stop=(k == 7)),
                                  bwg + [Bhn[bi]], [bp])
                            A(lambda e, ps=ps: e.activation(A1[0:4, o:o + n], ps[0:4, 0:n], AF.Identity, bias=bg[:, 0:1]), [bp, Bc], [b1])
                            ps, bp = PS()
                            for k in range(8):
                                P(lambda e, k=k, ps=ps: e.matmul(ps[0:4, 0:n], wgs[:, k, 4:8], hn[:, k, o:o + n], start=(k == 0), stop=(k == 7)),
                                  bwg + [Bhn[bi]], [bp])
                            A(lambda e, ps=ps: e.activation(A2[0:4, o:o + n], ps[0:4, 0:n], AF.Exp, scale=-1.0, bias=nbg[:, 0:1]), [bp, Bc], [b2])
                        A(lambda e: e.activation(A2[0:4, 0:NT], A2[0:4, 0:NT], AF.Ln, bias=CONE[0:4]), [b2], [b2])
                        V(lambda e: e.memset(A4[0:4, 0:NT], 1.0), (), [b4])
                        V(lambda e: e.tensor_tensor_scan(A3[0:4, 0:NP], A4[0:4, 0:NP], A2[0:4, 0:NP], carr[:, 0:1], ALU.mult, ALU.add),
                          [b2, b4, Bcar], [b3])
                        if has_s:
                            V(lambda e: e.tensor_tensor_scan(A3[0:4, NP:NT], segm[0:4, NP:NT], A2[0:4, NP:NT], 0.0, ALU.mult, ALU.add),
                              [b2, Bc], [b3])
                        V(lambda e: e.tensor_tensor(A1[0:4, 0:NT], A1[0:4, 0:NT], A3[0:4, 0:NT], ALU.add), [b1, b3], [b1])
                        V(lambda e: e.memset(A4[0:4, 0:NT], 0.0), (), [b4])
                        V(lambda e: e.tensor_tensor_scan(A2[0:4, 0:NP], A4[0:4, 0:NP], A1[0:4, 0:NP], carr[:, 1:2], ALU.add, ALU.max),
                          [b1, b4, Bcar], [b2])
                        V(lambda e: e.tensor_copy(mxe[:, 0:1], carr[:, 1:2]), [Bcar], [Bsm])
                        V(lambda e: e.tensor_copy(mxe[:, 1:1 + ntp], A2[0:4, 0:NP].rearrange("p (c t) -> p c t", t=128)[:, :, 127]), [b2], [Bsm])
                        if has_s:
                            LD(ms0[:], D["ms"], [Bsm])
                            V(lambda e: e.tensor_copy(A4[0:4, NP:NT], A1[0:4, NP:NT]), [b1], [b4])
                            g3 = A4[0:4, NP:NT].rearrange("p (j l) -> p j l", l=8)
                            V(lambda e: e.tensor_tensor(g3[:, :, 0], g3[:, :, 0], ms0[:], ALU.max), [b4, Bsm], [b4])
                            V(lambda e: e.tensor_tensor_scan(A2[0:4, NP:NT], negm[:], A4[0:4, NP:NT], 0.0, ALU.add, ALU.max), [b4, Bc], [b2])
                        V(lambda e: e.tensor_copy(A4[0:4, 0:NP].rearrange("p (c t) -> p c t", t=128),
                                                  mxe[:, 0:ntp].unsqueeze(2).broadcast_to([4, ntp, 128])), [Bsm], [b4])
                        if has_s:
                            V(lambda e: e.tensor_copy(A4[0:4, NP:NT].rearrange("p (j l) -> p j l", l=8),
                                                      ms0[:].unsqueeze(2).broadcast_to([4, 16, 8])), [Bsm], [b4])
                        V(lambda e: e.tensor_tensor(decrow[:, 0:ntp], mxe[:, 0:ntp], mxe[:, 1:1 + ntp], ALU.subtract), [Bsm], [Bsm])
                        if has_s:
                            V(lambda e: e.tensor_tensor(decrow[:, 8:24], ms0[:], A2[0:4, NP:NT].rearrange("p (j l) -> p j l", l=8)[:, :, 7],
                                                        ALU.subtract), [Bsm, b2], [Bsm])
                        else:
                            V(lambda e: e.memset(decrow[:, 8:24], 0.0), (), [Bsm])
                        if ntp < 8:
                            V(lambda e: e.memset(decrow[:, ntp:8], 0.0), (), [Bsm])
                        A(lambda e: e.activation(decrow[:], decrow[:], AF.Exp), [Bsm], [Bsm])
                        ps, bp = PSS()
                        for hh in range(4):
                            P(lambda e, hh=hh, ps=ps: e.matmul(ps[:, hh * 24:(hh + 1) * 24], sel[:, hh, :], decrow[:], start=True, stop=True),
                              [Bc, Bsm], [bp])
                        V(lambda e, ps=ps: e.tensor_copy(decbc[:].rearrange("p a b -> p (a b)"), ps[:, 0:96]), [bp], [Bdec])
                        if last:
                            V(lambda e: e.tensor_tensor(msout[:, 16:17], A2[0:4, NP - 1:NP], A3[0:4, NP - 1:NP], ALU.subtract), [b2, b3], [Bsm])
                            V(lambda e: e.tensor_tensor(msout[:, 0:16], A2[0:4, NP:NT].rearrange("p (j l) -> p j l", l=8)[:, :, 7],
                                                        A3[0:4, NP:NT].rearrange("p (j l) -> p j l", l=8)[:, :, 7], ALU.subtract), [b2, b3], [Bsm])
                            STO(D["mp"], msout[:, 16:17], [Bsm]); STO(D["ms_o"], msout[:, 0:16], [Bsm])
                        V(lambda e: e.tensor_copy(carr[:, 0:1], A3[0:4, NP - 1:NP]), [b3], [Bcar])
                        V(lambda e: e.tensor_copy(carr[:, 1:2], A2[0:4, NP - 1:NP]), [b2], [Bcar])
                        V(lambda e: e.tensor_tensor(A3[0:4, 0:NT], A4[0:4, 0:NT], A3[0:4, 0:NT], ALU.subtract), [b3, b4], [b3])
                        V(lambda e: e.tensor_tensor(A1[0:4, 0:NT], A1[0:4, 0:NT], A4[0:4, 0:NT], ALU.subtract), [b1, b4], [b1])
                        A(lambda e: e.activation(A1[0:4, 0:NT], A1[0:4, 0:NT], AF.Exp, bias=CLNK[0:4]), [b1], [b1])
                        ps, bp = PSS()
                        for c in range(NT // 128):
                            P(lambda e, c=c, ps=ps: e.matmul(ps[:, c * 4:(c + 1) * 4], A1[0:4, c * 128:(c + 1) * 128], ident[0:4, 0:4],
                                                             start=True, stop=True), [b1, Bc], [bp])
                        V(lambda e, ps=ps: e.tensor_copy(ektm[:, 0:NT // 128, :].rearrange("p a b -> p (a b)"), ps[:, 0:4 * (NT // 128)]),
                          [bp], [Bek])
                        ck(2)
                        C0, S0 = Fs[6], Fs[7]
                        LD(Fs[8][:, 0:NT], D["posrow"][sbi][:, 0:NT], [BF[8]])
                        V(lambda e: e.tensor_scalar(Fs[8][:, 0:NT], Fs[8][:, 0:NT], pv("invf"), None, ALU.mult), [BF[8], Bc], [BF[8]])
                        sin_of(C0[:, 0:NT], Fs[8][:, 0:NT], 0.5 * PI, Fs[2][:, 0:NT], [BF[8]], [BF[6]], BF[2])
                        sin_of(S0[:, 0:NT], Fs[8][:, 0:NT], 0.0, Fs[2][:, 0:NT], [BF[8]], [BF[7]], BF[2])
                        V(lambda e: e.tensor_scalar(S0[:, 0:NT], S0[:, 0:NT], pv("sgn"), None, ALU.mult), [BF[7], Bc], [BF[7]])
                        V(lambda e: e.memset(vtm[:, :, 128:129], 1.0), (), [Bv])
                        V(lambda e: e.memset(vtm2[:, :, 128:129], 1.0), (), [Bv2])
                        SETS = [(Hs[0], Hs[1], Hs[2], vtm, BH[0], BH[1], BH[2], Bv), (Hs[3], Hs[4], gT2, vtm2, BH[3], BH[4], Bg2, Bv2)]
                        xq, xk, qT, kT, gT = Fs[0], Fs[1], Hs[0], Hs[1], Hs[2]
                        bxq, bxk, bqT, bkT, bgT = BF[0], BF[1], BH[0], BH[1], BH[2]
                        W = D["w_in_ab"]
                        def front_m(hh):
                            qT, kT, gT, vt_, bqT, bkT, bgT, bv_ = SETS[hh % 2]
                            slot, bs = wload([(D["w_ab_h"][:, hh * 512:(hh + 1) * 512], 0)], 8)
                            for (xx, bx, cc) in ((xq, bxq, 0), (xk, bxk, 128)):
                                def ev(ps, bp, bi, o, n, xx=xx, bx=bx):
                                    if o < NP:
                                        A(lambda e: e.copy(xx[:, 3 + o:3 + o + n], ps[:, 0:n]), [bp], [bx])
                                    else:
                                        A(lambda e: e.copy(xx[:, NP + 3:NP + 3 + 176].rearrange("p (j l) -> p j l", l=11)[:, :, 3:11],
                                                           ps[:, 0:128].rearrange("p (j l) -> p j l", l=8)), [bp], [bx])
                                proj_fm(slot, bs, cc, NT, ev)
                                yield

                            def evg(ps, bp, bi, o, n):
                                A(lambda e: e.activation(gT[:, o:o + n], ps[:, 0:n], AF.Sigmoid), [bp], [bgT])
                            proj_fm(slot, bs, 384, NT, evg)
                            yield
                            vproj(slot, bs, 256, NT, 129, vt_, bv_)
                            yield
                            for (xx, bx, ch, oT, boT) in ((xq, bxq, hh, qT, bqT), (xk, bxk, 4 + hh, kT, bkT)):
                                V(lambda e, xx=xx, ch=ch: e.tensor_copy(xx[:, 0:3], tails[:, ch, :]), [Btl], [bx])
                                acc = Fs[2]
                                V(lambda e, xx=xx, ch=ch: e.tensor_scalar(acc[:, 0:NP], xx[:, 0:NP], pv("cw0", ch), pv("cb", ch), ALU.mult, ALU.add),
                                  [bx, Bc], [BF[2]])
                                for j in range(1, 4):
                                    V(lambda e, xx=xx, ch=ch, j=j: e.scalar_tensor_tensor(acc[:, 0:NP], xx[:, j:j + NP], pv("cw%d" % j, ch), acc[:, 0:NP],
                                                                                         ALU.mult, ALU.add), [bx, Bc, BF[2]], [BF[2]])
                                if has_s:
                                    LD(xx[:, NP + 3:NP + 3 + 176].rearrange("p (j l) -> p j l", l=11)[:, :, 0:3],
                                       D["convs"][ch * 128:(ch + 1) * 128], [bx])
                                    xs3 = xx[:, NP + 3:NP + 3 + 176].rearrange("p (j l) -> p j l", l=11)
                                    a3 = acc[:, NP:NT].rearrange("p (j l) -> p j l", l=8)
                                    V(lambda e, xs3=xs3, a3=a3, ch=ch: e.tensor_scalar(a3, xs3[:, :, 0:8], pv("cw0", ch), pv("cb", ch), ALU.mult, ALU.add),
                                      [bx, Bc], [BF[2]])
                                    for j in range(1, 4):
                                        V(lambda e, xs3=xs3, a3=a3, ch=ch, j=j: e.scalar_tensor_tensor(a3, xs3[:, :, j:j + 8], pv("cw%d" % j, ch), a3,
                                                                                                      ALU.mult, ALU.add), [bx, Bc, BF[2]], [BF[2]])
                                    STO(D["convs_o"][ch * 128:(ch + 1) * 128], xs3[:, :, 8:11], [bx])
                                    STO(D["convp"][ch * 128:(ch + 1) * 128], xx[:, NP:NP + 3], [bx])
                                A(lambda e, oT=oT: e.activation(oT[:, 0:NT], acc[:, 0:NT], AF.Silu), [BF[2]], [boT])
                                V(lambda e, xx=xx, ch=ch: e.tensor_copy(tails[:, ch, :], xx[:, NP:NP + 3]), [bx], [Btl])
                                yield

                        def back_m(hh):
                            qT, kT, gT, vt_, bqT, bkT, bgT, bv_ = SETS[hh % 2]
                            if has_s:
                                load_sample_state(D["Us"], hh, 129)
                                make_qz(qT, bqT, NP)
                                V(lambda e: e.tensor_copy(nbs[:], Usf[:, :, 128:129].broadcast_to([128, 16, 128])), [BUs], [Bnbs])
                            for bi, (o, n) in enumerate(blks):
                                PT, bPT = PS()
                                dps, bd = PS()
                                pinned.update((pb.index(PT), pb.index(dps)))
                                for c in range(n // 128):
                                    tcol = o + c * 128
                                    tix = tcol // 128
                                    smp = tcol >= NP
                                    ekf = lambda t_: ektm[:, t_ // 128, hh:hh + 1]
                                    ctx_ = CTX.pop(tcol, None) or att_pre(qT, kT, bqT, bkT, tcol, ekf(tcol), smp)
                                    if tcol + 128 < NT:
                                        CTX[tcol + 128] = att_pre(qT, kT, bqT, bkT, tcol + 128, ekf(tcol + 128), tcol + 128 >= NP)
                                    att_tile(qT, kT, bqT, bkT, tcol, vt_[:, tix, :], 129, hh, ektm[:, tix, hh:hh + 1],
                                             (lambda j, hh=hh: decbc[:, hh, 8 + j:9 + j]) if smp else decbc[:, hh, tix:tix + 1],
                                             PT, bPT, c * 128, smp, den=(dps, bd), mlstm_h=hh, usbuf=Usf, bv=bv_, ctx=ctx_)
                                    step()
                                ps, bp = PSS()
                                P(lambda e, ps=ps, hh=hh: e.matmul(ps[:, 0:n], sel[:, hh, :], A3[0:4, o:o + n], start=True, stop=True), [Bc, b3], [bp])
                                dn = NT5[0]
                                A(lambda e, ps=ps: e.activation(dn[:, 0:n], ps[:, 0:n], AF.Exp, scale=-1.0), [bp], [BN[0]])
                                ab = NT5[1]
                                A(lambda e, dps=dps: e.activation(ab[:, 0:n], dps[:, 0:n], AF.Abs), [bd], [BN[1]])
                                V(lambda e: e.tensor_tensor(ab[:, 0:n], ab[:, 0:n], dn[:, 0:n], ALU.max), [BN[0], BN[1]], [BN[1]])
                                V(lambda e: e.reciprocal(ab[:, 0:n], ab[:, 0:n]), [BN[1]], [BN[1]])
                                hv = NT5[2]
                                V(lambda e, PT=PT: e.tensor_tensor(hv[:, 0:n], PT[:, 0:n], ab[:, 0:n], ALU.mult), [bPT, BN[1]], [BN[2]])
                                V(lambda e: e.tensor_tensor(hv[:, 0:n], hv[:, 0:n], gT[:, o:o + n], ALU.mult), [BN[2], bgT], [BN[2]])
                                rs = rstd_from(lambda q: A(lambda e: e.activation(q[:, 0:n], hv[:, 0:n], AF.Square), [BN[2]], [Bsq[0]]), n, None)
                                V(lambda e, hh=hh: e.scalar_tensor_tensor(mix[:, hh, o:o + n], hv[:, 0:n], pv("gna", hh), rs[:, 0:n], ALU.mult, ALU.mult),
                                  [BN[2], BN[3], Bc], [Bmix[hh][bi]])
                                pinned.clear()
                                step()
                            if has_s:
                                STO(D["Us_o"][:, hh, :, :].rearrange("j d e -> d j e"), Usf[:, :, :], [BUs])
                            if last:
                                STO(D["Up"][hh], Up[:, hh, :], [BU[hh]])
                        for _ in front_m(0):
                            pass
                        for hh in range(4):
                            STEP[0] = front_m(hh + 1) if hh + 1 < 4 else None
                            back_m(hh)
                            while STEP[0] is not None:
                                step()
                        ck(3)
                        def front_r(hh):
                            qT, kT, gT, vt_, bqT, bkT, bgT, bv_ = SETS[hh % 2]
                            b0 = 2056
                            slot, bs = wload([(D["w_ab_h"][:, (4 + hh) * 512:(5 + hh) * 512], 0)], 8)
                            for (xx, bx, cc, oT, boT) in ((xq, bxq, 0, qT, bqT), (xk, bxk, 128, kT, bkT)):
                                def ev(ps, bp, bi, o, n, xx=xx, bx=bx):
                                    A(lambda e: e.copy(xx[:, o:o + n], ps[:, 0:n]), [bp], [bx])
                                proj_fm(slot, bs, cc, NT, ev)
                                yield
                                xsw, t1, t2 = Fs[2], Fs[3], Fs[4]
                                A(lambda e, xx=xx: e.copy(xsw[0:64, 0:NT], xx[64:128, 0:NT]), [bx], [BF[2]])
                                A(lambda e, xx=xx: e.copy(xsw[64:128, 0:NT], xx[0:64, 0:NT]), [bx], [BF[2]])
                                V(lambda e, xx=xx: e.tensor_tensor(t1[:, 0:NT], xx[:, 0:NT], C0[:, 0:NT], ALU.mult), [bx, BF[6]], [BF[3]])
                                V(lambda e: e.tensor_tensor(t2[:, 0:NT], xsw[:, 0:NT], S0[:, 0:NT], ALU.mult), [BF[2], BF[7]], [BF[4]])
                                V(lambda e, oT=oT: e.tensor_tensor(oT[:, 0:NT], t1[:, 0:NT], t2[:, 0:NT], ALU.add), [BF[3], BF[4]], [boT])

                            def evg(ps, bp, bi, o, n):
                                A(lambda e: e.activation(gT[:, o:o + n], ps[:, 0:n], AF.Silu), [bp], [bgT])
                            proj_fm(slot, bs, 384, NT, evg)
                            yield
                            vproj(slot, bs, 256, NT, 128, vt_, bv_)
                            yield

                        def back_r(hh):
                            qT, kT, gT, vt_, bqT, bkT, bgT, bv_ = SETS[hh % 2]
                            if has_s:
                                load_sample_state(D["rets"], hh, 128)
                                make_qz(qT, bqT, NP)
                            g128 = math.exp(128 * LG[hh]); g8 = math.exp(8 * LG[hh])
                            for bi, (o, n) in enumerate(blks):
                                PT, bPT = PS()
                                pinned.add(pb.index(PT))
                                for c in range(n // 128):
                                    tcol = o + c * 128
                                    tix = tcol // 128
                                    smp = tcol >= NP
                                    ekf = lambda t_: gk[:, 1 if t_ >= NP else 0, hh:hh + 1]
                                    ctx_ = CTX.pop(tcol, None) or att_pre(qT, kT, bqT, bkT, tcol, ekf(tcol), smp)
                                    if tcol + 128 < NT:
                                        CTX[tcol + 128] = att_pre(qT, kT, bqT, bkT, tcol + 128, ekf(tcol + 128), tcol + 128 >= NP)
                                    att_tile(qT, kT, bqT, bkT, tcol, vt_[:, tix, :], 128, 4 + hh, gk[:, 1 if smp else 0, hh:hh + 1],
                                             (lambda j, g8=g8: g8) if smp else g128, PT, bPT, c * 128, smp, usbuf=Usf, bv=bv_, ctx=ctx_)
                                    step()
                                hv = NT5[2]
                                if o < NP:
                                    V(lambda e, PT=PT, hh=hh: e.tensor_tensor(hv[:, 0:n].rearrange("p (c t) -> p c t", t=128),
                                                                             PT[:, 0:n].rearrange("p (c t) -> p c t", t=128),
                                                                             Gq[:, 0, hh, :].unsqueeze(1).broadcast_to([128, n // 128, 128]), ALU.mult),
                                      [bPT, Bc], [BN[2]])
                                else:
                                    V(lambda e, PT=PT, hh=hh: e.tensor_tensor(hv[:, 0:n], PT[:, 0:n], Gq[:, 1, hh, :], ALU.mult), [bPT, Bc], [BN[2]])
                                rs = rstd_from(lambda q: A(lambda e: e.activation(q[:, 0:n], hv[:, 0:n], AF.Square), [BN[2]], [Bsq[0]]), n, None)
                                V(lambda e: e.tensor_tensor(hv[:, 0:n], hv[:, 0:n], rs[:, 0:n], ALU.mult), [BN[2], BN[3]], [BN[2]])
                                V(lambda e, hh=hh: e.tensor_tensor(mix[:, 4 + hh, o:o + n], hv[:, 0:n], gT[:, o:o + n], ALU.mult),
                                  [BN[2], bgT], [Bmix[4 + hh][bi]])
                                pinned.clear()
                                step()
                            if has_s:
                                STO(D["rets_o"][:, hh, :, :].rearrange("j d e -> d j e"), Usf[:, :, 0:128], [BUs])
                            if last:
                                STO(D["retp"][hh], Up[:, 4 + hh, 0:128], [BU[4 + hh]])
                        for _ in front_r(0):
                            pass
                        for hh in range(4):
                            STEP[0] = front_r(hh + 1) if hh + 1 < 4 else None
                            back_r(hh)
                            while STEP[0] is not None:
                                step()
                        resid_proj(D["w_out_ab"][0] if False else D["w_out_ab"], NT, mix, Bmix)
                        ck(4)
                    else:
                        W = D["w_in_cd"]
                        qf, ff, eb, enb, tmp = Fs[0], Fs[1], Fs[2], Fs[3], Fs[4]
                        qT, kT, gT = Hs[0], Hs[1], Hs[2]
                        bqT, bkT, bgT = BH[0], BH[1], BH[2]
                        def front_h(hh):
                            qT, kT, gT, vt_, bqT, bkT, bgT, bv_ = SETS[hh % 2]
                            slot, bs = wload([(D["w_cd_h"][:, hh * 512:(hh + 1) * 512], 0)], 8)

                            def evq(ps, bp, bi, o, n):
                                A(lambda e: e.activation(qf[:, o:o + n], ps[:, 0:n], AF.Copy, scale=128.0 ** -0.5), [bp], [BF[0]])
                            proj_fm(slot, bs, 0, NT, evq)
                            yield

                            def evf(ps, bp, bi, o, n):
                                A(lambda e: e.activation(ff[:, o:o + n], ps[:, 0:n], AF.Sigmoid), [bp], [BF[1]])
                            proj_fm(slot, bs, 128, NT, evf)
                            yield

                            def evg(ps, bp, bi, o, n):
                                A(lambda e: e.activation(gT[:, o:o + n], ps[:, 0:n], AF.Silu), [bp], [bgT])
                            proj_fm(slot, bs, 384, NT, evg)
                            yield
                            vproj(slot, bs, 256, NT, 128, vt_, bv_)
                            yield
                            V(lambda e, hh=hh: e.tensor_scalar(ff[:, 0:NT], ff[:, 0:NT], oml[:, hh:hh + 1], lb[:, hh:hh + 1], ALU.mult, ALU.add),
                              [BF[1], Bc], [BF[1]])
                            A(lambda e: e.activation(tmp[:, 0:NT], ff[:, 0:NT], AF.Ln), [BF[1]], [BF[4]])
                            V(lambda e: e.tensor_tensor_scan(eb[:, 0:NT], segm[:, 0:NT], tmp[:, 0:NT], 0.0, ALU.mult, ALU.add), [BF[4], Bc], [BF[2]])
                            A(lambda e: e.activation(enb[:, 0:NT], eb[:, 0:NT], AF.Exp, scale=-1.0), [BF[2]], [BF[3]])
                            A(lambda e: e.activation(eb[:, 0:NT], eb[:, 0:NT], AF.Exp), [BF[2], BF[3]], [BF[2]])
                            V(lambda e, hh=hh: e.tensor_copy(hdec[:, hh % 2, 0:ntp], eb[:, 0:NP].rearrange("p (c t) -> p c t", t=128)[:, :, 127]), [BF[2]], [Bhd])
                            if has_s:
                                V(lambda e, hh=hh: e.tensor_copy(hdec[:, hh % 2, 8:24], eb[:, NP:NT].rearrange("p (j l) -> p j l", l=8)[:, :, 7]), [BF[2]], [Bhd])
                            V(lambda e: e.tensor_scalar(ff[:, 0:NT], ff[:, 0:NT], -1.0, 1.0, ALU.mult, ALU.add), [BF[1], BF[4]], [BF[1]])
                            V(lambda e: e.tensor_tensor(qT[:, 0:NT], qf[:, 0:NT], eb[:, 0:NT], ALU.mult), [BF[0], BF[2]], [bqT])
                            V(lambda e: e.tensor_tensor(kT[:, 0:NT], ff[:, 0:NT], enb[:, 0:NT], ALU.mult), [BF[1], BF[3]], [bkT])

                        def back_h(hh):
                            qT, kT, gT, vt_, bqT, bkT, bgT, bv_ = SETS[hh % 2]
                            if has_s:
                                load_sample_state(D["hgrns"], hh, 128)
                                make_qz(qT, bqT, NP)
                            for bi, (o, n) in enumerate(blks):
                                PT, bPT = PS()
                                pinned.add(pb.index(PT))
                                for c in range(n // 128):
                                    tcol = o + c * 128
                                    tix = tcol // 128
                                    smp = tcol >= NP
                                    ctx_ = CTX.pop(tcol, None) or att_pre(qT, kT, bqT, bkT, tcol, None, smp)
                                    if tcol + 128 < NT:
                                        CTX[tcol + 128] = att_pre(qT, kT, bqT, bkT, tcol + 128, None, tcol + 128 >= NP)
                                    att_tile(qT, kT, bqT, bkT, tcol, vt_[:, tix, :], 128, 8 + hh, None,
                                             (lambda j, hh=hh: hdec[:, hh % 2, 8 + j:9 + j]) if smp else hdec[:, hh % 2, tix:tix + 1],
                                             PT, bPT, c * 128, smp, usbuf=Usf, bv=bv_, ctx=ctx_)
                                    step()
                                rs = rstd_from(lambda q, PT=PT, bPT=bPT: A(lambda e: e.activation(q[:, 0:n], PT[:, 0:n], AF.Square), [bPT], [Bsq[0]]), n, None)
                                hv = NT5[2]
                                V(lambda e, PT=PT: e.tensor_tensor(hv[:, 0:n], PT[:, 0:n], rs[:, 0:n], ALU.mult), [bPT, BN[3]], [BN[2]])
                                V(lambda e, hh=hh: e.scalar_tensor_tensor(mix[:, hh, o:o + n], hv[:, 0:n], pv("gnc", hh), gT[:, o:o + n], ALU.mult, ALU.mult),
                                  [BN[2], bgT, Bc], [Bmix[hh][bi]])
                                pinned.clear()
                                step()
                            if has_s:
                                STO(D["hgrns_o"][:, hh, :, :].rearrange("j d e -> d j e"), Usf[:, :, 0:128], [BUs])
                            if last:
                                STO(D["hgrnp"][hh], Up[:, 8 + hh, 0:128], [BU[8 + hh]])
                        for _ in front_h(0):
                            pass
                        for hh in range(4):
                            STEP[0] = front_h(hh + 1) if hh + 1 < 4 else None
                            back_h(hh)
                            while STEP[0] is not None:
                                step()
                        ck(7)
                        suf = Fs[8]; bsuf = BF[8]
                        V(lambda e: e.memset(cch[:, 5, 0:1], 0.0), (), S5FINE + [BUs, Bkz, Bqz, Bcch, Bg2, Bv2])
                        pinned.clear(); S5MODE[0] = True
                        sub = Hs[0]; bsub = BH[0]
                        if has_s:
                            LD(x0s[:, 0], D["x0re"], [Bx0]); LD(x0s[:, 1], D["x0im"], [Bx0])
                            bz = lambda v: v.unsqueeze(2).broadcast_to([128, 16, 16])
                            u1 = Fs[6][:, 0:256].rearrange("p (a b) -> p a b", a=16); u2 = Fs[7][:, 0:256].rearrange("p (a b) -> p a b", a=16)
                            u3 = Fs[6][:, 256:512].rearrange("p (a b) -> p a b", a=16); u4 = Fs[7][:, 256:512].rearrange("p (a b) -> p a b", a=16)
                            for (cr, ci) in ((izr, izi), (ar_, ai_)):
                                V(lambda e, cr=cr: e.tensor_tensor(u1, x0s[:, 0], bz(cr), ALU.mult), [Bx0, Bc], [BF[6]])
                                V(lambda e, ci=ci: e.tensor_tensor(u2, x0s[:, 1], bz(ci), ALU.mult), [Bx0, Bc], [BF[7]])
                                V(lambda e, ci=ci: e.tensor_tensor(u3, x0s[:, 0], bz(ci), ALU.mult), [Bx0, Bc], [BF[6]])
                                V(lambda e, cr=cr: e.tensor_tensor(u4, x0s[:, 1], bz(cr), ALU.mult), [Bx0, Bc], [BF[7]])
                                V(lambda e: e.tensor_tensor(x0s[:, 0], u1, u2, ALU.subtract), [BF[6], BF[7]], [Bx0])
                                V(lambda e: e.tensor_tensor(x0s[:, 1], u3, u4, ALU.add), [BF[6], BF[7]], [Bx0])
                        ntl = NT // 128
                        slot_su, bs_su = wload([(W[:, 2048:2560], 0)], 8)
                        for oc in range(4):
                            slot, bs = slot_su, bs_su

                            def evs(ps, bp, bi, o, n):
                                A(lambda e: e.copy(suf[:, o:o + n], ps[:, 0:n]), [bp], [bsuf])
                                A(lambda e: e.copy(sub[:, o:o + n], ps[:, 0:n]), [bp], [bsub])
                            proj_fm(slot, bs, oc * 128, NT, evs)
                            for part in range(2):
                                S.dma("pool", lambda e, part=part, oc=oc: e.dma_start(out=bct[:, 0, part], in_=D["BT"][part, oc * 4:(oc + 1) * 4].rearrange("i k m -> k i m")),
                                      bcsem[part * 2], (), (Bbct if part == 0 else [Bbct[part * 2]]))
                                S.dma("pool", lambda e, part=part, oc=oc: e.dma_start(out=bct[:, 1, part], in_=D["CT"][part, oc * 4:(oc + 1) * 4].rearrange("i k m -> k i m")),
                                      bcsem[part * 2 + 1], (), [Bbct[part * 2 + 1]])
                            V(lambda e: e.tensor_scalar(bct[:, 1, 1], bct[:, 1, 1], -1.0, None, ALU.mult), Bbct, [Bbct[3]])
                            LD(tbe[:], tabE[:, :, oc * 512:(oc + 1) * 512], [Btab], r=[Bscr])
                            LD(tbz[:], tabZ[:, :, 4 * oc:4 * oc + 4, :], [Btab], r=[Bscr])
                            if has_s:
                                EsT = [Hs[1][:, 0:512], Hs[2][:, 0:512]]
                                ZsT = [Hs[3][:, 0:512], Hs[4][:, 0:512]]
                                build_tables(oc, pv("s8"), jrow[:, 3, :], lambda part: EsT[part],
                                             lambda part: ZsT[part].rearrange("p (a b) -> p a b", a=4), [Bes, BH[1], BH[2]], [Bes, BH[3], BH[4]])
                                for part in range(2):
                                    V(lambda e, part=part, oc=oc: e.tensor_copy(cbs[:, part].rearrange("p a (j l) -> p a j l", l=8),
                                                                             x0s[:, part, 4 * oc:4 * oc + 4, :].unsqueeze(3).broadcast_to([128, 4, 16, 8])),
                                      [Bx0], [Bcbs])
                            i4 = slice(4 * oc, 4 * oc + 4)
                            ypsh = [None]

                            def stA(c):
                                smp = c * 128 >= NP; tc0 = c * 128; k2 = c % 2
                                wrb, wib = wtb[k2][0], wtb[k2][1]; xrb, xib = xtb[k2][0], xtb[k2][1]
                                bE = Bes if smp else Btab
                                smp = c * 128 >= NP
                                tc0 = c * 128
                                Er = EsT[0] if smp else tbe[:, 0, :]
                                Ei = EsT[1] if smp else tbe[:, 1, :]
                                bE = Bes if smp else Btab
                                k2 = c % 2
                                pr, bpr = PS(); pi_, bpi = PS()
                                P(lambda e, pr=pr: e.matmul(pr[:, 0:512], sub[:, tc0:tc0 + 128], bct[:, 0, 0].rearrange("p a b -> p (a b)"), start=True, stop=True),
                                  Bbct + [bsub], [bpr])
                                P(lambda e, pi_=pi_: e.matmul(pi_[:, 0:512], sub[:, tc0:tc0 + 128], bct[:, 0, 1].rearrange("p a b -> p (a b)"), start=True, stop=True),
                                  Bbct + [bsub], [bpi])
                                prb = qzf[:, (2 * k2) * 512:(2 * k2 + 1) * 512]; pib = qzf[:, (2 * k2 + 1) * 512:(2 * k2 + 2) * 512]
                                A(lambda e, pr=pr: e.copy(prb, pr[:, 0:512]), [bpr], [Bprb[k2]])
                                A(lambda e, pi_=pi_: e.copy(pib, pi_[:, 0:512]), [bpi], [Bprb[k2]])
                                wrb, wib = wtb[k2][0], wtb[k2][1]
                                ta, tb, tc_, td = [kzf[:, q_ * 512:(q_ + 1) * 512] for q_ in range(4)]
                                V(lambda e: e.tensor_tensor(ta, prb, Er, ALU.mult), [Bprb[k2], bE], [Bt4[0]])
                                V(lambda e: e.tensor_tensor(tb, pib, Ei, ALU.mult), [Bprb[k2], bE], [Bt4[1]])
                                V(lambda e: e.tensor_tensor(wrb[:], ta, tb, ALU.subtract), [Bt4[0], Bt4[1]], [Bwt[k2]])
                                V(lambda e: e.tensor_tensor(tc_, pib, Er, ALU.mult), [Bprb[k2], bE], [Bt4[2]])
                                V(lambda e: e.tensor_tensor(td, prb, Ei, ALU.mult), [Bprb[k2], bE], [Bt4[3]])
                                V(lambda e: e.tensor_tensor(wib[:], tc_, td, ALU.add), [Bt4[2], Bt4[3]], [Bwt[k2]])

                            TC = {}

                            def stB1(c):
                                smp = c * 128 >= NP; tc0 = c * 128; k2 = c % 2
                                wrb, wib = wtb[k2][0], wtb[k2][1]; xrb, xib = xtb[k2][0], xtb[k2][1]
                                bE = Bes if smp else Btab
                                csr, bcsr = PSS(); csi, bcsi = PSS()
                                cur = S5PAR[oc]
                                TC[c] = (csr, bcsr, csi, bcsi, cur)
                                msk = maskb if smp else maskc
                                for il in range(4):
                                    P(lambda e, il=il, csr=csr: e.matmul(csr[:, il * 128:(il + 1) * 128], wrb[:, il * 128:(il + 1) * 128], msk[:], start=True, stop=(not smp)),
                                      [Bwt[k2], Bc], [bcsr])
                                    P(lambda e, il=il, csi=csi: e.matmul(csi[:, il * 128:(il + 1) * 128], wib[:, il * 128:(il + 1) * 128], msk[:], start=True, stop=(not smp)),
                                      [Bwt[k2], Bc], [bcsi])
                                    if smp:
                                        P(lambda e, il=il, csr=csr: e.matmul(csr[:, il * 128:(il + 1) * 128], identb[:], cbs[:, 0, il, :], start=False, stop=True), [Bc, Bcbs], [bcsr])
                                        P(lambda e, il=il, csi=csi: e.matmul(csi[:, il * 128:(il + 1) * 128], identb[:], cbs[:, 1, il, :], start=False, stop=True), [Bc, Bcbs], [bcsi])
                                cs3r = csr[:, 0:512].rearrange("p (a b) -> p a b", a=4); cs3i = csi[:, 0:512].rearrange("p (a b) -> p a b", a=4)
                                if not smp:
                                    nxt = 1 - cur
                                    V(lambda e: e.tensor_tensor(cch[:, 0, :], cs3r[:, :, 127], s5d[:, cur, 0, i4], ALU.add), [bcsr, Bs5d[cur]], [Bcr])
                                    V(lambda e: e.tensor_tensor(cch[:, 1, :], cs3i[:, :, 127], s5d[:, cur, 1, i4], ALU.add), [bcsi, Bs5d[cur]], [Bcr])
                                    V(lambda e: e.tensor_tensor(cch[:, 2, :], cch[:, 0, :], a128r[:, i4], ALU.mult), [Bcr, Bc], [Bcch])
                                    V(lambda e: e.tensor_tensor(cch[:, 3, :], cch[:, 1, :], a128i[:, i4], ALU.mult), [Bcr, Bc], [Bcch])
                                    V(lambda e: e.tensor_tensor(cch[:, 4, :], cch[:, 0, :], a128i[:, i4], ALU.mult), [Bcr, Bc], [Bcch])
                                    V(lambda e: e.tensor_tensor(cch[:, 5, :], cch[:, 1, :], a128r[:, i4], ALU.mult), [Bcr, Bc], [Bcch])
                                    V(lambda e: e.tensor_tensor(s5d[:, nxt, 0, i4], cch[:, 2, :], cch[:, 3, :], ALU.subtract), [Bcch], [Bs5d[nxt]])
                                    V(lambda e: e.tensor_tensor(s5d[:, nxt, 1, i4], cch[:, 4, :], cch[:, 5, :], ALU.add), [Bcch], [Bs5d[nxt]])
                                    S5PAR[oc] = nxt

                            def stB2(c):
                                smp = c * 128 >= NP; tc0 = c * 128; k2 = c % 2
                                wrb, wib = wtb[k2][0], wtb[k2][1]; xrb, xib = xtb[k2][0], xtb[k2][1]
                                csr, bcsr, csi, bcsi, cur = TC.pop(c)
                                csbr = usbf[:, (2 * k2) * 512:(2 * k2 + 1) * 512]; csbi = usbf[:, (2 * k2 + 1) * 512:(2 * k2 + 2) * 512]
                                if not smp:
                                    for il in range(4):
                                        i = 4 * oc + il
                                        sl = slice(il * 128, (il + 1) * 128)
                                        A(lambda e, sl=sl, i=i, csr=csr: e.activation(csbr[:, sl], csr[:, sl], AF.Identity, bias=s5d[:, cur, 0, i:i + 1]), [bcsr, Bs5d[cur], Bcr], [Bcsb[k2]])
                                        A(lambda e, sl=sl, i=i, csi=csi: e.activation(csbi[:, sl], csi[:, sl], AF.Identity, bias=s5d[:, cur, 1, i:i + 1]), [bcsi, Bs5d[cur], Bcr], [Bcsb[k2]])
                                    Zr = tbz[:, 0].rearrange("p a b -> p (a b)"); Zi = tbz[:, 1].rearrange("p a b -> p (a b)"); bZ = Btab
                                else:
                                    A(lambda e, csr=csr: e.copy(csbr, csr[:, 0:512]), [bcsr], [Bcsb[k2]])
                                    A(lambda e, csi=csi: e.copy(csbi, csi[:, 0:512]), [bcsi], [Bcsb[k2]])
                                    Zr = ZsT[0]; Zi = ZsT[1]; bZ = Bes
                                p1, p2, p3, p4 = [pt4[q_] for q_ in range(4)]
                                xrb, xib = xtb[k2][0], xtb[k2][1]
                                V(lambda e: e.tensor_tensor(p1[:], csbr, Zr, ALU.mult), [Bcsb[k2], bZ], [Bp4[0]])
                                V(lambda e: e.tensor_tensor(p2[:], csbi, Zi, ALU.mult), [Bcsb[k2], bZ], [Bp4[1]])
                                V(lambda e: e.tensor_tensor(xrb[:], p1[:], p2[:], ALU.subtract), [Bp4[0], Bp4[1]], [Bxt[k2]])
                                V(lambda e: e.tensor_tensor(p3[:], csbr, Zi, ALU.mult), [Bcsb[k2], bZ], [Bp4[2]])
                                V(lambda e: e.tensor_tensor(p4[:], csbi, Zr, ALU.mult), [Bcsb[k2], bZ], [Bp4[3]])
                                V(lambda e: e.tensor_tensor(xib[:], p3[:], p4[:], ALU.add), [Bp4[2], Bp4[3]], [Bxt[k2]])
                                if last and (smp or c == NP // 128 - 1):
                                    p13 = [q_[:].rearrange("p (a b) -> p a b", a=4) for q_ in (p1, p2, p3, p4)]
                                    if not smp:
                                        V(lambda e: e.tensor_tensor(s5po[:, 0, i4], p13[0][:, :, 127], p13[1][:, :, 127], ALU.subtract), [Bp4[0], Bp4[1]], [Bs5o])
                                        V(lambda e: e.tensor_tensor(s5po[:, 1, i4], p13[2][:, :, 127], p13[3][:, :, 127], ALU.add), [Bp4[2], Bp4[3]], [Bs5o])
                                    else:
                                        l7 = lambda q3: q3.rearrange("p a (j l) -> p a j l", l=8)[:, :, :, 7]
                                        V(lambda e: e.tensor_tensor(s5so[:, 0, i4, :], l7(p13[0]), l7(p13[1]), ALU.subtract), [Bp4[0], Bp4[1]], [Bs5o])
                                        V(lambda e: e.tensor_tensor(s5so[:, 1, i4, :], l7(p13[2]), l7(p13[3]), ALU.add), [Bp4[2], Bp4[3]], [Bs5o])

                            def stC(c):
                                smp = c * 128 >= NP; tc0 = c * 128; k2 = c % 2
                                wrb, wib = wtb[k2][0], wtb[k2][1]; xrb, xib = xtb[k2][0], xtb[k2][1]
                                bE = Bes if smp else Btab
                                if c % 4 == 0:
                                    pinned.clear()
                                    ypsh[0] = PS()
                                    pinned.add(pb.index(ypsh[0][0]))
                                yp, byp = ypsh[0]
                                yc = (c % 4) * 128
                                for il in range(4):
                                    P(lambda e, il=il, yp=yp: e.matmul(yp[:, yc:yc + 128], bct[:, 1, 0, il, :], xrb[:, il * 128:(il + 1) * 128], start=(il == 0), stop=False),
                                      Bbct + [Bxt[k2]], [byp])
                                    P(lambda e, il=il, yp=yp: e.matmul(yp[:, yc:yc + 128], bct[:, 1, 1, il, :], xib[:, il * 128:(il + 1) * 128], start=False, stop=(il == 3)),
                                      Bbct + [Bxt[k2]], [byp])
                                if c % 4 == 3 or c == ntl - 1:
                                    o = (c // 4) * 512
                                    n = (c % 4 + 1) * 128
                                    bi = o // 512
                                    yv = NT5[4]
                                    V(lambda e, yp=yp: e.scalar_tensor_tensor(yv[:, 0:n], suf[:, o:o + n], pv("s5d", oc), yp[:, 0:n], ALU.mult, ALU.add),
                                      [bsuf, byp, Bc], [BN[4]])
                                    A(lambda e: e.activation(mix[:, 4 + oc, o:o + n], yv[:, 0:n], AF.Gelu), [BN[4]], [Bmix[4 + oc][bi]])
                            stA(0)
                            stB1(0)
                            for c in range(ntl):
                                if c + 1 < ntl:
                                    stA(c + 1)
                                stB2(c)
                                if c + 1 < ntl:
                                    stB1(c + 1)
                                stC(c)
                        V(lambda e: e.memset(cch[:, 5, 0:1], 0.0), (), S5FINE + [BUs, Bkz, Bqz, Bcch, Bg2, Bv2])
                        pinned.clear()
                        S5MODE[0] = False
                        if last:
                            STO(D["s5rep"], s5po[:, 0, :], [Bs5o]); STO(D["s5imp"], s5po[:, 1, :], [Bs5o])
                            STO(D["s5res"], s5so[:, 0], [Bs5o]); STO(D["s5ims"], s5so[:, 1], [Bs5o])
                        ck(8)
                        slot, bs = wload([(D["w_glu"], 0)], 4)
                        for oc in range(4):
                            for bi, (o, n) in enumerate(blks):
                                ps, bp = PS()
                                for k in range(4):
                                    P(lambda e, k=k, ps=ps, oc=oc: e.matmul(ps[:, 0:n], slot[:, k, oc * 128:(oc + 1) * 128], mix[:, 4 + k, o:o + n],
                                                                            start=(k == 0), stop=(k == 3)), bs + [Bmix[4 + k][bi] for k in range(4)], [bp])
                                A(lambda e, ps=ps, oc=oc: e.activation(hn[:, oc, o:o + n], ps[:, 0:n], AF.Sigmoid, bias=pv("bglu", oc)), [bp, Bc], [Bhn[bi]])
                        for oc in range(4):
                            for bi, (o, n) in enumerate(blks):
                                V(lambda e, oc=oc: e.tensor_tensor(mix[:, 4 + oc, o:o + n], mix[:, 4 + oc, o:o + n], hn[:, oc, o:o + n], ALU.mult),
                                  [Bhn[bi], Bmix[4 + oc][bi]], [Bmix[4 + oc][bi]])
                        resid_proj(D["w_out_cd"], NT, mix, Bmix)
                    rmsnorm("nff%d" % layer, NT)
                    for q in range(4):
                        if sbi == 0 and layer == 0:
                            build_tab_kc(q)
                        for u in range(2):
                            c0 = q * 1024 + u * 512
                            slot, bs = wload([(D["w_ff1"][layer][:, c0:c0 + 512], 0)], 8)
                            for hc in range(4):
                                c = u * 4 + hc

                                def ev(ps, bp, bi, o, n, c=c):
                                    t = sqb[1]
                                    A(lambda e: e.activation(t[:, 0:n], ps[:, 0:n], AF.Relu), [bp], [Bsq[1]])
                                    V(lambda e: e.tensor_tensor(mix[:, c, o:o + n], t[:, 0:n], t[:, 0:n], ALU.mult), [Bsq[1]], [Bmix[c][bi]])
                                proj_fm(slot, bs, hc * 128, NT, ev)
                        for u in range(2):
                            slot, bs = wload([(D["w_ff2"][layer][q * 1024:(q + 1) * 1024, u * 512:(u + 1) * 512], 0)], 8)
                            for oc in range(4):
                                c = u * 4 + oc

                                def ev(ps, bp, bi, o, n, c=c):
                                    V(lambda e: e.tensor_tensor(h[:, c, o:o + n], h[:, c, o:o + n], ps[:, 0:n], ALU.add), [bp, Bh[c][bi]], [Bh[c][bi]])
                                proj_fm(slot, bs, oc * 128, NT, ev, rhs=mix, rbufs=lambda bi: [Bmix[k][bi] for k in range(8)])
                    ck(5)
                    rmsnorm("nple%d" % layer, NT)
                    S.dma("pool", lambda e, layer=layer: e.dma_start(out=pTb[:, :, 0:NT], in_=D["pT"][layer][:, tok0:tok0 + NT].rearrange("(k p) n -> p k n", p=128)),
                          pTsem, (), [BpT])
                    for u in range(2):
                        slot, bs = wload([(D["w_ple_gate"][layer][:, u * 512:(u + 1) * 512], 0)], 8)
                        slot2, bs2 = wload([(D["w_ple_proj"][layer][:, u * 512:(u + 1) * 512], 0)], 2)
                        for oc in range(4):
                            c = u * 4 + oc
                            for bi, (o, n) in enumerate(blks):
                                ps, bp = PS()
                                for k in range(8):
                                    P(lambda e, k=k, ps=ps: e.matmul(ps[:, 0:n], slot[:, k, oc * 128:(oc + 1) * 128], hn[:, k, o:o + n], start=(k == 0), stop=(k == 7)),
                                      bs + [Bhn[bi]], [bp])
                                gt = NT5[0]
                                A(lambda e, ps=ps: e.activation(gt[:, 0:n], ps[:, 0:n], AF.Sigmoid), [bp], [BN[0]])
                                ps2, bp2 = PS()
                                for k in range(2):
                                    P(lambda e, k=k, ps2=ps2: e.matmul(ps2[:, 0:n], slot2[:, k, oc * 128:(oc + 1) * 128], pTb[:, k, o:o + n], start=(k == 0), stop=(k == 1)),
                                      bs2 + [BpT], [bp2])
                                V(lambda e, ps2=ps2: e.tensor_tensor(gt[:, 0:n], gt[:, 0:n], ps2[:, 0:n], ALU.mult), [bp2, BN[0]], [BN[0]])
                                V(lambda e, c=c: e.tensor_tensor(h[:, c, o:o + n], h[:, c, o:o + n], gt[:, 0:n], ALU.add), [BN[0], Bh[c][bi]], [Bh[c][bi]])
                ck(9)
                rmsnorm("nfin", NT, final_out=D["yT"][:, tok0:tok0 + NT] if True else None)
        except _Stop:
            pass
        S.final_wait("sp", OUTS)
        S.emit(st)
    return nc


_NC = [None]


def kernel(**I):
    if _NC[0] is None:
        _NC[0] = build_program()
    nc = _NC[0]
    in_maps = prep_inputs(I)
    res = run_bass_kernel_spmd(nc, in_maps, core_ids=list(range(8)))
    return assemble(res.results)


def prep_inputs(I):
    f = lambda a: np.ascontiguousarray(np.asarray(a, np.float32))
    ident = np.eye(128, dtype=np.float32)
    s_ = np.arange(128)
    maskc = (s_[:, None] <= s_[None, :]).astype(np.float32)
    maskb = maskc * (s_[:, None] // 8 == s_[None, :] // 8)
    blk3 = np.broadcast_to((np.arange(16)[:, None] == (s_[None, :] // 8)).astype(np.float32)[None], (128, 16, 128)).copy()
    rowm = (s_[:, None] // 8 == np.arange(16)[None, :]).astype(np.float32)
    segm = np.ones((3, 128, NTM), np.float32); posrow = np.zeros((3, 128, NTM), np.float32); tau = np.ones((3, 128, NTM), np.float32)
    for i, (t0, NP, hs) in enumerate(SBS):
        segm[i, :, 0:NP:128] = 0.0
        posrow[i, :, 0:NP] = np.arange(t0, t0 + NP)[None]
        tau[i, :, 0:NP] = np.arange(1, NP + 1)[None]
        if hs:
            segm[i, :, NP:NP + 128:8] = 0.0
            posrow[i, :, NP:NP + 128] = (16384 + (np.arange(128) % 8))[None]
            tau[i, :, NP:NP + 128] = (1 + (np.arange(128) % 8))[None]
    negm = np.zeros((4, 128), np.float32); negm[:, 0::8] = -1e30
    sel = np.zeros((4, 4, 128), np.float32)
    for k in range(4):
        sel[k, k, :] = 1.0
    jrow = np.zeros((128, 4, 128), np.float32); jrow[:, 0, :] = (s_ + 1)[None]; jrow[:, 1, :] = (s_ % 8 + 1)[None]; jrow[:, 2, :] = s_[None]; jrow[:, 3, :] = (s_ % 8)[None]
    pvec = np.zeros((128, NPV), np.float32)

    def put(name, arr):
        o, w = PV[name]
        pvec[:, o:o + w] = arr
    for l in range(2):
        put("nmix%d" % l, _cols(I["norm_mix"][l])); put("nff%d" % l, _cols(I["norm_ff"][l])); put("nple%d" % l, _cols(I["norm_ple"][l]))
        put("lb%d" % l, _cols(I["lb_logits"][l]))
    put("nfin", _cols(I["norm_final"]))
    for j in range(4):
        put("cw%d" % j, _cols(I["conv_w_ab"][0][j]))
    put("cb", _cols(I["conv_b_ab"][0])); put("gna", _cols(I["gn_a"][0])); put("gnc", _cols(I["gn_c"][0]))
    put("s5d", _cols(I["s5_D"][0])); put("bglu", _cols(I["b_glu"][0]))
    st = lambda a: np.ascontiguousarray(np.asarray(a, np.float32).reshape(16, 2, 64).reshape(16, 128).T)
    put("are", st(I["s5_A_re"][0])); put("aim", st(I["s5_A_im"][0]))
    put("ldt", st(np.repeat(np.asarray(I["s5_log_dt"][0], np.float32)[:, None], 64, axis=1)))
    put("invf", (10000.0 ** (-(np.arange(128) % 64) / 64.0)).astype(np.float32)[:, None])
    put("sgn", np.where(s_ < 64, -1.0, 1.0).astype(np.float32)[:, None])
    put("s0", s_.astype(np.float32)[:, None]); put("s8", (s_ % 8).astype(np.float32)[:, None]); put("pidx", (s_ + 1).astype(np.float32)[:, None]); put("pidxs", (s_ % 8 + 1).astype(np.float32)[:, None]); put("rowm", rowm)
    rw = lambda a: np.broadcast_to(np.asarray(a, np.float32).reshape(1, 2048), (128, 2048))
    rowp = np.ascontiguousarray(np.stack([rw(I["s5_A_re"][0]), rw(I["s5_A_im"][0]),
                                          rw(np.repeat(np.asarray(I["s5_log_dt"][0], np.float32)[:, None], 64, axis=1))]))
    bgv = np.asarray(I["b_gate_ab"][0], np.float32)
    bg = np.stack([bgv[:4], bgv[4:]], axis=1).copy()
    BT = np.zeros((2, 16, 128, 128), np.float32); CT = np.zeros((2, 16, 128, 128), np.float32)
    for part, (Bm, Cm) in enumerate(((I["s5_B_re"][0], I["s5_C_re"][0]), (I["s5_B_im"][0], I["s5_C_im"][0]))):
        Bm = np.asarray(Bm, np.float32); Cm = np.asarray(Cm, np.float32)
        for g in range(32):
            i, gl = g // 2, g % 2
            k0 = (g % 8) * 16
            BT[part, i, k0:k0 + 16, gl * 64:(gl + 1) * 64] = Bm[g].T
            CT[part, i, gl * 64:(gl + 1) * 64, k0:k0 + 16] = Cm[g].T
    wab = np.asarray(I["w_in_ab"][0], np.float32); wcd = np.asarray(I["w_in_cd"][0], np.float32)
    w_ab_h = np.concatenate([wab[:, b0 + sec * 512 + hh * 128:b0 + sec * 512 + (hh + 1) * 128]
                             for b0 in (0, 2056) for hh in range(4) for sec in range(4)], axis=1)
    w_cd_h = np.concatenate([wcd[:, sec * 512 + hh * 128:sec * 512 + (hh + 1) * 128] for hh in range(4) for sec in range(4)], axis=1)
    common = dict(w_ab_h=f(w_ab_h), w_cd_h=f(w_cd_h), w_in_ab=f(I["w_in_ab"][0]), wg=f(I["w_in_ab"][0][:, 2048:2056]), w_out_ab=f(I["w_out_ab"][0]),
                  w_in_cd=f(I["w_in_cd"][0]), w_glu=f(I["w_glu"][0]), w_out_cd=f(I["w_out_cd"][0]),
                  w_ff1=f(I["w_ff1"]), w_ff2=f(I["w_ff2"]), w_ple_proj=f(I["w_ple_proj"]), w_ple_gate=f(I["w_ple_gate"]),
                  pvec=pvec, bg=bg, BT=BT, CT=CT, ident=ident, maskc=maskc, maskb=maskb.astype(np.float32), blk3=blk3,
                  segm=segm, negm=negm, sel=sel, posrow=posrow, rowp=rowp, jrow=jrow)
    in_maps = []
    for c in range(8):
        sl = slice(16 * c, 16 * c + 16)
        xT = np.concatenate([np.asarray(I["x_prompt"][c]).T, np.asarray(I["x_sample"][sl]).reshape(128, 1024).T], axis=1)
        pT = np.concatenate([np.transpose(np.asarray(I["p_prompt"][:, c]), (0, 2, 1)),
                             np.transpose(np.asarray(I["p_sample"][:, sl]).reshape(2, 128, 256), (0, 2, 1))], axis=2)
        Us = np.concatenate([np.asarray(I["state_mlstm_C"][0][sl]), np.asarray(I["state_mlstm_n"][0][sl])[..., None]], axis=-1)
        x0 = lambda a: np.transpose(np.asarray(a, np.float32).reshape(16, 16, 128), (2, 1, 0))
        m = dict(common)
        m.update(xT=f(xT), pT=f(pT), convs=f(np.transpose(np.asarray(I["state_mlstm_conv"][0][sl]), (2, 0, 1))), Us=f(Us),
                 ms=f(np.asarray(I["state_mlstm_m"][0][sl]).T), rets=f(I["state_ret"][0][sl]), hgrns=f(I["state_hgrn"][0][sl]),
                 x0re=f(x0(I["state_s5_re"][0][sl])), x0im=f(x0(I["state_s5_im"][0][sl])))
        in_maps.append(m)
    return in_maps


def assemble(R):
    yp = np.zeros((8, 2048, 1024), np.float32); ys = np.zeros((128, 8, 1024), np.float32)
    convp = np.zeros((1, 8, 3, 1024), np.float32); convs = np.zeros((1, 128, 3, 1024), np.float32)
    Cp = np.zeros((1, 8, 4, 128, 128), np.float32); Cs = np.zeros((1, 128, 4, 128, 128), np.float32)
    np_ = np.zeros((1, 8, 4, 128), np.float32); ns = np.zeros((1, 128, 4, 128), np.float32)
    mp = np.zeros((1, 8, 4), np.float32); ms = np.zeros((1, 128, 4), np.float32)
    retp = np.zeros((1, 8, 4, 128, 128), np.float32); rets = np.zeros((1, 128, 4, 128, 128), np.float32)
    hgp = np.zeros((1, 8, 4, 128, 128), np.float32); hgs = np.zeros((1, 128, 4, 128, 128), np.float32)
    s5rp = np.zeros((1, 8, 32, 64), np.float32); s5ip = np.zeros((1, 8, 32, 64), np.float32)
    s5rs = np.zeros((1, 128, 32, 64), np.float32); s5is = np.zeros((1, 128, 32, 64), np.float32)
    for c in range(len(R)):
        r = R[c]
        sl = slice(16 * c, 16 * c + 16)
        yp[c] = r["yT"][:, :2048].T
        ys[sl] = r["yT"][:, 2048:].T.reshape(16, 8, 1024)
        convp[0, c] = r["convp"].T
        convs[0, sl] = np.transpose(r["convs_o"], (1, 2, 0))
        Cp[0, c] = r["Up"][:, :, :128]; np_[0, c] = r["Up"][:, :, 128]
        Cs[0, sl] = r["Us_o"][..., :128]; ns[0, sl] = r["Us_o"][..., 128]
        mp[0, c] = r["mp"][:, 0]; ms[0, sl] = r["ms_o"].T
        retp[0, c] = r["retp"]; rets[0, sl] = r["rets_o"]; hgp[0, c] = r["hgrnp"]; hgs[0, sl] = r["hgrns_o"]
        s5rp[0, c] = r["s5rep"].T.reshape(32, 64); s5ip[0, c] = r["s5imp"].T.reshape(32, 64)
        s5rs[0, sl] = np.transpose(r["s5res"], (2, 1, 0)).reshape(16, 32, 64)
        s5is[0, sl] = np.transpose(r["s5ims"], (2, 1, 0)).reshape(16, 32, 64)
    return (yp, ys, convp, convs, Cp, Cs, np_, ns, mp, ms, retp, rets, hgp, hgs, s5rp, s5rs, s5ip, s5is)
```

```python
import math, contextlib, os
import numpy as np
import concourse.bass as bass
import concourse.mybir as mybir
from concourse.bass_utils import run_bass_kernel_spmd

F32 = mybir.dt.float32
BF16 = mybir.dt.bfloat16
AF = mybir.ActivationFunctionType
ALU = mybir.AluOpType

NTM = 768
FW = 776
SBS = [(0, 768, False), (768, 768, False), (1536, 512, True)]
NTOK = 2176
EPS = 1e-6
PI = math.pi
LG = [math.log1p(-2.0 ** (-5.0 - h)) for h in range(4)]
LNK = -0.5 * math.log(128.0)


class Buf:
    __slots__ = ("w", "r")

    def __init__(self):
        self.w = None
        self.r = []


class _Rec:
    def __init__(self):
        self.call = None

    def __getattr__(self, name):
        def f(*a, **k):
            self.call = (name, a, k)
            return self
        return f


def _record(fn):
    r = _Rec()
    fn(r)
    assert r.call is not None
    return r.call


class Sched:
    ENGS = ("pe", "act", "dve", "pool", "sp")

    def __init__(self, nc):
        self.nc = nc
        self.ops = {e: [] for e in self.ENGS}
        self.cnt = {e: 0 for e in self.ENGS}
        self.seen = {e: {} for e in self.ENGS}
        self.sems = {}
        self.dma_cnt = {}

    def new_dma_sem(self):
        k = "dma%d" % len(self.dma_cnt)
        self.dma_cnt[k] = 0
        return k

    def _deps(self, eng, reads, writes, is_dma):
        waits = {}

        def add(ev, kind):
            key, val, src_eng, src_dma = ev
            if (not src_dma) and (not is_dma) and src_eng == eng and eng == "pe":
                return
            if self.seen[eng].get(key, 0) >= val:
                return
            if waits.get(key, 0) < val:
                waits[key] = val
        for b in reads:
            if b.w is not None:
                add(b.w, "raw")
        for b in writes:
            if b.w is not None:
                add(b.w, "waw")
            for r in b.r:
                add(r, "war")
        for k, v in waits.items():
            self.seen[eng][k] = v
        return list(waits.items())

    def _post(self, ev, reads, writes):
        for b in writes:
            b.w = ev
            b.r = []
        for b in reads:
            if b.w is not ev:
                b.r.append(ev)
                if len(b.r) > 24:
                    b.r = b.r[-24:] if False else b.r

    def op(self, eng, fn, reads=(), writes=()):
        waits = self._deps(eng, reads, writes, False)
        self.cnt[eng] += 1
        ev = ("e_" + eng, self.cnt[eng], eng, False)
        self.ops[eng].append((waits, _record(fn), ("e_" + eng, 1)))
        self._post(ev, reads, writes)

    def dma(self, eng, fn, sem, reads=(), writes=()):
        waits = self._deps(eng, reads, writes, True)
        prev = self.dma_cnt[sem]
        if prev > 0 and self.seen[eng].get(sem, 0) < prev:
            waits = [w_ for w_ in waits if w_[0] != sem] + [(sem, prev)]
            self.seen[eng][sem] = prev
        self.dma_cnt[sem] += 16
        ev = (sem, self.dma_cnt[sem], eng, True)
        self.ops[eng].append((waits, _record(fn), (sem, 16)))
        self._post(ev, reads, writes)

    def final_wait(self, eng, bufs):
        waits = self._deps(eng, bufs, bufs, True)
        have = dict(waits)
        for k, v in self.dma_cnt.items():
            if v > 0 and self.seen[eng].get(k, 0) < v and have.get(k, 0) < v:
                have[k] = v
        for e2 in self.ENGS:
            if e2 != eng and self.cnt[e2] > 0:
                have["e_" + e2] = self.cnt[e2]
        self.ops[eng].append((list(have.items()), None, None))

    def emit(self, stack):
        nc = self.nc
        keys = ["e_" + e for e in self.ENGS] + list(self.dma_cnt.keys())
        for k in keys:
            self.sems[k] = stack.enter_context(nc.semaphore(k))
        block = stack.enter_context(nc.Block())
        engobj = {"pe": "tensor", "act": "scalar", "dve": "vector", "pool": "gpsimd", "sp": "sync"}

        def mk(e):
            def body(engine):
                for (waits, fn, inc) in self.ops[e]:
                    for (k, v) in waits:
                        engine.wait_ge(self.sems[k], v)
                    if fn is not None:
                        name, a, k = fn
                        getattr(engine, name)(*a, **k).then_inc(self.sems[inc[0]], inc[1])
            return body
        for e in self.ENGS:
            if self.ops[e]:
                getattr(block, engobj[e])(mk(e))


PV = {}
_o = 0
for _n, _w in [("nmix0", 8), ("nmix1", 8), ("nff0", 8), ("nff1", 8), ("nple0", 8), ("nple1", 8), ("nfin", 8),
               ("cw0", 8), ("cw1", 8), ("cw2", 8), ("cw3", 8), ("cb", 8), ("gna", 4), ("gnc", 4), ("s5d", 4),
               ("bglu", 4), ("lb0", 4), ("lb1", 4), ("are", 16), ("aim", 16), ("ldt", 16), ("invf", 1),
               ("sgn", 1), ("pidx", 1), ("pidxs", 1), ("rowm", 16), ("s0", 1), ("s8", 1)]:
    PV[_n] = (_o, _w)
    _o += _w
NPV = _o


def _cols(v):
    return np.ascontiguousarray(np.asarray(v, np.float32).reshape(-1, 128).T)


def build_program():
    nc = bass.Bass("TRN2", target_bir_lowering=False)
    D = {}

    def din(name, shape):
        D[name] = nc.dram_tensor(name, list(shape), F32, kind="ExternalInput").ap()
        return D[name]

    def dout(name, shape):
        D[name] = nc.dram_tensor(name, list(shape), F32, kind="ExternalOutput").ap()
        return D[name]
    din("xT", [1024, NTOK]); din("pT", [2, 256, NTOK])
    din("convs", [1024, 16, 3]); din("Us", [16, 4, 128, 129]); din("ms", [4, 16])
    din("rets", [16, 4, 128, 128]); din("hgrns", [16, 4, 128, 128])
    din("x0re", [128, 16, 16]); din("x0im", [128, 16, 16])
    din("w_ab_h", [1024, 4096]); din("w_cd_h", [1024, 2048]); din("w_in_ab", [1024, 4104]); din("wg", [1024, 8]); din("w_out_ab", [1024, 1024])
    din("w_in_cd", [1024, 2560]); din("w_glu", [512, 512]); din("w_out_cd", [1024, 1024])
    din("w_ff1", [2, 1024, 4096]); din("w_ff2", [2, 4096, 1024])
    din("w_ple_proj", [2, 256, 1024]); din("w_ple_gate", [2, 1024, 1024])
    din("pvec", [128, NPV]); din("bg", [4, 2])
    din("BT", [2, 16, 128, 128]); din("CT", [2, 16, 128, 128])
    din("ident", [128, 128]); din("maskc", [128, 128]); din("maskb", [128, 128])
    din("blk3", [128, 16, 128]); din("segm", [3, 128, NTM]); din("negm", [4, 128]); din("sel", [4, 4, 128])
    din("posrow", [3, 128, NTM]); din("rowp", [3, 128, 2048]); din("jrow", [128, 4, 128])
    dout("yT", [1024, NTOK]); dout("convp", [1024, 3]); dout("convs_o", [1024, 16, 3])
    dout("Up", [4, 128, 129]); dout("Us_o", [16, 4, 128, 129]); dout("mp", [4, 1]); dout("ms_o", [4, 16])
    dout("retp", [4, 128, 128]); dout("rets_o", [16, 4, 128, 128])
    dout("hgrnp", [4, 128, 128]); dout("hgrns_o", [16, 4, 128, 128])
    dout("s5rep", [128, 16]); dout("s5imp", [128, 16]); dout("s5res", [128, 16, 16]); dout("s5ims", [128, 16, 16])

    st = contextlib.ExitStack()
    with st:
        S = Sched(nc)
        cnt = [0]

        def sb(shape, dt=F32):
            cnt[0] += 1
            return st.enter_context(nc.sbuf_tensor("t%d" % cnt[0], list(shape), dt))

        def psum(shape, dt=F32):
            cnt[0] += 1
            return st.enter_context(nc.psum_tensor("p%d" % cnt[0], list(shape), dt))
        V = lambda fn, r=(), w=(): S.op("dve", fn, r, w)
        A = lambda fn, r=(), w=(): S.op("act", fn, r, w)
        G = lambda fn, r=(), w=(): S.op("pool", fn, r, w)
        P = lambda fn, r=(), w=(): S.op("pe", fn, r, w)
        msems = {"sp": [S.new_dma_sem() for _ in range(24)], "pool": [S.new_dma_sem() for _ in range(8)]}
        mi = {"sp": 0, "pool": 0}

        def LD(out, in_, w, r=(), eng="sp"):
            k = msems[eng][mi[eng] % len(msems[eng])]
            mi[eng] += 1
            S.dma(eng, lambda e: e.dma_start(out=out, in_=in_), k, r, w)
        OUTS = []

        def STO(out, in_, r):
            b_ = Buf()
            OUTS.append(b_)
            LD(out, in_, [b_], r)

        h = sb([128, 8, NTM]); hn = sb([128, 8, NTM], BF16); mix = sb([128, 8, NTM], BF16)
        Bh = [[Buf() for _ in range(2)] for _ in range(8)]
        Bhn = [Buf() for _ in range(2)]
        Bmix = [[Buf() for _ in range(2)] for _ in range(8)]
        NW = 2
        wr = [sb([128, 8, 512], BF16) for _ in range(NW)]
        Bwrp = [[Buf() for _ in range(4)] for _ in range(NW)]
        wsem = [[S.new_dma_sem() for _ in range(4)] for _ in range(NW)]
        wi = [0]

        def wload(parts, nk):
            i = wi[0] % NW
            wi[0] += 1
            for pi_, (ap, co) in enumerate(parts):
                ncol = ap.shape[1]
                S.dma("pool", lambda e, ap=ap, co=co, ncol=ncol, i=i: e.dma_start(
                    out=wr[i][:, 0:nk, co:co + ncol], in_=ap.rearrange("(k p) n -> p k n", p=128)),
                    wsem[i][pi_], (), (Bwrp[i] if pi_ == 0 else [Bwrp[i][pi_]]))
            return wr[i], Bwrp[i]
        Fs = [sb([128, FW]) for _ in range(9)]
        BF = [Buf() for _ in range(9)]
        Hs = [sb([128, NTM], BF16) for _ in range(5)]
        BH = [Buf() for _ in range(5)]
        vtm = sb([128, 6, 129], BF16); Bv = Buf()
        NT5 = [sb([128, 512]) for _ in range(5)]
        BN = [Buf() for _ in range(5)]
        sqb = [sb([128, 512], BF16) for _ in range(2)]
        Bsq = [Buf() for _ in range(2)]
        pb = [psum([128, 512]) for _ in range(7)]
        Bp = [Buf() for _ in range(7)]
        ptb = psum([128, 1024], BF16); Bpt = Buf()
        pbi = [0]

        pinned = set()

        S5MODE = [False]

        def PS():
            while True:
                i = pbi[0] % (3 if S5MODE[0] else 4)
                pbi[0] += 1
                if i not in pinned:
                    return pb[i], Bp[i]
        psi = [0]

        def PSS():
            if S5MODE[0]:
                i = (4, 5, 6, 3)[psi[0] % 4]
            else:
                i = 4 + psi[0] % 3
            psi[0] += 1
            return pb[i], Bp[i]
        ident = sb([128, 128]); identb = sb([128, 128], BF16); maskc = sb([128, 128], BF16); maskb = sb([128, 128], BF16)
        onesb = sb([128, 128], BF16); blk3 = sb([128, 16, 128], BF16); segm = sb([128, NTM]); negm = sb([4, 128])
        sel = sb([4, 4, 128]); pvec = sb([128, NPV]); bg = sb([4, 2]); nbg = sb([4, 1]); jrow = sb([128, 4, 128])
        Bc = Buf()
        LD(ident[:], D["ident"], [Bc]); LD(identb[:], D["ident"], [Bc], eng="pool")
        LD(maskc[:], D["maskc"], [Bc], eng="pool"); LD(maskb[:], D["maskb"], [Bc], eng="pool")
        LD(blk3[:], D["blk3"], [Bc], eng="pool"); LD(negm[:], D["negm"], [Bc]); LD(sel[:], D["sel"], [Bc])
        LD(pvec[:], D["pvec"], [Bc]); LD(bg[:], D["bg"], [Bc]); LD(jrow[:], D["jrow"], [Bc])
        V(lambda e: e.memset(onesb[:], 1.0), (), [Bc])
        V(lambda e: e.tensor_scalar(nbg[:], bg[:, 1:2], -1.0, None, ALU.mult), [Bc], [Bc])

        cb_ = sb([128, 8])
        CBV = [EPS, LNK, 1.0, 0.0, 0.5 * PI, 0.0, 0.0, 0.0]
        for _i, _v in enumerate(CBV):
            V(lambda e, _i=_i, _v=_v: e.memset(cb_[:, _i:_i + 1], _v), (), [Bc])
        CEPS, CLNK, CONE, CZERO, CHPI = [cb_[:, i:i + 1] for i in range(5)]
        RC = 12582912.0
        I2P = 1.0 / (2 * PI)

        def sin_of(dst, src, shift, tmp, rd, wr_, btmp, npart=128):
            V(lambda e: e.tensor_scalar(tmp, src, shift, I2P, ALU.add, ALU.mult), rd, [btmp])
            V(lambda e: e.tensor_scalar(tmp, tmp, RC, None, ALU.add), [btmp], [btmp])
            V(lambda e: e.tensor_scalar(tmp, tmp, -RC, None, ALU.add), [btmp], [btmp])
            V(lambda e: e.scalar_tensor_tensor(tmp, tmp, -2 * PI, src, ALU.mult, ALU.add), [btmp] + list(rd), [btmp])
            V(lambda e: e.tensor_scalar(tmp, tmp, -PI - shift + 4e-6, PI - shift - 4e-6, ALU.max, ALU.min), [btmp], [btmp])
            A(lambda e: e.activation(dst, tmp, AF.Sin, bias=(CHPI[0:npart] if shift != 0.0 else CZERO[0:npart])), [btmp, Bc], wr_)

        def pv(name, j=0, n=1):
            o, w = PV[name]
            return pvec[:, o + j:o + j + n]
        Gq = sb([128, 2, 4, 128], BF16); gk = sb([128, 2, 4])
        for v2 in range(2):
            for hh in range(4):
                A(lambda e, v2=v2, hh=hh: e.activation(Gq[:, v2, hh, :], jrow[:, v2, :], AF.Exp, scale=LG[hh]), [Bc], [Bc])
                A(lambda e, v2=v2, hh=hh: e.activation(gk[:, v2, hh:hh + 1], pv("pidxs" if v2 else "pidx"), AF.Exp,
                                                       scale=-LG[hh], bias=CLNK), [Bc], [Bc])
        lb = sb([128, 4]); oml = sb([128, 4])
        V(lambda e: e.tensor_tensor(lb[:], pv("lb1", 0, 4), pv("lb0", 0, 4), ALU.subtract), [Bc], [Bc])
        A(lambda e: e.activation(lb[:], lb[:], AF.Sigmoid), [Bc], [Bc])
        V(lambda e: e.tensor_scalar(oml[:], lb[:], -1.0, 1.0, ALU.mult, ALU.add), [Bc], [Bc])
        s5p = sb([128, 16, 16])
        th, rr, zr, zi, rho = s5p[:, 0, :], s5p[:, 1, :], s5p[:, 2, :], s5p[:, 3, :], s5p[:, 7, :]
        t4, t5, t6 = s5p[:, 4, :], s5p[:, 5, :], s5p[:, 6, :]
        ar_, ai_, a128r, a128i, izr, izi, t7 = (s5p[:, 8, :], s5p[:, 9, :], s5p[:, 10, :], s5p[:, 11, :], s5p[:, 12, :],
                                               s5p[:, 13, :], s5p[:, 14, :])
        are, aim = pv("are", 0, 16), pv("aim", 0, 16)
        A(lambda e: e.activation(t4, pv("ldt", 0, 16), AF.Exp), [Bc], [Bc])
        V(lambda e: e.tensor_tensor(th, t4, aim, ALU.mult), [Bc], [Bc])
        V(lambda e: e.tensor_tensor(rho, t4, are, ALU.mult), [Bc], [Bc])
        A(lambda e: e.activation(rr, rho, AF.Exp), [Bc], [Bc])
        sin_of(t4, th, 0.5 * PI, t6, [Bc], [Bc], Bc)
        sin_of(t5, th, 0.0, t6, [Bc], [Bc], Bc)
        V(lambda e: e.tensor_tensor(ar_, t4, rr, ALU.mult), [Bc], [Bc])
        V(lambda e: e.tensor_tensor(ai_, t5, rr, ALU.mult), [Bc], [Bc])
        V(lambda e: e.tensor_scalar(t4, ar_, -1.0, None, ALU.add), [Bc], [Bc])
        V(lambda e: e.tensor_copy(t5, ai_), [Bc], [Bc])
        V(lambda e: e.tensor_tensor(t6, are, are, ALU.mult), [Bc], [Bc])
        V(lambda e: e.tensor_tensor(zr, aim, aim, ALU.mult), [Bc], [Bc])
        V(lambda e: e.tensor_tensor(t6, t6, zr, ALU.add), [Bc], [Bc])
        V(lambda e: e.reciprocal(t6, t6), [Bc], [Bc])
        V(lambda e: e.tensor_tensor(zr, t4, are, ALU.mult), [Bc], [Bc])
        V(lambda e: e.tensor_tensor(zi, t5, aim, ALU.mult), [Bc], [Bc])
        V(lambda e: e.tensor_tensor(zr, zr, zi, ALU.add), [Bc], [Bc])
        V(lambda e: e.tensor_tensor(zi, t5, are, ALU.mult), [Bc], [Bc])
        V(lambda e: e.tensor_tensor(t7, t4, aim, ALU.mult), [Bc], [Bc])
        V(lambda e: e.tensor_tensor(zi, zi, t7, ALU.subtract), [Bc], [Bc])
        V(lambda e: e.tensor_tensor(zr, zr, t6, ALU.mult), [Bc], [Bc])
        V(lambda e: e.tensor_tensor(zi, zi, t6, ALU.mult), [Bc], [Bc])
        V(lambda e: e.tensor_tensor(t4, zr, zr, ALU.mult), [Bc], [Bc])
        V(lambda e: e.tensor_tensor(t5, zi, zi, ALU.mult), [Bc], [Bc])
        V(lambda e: e.tensor_tensor(t4, t4, t5, ALU.add), [Bc], [Bc])
        V(lambda e: e.reciprocal(t4, t4), [Bc], [Bc])
        V(lambda e: e.tensor_tensor(izr, zr, t4, ALU.mult), [Bc], [Bc])
        V(lambda e: e.scalar_tensor_tensor(izi, zi, -1.0, t4, ALU.mult, ALU.mult), [Bc], [Bc])
        V(lambda e: e.tensor_scalar(t7, th, 128.0, None, ALU.mult), [Bc], [Bc])
        sin_of(t4, t7, 0.5 * PI, t6, [Bc], [Bc], Bc)
        sin_of(t5, t7, 0.0, t6, [Bc], [Bc], Bc)
        A(lambda e: e.activation(t6, rho, AF.Exp, scale=128.0), [Bc], [Bc])
        V(lambda e: e.tensor_tensor(a128r, t4, t6, ALU.mult), [Bc], [Bc])
        V(lambda e: e.tensor_tensor(a128i, t5, t6, ALU.mult), [Bc], [Bc])
        tabE = nc.dram_tensor("tabE", [128, 2, 2048], BF16).ap(); tabZ = nc.dram_tensor("tabZ", [128, 2, 16, 128], BF16).ap()
        tbe = sb([128, 2, 512], BF16); tbz = sb([128, 2, 4, 128], BF16); Btab = Buf(); Bscr = Buf()

        def build_tables(kc, scol, jr, outE, outZ, wE, wZ):
            f0, f1, f2, f3, f4, f5 = [Fs[k][:, 0:512] for k in range(6)]
            b0_, b1_, b2_, b3_, b4_, b5_ = BF[0:6]
            for k in range(3):
                LD(Fs[k][:, 0:512], D["rowp"][k][:, kc * 512:(kc + 1) * 512], [BF[k]])
            A(lambda e: e.activation(f2, f2, AF.Exp), [b2_], [b2_])
            V(lambda e: e.tensor_tensor(f1, f1, f2, ALU.mult), [b1_, b2_], [b1_])
            V(lambda e: e.tensor_tensor(f0, f0, f2, ALU.mult), [b0_, b2_], [b0_])
            V(lambda e: e.tensor_scalar(f1, f1, scol, None, ALU.mult), [b1_, Bc], [b1_])
            A(lambda e: e.activation(f0, f0, AF.Exp, scale=scol), [b0_, Bc], [b0_])
            V(lambda e: e.reciprocal(f0, f0), [b0_], [b0_])
            sin_of(f3, f1, 0.5 * PI, f2, [b1_], [b3_], b2_)
            sin_of(f4, f1, 0.0, f2, [b1_], [b4_], b2_)
            V(lambda e: e.tensor_tensor(outE(0), f3, f0, ALU.mult), [b3_, b0_], wE)
            V(lambda e: e.scalar_tensor_tensor(outE(1), f4, -1.0, f0, ALU.mult, ALU.mult), [b4_, b0_], wE)
            g0, g1, g2, g3, g4 = [Fs[k][:, 0:512].rearrange("p (a b) -> p a b", a=4) for k in range(5)]
            i4 = slice(4 * kc, 4 * kc + 4)
            jb = jr.unsqueeze(1).broadcast_to([128, 4, 128])
            bc4 = lambda v: v[:, i4].unsqueeze(2).broadcast_to([128, 4, 128])
            V(lambda e: e.tensor_tensor(g1, jb, bc4(th), ALU.mult), [Bc], [b1_])
            V(lambda e: e.tensor_tensor(g0, jb, bc4(rho), ALU.mult), [Bc], [b0_])
            A(lambda e: e.activation(Fs[0][:, 0:512], Fs[0][:, 0:512], AF.Exp), [b0_], [b0_])
            sin_of(f3, f1, 0.5 * PI, f2, [b1_], [b3_], b2_)
            sin_of(f4, f1, 0.0, f2, [b1_], [b4_], b2_)
            V(lambda e: e.tensor_tensor(f3, f3, f0, ALU.mult), [b3_, b0_], [b3_])
            V(lambda e: e.tensor_tensor(f4, f4, f0, ALU.mult), [b4_, b0_], [b4_])
            V(lambda e: e.tensor_tensor(g0, g3, bc4(zr), ALU.mult), [b3_, Bc], [b0_])
            V(lambda e: e.tensor_tensor(g1, g4, bc4(zi), ALU.mult), [b4_, Bc], [b1_])
            V(lambda e: e.tensor_tensor(outZ(0), g0, g1, ALU.subtract), [b0_, b1_], wZ)
            V(lambda e: e.tensor_tensor(g0, g3, bc4(zi), ALU.mult), [b3_, Bc], [b0_])
            V(lambda e: e.tensor_tensor(g1, g4, bc4(zr), ALU.mult), [b4_, Bc], [b1_])
            V(lambda e: e.tensor_tensor(outZ(1), g0, g1, ALU.add), [b0_, b1_], wZ)
        Up = sb([128, 12, 129]); Upb = sb([128, 12, 129], BF16); nbc = sb([128, 4, 128], BF16)
        BU = [Buf() for _ in range(12)]
        V(lambda e: e.memset(Up[:], 0.0), (), BU); V(lambda e: e.memset(Upb[:], 0.0), (), BU)
        V(lambda e: e.memset(nbc[:], 0.0), (), BU)
        tails = sb([128, 8, 3]); Btl = Buf()
        V(lambda e: e.memset(tails[:], 0.0), (), [Btl])
        carr = sb([4, 2]); Bcar = Buf()
        V(lambda e: e.memset(carr[:], 0.0), (), [Bcar])
        s5c = sb([128, 2, 16]); Bs5c = Buf()
        s5d = sb([128, 2, 2, 16]); Bs5d = [Buf(), Buf()]; S5PAR = [0, 0, 0, 0]; Bcr = Buf()
        V(lambda e: e.memset(s5c[:], 0.0), (), [Bs5c])
        V(lambda e: e.memset(s5d[:], 0.0), (), Bs5d)
        Usf = sb([128, 16, 129]); Usb = sb([128, 16, 129], BF16); BUs = Buf()
        qz = sb([128, 16, 128], BF16); kz = sb([128, 16, 128], BF16); nbs = kz
        Bqz = Buf(); Bkz = Buf(); Bnbs = Bkz
        stb = [sb([128, 128], BF16) for _ in range(2)]; Bst = [Buf() for _ in range(2)]
        khb = [sb([128, 128], BF16) for _ in range(2)]; Bkh = [Buf() for _ in range(2)]
        ektm = sb([128, 6, 4]); Bek = Buf()
        decbc = sb([128, 4, 24]); Bdec = Buf()
        decrow = sb([4, 24]); mxe = sb([4, 8]); ms0 = sb([4, 16]); msout = sb([4, 17]); Bsm = Buf()
        x0s = Fs[5][:, 0:512].rearrange("p (a b c) -> p a b c", a=2, b=16); Bx0 = BF[5]
        s5so = Usf[:].rearrange("p a b -> p (a b)")[:, 0:512].rearrange("p (a b c) -> p a b c", a=2, b=16); s5po = sb([128, 2, 16]); Bs5o = Buf()
        Bes = Buf()
        cbs = sb([128, 2, 4, 128], BF16); Bcbs = Buf()
        pt4all = sb([128, 2048], BF16)
        pt4 = [pt4all[:, q_ * 512:(q_ + 1) * 512] for q_ in range(4)]
        gT2 = pt4all[:, 0:NTM]; vtm2 = pt4all[:, NTM:NTM + 774].rearrange("p (a b) -> p a b", a=6); Bg2 = Buf(); Bv2 = Buf()
        hdec = sb([128, 2, 24]); Bhd = Buf()
        Bprb = [Buf() for _ in range(2)]; Bt4 = [Buf() for _ in range(4)]; Bcsb = [Buf() for _ in range(2)]; Bp4 = [Buf() for _ in range(4)]
        S5FINE = Bprb + Bt4 + Bcsb + Bp4
        qzf = qz[:].rearrange("p a b -> p (a b)"); kzf = kz[:].rearrange("p a b -> p (a b)"); usbf = Usb[:].rearrange("p a b -> p (a b)")
        wtb = [[sb([128, 512], BF16) for _ in range(2)] for _ in range(2)]; Bwt = [Buf() for _ in range(2)]
        xtb = [[sb([128, 512], BF16) for _ in range(2)] for _ in range(2)]; Bxt = [Buf() for _ in range(2)]
        cch = sb([128, 6, 4]); Bcch = Buf()
        bct = sb([128, 2, 2, 4, 128], BF16); Bbct = [Buf() for _ in range(4)]; bcsem = [S.new_dma_sem() for _ in range(4)]
        pTb = sb([128, 2, NTM], BF16); BpT = Buf(); pTsem = S.new_dma_sem()

        def blocks(ntot):
            out = []
            o = 0
            while o < ntot:
                n = min(512, ntot - o)
                out.append((o, n)); o += n
            return out

        def rmsnorm(gname, NT, final_out=None):
            for bi, (o, n) in enumerate(blocks(NT)):
                ps, bp = PS()
                for c in range(8):
                    q = sqb[c % 2]; bq = Bsq[c % 2]
                    A(lambda e, c=c, q=q: e.activation(q[:, 0:n], h[:, c, o:o + n], AF.Square), [Bh[c][bi]], [bq])
                    P(lambda e, c=c, q=q, ps=ps: e.matmul(ps[:, 0:n], onesb[:], q[:, 0:n], start=(c == 0), stop=(c == 7)),
                      [bq, Bc], [bp])
                rs = NT5[4]
                A(lambda e, ps=ps: e.activation(rs[:, 0:n], ps[:, 0:n], AF.Ln, scale=1.0 / 1024, bias=CEPS), [bp], [BN[4]])
                A(lambda e: e.activation(rs[:, 0:n], rs[:, 0:n], AF.Exp, scale=-0.5), [BN[4]], [BN[4]])
                for c in range(8):
                    if final_out is None:
                        V(lambda e, c=c: e.scalar_tensor_tensor(hn[:, c, o:o + n], h[:, c, o:o + n], pv(gname, c), rs[:, 0:n],
                                                                ALU.mult, ALU.mult), [Bh[c][bi], BN[4], Bc], [Bhn[bi]])
                    else:
                        t = NT5[c % 2]
                        V(lambda e, c=c, t=t: e.scalar_tensor_tensor(t[:, 0:n], h[:, c, o:o + n], pv(gname, c), rs[:, 0:n],
                                                                     ALU.mult, ALU.mult), [Bh[c][bi], BN[4], Bc], [BN[c % 2]])
                        STO(final_out[c * 128:(c + 1) * 128, o:o + n], t[:, 0:n], [BN[c % 2]])

        def proj_fm(slot, bs, col, NT, evac, rhs=None, nk=8, rbufs=None):
            for bi, (o, n) in enumerate(blocks(NT)):
                ps, bp = PS()
                for k in range(nk):
                    src = hn if rhs is None else rhs
                    P(lambda e, k=k, ps=ps, src=src: e.matmul(ps[:, 0:n], slot[:, k, col:col + 128], src[:, k, o:o + n],
                                                             start=(k == 0), stop=(k == nk - 1)),
                      bs + ([Bhn[bi]] if rbufs is None else rbufs(bi)), [bp])
                evac(ps, bp, bi, o, n)

        def resid_proj(w_ap, NT, src, srcb):
            for u in range(2):
                slot, bs = wload([(w_ap[:, u * 512:(u + 1) * 512], 0)], 8)
                for oc in range(4):
                    c = u * 4 + oc

                    def ev(ps, bp, bi, o, n, c=c):
                        V(lambda e: e.tensor_tensor(h[:, c, o:o + n], h[:, c, o:o + n], ps[:, 0:n], ALU.add),
                          [bp, Bh[c][bi]], [Bh[c][bi]])
                    proj_fm(slot, bs, oc * 128, NT, ev, rhs=src, rbufs=lambda bi: [srcb[k][bi] for k in range(8)])

        def att_pre(qT, kT, bq, bk, col, ek, sample):
            ps, bp = PSS()
            P(lambda e: e.matmul(ps[:, 0:128], kT[:, col:col + 128], qT[:, col:col + 128], start=True, stop=True),
              [bq, bk], [bp])
            i2 = att_tile.k % 2
            att_tile.k += 1
            sT = stb[i2]; bsT = Bst[i2]
            msk = maskb if sample else maskc
            if ek is not None:
                V(lambda e: e.scalar_tensor_tensor(sT[:], ps[:, 0:128], ek, msk[:], ALU.mult, ALU.mult), [bp, Bek, Bc], [bsT])
            else:
                V(lambda e: e.tensor_tensor(sT[:], ps[:, 0:128], msk[:], ALU.mult), [bp, Bc], [bsT])
            P(lambda e: e.transpose(ptb[:, 0:128], kT[:, col:col + 128], identb[:]), [bk, Bc], [Bpt])
            kh = khb[i2]; bkh = Bkh[i2]
            if ek is not None:
                A(lambda e: e.activation(kh[:], ptb[:, 0:128], AF.Copy, scale=ek), [Bpt, Bek], [bkh])
            else:
                A(lambda e: e.copy(kh[:], ptb[:, 0:128]), [Bpt], [bkh])
            return (sT, bsT, kh, bkh)

        def att_tile(qT, kT, bq, bk, col, vt, E, si, ek, dec, PT, bPT, pcol, sample, den=None, mlstm_h=None, usbuf=None, bv=None, ctx=None):
            Bv = bv
            if ctx is None:
                ctx = att_pre(qT, kT, bq, bk, col, ek, sample)
            sT, bsT, kh, bkh = ctx
            P(lambda e: e.matmul(PT[:, pcol:pcol + 128], vt[:, 0:128], sT[:], start=True, stop=False), [Bv, bsT], [bPT])
            if not sample:
                P(lambda e: e.matmul(PT[:, pcol:pcol + 128], Upb[:, si, 0:128], qT[:, col:col + 128], start=False, stop=True),
                  [BU[si], bq], [bPT])
            else:
                for j in range(16):
                    P(lambda e, j=j: e.matmul(PT[:, pcol:pcol + 128], Usb[:, j, 0:128], qz[:, j, :], start=False, stop=(j == 15)),
                      [BUs, Bqz], [bPT])
            if den is not None:
                dps, bd = den
                P(lambda e: e.matmul(dps[:, pcol:pcol + 128], onesb[:], sT[:], start=True, stop=False), [Bc, bsT], [bd])
                if not sample:
                    P(lambda e: e.matmul(dps[:, pcol:pcol + 128], nbc[:, mlstm_h, :], qT[:, col:col + 128], start=False, stop=True),
                      [BU[si], bq], [bd])
                else:
                    for j in range(16):
                        P(lambda e, j=j: e.matmul(dps[:, pcol:pcol + 128], nbs[:, j, :], qz[:, j, :], start=False, stop=(j == 15)),
                          [Bnbs, Bqz], [bd])
            if not sample:
                ps2, bp2 = PSS()
                P(lambda e: e.matmul(ps2[:, 0:E], ident[:], Up[:, si, 0:E], start=True, stop=False), [Bc, BU[si]], [bp2])
                P(lambda e: e.matmul(ps2[:, 0:E], kh[:], vt[:, 0:E], start=False, stop=True), [bkh, Bv], [bp2])
                A(lambda e: e.activation(Up[:, si, 0:E], ps2[:, 0:E], AF.Copy, scale=dec), [bp2, Bdec, Bhd], [BU[si]])
                V(lambda e: e.tensor_copy(Upb[:, si, 0:E], Up[:, si, 0:E]), [BU[si]], [BU[si]])
                if mlstm_h is not None:
                    V(lambda e: e.tensor_copy(nbc[:, mlstm_h, :], Up[:, si, 128:129].broadcast_to([128, 128])), [BU[si]], [BU[si]])
            else:
                V(lambda e: e.tensor_tensor(kz[:], kh[:].unsqueeze(1).broadcast_to([128, 16, 128]),
                                            pv("rowm", 0, 16).unsqueeze(2).broadcast_to([128, 16, 128]), ALU.mult),
                  [bkh, Bc], [Bkz])
                for j in range(16):
                    ps2, bp2 = PSS()
                    P(lambda e, j=j, ps2=ps2: e.matmul(ps2[:, 0:E], ident[:], Usf[:, j, 0:E], start=True, stop=False), [Bc, BUs], [bp2])
                    P(lambda e, j=j, ps2=ps2: e.matmul(ps2[:, 0:E], kz[:, j, :], vt[:, 0:E], start=False, stop=True), [Bkz, Bv], [bp2])
                    A(lambda e, j=j, ps2=ps2: e.activation(usbuf[:, j, 0:E], ps2[:, 0:E], AF.Copy, scale=dec(j)),
                      [bp2, Bdec, Bhd], [BUs])
        att_tile.k = 0
        CTX = {}
        PEND = [None]

        def load_sample_state(src, hh, E):
            LD(Usf[:, :, 0:E], src[:, hh, :, :].rearrange("j d e -> d j e"), [BUs])
            V(lambda e: e.tensor_copy(Usb[:, :, 0:E], Usf[:, :, 0:E]), [BUs], [BUs])

        def make_qz(qT, bq, col):
            V(lambda e: e.tensor_tensor(qz[:], qT[:, col:col + 128].unsqueeze(1).broadcast_to([128, 16, 128]), blk3[:], ALU.mult),
              [bq, Bc], [Bqz])

        def vproj(slot, bs, col, NT, E, vt_, bv_):
            nt = NT // 128
            for c in range(nt):
                ps, bp = PS()
                for k in range(8):
                    P(lambda e, k=k, ps=ps: e.matmul(ps[:, 0:128], hn[:, k, c * 128:(c + 1) * 128], slot[:, k, col:col + 128],
                                                     start=(k == 0), stop=(k == 7)), bs + [Bhn[(c * 128) // 512]], [bp])
                A(lambda e, ps=ps: e.copy(vt_[:, c, 0:128], ps[:, 0:128]), [bp], [bv_])

        def rstd_from(sq_src_fn, n, srcb):
            q = sqb[0]
            sq_src_fn(q)
            ps, bp = PSS()
            P(lambda e: e.matmul(ps[:, 0:n], onesb[:], q[:, 0:n], start=True, stop=True), [Bsq[0], Bc], [bp])
            rs = NT5[3]
            A(lambda e: e.activation(rs[:, 0:n], ps[:, 0:n], AF.Ln, scale=1.0 / 128, bias=CEPS), [bp], [BN[3]])
            A(lambda e: e.activation(rs[:, 0:n], rs[:, 0:n], AF.Exp, scale=-0.5), [BN[3]], [BN[3]])
            return rs

        def build_tab_kc(kc_):
            build_tables(kc_, pv("s0"), jrow[:, 2, :], lambda part: tbe[:, part, :], lambda part: tbz[:, part, :, :], [Btab], [Btab])
            LD(tabE[:, :, kc_ * 512:(kc_ + 1) * 512], tbe[:], [Bscr], r=[Btab])
            LD(tabZ[:, :, 4 * kc_:4 * kc_ + 4, :], tbz[:], [Bscr], r=[Btab])
        STEP = [None]

        def step():
            g = STEP[0]
            if g is not None:
                try:
                    next(g)
                except StopIteration:
                    STEP[0] = None
        CUT = int(os.environ.get("KCUT", "0"))

        class _Stop(Exception):
            pass

        def ck(k):
            if CUT == k:
                raise _Stop()
        try:
            for sbi, (tok0, NP, has_s) in enumerate(SBS):
                NT = NP + (128 if has_s else 0)
                ntp = NP // 128
                blks = blocks(NT)
                last = (sbi == len(SBS) - 1)
                LD(segm[:, 0:NTM], D["segm"][sbi], [Bc], r=[Bc])
                for c in range(8):
                    for bi, (o, n) in enumerate(blks):
                        LD(h[:, c, o:o + n], D["xT"][c * 128:(c + 1) * 128, tok0 + o:tok0 + o + n], [Bh[c][bi]])
                for layer in range(2):
                    rmsnorm("nmix%d" % layer, NT)
                    ck(1)
                    if layer == 0:
                        wgs, bwg = wload([(D["wg"], 0)], 8)
                        A1, A2, A3, A4 = Fs[2], Fs[3], Fs[5], Fs[4]
                        b1, b2, b3, b4 = BF[2], BF[3], BF[5], BF[4]
                        for bi, (o, n) in enumerate(blks):
                            ps, bp = PS()
                            for k in range(8):
                                P(lambda e, k=k, ps=ps: e.matmul(ps[0:4, 0:n], wgs[:, k, 0:4], hn[:, k, o:o + n], start=(k == 0), stop=(k == 7)),
                                  bwg + [Bhn[bi]], [bp])
                            A(lambda e, ps=ps: e.activation(A1[0:4, o:o + n], ps[0:4, 0:n], AF.Identity, bias=bg[:, 0:1]), [bp, Bc], [b1])
                            ps, bp = PS()
                            for k in range(8):
                                P(lambda e, k=k, ps=ps: e.matmul(ps[0:4, 0:n], wgs[:, k, 4:8], hn[:, k, o:o + n], start=(k == 0), stop=(k == 7)),
                                  bwg + [Bhn[bi]], [bp])
                            A(lambda e, ps=ps: e.activation(A2[0:4, o:o + n], ps[0:4, 0:n], AF.Exp, scale=-1.0, bias=nbg[:, 0:1]), [bp, Bc], [b2])
                        A(lambda e: e.activation(A2[0:4, 0:NT], A2[0:4, 0:NT], AF.Ln, bias=CONE[0:4]), [b2], [b2])
                        V(lambda e: e.memset(A4[0:4, 0:NT], 1.0), (), [b4])
                        V(lambda e: e.tensor_tensor_scan(A3[0:4, 0:NP], A4[0:4, 0:NP], A2[0:4, 0:NP], carr[:, 0:1], ALU.mult, ALU.add),
                          [b2, b4, Bcar], [b3])
                        if has_s:
                            V(lambda e: e.tensor_tensor_scan(A3[0:4, NP:NT], segm[0:4, NP:NT], A2[0:4, NP:NT], 0.0, ALU.mult, ALU.add),
                              [b2, Bc], [b3])
                        V(lambda e: e.tensor_tensor(A1[0:4, 0:NT], A1[0:4, 0:NT], A3[0:4, 0:NT], ALU.add), [b1, b3], [b1])
                        V(lambda e: e.memset(A4[0:4, 0:NT], 0.0), (), [b4])
                        V(lambda e: e.tensor_tensor_scan(A2[0:4, 0:NP], A4[0:4, 0:NP], A1[0:4, 0:NP], carr[:, 1:2], ALU.add, ALU.max),
                          [b1, b4, Bcar], [b2])
                        V(lambda e: e.tensor_copy(mxe[:, 0:1], carr[:, 1:2]), [Bcar], [Bsm])
                        V(lambda e: e.tensor_copy(mxe[:, 1:1 + ntp], A2[0:4, 0:NP].rearrange("p (c t) -> p c t", t=128)[:, :, 127]), [b2], [Bsm])
                        if has_s:
                            LD(ms0[:], D["ms"], [Bsm])
                            V(lambda e: e.tensor_copy(A4[0:4, NP:NT], A1[0:4, NP:NT]), [b1], [b4])
                            g3 = A4[0:4, NP:NT].rearrange("p (j l) -> p j l", l=8)
                            V(lambda e: e.tensor_tensor(g3[:, :, 0], g3[:, :, 0], ms0[:], ALU.max), [b4, Bsm], [b4])
                            V(lambda e: e.tensor_tensor_scan(A2[0:4, NP:NT], negm[:], A4[0:4, NP:NT], 0.0, ALU.add, ALU.max), [b4, Bc], [b2])
                        V(lambda e: e.tensor_copy(A4[0:4, 0:NP].rearrange("p (c t) -> p c t", t=128),
                                                  mxe[:, 0:ntp].unsqueeze(2).broadcast_to([4, ntp, 128])), [Bsm], [b4])
                        if has_s:
                            V(lambda e: e.tensor_copy(A4[0:4, NP:NT].rearrange("p (j l) -> p j l", l=8),
                                                      ms0[:].unsqueeze(2).broadcast_to([4, 16, 8])), [Bsm], [b4])
                        V(lambda e: e.tensor_tensor(decrow[:, 0:ntp], mxe[:, 0:ntp], mxe[:, 1:1 + ntp], ALU.subtract), [Bsm], [Bsm])
                        if has_s:
                            V(lambda e: e.tensor_tensor(decrow[:, 8:24], ms0[:], A2[0:4, NP:NT].rearrange("p (j l) -> p j l", l=8)[:, :, 7],
                                                        ALU.subtract), [Bsm, b2], [Bsm])
                        else:
                            V(lambda e: e.memset(decrow[:, 8:24], 0.0), (), [Bsm])
                        if ntp < 8:
                            V(lambda e: e.memset(decrow[:, ntp:8], 0.0), (), [Bsm])
                        A(lambda e: e.activation(decrow[:], decrow[:], AF.Exp), [Bsm], [Bsm])
                        ps, bp = PSS()
                        for hh in range(4):
                            P(lambda e, hh=hh, ps=ps: e.matmul(ps[:, hh * 24:(hh + 1) * 24], sel[:, hh, :], decrow[:], start=True, stop=True),
                              [Bc, Bsm], [bp])
                        V(lambda e, ps=ps: e.tensor_copy(decbc[:].rearrange("p a b -> p (a b)"), ps[:, 0:96]), [bp], [Bdec])
                        if last:
                            V(lambda e: e.tensor_tensor(msout[:, 16:17], A2[0:4, NP - 1:NP], A3[0:4, NP - 1:NP], ALU.subtract), [b2, b3], [Bsm])
                            V(lambda e: e.tensor_tensor(msout[:, 0:16], A2[0:4, NP:NT].rearrange("p (j l) -> p j l", l=8)[:, :, 7],
                                                        A3[0:4, NP:NT].rearrange("p (j l) -> p j l", l=8)[:, :, 7], ALU.subtract), [b2, b3], [Bsm])
                            STO(D["mp"], msout[:, 16:17], [Bsm]); STO(D["ms_o"], msout[:, 0:16], [Bsm])
                        V(lambda e: e.tensor_copy(carr[:, 0:1], A3[0:4, NP - 1:NP]), [b3], [Bcar])
                        V(lambda e: e.tensor_copy(carr[:, 1:2], A2[0:4, NP - 1:NP]), [b2], [Bcar])
                        V(lambda e: e.tensor_tensor(A3[0:4, 0:NT], A4[0:4, 0:NT], A3[0:4, 0:NT], ALU.subtract), [b3, b4], [b3])
                        V(lambda e: e.tensor_tensor(A1[0:4, 0:NT], A1[0:4, 0:NT], A4[0:4, 0:NT], ALU.subtract), [b1, b4], [b1])
                        A(lambda e: e.activation(A1[0:4, 0:NT], A1[0:4, 0:NT], AF.Exp, bias=CLNK[0:4]), [b1], [b1])
                        ps, bp = PSS()
                        for c in range(NT // 128):
                            P(lambda e, c=c, ps=ps: e.matmul(ps[:, c * 4:(c + 1) * 4], A1[0:4, c * 128:(c + 1) * 128], ident[0:4, 0:4],
                                                             start=True, stop=True), [b1, Bc], [bp])
                        V(lambda e, ps=ps: e.tensor_copy(ektm[:, 0:NT // 128, :].rearrange("p a b -> p (a b)"), ps[:, 0:4 * (NT // 128)]),
                          [bp], [Bek])
                        ck(2)
                        C0, S0 = Fs[6], Fs[7]
                        LD(Fs[8][:, 0:NT], D["posrow"][sbi][:, 0:NT], [BF[8]])
                        V(lambda e: e.tensor_scalar(Fs[8][:, 0:NT], Fs[8][:, 0:NT], pv("invf"), None, ALU.mult), [BF[8], Bc], [BF[8]])
                        sin_of(C0[:, 0:NT], Fs[8][:, 0:NT], 0.5 * PI, Fs[2][:, 0:NT], [BF[8]], [BF[6]], BF[2])
                        sin_of(S0[:, 0:NT], Fs[8][:, 0:NT], 0.0, Fs[2][:, 0:NT], [BF[8]], [BF[7]], BF[2])
                        V(lambda e: e.tensor_scalar(S0[:, 0:NT], S0[:, 0:NT], pv("sgn"), None, ALU.mult), [BF[7], Bc], [BF[7]])
                        V(lambda e: e.memset(vtm[:, :, 128:129], 1.0), (), [Bv])
                        V(lambda e: e.memset(vtm2[:, :, 128:129], 1.0), (), [Bv2])
                        SETS = [(Hs[0], Hs[1], Hs[2], vtm, BH[0], BH[1], BH[2], Bv), (Hs[3], Hs[4], gT2, vtm2, BH[3], BH[4], Bg2, Bv2)]
                        xq, xk, qT, kT, gT = Fs[0], Fs[1], Hs[0], Hs[1], Hs[2]
                        bxq, bxk, bqT, bkT, bgT = BF[0], BF[1], BH[0], BH[1], BH[2]
                        W = D["w_in_ab"]
                        def front_m(hh):
                            qT, kT, gT, vt_, bqT, bkT, bgT, bv_ = SETS[hh % 2]
                            slot, bs = wload([(D["w_ab_h"][:, hh * 512:(hh + 1) * 512], 0)], 8)
                            for (xx, bx, cc) in ((xq, bxq, 0), (xk, bxk, 128)):
                                def ev(ps, bp, bi, o, n, xx=xx, bx=bx):
                                    if o < NP:
                                        A(lambda e: e.copy(xx[:, 3 + o:3 + o + n], ps[:, 0:n]), [bp], [bx])
                                    else:
                                        A(lambda e: e.copy(xx[:, NP + 3:NP + 3 + 176].rearrange("p (j l) -> p j l", l=11)[:, :, 3:11],
                                                           ps[:, 0:128].rearrange("p (j l) -> p j l", l=8)), [bp], [bx])
                                proj_fm(slot, bs, cc, NT, ev)
                                yield

                            def evg(ps, bp, bi, o, n):
                                A(lambda e: e.activation(gT[:, o:o + n], ps[:, 0:n], AF.Sigmoid), [bp], [bgT])
                            proj_fm(slot, bs, 384, NT, evg)
                            yield
                            vproj(slot, bs, 256, NT, 129, vt_, bv_)
                            yield
                            for (xx, bx, ch, oT, boT) in ((xq, bxq, hh, qT, bqT), (xk, bxk, 4 + hh, kT, bkT)):
                                V(lambda e, xx=xx, ch=ch: e.tensor_copy(xx[:, 0:3], tails[:, ch, :]), [Btl], [bx])
                                acc = Fs[2]
                                V(lambda e, xx=xx, ch=ch: e.tensor_scalar(acc[:, 0:NP], xx[:, 0:NP], pv("cw0", ch), pv("cb", ch), ALU.mult, ALU.add),
                                  [bx, Bc], [BF[2]])
                                for j in range(1, 4):
                                    V(lambda e, xx=xx, ch=ch, j=j: e.scalar_tensor_tensor(acc[:, 0:NP], xx[:, j:j + NP], pv("cw%d" % j, ch), acc[:, 0:NP],
                                                                                         ALU.mult, ALU.add), [bx, Bc, BF[2]], [BF[2]])
                                if has_s:
                                    LD(xx[:, NP + 3:NP + 3 + 176].rearrange("p (j l) -> p j l", l=11)[:, :, 0:3],
                                       D["convs"][ch * 128:(ch + 1) * 128], [bx])
                                    xs3 = xx[:, NP + 3:NP + 3 + 176].rearrange("p (j l) -> p j l", l=11)
                                    a3 = acc[:, NP:NT].rearrange("p (j l) -> p j l", l=8)
                                    V(lambda e, xs3=xs3, a3=a3, ch=ch: e.tensor_scalar(a3, xs3[:, :, 0:8], pv("cw0", ch), pv("cb", ch), ALU.mult, ALU.add),
                                      [bx, Bc], [BF[2]])
                                    for j in range(1, 4):
                                        V(lambda e, xs3=xs3, a3=a3, ch=ch, j=j: e.scalar_tensor_tensor(a3, xs3[:, :, j:j + 8], pv("cw%d" % j, ch), a3,
                                                                                                      ALU.mult, ALU.add), [bx, Bc, BF[2]], [BF[2]])
                                    STO(D["convs_o"][ch * 128:(ch + 1) * 128], xs3[:, :, 8:11], [bx])
                                    STO(D["convp"][ch * 128:(ch + 1) * 128], xx[:, NP:NP + 3], [bx])
                                A(lambda e, oT=oT: e.activation(oT[:, 0:NT], acc[:, 0:NT], AF.Silu), [BF[2]], [boT])
                                V(lambda e, xx=xx, ch=ch: e.tensor_copy(tails[:, ch, :], xx[:, NP:NP + 3]), [bx], [Btl])
                                yield

                        def back_m(hh):
                            qT, kT, gT, vt_, bqT, bkT, bgT, bv_ = SETS[hh % 2]
                            if has_s:
                                load_sample_state(D["Us"], hh, 129)
                                make_qz(qT, bqT, NP)
                                V(lambda e: e.tensor_copy(nbs[:], Usf[:, :, 128:129].broadcast_to([128, 16, 128])), [BUs], [Bnbs])
                            for bi, (o, n) in enumerate(blks):
                                PT, bPT = PS()
                                dps, bd = PS()
                                pinned.update((pb.index(PT), pb.index(dps)))
                                for c in range(n // 128):
                                    tcol = o + c * 128
                                    tix = tcol // 128
                                    smp = tcol >= NP
                                    ekf = lambda t_: ektm[:, t_ // 128, hh:hh + 1]
                                    ctx_ = CTX.pop(tcol, None) or att_pre(qT, kT, bqT, bkT, tcol, ekf(tcol), smp)
                                    if tcol + 128 < NT:
                                        CTX[tcol + 128] = att_pre(qT, kT, bqT, bkT, tcol + 128, ekf(tcol + 128), tcol + 128 >= NP)
                                    att_tile(qT, kT, bqT, bkT, tcol, vt_[:, tix, :], 129, hh, ektm[:, tix, hh:hh + 1],
                                             (lambda j, hh=hh: decbc[:, hh, 8 + j:9 + j]) if smp else decbc[:, hh, tix:tix + 1],
                                             PT, bPT, c * 128, smp, den=(dps, bd), mlstm_h=hh, usbuf=Usf, bv=bv_, ctx=ctx_)
                                    step()
                                ps, bp = PSS()
                                P(lambda e, ps=ps, hh=hh: e.matmul(ps[:, 0:n], sel[:, hh, :], A3[0:4, o:o + n], start=True, stop=True), [Bc, b3], [bp])
                                dn = NT5[0]
                                A(lambda e, ps=ps: e.activation(dn[:, 0:n], ps[:, 0:n], AF.Exp, scale=-1.0), [bp], [BN[0]])
                                ab = NT5[1]
                                A(lambda e, dps=dps: e.activation(ab[:, 0:n], dps[:, 0:n], AF.Abs), [bd], [BN[1]])
                                V(lambda e: e.tensor_tensor(ab[:, 0:n], ab[:, 0:n], dn[:, 0:n], ALU.max), [BN[0], BN[1]], [BN[1]])
                                V(lambda e: e.reciprocal(ab[:, 0:n], ab[:, 0:n]), [BN[1]], [BN[1]])
                                hv = NT5[2]
                                V(lambda e, PT=PT: e.tensor_tensor(hv[:, 0:n], PT[:, 0:n], ab[:, 0:n], ALU.mult), [bPT, BN[1]], [BN[2]])
                                V(lambda e: e.tensor_tensor(hv[:, 0:n], hv[:, 0:n], gT[:, o:o + n], ALU.mult), [BN[2], bgT], [BN[2]])
                                rs = rstd_from(lambda q: A(lambda e: e.activation(q[:, 0:n], hv[:, 0:n], AF.Square), [BN[2]], [Bsq[0]]), n, None)
                                V(lambda e, hh=hh: e.scalar_tensor_tensor(mix[:, hh, o:o + n], hv[:, 0:n], pv("gna", hh), rs[:, 0:n], ALU.mult, ALU.mult),
                                  [BN[2], BN[3], Bc], [Bmix[hh][bi]])
                                pinned.clear()
                                step()
                            if has_s:
                                STO(D["Us_o"][:, hh, :, :].rearrange("j d e -> d j e"), Usf[:, :, :], [BUs])
                            if last:
                                STO(D["Up"][hh], Up[:, hh, :], [BU[hh]])
                        for _ in front_m(0):
                            pass
                        for hh in range(4):
                            STEP[0] = front_m(hh + 1) if hh + 1 < 4 else None
                            back_m(hh)
                            while STEP[0] is not None:
                                step()
                        ck(3)
                        def front_r(hh):
                            qT, kT, gT, vt_, bqT, bkT, bgT, bv_ = SETS[hh % 2]
                            b0 = 2056
                            slot, bs = wload([(D["w_ab_h"][:, (4 + hh) * 512:(5 + hh) * 512], 0)], 8)
                            for (xx, bx, cc, oT, boT) in ((xq, bxq, 0, qT, bqT), (xk, bxk, 128, kT, bkT)):
                                def ev(ps, bp, bi, o, n, xx=xx, bx=bx):
                                    A(lambda e: e.copy(xx[:, o:o + n], ps[:, 0:n]), [bp], [bx])
                                proj_fm(slot, bs, cc, NT, ev)
                                yield
                                xsw, t1, t2 = Fs[2], Fs[3], Fs[4]
                                A(lambda e, xx=xx: e.copy(xsw[0:64, 0:NT], xx[64:128, 0:NT]), [bx], [BF[2]])
                                A(lambda e, xx=xx: e.copy(xsw[64:128, 0:NT], xx[0:64, 0:NT]), [bx], [BF[2]])
                                V(lambda e, xx=xx: e.tensor_tensor(t1[:, 0:NT], xx[:, 0:NT], C0[:, 0:NT], ALU.mult), [bx, BF[6]], [BF[3]])
                                V(lambda e: e.tensor_tensor(t2[:, 0:NT], xsw[:, 0:NT], S0[:, 0:NT], ALU.mult), [BF[2], BF[7]], [BF[4]])
                                V(lambda e, oT=oT: e.tensor_tensor(oT[:, 0:NT], t1[:, 0:NT], t2[:, 0:NT], ALU.add), [BF[3], BF[4]], [boT])

                            def evg(ps, bp, bi, o, n):
                                A(lambda e: e.activation(gT[:, o:o + n], ps[:, 0:n], AF.Silu), [bp], [bgT])
                            proj_fm(slot, bs, 384, NT, evg)
                            yield
                            vproj(slot, bs, 256, NT, 128, vt_, bv_)
                            yield

                        def back_r(hh):
                            qT, kT, gT, vt_, bqT, bkT, bgT, bv_ = SETS[hh % 2]
                            if has_s:
                                load_sample_state(D["rets"], hh, 128)
                                make_qz(qT, bqT, NP)
                            g128 = math.exp(128 * LG[hh]); g8 = math.exp(8 * LG[hh])
                            for bi, (o, n) in enumerate(blks):
                                PT, bPT = PS()
                                pinned.add(pb.index(PT))
                                for c in range(n // 128):
                                    tcol = o + c * 128
                                    tix = tcol // 128
                                    smp = tcol >= NP
                                    ekf = lambda t_: gk[:, 1 if t_ >= NP else 0, hh:hh + 1]
                                    ctx_ = CTX.pop(tcol, None) or att_pre(qT, kT, bqT, bkT, tcol, ekf(tcol), smp)
                                    if tcol + 128 < NT:
                                        CTX[tcol + 128] = att_pre(qT, kT, bqT, bkT, tcol + 128, ekf(tcol + 128), tcol + 128 >= NP)
                                    att_tile(qT, kT, bqT, bkT, tcol, vt_[:, tix, :], 128, 4 + hh, gk[:, 1 if smp else 0, hh:hh + 1],
                                             (lambda j, g8=g8: g8) if smp else g128, PT, bPT, c * 128, smp, usbuf=Usf, bv=bv_, ctx=ctx_)
                                    step()
                                    if PEND[0] is not None and c == 0:
                                        PEND[0]()
                                        PEND[0] = None
                                def chain_(PT=PT, bPT=bPT, o=o, n=n, bi=bi):
                                    hv = NT5[2]
                                    if o < NP:
                                        V(lambda e, PT=PT, hh=hh: e.tensor_tensor(hv[:, 0:n].rearrange("p (c t) -> p c t", t=128),
                                                                                 PT[:, 0:n].rearrange("p (c t) -> p c t", t=128),
                                                                                 Gq[:, 0, hh, :].unsqueeze(1).broadcast_to([128, n // 128, 128]), ALU.mult),
                                          [bPT, Bc], [BN[2]])
                                    else:
                                        V(lambda e, PT=PT, hh=hh: e.tensor_tensor(hv[:, 0:n], PT[:, 0:n], Gq[:, 1, hh, :], ALU.mult), [bPT, Bc], [BN[2]])
                                    rs = rstd_from(lambda q: A(lambda e: e.activation(q[:, 0:n], hv[:, 0:n], AF.Square), [BN[2]], [Bsq[0]]), n, None)
                                    V(lambda e: e.tensor_tensor(hv[:, 0:n], hv[:, 0:n], rs[:, 0:n], ALU.mult), [BN[2], BN[3]], [BN[2]])
                                    V(lambda e, hh=hh: e.tensor_tensor(mix[:, 4 + hh, o:o + n], hv[:, 0:n], gT[:, o:o + n], ALU.mult),
                                      [BN[2], bgT], [Bmix[4 + hh][bi]])
                                    pinned.discard(pb.index(PT))
                                    step()
                                PEND[0] = chain_
                            if PEND[0] is not None:
                                PEND[0]()
                                PEND[0] = None
                            if has_s:
                                STO(D["rets_o"][:, hh, :, :].rearrange("j d e -> d j e"), Usf[:, :, 0:128], [BUs])
                            if last:
                                STO(D["retp"][hh], Up[:, 4 + hh, 0:128], [BU[4 + hh]])
                        for _ in front_r(0):
                            pass
                        for hh in range(4):
                            STEP[0] = front_r(hh + 1) if hh + 1 < 4 else None
                            back_r(hh)
                            while STEP[0] is not None:
                                step()
                        resid_proj(D["w_out_ab"][0] if False else D["w_out_ab"], NT, mix, Bmix)
                        ck(4)
                    else:
                        W = D["w_in_cd"]
                        qf, ff, eb, enb, tmp = Fs[0], Fs[1], Fs[2], Fs[3], Fs[4]
                        qT, kT, gT = Hs[0], Hs[1], Hs[2]
                        bqT, bkT, bgT = BH[0], BH[1], BH[2]
                        def front_h(hh):
                            qT, kT, gT, vt_, bqT, bkT, bgT, bv_ = SETS[hh % 2]
                            slot, bs = wload([(D["w_cd_h"][:, hh * 512:(hh + 1) * 512], 0)], 8)

                            def evq(ps, bp, bi, o, n):
                                A(lambda e: e.activation(qf[:, o:o + n], ps[:, 0:n], AF.Copy, scale=128.0 ** -0.5), [bp], [BF[0]])
                            proj_fm(slot, bs, 0, NT, evq)
                            yield

                            def evf(ps, bp, bi, o, n):
                                A(lambda e: e.activation(ff[:, o:o + n], ps[:, 0:n], AF.Sigmoid), [bp], [BF[1]])
                            proj_fm(slot, bs, 128, NT, evf)
                            yield

                            def evg(ps, bp, bi, o, n):
                                A(lambda e: e.activation(gT[:, o:o + n], ps[:, 0:n], AF.Silu), [bp], [bgT])
                            proj_fm(slot, bs, 384, NT, evg)
                            yield
                            vproj(slot, bs, 256, NT, 128, vt_, bv_)
                            yield
                            V(lambda e, hh=hh: e.tensor_scalar(ff[:, 0:NT], ff[:, 0:NT], oml[:, hh:hh + 1], lb[:, hh:hh + 1], ALU.mult, ALU.add),
                              [BF[1], Bc], [BF[1]])
                            A(lambda e: e.activation(tmp[:, 0:NT], ff[:, 0:NT], AF.Ln), [BF[1]], [BF[4]])
                            V(lambda e: e.tensor_tensor_scan(eb[:, 0:NT], segm[:, 0:NT], tmp[:, 0:NT], 0.0, ALU.mult, ALU.add), [BF[4], Bc], [BF[2]])
                            A(lambda e: e.activation(enb[:, 0:NT], eb[:, 0:NT], AF.Exp, scale=-1.0), [BF[2]], [BF[3]])
                            A(lambda e: e.activation(eb[:, 0:NT], eb[:, 0:NT], AF.Exp), [BF[2], BF[3]], [BF[2]])
                            V(lambda e, hh=hh: e.tensor_copy(hdec[:, hh % 2, 0:ntp], eb[:, 0:NP].rearrange("p (c t) -> p c t", t=128)[:, :, 127]), [BF[2]], [Bhd])
                            if has_s:
                                V(lambda e, hh=hh: e.tensor_copy(hdec[:, hh % 2, 8:24], eb[:, NP:NT].rearrange("p (j l) -> p j l", l=8)[:, :, 7]), [BF[2]], [Bhd])
                            V(lambda e: e.tensor_scalar(ff[:, 0:NT], ff[:, 0:NT], -1.0, 1.0, ALU.mult, ALU.add), [BF[1], BF[4]], [BF[1]])
                            V(lambda e: e.tensor_tensor(qT[:, 0:NT], qf[:, 0:NT], eb[:, 0:NT], ALU.mult), [BF[0], BF[2]], [bqT])
                            V(lambda e: e.tensor_tensor(kT[:, 0:NT], ff[:, 0:NT], enb[:, 0:NT], ALU.mult), [BF[1], BF[3]], [bkT])

                        def back_h(hh):
                            qT, kT, gT, vt_, bqT, bkT, bgT, bv_ = SETS[hh % 2]
                            if has_s:
                                load_sample_state(D["hgrns"], hh, 128)
                                make_qz(qT, bqT, NP)
                            for bi, (o, n) in enumerate(blks):
                                PT, bPT = PS()
                                pinned.add(pb.index(PT))
                                for c in range(n // 128):
                                    tcol = o + c * 128
                                    tix = tcol // 128
                                    smp = tcol >= NP
                                    ctx_ = CTX.pop(tcol, None) or att_pre(qT, kT, bqT, bkT, tcol, None, smp)
                                    if tcol + 128 < NT:
                                        CTX[tcol + 128] = att_pre(qT, kT, bqT, bkT, tcol + 128, None, tcol + 128 >= NP)
                                    att_tile(qT, kT, bqT, bkT, tcol, vt_[:, tix, :], 128, 8 + hh, None,
                                             (lambda j, hh=hh: hdec[:, hh % 2, 8 + j:9 + j]) if smp else hdec[:, hh % 2, tix:tix + 1],
                                             PT, bPT, c * 128, smp, usbuf=Usf, bv=bv_, ctx=ctx_)
                                    step()
                                    if PEND[0] is not None and c == 0:
                                        PEND[0]()
                                        PEND[0] = None
                                def chain_(PT=PT, bPT=bPT, o=o, n=n, bi=bi):
                                    rs = rstd_from(lambda q, PT=PT, bPT=bPT: A(lambda e: e.activation(q[:, 0:n], PT[:, 0:n], AF.Square), [bPT], [Bsq[0]]), n, None)
                                    hv = NT5[2]
                                    V(lambda e, PT=PT: e.tensor_tensor(hv[:, 0:n], PT[:, 0:n], rs[:, 0:n], ALU.mult), [bPT, BN[3]], [BN[2]])
                                    V(lambda e, hh=hh: e.scalar_tensor_tensor(mix[:, hh, o:o + n], hv[:, 0:n], pv("gnc", hh), gT[:, o:o + n], ALU.mult, ALU.mult),
                                      [BN[2], bgT, Bc], [Bmix[hh][bi]])
                                    pinned.discard(pb.index(PT))
                                    step()
                                PEND[0] = chain_
                            if PEND[0] is not None:
                                PEND[0]()
                                PEND[0] = None
                            if has_s:
                                STO(D["hgrns_o"][:, hh, :, :].rearrange("j d e -> d j e"), Usf[:, :, 0:128], [BUs])
                            if last:
                                STO(D["hgrnp"][hh], Up[:, 8 + hh, 0:128], [BU[8 + hh]])
                        for _ in front_h(0):
                            pass
                        for hh in range(4):
                            STEP[0] = front_h(hh + 1) if hh + 1 < 4 else None
                            back_h(hh)
                            while STEP[0] is not None:
                                step()
                        ck(7)
                        suf = Fs[8]; bsuf = BF[8]
                        V(lambda e: e.memset(cch[:, 5, 0:1], 0.0), (), S5FINE + [BUs, Bkz, Bqz, Bcch, Bg2, Bv2])
                        pinned.clear(); S5MODE[0] = True
                        sub = Hs[0]; bsub = BH[0]
                        if has_s:
                            LD(x0s[:, 0], D["x0re"], [Bx0]); LD(x0s[:, 1], D["x0im"], [Bx0])
                            bz = lambda v: v.unsqueeze(2).broadcast_to([128, 16, 16])
                            u1 = Fs[6][:, 0:256].rearrange("p (a b) -> p a b", a=16); u2 = Fs[7][:, 0:256].rearrange("p (a b) -> p a b", a=16)
                            u3 = Fs[6][:, 256:512].rearrange("p (a b) -> p a b", a=16); u4 = Fs[7][:, 256:512].rearrange("p (a b) -> p a b", a=16)
                            for (cr, ci) in ((izr, izi), (ar_, ai_)):
                                V(lambda e, cr=cr: e.tensor_tensor(u1, x0s[:, 0], bz(cr), ALU.mult), [Bx0, Bc], [BF[6]])
                                V(lambda e, ci=ci: e.tensor_tensor(u2, x0s[:, 1], bz(ci), ALU.mult), [Bx0, Bc], [BF[7]])
                                V(lambda e, ci=ci: e.tensor_tensor(u3, x0s[:, 0], bz(ci), ALU.mult), [Bx0, Bc], [BF[6]])
                                V(lambda e, cr=cr: e.tensor_tensor(u4, x0s[:, 1], bz(cr), ALU.mult), [Bx0, Bc], [BF[7]])
                                V(lambda e: e.tensor_tensor(x0s[:, 0], u1, u2, ALU.subtract), [BF[6], BF[7]], [Bx0])
                                V(lambda e: e.tensor_tensor(x0s[:, 1], u3, u4, ALU.add), [BF[6], BF[7]], [Bx0])
                        ntl = NT // 128
                        slot_su, bs_su = wload([(W[:, 2048:2560], 0)], 8)
                        for oc in range(4):
                            slot, bs = slot_su, bs_su

                            def evs(ps, bp, bi, o, n):
                                A(lambda e: e.copy(suf[:, o:o + n], ps[:, 0:n]), [bp], [bsuf])
                                A(lambda e: e.copy(sub[:, o:o + n], ps[:, 0:n]), [bp], [bsub])
                            proj_fm(slot, bs, oc * 128, NT, evs)
                            for part in range(2):
                                S.dma("pool", lambda e, part=part, oc=oc: e.dma_start(out=bct[:, 0, part], in_=D["BT"][part, oc * 4:(oc + 1) * 4].rearrange("i k m -> k i m")),
                                      bcsem[part * 2], (), (Bbct if part == 0 else [Bbct[part * 2]]))
                                S.dma("pool", lambda e, part=part, oc=oc: e.dma_start(out=bct[:, 1, part], in_=D["CT"][part, oc * 4:(oc + 1) * 4].rearrange("i k m -> k i m")),
                                      bcsem[part * 2 + 1], (), [Bbct[part * 2 + 1]])
                            V(lambda e: e.tensor_scalar(bct[:, 1, 1], bct[:, 1, 1], -1.0, None, ALU.mult), Bbct, [Bbct[3]])
                            LD(tbe[:], tabE[:, :, oc * 512:(oc + 1) * 512], [Btab], r=[Bscr])
                            LD(tbz[:], tabZ[:, :, 4 * oc:4 * oc + 4, :], [Btab], r=[Bscr])
                            if has_s:
                                EsT = [Hs[1][:, 0:512], Hs[2][:, 0:512]]
                                ZsT = [Hs[3][:, 0:512], Hs[4][:, 0:512]]
                                build_tables(oc, pv("s8"), jrow[:, 3, :], lambda part: EsT[part],
                                             lambda part: ZsT[part].rearrange("p (a b) -> p a b", a=4), [Bes, BH[1], BH[2]], [Bes, BH[3], BH[4]])
                                for part in range(2):
                                    V(lambda e, part=part, oc=oc: e.tensor_copy(cbs[:, part].rearrange("p a (j l) -> p a j l", l=8),
                                                                             x0s[:, part, 4 * oc:4 * oc + 4, :].unsqueeze(3).broadcast_to([128, 4, 16, 8])),
                                      [Bx0], [Bcbs])
                            i4 = slice(4 * oc, 4 * oc + 4)
                            ypsh = [None]

                            def stA(c):
                                smp = c * 128 >= NP; tc0 = c * 128; k2 = c % 2
                                wrb, wib = wtb[k2][0], wtb[k2][1]; xrb, xib = xtb[k2][0], xtb[k2][1]
                                bE = Bes if smp else Btab
                                smp = c * 128 >= NP
                                tc0 = c * 128
                                Er = EsT[0] if smp else tbe[:, 0, :]
                                Ei = EsT[1] if smp else tbe[:, 1, :]
                                bE = Bes if smp else Btab
                                k2 = c % 2
                                pr, bpr = PS(); pi_, bpi = PS()
                                P(lambda e, pr=pr: e.matmul(pr[:, 0:512], sub[:, tc0:tc0 + 128], bct[:, 0, 0].rearrange("p a b -> p (a b)"), start=True, stop=True),
                                  Bbct + [bsub], [bpr])
                                P(lambda e, pi_=pi_: e.matmul(pi_[:, 0:512], sub[:, tc0:tc0 + 128], bct[:, 0, 1].rearrange("p a b -> p (a b)"), start=True, stop=True),
                                  Bbct + [bsub], [bpi])
                                prb = qzf[:, (2 * k2) * 512:(2 * k2 + 1) * 512]; pib = qzf[:, (2 * k2 + 1) * 512:(2 * k2 + 2) * 512]
                                A(lambda e, pr=pr: e.copy(prb, pr[:, 0:512]), [bpr], [Bprb[k2]])
                                A(lambda e, pi_=pi_: e.copy(pib, pi_[:, 0:512]), [bpi], [Bprb[k2]])
                                wrb, wib = wtb[k2][0], wtb[k2][1]
                                ta, tb, tc_, td = [kzf[:, q_ * 512:(q_ + 1) * 512] for q_ in range(4)]
                                V(lambda e: e.tensor_tensor(ta, prb, Er, ALU.mult), [Bprb[k2], bE], [Bt4[0]])
                                V(lambda e: e.tensor_tensor(tb, pib, Ei, ALU.mult), [Bprb[k2], bE], [Bt4[1]])
                                V(lambda e: e.tensor_tensor(wrb[:], ta, tb, ALU.subtract), [Bt4[0], Bt4[1]], [Bwt[k2]])
                                V(lambda e: e.tensor_tensor(tc_, pib, Er, ALU.mult), [Bprb[k2], bE], [Bt4[2]])
                                V(lambda e: e.tensor_tensor(td, prb, Ei, ALU.mult), [Bprb[k2], bE], [Bt4[3]])
                                V(lambda e: e.tensor_tensor(wib[:], tc_, td, ALU.add), [Bt4[2], Bt4[3]], [Bwt[k2]])

                            TC = {}

                            def stB1(c):
                                smp = c * 128 >= NP; tc0 = c * 128; k2 = c % 2
                                wrb, wib = wtb[k2][0], wtb[k2][1]; xrb, xib = xtb[k2][0], xtb[k2][1]
                                bE = Bes if smp else Btab
                                csr, bcsr = PSS(); csi, bcsi = PSS()
                                cur = S5PAR[oc]
                                TC[c] = (csr, bcsr, csi, bcsi, cur)
                                msk = maskb if smp else maskc
                                for il in range(4):
                                    P(lambda e, il=il, csr=csr: e.matmul(csr[:, il * 128:(il + 1) * 128], wrb[:, il * 128:(il + 1) * 128], msk[:], start=True, stop=(not smp)),
                                      [Bwt[k2], Bc], [bcsr])
                                    P(lambda e, il=il, csi=csi: e.matmul(csi[:, il * 128:(il + 1) * 128], wib[:, il * 128:(il + 1) * 128], msk[:], start=True, stop=(not smp)),
                                      [Bwt[k2], Bc], [bcsi])
                                    if smp:
                                        P(lambda e, il=il, csr=csr: e.matmul(csr[:, il * 128:(il + 1) * 128], identb[:], cbs[:, 0, il, :], start=False, stop=True), [Bc, Bcbs], [bcsr])
                                        P(lambda e, il=il, csi=csi: e.matmul(csi[:, il * 128:(il + 1) * 128], identb[:], cbs[:, 1, il, :], start=False, stop=True), [Bc, Bcbs], [bcsi])
                                cs3r = csr[:, 0:512].rearrange("p (a b) -> p a b", a=4); cs3i = csi[:, 0:512].rearrange("p (a b) -> p a b", a=4)
                                if not smp:
                                    nxt = 1 - cur
                                    V(lambda e: e.tensor_tensor(cch[:, 0, :], cs3r[:, :, 127], s5d[:, cur, 0, i4], ALU.add), [bcsr, Bs5d[cur]], [Bcr])
                                    V(lambda e: e.tensor_tensor(cch[:, 1, :], cs3i[:, :, 127], s5d[:, cur, 1, i4], ALU.add), [bcsi, Bs5d[cur]], [Bcr])
                                    V(lambda e: e.tensor_tensor(cch[:, 2, :], cch[:, 0, :], a128r[:, i4], ALU.mult), [Bcr, Bc], [Bcch])
                                    V(lambda e: e.tensor_tensor(cch[:, 3, :], cch[:, 1, :], a128i[:, i4], ALU.mult), [Bcr, Bc], [Bcch])
                                    V(lambda e: e.tensor_tensor(cch[:, 4, :], cch[:, 0, :], a128i[:, i4], ALU.mult), [Bcr, Bc], [Bcch])
                                    V(lambda e: e.tensor_tensor(cch[:, 5, :], cch[:, 1, :], a128r[:, i4], ALU.mult), [Bcr, Bc], [Bcch])
                                    V(lambda e: e.tensor_tensor(s5d[:, nxt, 0, i4], cch[:, 2, :], cch[:, 3, :], ALU.subtract), [Bcch], [Bs5d[nxt]])
                                    V(lambda e: e.tensor_tensor(s5d[:, nxt, 1, i4], cch[:, 4, :], cch[:, 5, :], ALU.add), [Bcch], [Bs5d[nxt]])
                                    S5PAR[oc] = nxt

                            def stB2(c):
                                smp = c * 128 >= NP; tc0 = c * 128; k2 = c % 2
                                wrb, wib = wtb[k2][0], wtb[k2][1]; xrb, xib = xtb[k2][0], xtb[k2][1]
                                csr, bcsr, csi, bcsi, cur = TC.pop(c)
                                csbr = usbf[:, (2 * k2) * 512:(2 * k2 + 1) * 512]; csbi = usbf[:, (2 * k2 + 1) * 512:(2 * k2 + 2) * 512]
                                if not smp:
                                    for il in range(4):
                                        i = 4 * oc + il
                                        sl = slice(il * 128, (il + 1) * 128)
                                        A(lambda e, sl=sl, i=i, csr=csr: e.activation(csbr[:, sl], csr[:, sl], AF.Identity, bias=s5d[:, cur, 0, i:i + 1]), [bcsr, Bs5d[cur], Bcr], [Bcsb[k2]])
                                        A(lambda e, sl=sl, i=i, csi=csi: e.activation(csbi[:, sl], csi[:, sl], AF.Identity, bias=s5d[:, cur, 1, i:i + 1]), [bcsi, Bs5d[cur], Bcr], [Bcsb[k2]])
                                    Zr = tbz[:, 0].rearrange("p a b -> p (a b)"); Zi = tbz[:, 1].rearrange("p a b -> p (a b)"); bZ = Btab
                                else:
                                    A(lambda e, csr=csr: e.copy(csbr, csr[:, 0:512]), [bcsr], [Bcsb[k2]])
                                    A(lambda e, csi=csi: e.copy(csbi, csi[:, 0:512]), [bcsi], [Bcsb[k2]])
                                    Zr = ZsT[0]; Zi = ZsT[1]; bZ = Bes
                                p1, p2, p3, p4 = [pt4[q_] for q_ in range(4)]
                                xrb, xib = xtb[k2][0], xtb[k2][1]
                                V(lambda e: e.tensor_tensor(p1[:], csbr, Zr, ALU.mult), [Bcsb[k2], bZ], [Bp4[0]])
                                V(lambda e: e.tensor_tensor(p2[:], csbi, Zi, ALU.mult), [Bcsb[k2], bZ], [Bp4[1]])
                                V(lambda e: e.tensor_tensor(xrb[:], p1[:], p2[:], ALU.subtract), [Bp4[0], Bp4[1]], [Bxt[k2]])
                                V(lambda e: e.tensor_tensor(p3[:], csbr, Zi, ALU.mult), [Bcsb[k2], bZ], [Bp4[2]])
                                V(lambda e: e.tensor_tensor(p4[:], csbi, Zr, ALU.mult), [Bcsb[k2], bZ], [Bp4[3]])
                                V(lambda e: e.tensor_tensor(xib[:], p3[:], p4[:], ALU.add), [Bp4[2], Bp4[3]], [Bxt[k2]])
                                if last and (smp or c == NP // 128 - 1):
                                    p13 = [q_[:].rearrange("p (a b) -> p a b", a=4) for q_ in (p1, p2, p3, p4)]
                                    if not smp:
                                        V(lambda e: e.tensor_tensor(s5po[:, 0, i4], p13[0][:, :, 127], p13[1][:, :, 127], ALU.subtract), [Bp4[0], Bp4[1]], [Bs5o])
                                        V(lambda e: e.tensor_tensor(s5po[:, 1, i4], p13[2][:, :, 127], p13[3][:, :, 127], ALU.add), [Bp4[2], Bp4[3]], [Bs5o])
                                    else:
                                        l7 = lambda q3: q3.rearrange("p a (j l) -> p a j l", l=8)[:, :, :, 7]
                                        V(lambda e: e.tensor_tensor(s5so[:, 0, i4, :], l7(p13[0]), l7(p13[1]), ALU.subtract), [Bp4[0], Bp4[1]], [Bs5o])
                                        V(lambda e: e.tensor_tensor(s5so[:, 1, i4, :], l7(p13[2]), l7(p13[3]), ALU.add), [Bp4[2], Bp4[3]], [Bs5o])

                            def stC(c):
                                smp = c * 128 >= NP; tc0 = c * 128; k2 = c % 2
                                wrb, wib = wtb[k2][0], wtb[k2][1]; xrb, xib = xtb[k2][0], xtb[k2][1]
                                bE = Bes if smp else Btab
                                if c % 4 == 0:
                                    pinned.clear()
                                    ypsh[0] = PS()
                                    pinned.add(pb.index(ypsh[0][0]))
                                yp, byp = ypsh[0]
                                yc = (c % 4) * 128
                                for il in range(4):
                                    P(lambda e, il=il, yp=yp: e.matmul(yp[:, yc:yc + 128], bct[:, 1, 0, il, :], xrb[:, il * 128:(il + 1) * 128], start=(il == 0), stop=False),
                                      Bbct + [Bxt[k2]], [byp])
                                    P(lambda e, il=il, yp=yp: e.matmul(yp[:, yc:yc + 128], bct[:, 1, 1, il, :], xib[:, il * 128:(il + 1) * 128], start=False, stop=(il == 3)),
                                      Bbct + [Bxt[k2]], [byp])
                                if c % 4 == 3 or c == ntl - 1:
                                    o = (c // 4) * 512
                                    n = (c % 4 + 1) * 128
                                    bi = o // 512
                                    yv = NT5[4]
                                    V(lambda e, yp=yp: e.scalar_tensor_tensor(yv[:, 0:n], suf[:, o:o + n], pv("s5d", oc), yp[:, 0:n], ALU.mult, ALU.add),
                                      [bsuf, byp, Bc], [BN[4]])
                                    A(lambda e: e.activation(mix[:, 4 + oc, o:o + n], yv[:, 0:n], AF.Gelu), [BN[4]], [Bmix[4 + oc][bi]])
                            stA(0)
                            stB1(0)
                            for c in range(ntl):
                                if c + 1 < ntl:
                                    stA(c + 1)
                                stB2(c)
                                if c + 1 < ntl:
                                    stB1(c + 1)
                                stC(c)
                        V(lambda e: e.memset(cch[:, 5, 0:1], 0.0), (), S5FINE + [BUs, Bkz, Bqz, Bcch, Bg2, Bv2])
                        pinned.clear()
                        S5MODE[0] = False
                        if last:
                            STO(D["s5rep"], s5po[:, 0, :], [Bs5o]); STO(D["s5imp"], s5po[:, 1, :], [Bs5o])
                            STO(D["s5res"], s5so[:, 0], [Bs5o]); STO(D["s5ims"], s5so[:, 1], [Bs5o])
                        ck(8)
                        slot, bs = wload([(D["w_glu"], 0)], 4)
                        for oc in range(4):
                            for bi, (o, n) in enumerate(blks):
                                ps, bp = PS()
                                for k in range(4):
                                    P(lambda e, k=k, ps=ps, oc=oc: e.matmul(ps[:, 0:n], slot[:, k, oc * 128:(oc + 1) * 128], mix[:, 4 + k, o:o + n],
                                                                            start=(k == 0), stop=(k == 3)), bs + [Bmix[4 + k][bi] for k in range(4)], [bp])
                                A(lambda e, ps=ps, oc=oc: e.activation(hn[:, oc, o:o + n], ps[:, 0:n], AF.Sigmoid, bias=pv("bglu", oc)), [bp, Bc], [Bhn[bi]])
                        for oc in range(4):
                            for bi, (o, n) in enumerate(blks):
                                V(lambda e, oc=oc: e.tensor_tensor(mix[:, 4 + oc, o:o + n], mix[:, 4 + oc, o:o + n], hn[:, oc, o:o + n], ALU.mult),
                                  [Bhn[bi], Bmix[4 + oc][bi]], [Bmix[4 + oc][bi]])
                        resid_proj(D["w_out_cd"], NT, mix, Bmix)
                    rmsnorm("nff%d" % layer, NT)
                    for q in range(4):
                        if sbi == 0 and layer == 0:
                            build_tab_kc(q)
                        for u in range(2):
                            c0 = q * 1024 + u * 512
                            slot, bs = wload([(D["w_ff1"][layer][:, c0:c0 + 512], 0)], 8)
                            for hc in range(4):
                                c = u * 4 + hc

                                def ev(ps, bp, bi, o, n, c=c):
                                    t = sqb[1]
                                    A(lambda e: e.activation(t[:, 0:n], ps[:, 0:n], AF.Relu), [bp], [Bsq[1]])
                                    V(lambda e: e.tensor_tensor(mix[:, c, o:o + n], t[:, 0:n], t[:, 0:n], ALU.mult), [Bsq[1]], [Bmix[c][bi]])
                                proj_fm(slot, bs, hc * 128, NT, ev)
                        for u in range(2):
                            slot, bs = wload([(D["w_ff2"][layer][q * 1024:(q + 1) * 1024, u * 512:(u + 1) * 512], 0)], 8)
                            for oc in range(4):
                                c = u * 4 + oc

                                def ev(ps, bp, bi, o, n, c=c):
                                    V(lambda e: e.tensor_tensor(h[:, c, o:o + n], h[:, c, o:o + n], ps[:, 0:n], ALU.add), [bp, Bh[c][bi]], [Bh[c][bi]])
                                proj_fm(slot, bs, oc * 128, NT, ev, rhs=mix, rbufs=lambda bi: [Bmix[k][bi] for k in range(8)])
                    ck(5)
                    rmsnorm("nple%d" % layer, NT)
                    S.dma("pool", lambda e, layer=layer: e.dma_start(out=pTb[:, :, 0:NT], in_=D["pT"][layer][:, tok0:tok0 + NT].rearrange("(k p) n -> p k n", p=128)),
                          pTsem, (), [BpT])
                    for u in range(2):
                        slot, bs = wload([(D["w_ple_gate"][layer][:, u * 512:(u + 1) * 512], 0)], 8)
                        slot2, bs2 = wload([(D["w_ple_proj"][layer][:, u * 512:(u + 1) * 512], 0)], 2)
                        for oc in range(4):
                            c = u * 4 + oc
                            for bi, (o, n) in enumerate(blks):
                                ps, bp = PS()
                                for k in range(8):
                                    P(lambda e, k=k, ps=ps: e.matmul(ps[:, 0:n], slot[:, k, oc * 128:(oc + 1) * 128], hn[:, k, o:o + n], start=(k == 0), stop=(k == 7)),
                                      bs + [Bhn[bi]], [bp])
                                gt = NT5[0]
                                A(lambda e, ps=ps: e.activation(gt[:, 0:n], ps[:, 0:n], AF.Sigmoid), [bp], [BN[0]])
                                ps2, bp2 = PS()
                                for k in range(2):
                                    P(lambda e, k=k, ps2=ps2: e.matmul(ps2[:, 0:n], slot2[:, k, oc * 128:(oc + 1) * 128], pTb[:, k, o:o + n], start=(k == 0), stop=(k == 1)),
                                      bs2 + [BpT], [bp2])
                                V(lambda e, ps2=ps2: e.tensor_tensor(gt[:, 0:n], gt[:, 0:n], ps2[:, 0:n], ALU.mult), [bp2, BN[0]], [BN[0]])
                                V(lambda e, c=c: e.tensor_tensor(h[:, c, o:o + n], h[:, c, o:o + n], gt[:, 0:n], ALU.add), [BN[0], Bh[c][bi]], [Bh[c][bi]])
                ck(9)
                rmsnorm("nfin", NT, final_out=D["yT"][:, tok0:tok0 + NT] if True else None)
        except _Stop:
            pass
        S.final_wait("sp", OUTS)
        S.emit(st)
    return nc


_NC = [None]


def kernel(**I):
    if _NC[0] is None:
        _NC[0] = build_program()
    nc = _NC[0]
    in_maps = prep_inputs(I)
    res = run_bass_kernel_spmd(nc, in_maps, core_ids=list(range(8)))
    return assemble(res.results)


def prep_inputs(I):
    f = lambda a: np.ascontiguousarray(np.asarray(a, np.float32))
    ident = np.eye(128, dtype=np.float32)
    s_ = np.arange(128)
    maskc = (s_[:, None] <= s_[None, :]).astype(np.float32)
    maskb = maskc * (s_[:, None] // 8 == s_[None, :] // 8)
    blk3 = np.broadcast_to((np.arange(16)[:, None] == (s_[None, :] // 8)).astype(np.float32)[None], (128, 16, 128)).copy()
    rowm = (s_[:, None] // 8 == np.arange(16)[None, :]).astype(np.float32)
    segm = np.ones((3, 128, NTM), np.float32); posrow = np.zeros((3, 128, NTM), np.float32); tau = np.ones((3, 128, NTM), np.float32)
    for i, (t0, NP, hs) in enumerate(SBS):
        segm[i, :, 0:NP:128] = 0.0
        posrow[i, :, 0:NP] = np.arange(t0, t0 + NP)[None]
        tau[i, :, 0:NP] = np.arange(1, NP + 1)[None]
        if hs:
            segm[i, :, NP:NP + 128:8] = 0.0
            posrow[i, :, NP:NP + 128] = (16384 + (np.arange(128) % 8))[None]
            tau[i, :, NP:NP + 128] = (1 + (np.arange(128) % 8))[None]
    negm = np.zeros((4, 128), np.float32); negm[:, 0::8] = -1e30
    sel = np.zeros((4, 4, 128), np.float32)
    for k in range(4):
        sel[k, k, :] = 1.0
    jrow = np.zeros((128, 4, 128), np.float32); jrow[:, 0, :] = (s_ + 1)[None]; jrow[:, 1, :] = (s_ % 8 + 1)[None]; jrow[:, 2, :] = s_[None]; jrow[:, 3, :] = (s_ % 8)[None]
    pvec = np.zeros((128, NPV), np.float32)

    def put(name, arr):
        o, w = PV[name]
        pvec[:, o:o + w] = arr
    for l in range(2):
        put("nmix%d" % l, _cols(I["norm_mix"][l])); put("nff%d" % l, _cols(I["norm_ff"][l])); put("nple%d" % l, _cols(I["norm_ple"][l]))
        put("lb%d" % l, _cols(I["lb_logits"][l]))
    put("nfin", _cols(I["norm_final"]))
    for j in range(4):
        put("cw%d" % j, _cols(I["conv_w_ab"][0][j]))
    put("cb", _cols(I["conv_b_ab"][0])); put("gna", _cols(I["gn_a"][0])); put("gnc", _cols(I["gn_c"][0]))
    put("s5d", _cols(I["s5_D"][0])); put("bglu", _cols(I["b_glu"][0]))
    st = lambda a: np.ascontiguousarray(np.asarray(a, np.float32).reshape(16, 2, 64).reshape(16, 128).T)
    put("are", st(I["s5_A_re"][0])); put("aim", st(I["s5_A_im"][0]))
    put("ldt", st(np.repeat(np.asarray(I["s5_log_dt"][0], np.float32)[:, None], 64, axis=1)))
    put("invf", (10000.0 ** (-(np.arange(128) % 64) / 64.0)).astype(np.float32)[:, None])
    put("sgn", np.where(s_ < 64, -1.0, 1.0).astype(np.float32)[:, None])
    put("s0", s_.astype(np.float32)[:, None]); put("s8", (s_ % 8).astype(np.float32)[:, None]); put("pidx", (s_ + 1).astype(np.float32)[:, None]); put("pidxs", (s_ % 8 + 1).astype(np.float32)[:, None]); put("rowm", rowm)
    rw = lambda a: np.broadcast_to(np.asarray(a, np.float32).reshape(1, 2048), (128, 2048))
    rowp = np.ascontiguousarray(np.stack([rw(I["s5_A_re"][0]), rw(I["s5_A_im"][0]),
                                          rw(np.repeat(np.asarray(I["s5_log_dt"][0], np.float32)[:, None], 64, axis=1))]))
    bgv = np.asarray(I["b_gate_ab"][0], np.float32)
    bg = np.stack([bgv[:4], bgv[4:]], axis=1).copy()
    BT = np.zeros((2, 16, 128, 128), np.float32); CT = np.zeros((2, 16, 128, 128), np.float32)
    for part, (Bm, Cm) in enumerate(((I["s5_B_re"][0], I["s5_C_re"][0]), (I["s5_B_im"][0], I["s5_C_im"][0]))):
        Bm = np.asarray(Bm, np.float32); Cm = np.asarray(Cm, np.float32)
        for g in range(32):
            i, gl = g // 2, g % 2
            k0 = (g % 8) * 16
            BT[part, i, k0:k0 + 16, gl * 64:(gl + 1) * 64] = Bm[g].T
            CT[part, i, gl * 64:(gl + 1) * 64, k0:k0 + 16] = Cm[g].T
    wab = np.asarray(I["w_in_ab"][0], np.float32); wcd = np.asarray(I["w_in_cd"][0], np.float32)
    w_ab_h = np.concatenate([wab[:, b0 + sec * 512 + hh * 128:b0 + sec * 512 + (hh + 1) * 128]
                             for b0 in (0, 2056) for hh in range(4) for sec in range(4)], axis=1)
    w_cd_h = np.concatenate([wcd[:, sec * 512 + hh * 128:sec * 512 + (hh + 1) * 128] for hh in range(4) for sec in range(4)], axis=1)
    common = dict(w_ab_h=f(w_ab_h), w_cd_h=f(w_cd_h), w_in_ab=f(I["w_in_ab"][0]), wg=f(I["w_in_ab"][0][:, 2048:2056]), w_out_ab=f(I["w_out_ab"][0]),
                  w_in_cd=f(I["w_in_cd"][0]), w_glu=f(I["w_glu"][0]), w_out_cd=f(I["w_out_cd"][0]),
                  w_ff1=f(I["w_ff1"]), w_ff2=f(I["w_ff2"]), w_ple_proj=f(I["w_ple_proj"]), w_ple_gate=f(I["w_ple_gate"]),
                  pvec=pvec, bg=bg, BT=BT, CT=CT, ident=ident, maskc=maskc, maskb=maskb.astype(np.float32), blk3=blk3,
                  segm=segm, negm=negm, sel=sel, posrow=posrow, rowp=rowp, jrow=jrow)
    in_maps = []
    for c in range(8):
        sl = slice(16 * c, 16 * c + 16)
        xT = np.concatenate([np.asarray(I["x_prompt"][c]).T, np.asarray(I["x_sample"][sl]).reshape(128, 1024).T], axis=1)
        pT = np.concatenate([np.transpose(np.asarray(I["p_prompt"][:, c]), (0, 2, 1)),
                             np.transpose(np.asarray(I["p_sample"][:, sl]).reshape(2, 128, 256), (0, 2, 1))], axis=2)
        Us = np.concatenate([np.asarray(I["state_mlstm_C"][0][sl]), np.asarray(I["state_mlstm_n"][0][sl])[..., None]], axis=-1)
        x0 = lambda a: np.transpose(np.asarray(a, np.float32).reshape(16, 16, 128), (2, 1, 0))
        m = dict(common)
        m.update(xT=f(xT), pT=f(pT), convs=f(np.transpose(np.asarray(I["state_mlstm_conv"][0][sl]), (2, 0, 1))), Us=f(Us),
                 ms=f(np.asarray(I["state_mlstm_m"][0][sl]).T), rets=f(I["state_ret"][0][sl]), hgrns=f(I["state_hgrn"][0][sl]),
                 x0re=f(x0(I["state_s5_re"][0][sl])), x0im=f(x0(I["state_s5_im"][0][sl])))
        in_maps.append(m)
    return in_maps


def assemble(R):
    yp = np.zeros((8, 2048, 1024), np.float32); ys = np.zeros((128, 8, 1024), np.float32)
    convp = np.zeros((1, 8, 3, 1024), np.float32); convs = np.zeros((1, 128, 3, 1024), np.float32)
    Cp = np.zeros((1, 8, 4, 128, 128), np.float32); Cs = np.zeros((1, 128, 4, 128, 128), np.float32)
    np_ = np.zeros((1, 8, 4, 128), np.float32); ns = np.zeros((1, 128, 4, 128), np.float32)
    mp = np.zeros((1, 8, 4), np.float32); ms = np.zeros((1, 128, 4), np.float32)
    retp = np.zeros((1, 8, 4, 128, 128), np.float32); rets = np.zeros((1, 128, 4, 128, 128), np.float32)
    hgp = np.zeros((1, 8, 4, 128, 128), np.float32); hgs = np.zeros((1, 128, 4, 128, 128), np.float32)
    s5rp = np.zeros((1, 8, 32, 64), np.float32); s5ip = np.zeros((1, 8, 32, 64), np.float32)
    s5rs = np.zeros((1, 128, 32, 64), np.float32); s5is = np.zeros((1, 128, 32, 64), np.float32)
    for c in range(len(R)):
        r = R[c]
        sl = slice(16 * c, 16 * c + 16)
        yp[c] = r["yT"][:, :2048].T
        ys[sl] = r["yT"][:, 2048:].T.reshape(16, 8, 1024)
        convp[0, c] = r["convp"].T
        convs[0, sl] = np.transpose(r["convs_o"], (1, 2, 0))
        Cp[0, c] = r["Up"][:, :, :128]; np_[0, c] = r["Up"][:, :, 128]
        Cs[0, sl] = r["Us_o"][..., :128]; ns[0, sl] = r["Us_o"][..., 128]
        mp[0, c] = r["mp"][:, 0]; ms[0, sl] = r["ms_o"].T
        retp[0, c] = r["retp"]; rets[0, sl] = r["rets_o"]; hgp[0, c] = r["hgrnp"]; hgs[0, sl] = r["hgrns_o"]
        s5rp[0, c] = r["s5rep"].T.reshape(32, 64); s5ip[0, c] = r["s5imp"].T.reshape(32, 64)
        s5rs[0, sl] = np.transpose(r["s5res"], (2, 1, 0)).reshape(16, 32, 64)
        s5is[0, sl] = np.transpose(r["s5ims"], (2, 1, 0)).reshape(16, 32, 64)
    return (yp, ys, convp, convs, Cp, Cs, np_, ns, mp, ms, retp, rets, hgp, hgs, s5rp, s5rs, s5ip, s5is)
```

```python
import math, contextlib, os
import numpy as np
import concourse.bass as bass
import concourse.mybir as mybir
from concourse.bass_utils import run_bass_kernel_spmd

F32 = mybir.dt.float32
BF16 = mybir.dt.bfloat16
AF = mybir.ActivationFunctionType
ALU = mybir.AluOpType

NTM = 768
FW = 776
SBS = [(0, 768, False), (768, 768, False), (1536, 512, True)]
NTOK = 2176
EPS = 1e-6
PI = math.pi
LG = [math.log1p(-2.0 ** (-5.0 - h)) for h in range(4)]
LNK = -0.5 * math.log(128.0)


class Buf:
    __slots__ = ("w", "r")

    def __init__(self):
        self.w = None
        self.r = []


class _Rec:
    def __init__(self):
        self.call = None

    def __getattr__(self, name):
        def f(*a, **k):
            self.call = (name, a, k)
            return self
        return f


def _record(fn):
    r = _Rec()
    fn(r)
    assert r.call is not None
    return r.call


class Sched:
    ENGS = ("pe", "act", "dve", "pool", "sp")

    def __init__(self, nc):
        self.nc = nc
        self.ops = {e: [] for e in self.ENGS}
        self.cnt = {e: 0 for e in self.ENGS}
        self.seen = {e: {} for e in self.ENGS}
        self.sems = {}
        self.dma_cnt = {}

    def new_dma_sem(self):
        k = "dma%d" % len(self.dma_cnt)
        self.dma_cnt[k] = 0
        return k

    def _deps(self, eng, reads, writes, is_dma):
        waits = {}

        def add(ev, kind):
            key, val, src_eng, src_dma = ev
            if (not src_dma) and (not is_dma) and src_eng == eng and eng == "pe":
                return
            if self.seen[eng].get(key, 0) >= val:
                return
            if waits.get(key, 0) < val:
                waits[key] = val
        for b in reads:
            if b.w is not None:
                add(b.w, "raw")
        for b in writes:
            if b.w is not None:
                add(b.w, "waw")
            for r in b.r:
                add(r, "war")
        for k, v in waits.items():
            self.seen[eng][k] = v
        return list(waits.items())

    def _post(self, ev, reads, writes):
        for b in writes:
            b.w = ev
            b.r = []
        for b in reads:
            if b.w is not ev:
                b.r.append(ev)
                if len(b.r) > 24:
                    b.r = b.r[-24:] if False else b.r

    def op(self, eng, fn, reads=(), writes=()):
        waits = self._deps(eng, reads, writes, False)
        self.cnt[eng] += 1
        ev = ("e_" + eng, self.cnt[eng], eng, False)
        self.ops[eng].append((waits, _record(fn), ("e_" + eng, 1)))
        self._post(ev, reads, writes)

    def dma(self, eng, fn, sem, reads=(), writes=()):
        waits = self._deps(eng, reads, writes, True)
        prev = self.dma_cnt[sem]
        if prev > 0 and self.seen[eng].get(sem, 0) < prev:
            waits = [w_ for w_ in waits if w_[0] != sem] + [(sem, prev)]
            self.seen[eng][sem] = prev
        self.dma_cnt[sem] += 16
        ev = (sem, self.dma_cnt[sem], eng, True)
        self.ops[eng].append((waits, _record(fn), (sem, 16)))
        self._post(ev, reads, writes)

    def final_wait(self, eng, bufs):
        waits = self._deps(eng, bufs, bufs, True)
        have = dict(waits)
        for k, v in self.dma_cnt.items():
            if v > 0 and self.seen[eng].get(k, 0) < v and have.get(k, 0) < v:
                have[k] = v
        for e2 in self.ENGS:
            if e2 != eng and self.cnt[e2] > 0:
                have["e_" + e2] = self.cnt[e2]
        self.ops[eng].append((list(have.items()), None, None))

    def emit(self, stack):
        nc = self.nc
        keys = ["e_" + e for e in self.ENGS] + list(self.dma_cnt.keys())
        for k in keys:
            self.sems[k] = stack.enter_context(nc.semaphore(k))
        block = stack.enter_context(nc.Block())
        engobj = {"pe": "tensor", "act": "scalar", "dve": "vector", "pool": "gpsimd", "sp": "sync"}

        def mk(e):
            def body(engine):
                for (waits, fn, inc) in self.ops[e]:
                    for (k, v) in waits:
                        engine.wait_ge(self.sems[k], v)
                    if fn is not None:
                        name, a, k = fn
                        getattr(engine, name)(*a, **k).then_inc(self.sems[inc[0]], inc[1])
            return body
        for e in self.ENGS:
            if self.ops[e]:
                getattr(block, engobj[e])(mk(e))


PV = {}
_o = 0
for _n, _w in [("nmix0", 8), ("nmix1", 8), ("nff0", 8), ("nff1", 8), ("nple0", 8), ("nple1", 8), ("nfin", 8),
               ("cw0", 8), ("cw1", 8), ("cw2", 8), ("cw3", 8), ("cb", 8), ("gna", 4), ("gnc", 4), ("s5d", 4),
               ("bglu", 4), ("lb0", 4), ("lb1", 4), ("are", 16), ("aim", 16), ("ldt", 16), ("invf", 1),
               ("sgn", 1), ("pidx", 1), ("pidxs", 1), ("rowm", 16), ("s0", 1), ("s8", 1)]:
    PV[_n] = (_o, _w)
    _o += _w
NPV = _o


def _cols(v):
    return np.ascontiguousarray(np.asarray(v, np.float32).reshape(-1, 128).T)


def build_program():
    nc = bass.Bass("TRN2", target_bir_lowering=False)
    D = {}

    def din(name, shape):
        D[name] = nc.dram_tensor(name, list(shape), F32, kind="ExternalInput").ap()
        return D[name]

    def dout(name, shape):
        D[name] = nc.dram_tensor(name, list(shape), F32, kind="ExternalOutput").ap()
        return D[name]
    din("xT", [1024, NTOK]); din("pT", [2, 256, NTOK])
    din("convs", [1024, 16, 3]); din("Us", [16, 4, 128, 129]); din("ms", [4, 16])
    din("rets", [16, 4, 128, 128]); din("hgrns", [16, 4, 128, 128])
    din("x0re", [128, 16, 16]); din("x0im", [128, 16, 16])
    din("w_ab_h", [1024, 4096]); din("w_cd_h", [1024, 2048]); din("w_in_ab", [1024, 4104]); din("wg", [1024, 8]); din("w_out_ab", [1024, 1024])
    din("w_in_cd", [1024, 2560]); din("w_glu", [512, 512]); din("w_out_cd", [1024, 1024])
    din("w_ff1", [2, 1024, 4096]); din("w_ff2", [2, 4096, 1024])
    din("w_ple_proj", [2, 256, 1024]); din("w_ple_gate", [2, 1024, 1024])
    din("pvec", [128, NPV]); din("bg", [4, 2])
    din("BT", [2, 16, 128, 128]); din("CT", [2, 16, 128, 128])
    din("ident", [128, 128]); din("maskc", [128, 128]); din("maskb", [128, 128])
    din("blk3", [128, 16, 128]); din("segm", [3, 128, NTM]); din("negm", [4, 128]); din("sel", [4, 4, 128])
    din("posrow", [3, 128, NTM]); din("rowp", [3, 128, 2048]); din("jrow", [128, 4, 128])
    dout("yT", [1024, NTOK]); dout("convp", [1024, 3]); dout("convs_o", [1024, 16, 3])
    dout("Up", [4, 128, 129]); dout("Us_o", [16, 4, 128, 129]); dout("mp", [4, 1]); dout("ms_o", [4, 16])
    dout("retp", [4, 128, 128]); dout("rets_o", [16, 4, 128, 128])
    dout("hgrnp", [4, 128, 128]); dout("hgrns_o", [16, 4, 128, 128])
    dout("s5rep", [128, 16]); dout("s5imp", [128, 16]); dout("s5res", [128, 16, 16]); dout("s5ims", [128, 16, 16])

    st = contextlib.ExitStack()
    with st:
        S = Sched(nc)
        cnt = [0]

        def sb(shape, dt=F32):
            cnt[0] += 1
            return st.enter_context(nc.sbuf_tensor("t%d" % cnt[0], list(shape), dt))

        def psum(shape, dt=F32):
            cnt[0] += 1
            return st.enter_context(nc.psum_tensor("p%d" % cnt[0], list(shape), dt))
        V = lambda fn, r=(), w=(): S.op("dve", fn, r, w)
        A = lambda fn, r=(), w=(): S.op("act", fn, r, w)
        G = lambda fn, r=(), w=(): S.op("pool", fn, r, w)
        P = lambda fn, r=(), w=(): S.op("pe", fn, r, w)
        msems = {"sp": [S.new_dma_sem() for _ in range(24)], "pool": [S.new_dma_sem() for _ in range(8)]}
        mi = {"sp": 0, "pool": 0}

        def LD(out, in_, w, r=(), eng="sp"):
            k = msems[eng][mi[eng] % len(msems[eng])]
            mi[eng] += 1
            S.dma(eng, lambda e: e.dma_start(out=out, in_=in_), k, r, w)
        OUTS = []

        def STO(out, in_, r):
            b_ = Buf()
            OUTS.append(b_)
            LD(out, in_, [b_], r)

        h = sb([128, 8, NTM]); hn = sb([128, 8, NTM], BF16); mix = sb([128, 8, NTM], BF16)
        Bh = [[Buf() for _ in range(2)] for _ in range(8)]
        Bhn = [Buf() for _ in range(2)]
        Bmix = [[Buf() for _ in range(2)] for _ in range(8)]
        NW = 2
        wr = [sb([128, 8, 512], BF16) for _ in range(NW)]
        Bwrp = [[Buf() for _ in range(4)] for _ in range(NW)]
        wsem = [[S.new_dma_sem() for _ in range(4)] for _ in range(NW)]
        wi = [0]

        def wload(parts, nk):
            i = wi[0] % NW
            wi[0] += 1
            for pi_, (ap, co) in enumerate(parts):
                ncol = ap.shape[1]
                S.dma("pool", lambda e, ap=ap, co=co, ncol=ncol, i=i: e.dma_start(
                    out=wr[i][:, 0:nk, co:co + ncol], in_=ap.rearrange("(k p) n -> p k n", p=128)),
                    wsem[i][pi_], (), (Bwrp[i] if pi_ == 0 else [Bwrp[i][pi_]]))
            return wr[i], Bwrp[i]
        Fs = [sb([128, FW]) for _ in range(9)]
        BF = [Buf() for _ in range(9)]
        Hs = [sb([128, NTM], BF16) for _ in range(5)]
        BH = [Buf() for _ in range(5)]
        vtm = sb([128, 6, 129], BF16); Bv = Buf()
        NT5 = [sb([128, 512]) for _ in range(5)]
        BN = [Buf() for _ in range(5)]
        sqb = [sb([128, 512], BF16) for _ in range(2)]
        Bsq = [Buf() for _ in range(2)]
        pb = [psum([128, 512]) for _ in range(7)]
        Bp = [Buf() for _ in range(7)]
        ptb = psum([128, 1024], BF16); Bpt = Buf()
        pbi = [0]

        pinned = set()

        S5MODE = [False]

        def PS():
            while True:
                i = pbi[0] % (3 if S5MODE[0] else 4)
                pbi[0] += 1
                if i not in pinned:
                    return pb[i], Bp[i]
        psi = [0]

        def PSS():
            if S5MODE[0]:
                i = (4, 5, 6, 3)[psi[0] % 4]
            else:
                i = 4 + psi[0] % 3
            psi[0] += 1
            return pb[i], Bp[i]
        ident = sb([128, 128]); identb = sb([128, 128], BF16); maskc = sb([128, 128], BF16); maskb = sb([128, 128], BF16)
        onesb = sb([128, 128], BF16); blk3 = sb([128, 16, 128], BF16); segm = sb([128, NTM]); negm = sb([4, 128])
        sel = sb([4, 4, 128]); pvec = sb([128, NPV]); bg = sb([4, 2]); nbg = sb([4, 1]); jrow = sb([128, 4, 128])
        Bc = Buf()
        LD(ident[:], D["ident"], [Bc]); LD(identb[:], D["ident"], [Bc], eng="pool")
        LD(maskc[:], D["maskc"], [Bc], eng="pool"); LD(maskb[:], D["maskb"], [Bc], eng="pool")
        LD(blk3[:], D["blk3"], [Bc], eng="pool"); LD(negm[:], D["negm"], [Bc]); LD(sel[:], D["sel"], [Bc])
        LD(pvec[:], D["pvec"], [Bc]); LD(bg[:], D["bg"], [Bc]); LD(jrow[:], D["jrow"], [Bc])
        V(lambda e: e.memset(onesb[:], 1.0), (), [Bc])
        V(lambda e: e.tensor_scalar(nbg[:], bg[:, 1:2], -1.0, None, ALU.mult), [Bc], [Bc])

        cb_ = sb([128, 8])
        CBV = [EPS, LNK, 1.0, 0.0, 0.5 * PI, 0.0, 0.0, 0.0]
        for _i, _v in enumerate(CBV):
            V(lambda e, _i=_i, _v=_v: e.memset(cb_[:, _i:_i + 1], _v), (), [Bc])
        CEPS, CLNK, CONE, CZERO, CHPI = [cb_[:, i:i + 1] for i in range(5)]
        RC = 12582912.0
        I2P = 1.0 / (2 * PI)

        def sin_of(dst, src, shift, tmp, rd, wr_, btmp, npart=128):
            V(lambda e: e.tensor_scalar(tmp, src, shift, I2P, ALU.add, ALU.mult), rd, [btmp])
            V(lambda e: e.tensor_scalar(tmp, tmp, RC, None, ALU.add), [btmp], [btmp])
            V(lambda e: e.tensor_scalar(tmp, tmp, -RC, None, ALU.add), [btmp], [btmp])
            V(lambda e: e.scalar_tensor_tensor(tmp, tmp, -2 * PI, src, ALU.mult, ALU.add), [btmp] + list(rd), [btmp])
            V(lambda e: e.tensor_scalar(tmp, tmp, -PI - shift + 4e-6, PI - shift - 4e-6, ALU.max, ALU.min), [btmp], [btmp])
            A(lambda e: e.activation(dst, tmp, AF.Sin, bias=(CHPI[0:npart] if shift != 0.0 else CZERO[0:npart])), [btmp, Bc], wr_)

        def pv(name, j=0, n=1):
            o, w = PV[name]
            return pvec[:, o + j:o + j + n]
        Gq = sb([128, 2, 4, 128], BF16); gk = sb([128, 2, 4])
        for v2 in range(2):
            for hh in range(4):
                A(lambda e, v2=v2, hh=hh: e.activation(Gq[:, v2, hh, :], jrow[:, v2, :], AF.Exp, scale=LG[hh]), [Bc], [Bc])
                A(lambda e, v2=v2, hh=hh: e.activation(gk[:, v2, hh:hh + 1], pv("pidxs" if v2 else "pidx"), AF.Exp,
                                                       scale=-LG[hh], bias=CLNK), [Bc], [Bc])
        lb = sb([128, 4]); oml = sb([128, 4])
        V(lambda e: e.tensor_tensor(lb[:], pv("lb1", 0, 4), pv("lb0", 0, 4), ALU.subtract), [Bc], [Bc])
        A(lambda e: e.activation(lb[:], lb[:], AF.Sigmoid), [Bc], [Bc])
        V(lambda e: e.tensor_scalar(oml[:], lb[:], -1.0, 1.0, ALU.mult, ALU.add), [Bc], [Bc])
        s5p = sb([128, 16, 16])
        th, rr, zr, zi, rho = s5p[:, 0, :], s5p[:, 1, :], s5p[:, 2, :], s5p[:, 3, :], s5p[:, 7, :]
        t4, t5, t6 = s5p[:, 4, :], s5p[:, 5, :], s5p[:, 6, :]
        ar_, ai_, a128r, a128i, izr, izi, t7 = (s5p[:, 8, :], s5p[:, 9, :], s5p[:, 10, :], s5p[:, 11, :], s5p[:, 12, :],
                                               s5p[:, 13, :], s5p[:, 14, :])
        are, aim = pv("are", 0, 16), pv("aim", 0, 16)
        A(lambda e: e.activation(t4, pv("ldt", 0, 16), AF.Exp), [Bc], [Bc])
        V(lambda e: e.tensor_tensor(th, t4, aim, ALU.mult), [Bc], [Bc])
        V(lambda e: e.tensor_tensor(rho, t4, are, ALU.mult), [Bc], [Bc])
        A(lambda e: e.activation(rr, rho, AF.Exp), [Bc], [Bc])
        sin_of(t4, th, 0.5 * PI, t6, [Bc], [Bc], Bc)
        sin_of(t5, th, 0.0, t6, [Bc], [Bc], Bc)
        V(lambda e: e.tensor_tensor(ar_, t4, rr, ALU.mult), [Bc], [Bc])
        V(lambda e: e.tensor_tensor(ai_, t5, rr, ALU.mult), [Bc], [Bc])
        V(lambda e: e.tensor_scalar(t4, ar_, -1.0, None, ALU.add), [Bc], [Bc])
        V(lambda e: e.tensor_copy(t5, ai_), [Bc], [Bc])
        V(lambda e: e.tensor_tensor(t6, are, are, ALU.mult), [Bc], [Bc])
        V(lambda e: e.tensor_tensor(zr, aim, aim, ALU.mult), [Bc], [Bc])
        V(lambda e: e.tensor_tensor(t6, t6, zr, ALU.add), [Bc], [Bc])
        V(lambda e: e.reciprocal(t6, t6), [Bc], [Bc])
        V(lambda e: e.tensor_tensor(zr, t4, are, ALU.mult), [Bc], [Bc])
        V(lambda e: e.tensor_tensor(zi, t5, aim, ALU.mult), [Bc], [Bc])
        V(lambda e: e.tensor_tensor(zr, zr, zi, ALU.add), [Bc], [Bc])
        V(lambda e: e.tensor_tensor(zi, t5, are, ALU.mult), [Bc], [Bc])
        V(lambda e: e.tensor_tensor(t7, t4, aim, ALU.mult), [Bc], [Bc])
        V(lambda e: e.tensor_tensor(zi, zi, t7, ALU.subtract), [Bc], [Bc])
        V(lambda e: e.tensor_tensor(zr, zr, t6, ALU.mult), [Bc], [Bc])
        V(lambda e: e.tensor_tensor(zi, zi, t6, ALU.mult), [Bc], [Bc])
        V(lambda e: e.tensor_tensor(t4, zr, zr, ALU.mult), [Bc], [Bc])
        V(lambda e: e.tensor_tensor(t5, zi, zi, ALU.mult), [Bc], [Bc])
        V(lambda e: e.tensor_tensor(t4, t4, t5, ALU.add), [Bc], [Bc])
        V(lambda e: e.reciprocal(t4, t4), [Bc], [Bc])
        V(lambda e: e.tensor_tensor(izr, zr, t4, ALU.mult), [Bc], [Bc])
        V(lambda e: e.scalar_tensor_tensor(izi, zi, -1.0, t4, ALU.mult, ALU.mult), [Bc], [Bc])
        V(lambda e: e.tensor_scalar(t7, th, 128.0, None, ALU.mult), [Bc], [Bc])
        sin_of(t4, t7, 0.5 * PI, t6, [Bc], [Bc], Bc)
        sin_of(t5, t7, 0.0, t6, [Bc], [Bc], Bc)
        A(lambda e: e.activation(t6, rho, AF.Exp, scale=128.0), [Bc], [Bc])
        V(lambda e: e.tensor_tensor(a128r, t4, t6, ALU.mult), [Bc], [Bc])
        V(lambda e: e.tensor_tensor(a128i, t5, t6, ALU.mult), [Bc], [Bc])
        tabE = nc.dram_tensor("tabE", [128, 2, 2048], BF16).ap(); tabZ = nc.dram_tensor("tabZ", [128, 2, 16, 128], BF16).ap()
        tbe = sb([128, 2, 512], BF16); tbz = sb([128, 2, 4, 128], BF16); Btab = Buf(); Bscr = Buf()

        def build_tables(kc, scol, jr, outE, outZ, wE, wZ):
            f0, f1, f2, f3, f4, f5 = [Fs[k][:, 0:512] for k in range(6)]
            b0_, b1_, b2_, b3_, b4_, b5_ = BF[0:6]
            for k in range(3):
                LD(Fs[k][:, 0:512], D["rowp"][k][:, kc * 512:(kc + 1) * 512], [BF[k]])
            A(lambda e: e.activation(f2, f2, AF.Exp), [b2_], [b2_])
            V(lambda e: e.tensor_tensor(f1, f1, f2, ALU.mult), [b1_, b2_], [b1_])
            V(lambda e: e.tensor_tensor(f0, f0, f2, ALU.mult), [b0_, b2_], [b0_])
            V(lambda e: e.tensor_scalar(f1, f1, scol, None, ALU.mult), [b1_, Bc], [b1_])
            A(lambda e: e.activation(f0, f0, AF.Exp, scale=scol), [b0_, Bc], [b0_])
            V(lambda e: e.reciprocal(f0, f0), [b0_], [b0_])
            sin_of(f3, f1, 0.5 * PI, f2, [b1_], [b3_], b2_)
            sin_of(f4, f1, 0.0, f2, [b1_], [b4_], b2_)
            V(lambda e: e.tensor_tensor(outE(0), f3, f0, ALU.mult), [b3_, b0_], wE)
            V(lambda e: e.scalar_tensor_tensor(outE(1), f4, -1.0, f0, ALU.mult, ALU.mult), [b4_, b0_], wE)
            g0, g1, g2, g3, g4 = [Fs[k][:, 0:512].rearrange("p (a b) -> p a b", a=4) for k in range(5)]
            i4 = slice(4 * kc, 4 * kc + 4)
            jb = jr.unsqueeze(1).broadcast_to([128, 4, 128])
            bc4 = lambda v: v[:, i4].unsqueeze(2).broadcast_to([128, 4, 128])
            V(lambda e: e.tensor_tensor(g1, jb, bc4(th), ALU.mult), [Bc], [b1_])
            V(lambda e: e.tensor_tensor(g0, jb, bc4(rho), ALU.mult), [Bc], [b0_])
            A(lambda e: e.activation(Fs[0][:, 0:512], Fs[0][:, 0:512], AF.Exp), [b0_], [b0_])
            sin_of(f3, f1, 0.5 * PI, f2, [b1_], [b3_], b2_)
            sin_of(f4, f1, 0.0, f2, [b1_], [b4_], b2_)
            V(lambda e: e.tensor_tensor(f3, f3, f0, ALU.mult), [b3_, b0_], [b3_])
            V(lambda e: e.tensor_tensor(f4, f4, f0, ALU.mult), [b4_, b0_], [b4_])
            V(lambda e: e.tensor_tensor(g0, g3, bc4(zr), ALU.mult), [b3_, Bc], [b0_])
            V(lambda e: e.tensor_tensor(g1, g4, bc4(zi), ALU.mult), [b4_, Bc], [b1_])
            V(lambda e: e.tensor_tensor(outZ(0), g0, g1, ALU.subtract), [b0_, b1_], wZ)
            V(lambda e: e.tensor_tensor(g0, g3, bc4(zi), ALU.mult), [b3_, Bc], [b0_])
            V(lambda e: e.tensor_tensor(g1, g4, bc4(zr), ALU.mult), [b4_, Bc], [b1_])
            V(lambda e: e.tensor_tensor(outZ(1), g0, g1, ALU.add), [b0_, b1_], wZ)
        Up = sb([128, 12, 129]); Upb = sb([128, 12, 129], BF16); nbc = sb([128, 4, 128], BF16)
        BU = [Buf() for _ in range(12)]
        V(lambda e: e.memset(Up[:], 0.0), (), BU); V(lambda e: e.memset(Upb[:], 0.0), (), BU)
        V(lambda e: e.memset(nbc[:], 0.0), (), BU)
        tails = sb([128, 8, 3]); Btl = Buf()
        V(lambda e: e.memset(tails[:], 0.0), (), [Btl])
        carr = sb([4, 2]); Bcar = Buf()
        V(lambda e: e.memset(carr[:], 0.0), (), [Bcar])
        s5c = sb([128, 2, 16]); Bs5c = Buf()
        s5d = sb([128, 2, 2, 16]); Bs5d = [Buf(), Buf()]; S5PAR = [0, 0, 0, 0]; Bcr = Buf()
        V(lambda e: e.memset(s5c[:], 0.0), (), [Bs5c])
        V(lambda e: e.memset(s5d[:], 0.0), (), Bs5d)
        Usf = sb([128, 16, 129]); Usb = sb([128, 16, 129], BF16); BUs = Buf()
        qz = sb([128, 16, 128], BF16); kz = sb([128, 16, 128], BF16); nbs = kz
        Bqz = Buf(); Bkz = Buf(); Bnbs = Bkz
        stb = [sb([128, 128], BF16) for _ in range(2)]; Bst = [Buf() for _ in range(2)]
        khb = [sb([128, 128], BF16) for _ in range(2)]; Bkh = [Buf() for _ in range(2)]
        ektm = sb([128, 6, 4]); Bek = Buf()
        decbc = sb([128, 4, 24]); Bdec = Buf()
        decrow = sb([4, 24]); mxe = sb([4, 8]); ms0 = sb([4, 16]); msout = sb([4, 17]); Bsm = Buf()
        x0s = Fs[5][:, 0:512].rearrange("p (a b c) -> p a b c", a=2, b=16); Bx0 = BF[5]
        s5so = Usf[:].rearrange("p a b -> p (a b)")[:, 0:512].rearrange("p (a b c) -> p a b c", a=2, b=16); s5po = sb([128, 2, 16]); Bs5o = Buf()
        Bes = Buf()
        cbs = sb([128, 2, 4, 128], BF16); Bcbs = Buf()
        pt4all = sb([128, 2048], BF16)
        pt4 = [pt4all[:, q_ * 512:(q_ + 1) * 512] for q_ in range(4)]
        gT2 = pt4all[:, 0:NTM]; vtm2 = pt4all[:, NTM:NTM + 774].rearrange("p (a b) -> p a b", a=6); Bg2 = Buf(); Bv2 = Buf()
        hdec = sb([128, 2, 24]); Bhd = Buf()
        Bprb = [Buf() for _ in range(2)]; Bt4 = [Buf() for _ in range(4)]; Bcsb = [Buf() for _ in range(2)]; Bp4 = [Buf() for _ in range(4)]
        S5FINE = Bprb + Bt4 + Bcsb + Bp4
        qzf = qz[:].rearrange("p a b -> p (a b)"); kzf = kz[:].rearrange("p a b -> p (a b)"); usbf = Usb[:].rearrange("p a b -> p (a b)")
        wtb = [[sb([128, 512], BF16) for _ in range(2)] for _ in range(2)]; Bwt = [Buf() for _ in range(2)]
        xtb = [[sb([128, 512], BF16) for _ in range(2)] for _ in range(2)]; Bxt = [Buf() for _ in range(2)]
        cch = sb([128, 6, 4]); Bcch = Buf()
        bct = sb([128, 2, 2, 4, 128], BF16); Bbct = [Buf() for _ in range(4)]; bcsem = [S.new_dma_sem() for _ in range(4)]
        pTb = sb([128, 2, NTM], BF16); BpT = Buf(); pTsem = S.new_dma_sem()

        def blocks(ntot):
            out = []
            o = 0
            while o < ntot:
                n = min(512, ntot - o)
                out.append((o, n)); o += n
            return out

        def rmsnorm(gname, NT, final_out=None):
            for bi, (o, n) in enumerate(blocks(NT)):
                ps, bp = PS()
                for c in range(8):
                    q = sqb[c % 2]; bq = Bsq[c % 2]
                    A(lambda e, c=c, q=q: e.activation(q[:, 0:n], h[:, c, o:o + n], AF.Square), [Bh[c][bi]], [bq])
                    P(lambda e, c=c, q=q, ps=ps: e.matmul(ps[:, 0:n], onesb[:], q[:, 0:n], start=(c == 0), stop=(c == 7)),
                      [bq, Bc], [bp])
                rs = NT5[4]
                A(lambda e, ps=ps: e.activation(rs[:, 0:n], ps[:, 0:n], AF.Ln, scale=1.0 / 1024, bias=CEPS), [bp], [BN[4]])
                A(lambda e: e.activation(rs[:, 0:n], rs[:, 0:n], AF.Exp, scale=-0.5), [BN[4]], [BN[4]])
                for c in range(8):
                    if final_out is None:
                        V(lambda e, c=c: e.scalar_tensor_tensor(hn[:, c, o:o + n], h[:, c, o:o + n], pv(gname, c), rs[:, 0:n],
                                                                ALU.mult, ALU.mult), [Bh[c][bi], BN[4], Bc], [Bhn[bi]])
                    else:
                        t = NT5[c % 2]
                        V(lambda e, c=c, t=t: e.scalar_tensor_tensor(t[:, 0:n], h[:, c, o:o + n], pv(gname, c), rs[:, 0:n],
                                                                     ALU.mult, ALU.mult), [Bh[c][bi], BN[4], Bc], [BN[c % 2]])
                        STO(final_out[c * 128:(c + 1) * 128, o:o + n], t[:, 0:n], [BN[c % 2]])

        def proj_fm(slot, bs, col, NT, evac, rhs=None, nk=8, rbufs=None):
            for bi, (o, n) in enumerate(blocks(NT)):
                ps, bp = PS()
                for k in range(nk):
                    src = hn if rhs is None else rhs
                    P(lambda e, k=k, ps=ps, src=src: e.matmul(ps[:, 0:n], slot[:, k, col:col + 128], src[:, k, o:o + n],
                                                             start=(k == 0), stop=(k == nk - 1)),
                      bs + ([Bhn[bi]] if rbufs is None else rbufs(bi)), [bp])
                evac(ps, bp, bi, o, n)

        def resid_proj(w_ap, NT, src, srcb):
            for u in range(2):
                slot, bs = wload([(w_ap[:, u * 512:(u + 1) * 512], 0)], 8)
                for oc in range(4):
                    c = u * 4 + oc

                    def ev(ps, bp, bi, o, n, c=c):
                        V(lambda e: e.tensor_tensor(h[:, c, o:o + n], h[:, c, o:o + n], ps[:, 0:n], ALU.add),
                          [bp, Bh[c][bi]], [Bh[c][bi]])
                    proj_fm(slot, bs, oc * 128, NT, ev, rhs=src, rbufs=lambda bi: [srcb[k][bi] for k in range(8)])

        def att_pre(qT, kT, bq, bk, col, ek, sample):
            ps, bp = PSS()
            P(lambda e: e.matmul(ps[:, 0:128], kT[:, col:col + 128], qT[:, col:col + 128], start=True, stop=True),
              [bq, bk], [bp])
            i2 = att_tile.k % 2
            att_tile.k += 1
            sT = stb[i2]; bsT = Bst[i2]
            msk = maskb if sample else maskc
            if ek is not None:
                V(lambda e: e.scalar_tensor_tensor(sT[:], ps[:, 0:128], ek, msk[:], ALU.mult, ALU.mult), [bp, Bek, Bc], [bsT])
            else:
                V(lambda e: e.tensor_tensor(sT[:], ps[:, 0:128], msk[:], ALU.mult), [bp, Bc], [bsT])
            P(lambda e: e.transpose(ptb[:, 0:128], kT[:, col:col + 128], identb[:]), [bk, Bc], [Bpt])
            kh = khb[i2]; bkh = Bkh[i2]
            if ek is not None:
                A(lambda e: e.activation(kh[:], ptb[:, 0:128], AF.Copy, scale=ek), [Bpt, Bek], [bkh])
            else:
                A(lambda e: e.copy(kh[:], ptb[:, 0:128]), [Bpt], [bkh])
            return (sT, bsT, kh, bkh)

        def att_tile(qT, kT, bq, bk, col, vt, E, si, ek, dec, PT, bPT, pcol, sample, den=None, mlstm_h=None, usbuf=None, bv=None, ctx=None):
            Bv = bv
            if ctx is None:
                ctx = att_pre(qT, kT, bq, bk, col, ek, sample)
            sT, bsT, kh, bkh = ctx
            P(lambda e: e.matmul(PT[:, pcol:pcol + 128], vt[:, 0:128], sT[:], start=True, stop=False), [Bv, bsT], [bPT])
            if not sample:
                P(lambda e: e.matmul(PT[:, pcol:pcol + 128], Upb[:, si, 0:128], qT[:, col:col + 128], start=False, stop=True),
                  [BU[si], bq], [bPT])
            else:
                for j in range(16):
                    P(lambda e, j=j: e.matmul(PT[:, pcol:pcol + 128], Usb[:, j, 0:128], qz[:, j, :], start=False, stop=(j == 15)),
                      [BUs, Bqz], [bPT])
            if den is not None:
                dps, bd = den
                P(lambda e: e.matmul(dps[:, pcol:pcol + 128], onesb[:], sT[:], start=True, stop=False), [Bc, bsT], [bd])
                if not sample:
                    P(lambda e: e.matmul(dps[:, pcol:pcol + 128], nbc[:, mlstm_h, :], qT[:, col:col + 128], start=False, stop=True),
                      [BU[si], bq], [bd])
                else:
                    for j in range(16):
                        P(lambda e, j=j: e.matmul(dps[:, pcol:pcol + 128], nbs[:, j, :], qz[:, j, :], start=False, stop=(j == 15)),
                          [Bnbs, Bqz], [bd])
            if not sample:
                ps2, bp2 = PSS()
                P(lambda e: e.matmul(ps2[:, 0:E], ident[:], Up[:, si, 0:E], start=True, stop=False), [Bc, BU[si]], [bp2])
                P(lambda e: e.matmul(ps2[:, 0:E], kh[:], vt[:, 0:E], start=False, stop=True), [bkh, Bv], [bp2])
                A(lambda e: e.activation(Up[:, si, 0:E], ps2[:, 0:E], AF.Copy, scale=dec), [bp2, Bdec, Bhd], [BU[si]])
                V(lambda e: e.tensor_copy(Upb[:, si, 0:E], Up[:, si, 0:E]), [BU[si]], [BU[si]])
                if mlstm_h is not None:
                    V(lambda e: e.tensor_copy(nbc[:, mlstm_h, :], Up[:, si, 128:129].broadcast_to([128, 128])), [BU[si]], [BU[si]])
            else:
                V(lambda e: e.tensor_tensor(kz[:], kh[:].unsqueeze(1).broadcast_to([128, 16, 128]),
                                            pv("rowm", 0, 16).unsqueeze(2).broadcast_to([128, 16, 128]), ALU.mult),
                  [bkh, Bc], [Bkz])
                for j in range(16):
                    ps2, bp2 = PSS()
                    P(lambda e, j=j, ps2=ps2: e.matmul(ps2[:, 0:E], ident[:], Usf[:, j, 0:E], start=True, stop=False), [Bc, BUs], [bp2])
                    P(lambda e, j=j, ps2=ps2: e.matmul(ps2[:, 0:E], kz[:, j, :], vt[:, 0:E], start=False, stop=True), [Bkz, Bv], [bp2])
                    A(lambda e, j=j, ps2=ps2: e.activation(usbuf[:, j, 0:E], ps2[:, 0:E], AF.Copy, scale=dec(j)),
                      [bp2, Bdec, Bhd], [BUs])
        att_tile.k = 0
        CTX = {}
        FFK = [0]
        PEND = [None]

        def load_sample_state(src, hh, E):
            LD(Usf[:, :, 0:E], src[:, hh, :, :].rearrange("j d e -> d j e"), [BUs])
            V(lambda e: e.tensor_copy(Usb[:, :, 0:E], Usf[:, :, 0:E]), [BUs], [BUs])

        def make_qz(qT, bq, col):
            V(lambda e: e.tensor_tensor(qz[:], qT[:, col:col + 128].unsqueeze(1).broadcast_to([128, 16, 128]), blk3[:], ALU.mult),
              [bq, Bc], [Bqz])

        def vproj(slot, bs, col, NT, E, vt_, bv_):
            nt = NT // 128
            for c in range(nt):
                ps, bp = PS()
                for k in range(8):
                    P(lambda e, k=k, ps=ps: e.matmul(ps[:, 0:128], hn[:, k, c * 128:(c + 1) * 128], slot[:, k, col:col + 128],
                                                     start=(k == 0), stop=(k == 7)), bs + [Bhn[(c * 128) // 512]], [bp])
                A(lambda e, ps=ps: e.copy(vt_[:, c, 0:128], ps[:, 0:128]), [bp], [bv_])

        def rstd_from(sq_src_fn, n, srcb):
            q = sqb[0]
            sq_src_fn(q)
            ps, bp = PSS()
            P(lambda e: e.matmul(ps[:, 0:n], onesb[:], q[:, 0:n], start=True, stop=True), [Bsq[0], Bc], [bp])
            rs = NT5[3]
            A(lambda e: e.activation(rs[:, 0:n], ps[:, 0:n], AF.Ln, scale=1.0 / 128, bias=CEPS), [bp], [BN[3]])
            A(lambda e: e.activation(rs[:, 0:n], rs[:, 0:n], AF.Exp, scale=-0.5), [BN[3]], [BN[3]])
            return rs

        def build_tab_kc(kc_):
            build_tables(kc_, pv("s0"), jrow[:, 2, :], lambda part: tbe[:, part, :], lambda part: tbz[:, part, :, :], [Btab], [Btab])
            LD(tabE[:, :, kc_ * 512:(kc_ + 1) * 512], tbe[:], [Bscr], r=[Btab])
            LD(tabZ[:, :, 4 * kc_:4 * kc_ + 4, :], tbz[:], [Bscr], r=[Btab])
        STEP = [None]

        def step():
            g = STEP[0]
            if g is not None:
                try:
                    next(g)
                except StopIteration:
                    STEP[0] = None
        CUT = int(os.environ.get("KCUT", "0"))

        class _Stop(Exception):
            pass

        def ck(k):
            if CUT == k:
                raise _Stop()
        try:
            for sbi, (tok0, NP, has_s) in enumerate(SBS):
                NT = NP + (128 if has_s else 0)
                ntp = NP // 128
                blks = blocks(NT)
                last = (sbi == len(SBS) - 1)
                LD(segm[:, 0:NTM], D["segm"][sbi], [Bc], r=[Bc])
                for c in range(8):
                    for bi, (o, n) in enumerate(blks):
                        LD(h[:, c, o:o + n], D["xT"][c * 128:(c + 1) * 128, tok0 + o:tok0 + o + n], [Bh[c][bi]])
                for layer in range(2):
                    rmsnorm("nmix%d" % layer, NT)
                    ck(1)
                    if layer == 0:
                        wgs, bwg = wload([(D["wg"], 0)], 8)
                        A1, A2, A3, A4 = Fs[2], Fs[3], Fs[5], Fs[4]
                        b1, b2, b3, b4 = BF[2], BF[3], BF[5], BF[4]
                        for bi, (o, n) in enumerate(blks):
                            ps, bp = PS()
                            for k in range(8):
                                P(lambda e, k=k, ps=ps: e.matmul(ps[0:4, 0:n], wgs[:, k, 0:4], hn[:, k, o:o + n], start=(k == 0), stop=(k == 7)),
                                  bwg + [Bhn[bi]], [bp])
                            A(lambda e, ps=ps: e.activation(A1[0:4, o:o + n], ps[0:4, 0:n], AF.Identity, bias=bg[:, 0:1]), [bp, Bc], [b1])
                            ps, bp = PS()
                            for k in range(8):
                                P(lambda e, k=k, ps=ps: e.matmul(ps[0:4, 0:n], wgs[:, k, 4:8], hn[:, k, o:o + n], start=(k == 0), stop=(k == 7)),
                                  bwg + [Bhn[bi]], [bp])
                            A(lambda e, ps=ps: e.activation(A2[0:4, o:o + n], ps[0:4, 0:n], AF.Exp, scale=-1.0, bias=nbg[:, 0:1]), [bp, Bc], [b2])
                        A(lambda e: e.activation(A2[0:4, 0:NT], A2[0:4, 0:NT], AF.Ln, bias=CONE[0:4]), [b2], [b2])
                        V(lambda e: e.memset(A4[0:4, 0:NT], 1.0), (), [b4])
                        V(lambda e: e.tensor_tensor_scan(A3[0:4, 0:NP], A4[0:4, 0:NP], A2[0:4, 0:NP], carr[:, 0:1], ALU.mult, ALU.add),
                          [b2, b4, Bcar], [b3])
                        if has_s:
                            V(lambda e: e.tensor_tensor_scan(A3[0:4, NP:NT], segm[0:4, NP:NT], A2[0:4, NP:NT], 0.0, ALU.mult, ALU.add),
                              [b2, Bc], [b3])
                        V(lambda e: e.tensor_tensor(A1[0:4, 0:NT], A1[0:4, 0:NT], A3[0:4, 0:NT], ALU.add), [b1, b3], [b1])
                        V(lambda e: e.memset(A4[0:4, 0:NT], 0.0), (), [b4])
                        V(lambda e: e.tensor_tensor_scan(A2[0:4, 0:NP], A4[0:4, 0:NP], A1[0:4, 0:NP], carr[:, 1:2], ALU.add, ALU.max),
                          [b1, b4, Bcar], [b2])
                        V(lambda e: e.tensor_copy(mxe[:, 0:1], carr[:, 1:2]), [Bcar], [Bsm])
                        V(lambda e: e.tensor_copy(mxe[:, 1:1 + ntp], A2[0:4, 0:NP].rearrange("p (c t) -> p c t", t=128)[:, :, 127]), [b2], [Bsm])
                        if has_s:
                            LD(ms0[:], D["ms"], [Bsm])
                            V(lambda e: e.tensor_copy(A4[0:4, NP:NT], A1[0:4, NP:NT]), [b1], [b4])
                            g3 = A4[0:4, NP:NT].rearrange("p (j l) -> p j l", l=8)
                            V(lambda e: e.tensor_tensor(g3[:, :, 0], g3[:, :, 0], ms0[:], ALU.max), [b4, Bsm], [b4])
                            V(lambda e: e.tensor_tensor_scan(A2[0:4, NP:NT], negm[:], A4[0:4, NP:NT], 0.0, ALU.add, ALU.max), [b4, Bc], [b2])
                        V(lambda e: e.tensor_copy(A4[0:4, 0:NP].rearrange("p (c t) -> p c t", t=128),
                                                  mxe[:, 0:ntp].unsqueeze(2).broadcast_to([4, ntp, 128])), [Bsm], [b4])
                        if has_s:
                            V(lambda e: e.tensor_copy(A4[0:4, NP:NT].rearrange("p (j l) -> p j l", l=8),
                                                      ms0[:].unsqueeze(2).broadcast_to([4, 16, 8])), [Bsm], [b4])
                        V(lambda e: e.tensor_tensor(decrow[:, 0:ntp], mxe[:, 0:ntp], mxe[:, 1:1 + ntp], ALU.subtract), [Bsm], [Bsm])
                        if has_s:
                            V(lambda e: e.tensor_tensor(decrow[:, 8:24], ms0[:], A2[0:4, NP:NT].rearrange("p (j l) -> p j l", l=8)[:, :, 7],
                                                        ALU.subtract), [Bsm, b2], [Bsm])
                        else:
                            V(lambda e: e.memset(decrow[:, 8:24], 0.0), (), [Bsm])
                        if ntp < 8:
                            V(lambda e: e.memset(decrow[:, ntp:8], 0.0), (), [Bsm])
                        A(lambda e: e.activation(decrow[:], decrow[:], AF.Exp), [Bsm], [Bsm])
                        ps, bp = PSS()
                        for hh in range(4):
                            P(lambda e, hh=hh, ps=ps: e.matmul(ps[:, hh * 24:(hh + 1) * 24], sel[:, hh, :], decrow[:], start=True, stop=True),
                              [Bc, Bsm], [bp])
                        V(lambda e, ps=ps: e.tensor_copy(decbc[:].rearrange("p a b -> p (a b)"), ps[:, 0:96]), [bp], [Bdec])
                        if last:
                            V(lambda e: e.tensor_tensor(msout[:, 16:17], A2[0:4, NP - 1:NP], A3[0:4, NP - 1:NP], ALU.subtract), [b2, b3], [Bsm])
                            V(lambda e: e.tensor_tensor(msout[:, 0:16], A2[0:4, NP:NT].rearrange("p (j l) -> p j l", l=8)[:, :, 7],
                                                        A3[0:4, NP:NT].rearrange("p (j l) -> p j l", l=8)[:, :, 7], ALU.subtract), [b2, b3], [Bsm])
                            STO(D["mp"], msout[:, 16:17], [Bsm]); STO(D["ms_o"], msout[:, 0:16], [Bsm])
                        V(lambda e: e.tensor_copy(carr[:, 0:1], A3[0:4, NP - 1:NP]), [b3], [Bcar])
                        V(lambda e: e.tensor_copy(carr[:, 1:2], A2[0:4, NP - 1:NP]), [b2], [Bcar])
                        V(lambda e: e.tensor_tensor(A3[0:4, 0:NT], A4[0:4, 0:NT], A3[0:4, 0:NT], ALU.subtract), [b3, b4], [b3])
                        V(lambda e: e.tensor_tensor(A1[0:4, 0:NT], A1[0:4, 0:NT], A4[0:4, 0:NT], ALU.subtract), [b1, b4], [b1])
                        A(lambda e: e.activation(A1[0:4, 0:NT], A1[0:4, 0:NT], AF.Exp, bias=CLNK[0:4]), [b1], [b1])
                        ps, bp = PSS()
                        for c in range(NT // 128):
                            P(lambda e, c=c, ps=ps: e.matmul(ps[:, c * 4:(c + 1) * 4], A1[0:4, c * 128:(c + 1) * 128], ident[0:4, 0:4],
                                                             start=True, stop=True), [b1, Bc], [bp])
                        V(lambda e, ps=ps: e.tensor_copy(ektm[:, 0:NT // 128, :].rearrange("p a b -> p (a b)"), ps[:, 0:4 * (NT // 128)]),
                          [bp], [Bek])
                        ck(2)
                        C0, S0 = Fs[6], Fs[7]
                        LD(Fs[8][:, 0:NT], D["posrow"][sbi][:, 0:NT], [BF[8]])
                        V(lambda e: e.tensor_scalar(Fs[8][:, 0:NT], Fs[8][:, 0:NT], pv("invf"), None, ALU.mult), [BF[8], Bc], [BF[8]])
                        sin_of(C0[:, 0:NT], Fs[8][:, 0:NT], 0.5 * PI, Fs[2][:, 0:NT], [BF[8]], [BF[6]], BF[2])
                        sin_of(S0[:, 0:NT], Fs[8][:, 0:NT], 0.0, Fs[2][:, 0:NT], [BF[8]], [BF[7]], BF[2])
                        V(lambda e: e.tensor_scalar(S0[:, 0:NT], S0[:, 0:NT], pv("sgn"), None, ALU.mult), [BF[7], Bc], [BF[7]])
                        V(lambda e: e.memset(vtm[:, :, 128:129], 1.0), (), [Bv])
                        V(lambda e: e.memset(vtm2[:, :, 128:129], 1.0), (), [Bv2])
                        SETS = [(Hs[0], Hs[1], Hs[2], vtm, BH[0], BH[1], BH[2], Bv), (Hs[3], Hs[4], gT2, vtm2, BH[3], BH[4], Bg2, Bv2)]
                        xq, xk, qT, kT, gT = Fs[0], Fs[1], Hs[0], Hs[1], Hs[2]
                        bxq, bxk, bqT, bkT, bgT = BF[0], BF[1], BH[0], BH[1], BH[2]
                        W = D["w_in_ab"]
                        def front_m(hh):
                            qT, kT, gT, vt_, bqT, bkT, bgT, bv_ = SETS[hh % 2]
                            slot, bs = wload([(D["w_ab_h"][:, hh * 512:(hh + 1) * 512], 0)], 8)
                            for (xx, bx, cc) in ((xq, bxq, 0), (xk, bxk, 128)):
                                def ev(ps, bp, bi, o, n, xx=xx, bx=bx):
                                    if o < NP:
                                        A(lambda e: e.copy(xx[:, 3 + o:3 + o + n], ps[:, 0:n]), [bp], [bx])
                                    else:
                                        A(lambda e: e.copy(xx[:, NP + 3:NP + 3 + 176].rearrange("p (j l) -> p j l", l=11)[:, :, 3:11],
                                                           ps[:, 0:128].rearrange("p (j l) -> p j l", l=8)), [bp], [bx])
                                proj_fm(slot, bs, cc, NT, ev)
                                yield

                            def evg(ps, bp, bi, o, n):
                                A(lambda e: e.activation(gT[:, o:o + n], ps[:, 0:n], AF.Sigmoid), [bp], [bgT])
                            proj_fm(slot, bs, 384, NT, evg)
                            yield
                            vproj(slot, bs, 256, NT, 129, vt_, bv_)
                            yield
                            for (xx, bx, ch, oT, boT) in ((xq, bxq, hh, qT, bqT), (xk, bxk, 4 + hh, kT, bkT)):
                                V(lambda e, xx=xx, ch=ch: e.tensor_copy(xx[:, 0:3], tails[:, ch, :]), [Btl], [bx])
                                acc = Fs[2]
                                V(lambda e, xx=xx, ch=ch: e.tensor_scalar(acc[:, 0:NP], xx[:, 0:NP], pv("cw0", ch), pv("cb", ch), ALU.mult, ALU.add),
                                  [bx, Bc], [BF[2]])
                                for j in range(1, 4):
                                    V(lambda e, xx=xx, ch=ch, j=j: e.scalar_tensor_tensor(acc[:, 0:NP], xx[:, j:j + NP], pv("cw%d" % j, ch), acc[:, 0:NP],
                                                                                         ALU.mult, ALU.add), [bx, Bc, BF[2]], [BF[2]])
                                if has_s:
                                    LD(xx[:, NP + 3:NP + 3 + 176].rearrange("p (j l) -> p j l", l=11)[:, :, 0:3],
                                       D["convs"][ch * 128:(ch + 1) * 128], [bx])
                                    xs3 = xx[:, NP + 3:NP + 3 + 176].rearrange("p (j l) -> p j l", l=11)
                                    a3 = acc[:, NP:NT].rearrange("p (j l) -> p j l", l=8)
                                    V(lambda e, xs3=xs3, a3=a3, ch=ch: e.tensor_scalar(a3, xs3[:, :, 0:8], pv("cw0", ch), pv("cb", ch), ALU.mult, ALU.add),
                                      [bx, Bc], [BF[2]])
                                    for j in range(1, 4):
                                        V(lambda e, xs3=xs3, a3=a3, ch=ch, j=j: e.scalar_tensor_tensor(a3, xs3[:, :, j:j + 8], pv("cw%d" % j, ch), a3,
                                                                                                      ALU.mult, ALU.add), [bx, Bc, BF[2]], [BF[2]])
                                    STO(D["convs_o"][ch * 128:(ch + 1) * 128], xs3[:, :, 8:11], [bx])
                                    STO(D["convp"][ch * 128:(ch + 1) * 128], xx[:, NP:NP + 3], [bx])
                                A(lambda e, oT=oT: e.activation(oT[:, 0:NT], acc[:, 0:NT], AF.Silu), [BF[2]], [boT])
                                V(lambda e, xx=xx, ch=ch: e.tensor_copy(tails[:, ch, :], xx[:, NP:NP + 3]), [bx], [Btl])
                                yield

                        def back_m(hh):
                            qT, kT, gT, vt_, bqT, bkT, bgT, bv_ = SETS[hh % 2]
                            if has_s:
                                load_sample_state(D["Us"], hh, 129)
                                make_qz(qT, bqT, NP)
                                V(lambda e: e.tensor_copy(nbs[:], Usf[:, :, 128:129].broadcast_to([128, 16, 128])), [BUs], [Bnbs])
                            for bi, (o, n) in enumerate(blks):
                                PT, bPT = PS()
                                dps, bd = PS()
                                pinned.update((pb.index(PT), pb.index(dps)))
                                for c in range(n // 128):
                                    tcol = o + c * 128
                                    tix = tcol // 128
                                    smp = tcol >= NP
                                    ekf = lambda t_: ektm[:, t_ // 128, hh:hh + 1]
                                    ctx_ = CTX.pop(tcol, None) or att_pre(qT, kT, bqT, bkT, tcol, ekf(tcol), smp)
                                    if tcol + 128 < NT:
                                        CTX[tcol + 128] = att_pre(qT, kT, bqT, bkT, tcol + 128, ekf(tcol + 128), tcol + 128 >= NP)
                                    att_tile(qT, kT, bqT, bkT, tcol, vt_[:, tix, :], 129, hh, ektm[:, tix, hh:hh + 1],
                                             (lambda j, hh=hh: decbc[:, hh, 8 + j:9 + j]) if smp else decbc[:, hh, tix:tix + 1],
                                             PT, bPT, c * 128, smp, den=(dps, bd), mlstm_h=hh, usbuf=Usf, bv=bv_, ctx=ctx_)
                                    step()
                                ps, bp = PSS()
                                P(lambda e, ps=ps, hh=hh: e.matmul(ps[:, 0:n], sel[:, hh, :], A3[0:4, o:o + n], start=True, stop=True), [Bc, b3], [bp])
                                dn = NT5[0]
                                A(lambda e, ps=ps: e.activation(dn[:, 0:n], ps[:, 0:n], AF.Exp, scale=-1.0), [bp], [BN[0]])
                                ab = NT5[1]
                                A(lambda e, dps=dps: e.activation(ab[:, 0:n], dps[:, 0:n], AF.Abs), [bd], [BN[1]])
                                V(lambda e: e.tensor_tensor(ab[:, 0:n], ab[:, 0:n], dn[:, 0:n], ALU.max), [BN[0], BN[1]], [BN[1]])
                                V(lambda e: e.reciprocal(ab[:, 0:n], ab[:, 0:n]), [BN[1]], [BN[1]])
                                hv = NT5[2]
                                V(lambda e, PT=PT: e.tensor_tensor(hv[:, 0:n], PT[:, 0:n], ab[:, 0:n], ALU.mult), [bPT, BN[1]], [BN[2]])
                                V(lambda e: e.tensor_tensor(hv[:, 0:n], hv[:, 0:n], gT[:, o:o + n], ALU.mult), [BN[2], bgT], [BN[2]])
                                rs = rstd_from(lambda q: A(lambda e: e.activation(q[:, 0:n], hv[:, 0:n], AF.Square), [BN[2]], [Bsq[0]]), n, None)
                                V(lambda e, hh=hh: e.scalar_tensor_tensor(mix[:, hh, o:o + n], hv[:, 0:n], pv("gna", hh), rs[:, 0:n], ALU.mult, ALU.mult),
                                  [BN[2], BN[3], Bc], [Bmix[hh][bi]])
                                pinned.clear()
                                step()
                            if has_s:
                                STO(D["Us_o"][:, hh, :, :].rearrange("j d e -> d j e"), Usf[:, :, :], [BUs])
                            if last:
                                STO(D["Up"][hh], Up[:, hh, :], [BU[hh]])
                        for _ in front_m(0):
                            pass
                        for hh in range(4):
                            STEP[0] = front_m(hh + 1) if hh + 1 < 4 else None
                            back_m(hh)
                            while STEP[0] is not None:
                                step()
                        ck(3)
                        def front_r(hh):
                            qT, kT, gT, vt_, bqT, bkT, bgT, bv_ = SETS[hh % 2]
                            b0 = 2056
                            slot, bs = wload([(D["w_ab_h"][:, (4 + hh) * 512:(5 + hh) * 512], 0)], 8)
                            for (xx, bx, cc, oT, boT) in ((xq, bxq, 0, qT, bqT), (xk, bxk, 128, kT, bkT)):
                                def ev(ps, bp, bi, o, n, xx=xx, bx=bx):
                                    A(lambda e: e.copy(xx[:, o:o + n], ps[:, 0:n]), [bp], [bx])
                                proj_fm(slot, bs, cc, NT, ev)
                                yield
                                xsw, t1, t2 = Fs[2], Fs[3], Fs[4]
                                A(lambda e, xx=xx: e.copy(xsw[0:64, 0:NT], xx[64:128, 0:NT]), [bx], [BF[2]])
                                A(lambda e, xx=xx: e.copy(xsw[64:128, 0:NT], xx[0:64, 0:NT]), [bx], [BF[2]])
                                V(lambda e, xx=xx: e.tensor_tensor(t1[:, 0:NT], xx[:, 0:NT], C0[:, 0:NT], ALU.mult), [bx, BF[6]], [BF[3]])
                                V(lambda e: e.tensor_tensor(t2[:, 0:NT], xsw[:, 0:NT], S0[:, 0:NT], ALU.mult), [BF[2], BF[7]], [BF[4]])
                                V(lambda e, oT=oT: e.tensor_tensor(oT[:, 0:NT], t1[:, 0:NT], t2[:, 0:NT], ALU.add), [BF[3], BF[4]], [boT])

                            def evg(ps, bp, bi, o, n):
                                A(lambda e: e.activation(gT[:, o:o + n], ps[:, 0:n], AF.Silu), [bp], [bgT])
                            proj_fm(slot, bs, 384, NT, evg)
                            yield
                            vproj(slot, bs, 256, NT, 128, vt_, bv_)
                            yield

                        def back_r(hh):
                            qT, kT, gT, vt_, bqT, bkT, bgT, bv_ = SETS[hh % 2]
                            if has_s:
                                load_sample_state(D["rets"], hh, 128)
                                make_qz(qT, bqT, NP)
                            g128 = math.exp(128 * LG[hh]); g8 = math.exp(8 * LG[hh])
                            for bi, (o, n) in enumerate(blks):
                                PT, bPT = PS()
                                pinned.add(pb.index(PT))
                                for c in range(n // 128):
                                    tcol = o + c * 128
                                    tix = tcol // 128
                                    smp = tcol >= NP
                                    ekf = lambda t_: gk[:, 1 if t_ >= NP else 0, hh:hh + 1]
                                    ctx_ = CTX.pop(tcol, None) or att_pre(qT, kT, bqT, bkT, tcol, ekf(tcol), smp)
                                    if tcol + 128 < NT:
                                        CTX[tcol + 128] = att_pre(qT, kT, bqT, bkT, tcol + 128, ekf(tcol + 128), tcol + 128 >= NP)
                                    att_tile(qT, kT, bqT, bkT, tcol, vt_[:, tix, :], 128, 4 + hh, gk[:, 1 if smp else 0, hh:hh + 1],
                                             (lambda j, g8=g8: g8) if smp else g128, PT, bPT, c * 128, smp, usbuf=Usf, bv=bv_, ctx=ctx_)
                                    step()
                                    if PEND[0] is not None and c == 0:
                                        PEND[0]()
                                        PEND[0] = None
                                def chain_(PT=PT, bPT=bPT, o=o, n=n, bi=bi):
                                    hv = NT5[2]
                                    if o < NP:
                                        V(lambda e, PT=PT, hh=hh: e.tensor_tensor(hv[:, 0:n].rearrange("p (c t) -> p c t", t=128),
                                                                                 PT[:, 0:n].rearrange("p (c t) -> p c t", t=128),
                                                                                 Gq[:, 0, hh, :].unsqueeze(1).broadcast_to([128, n // 128, 128]), ALU.mult),
                                          [bPT, Bc], [BN[2]])
                                    else:
                                        V(lambda e, PT=PT, hh=hh: e.tensor_tensor(hv[:, 0:n], PT[:, 0:n], Gq[:, 1, hh, :], ALU.mult), [bPT, Bc], [BN[2]])
                                    rs = rstd_from(lambda q: A(lambda e: e.activation(q[:, 0:n], hv[:, 0:n], AF.Square), [BN[2]], [Bsq[0]]), n, None)
                                    V(lambda e: e.tensor_tensor(hv[:, 0:n], hv[:, 0:n], rs[:, 0:n], ALU.mult), [BN[2], BN[3]], [BN[2]])
                                    V(lambda e, hh=hh: e.tensor_tensor(mix[:, 4 + hh, o:o + n], hv[:, 0:n], gT[:, o:o + n], ALU.mult),
                                      [BN[2], bgT], [Bmix[4 + hh][bi]])
                                    pinned.discard(pb.index(PT))
                                    step()
                                PEND[0] = chain_
                            if PEND[0] is not None:
                                PEND[0]()
                                PEND[0] = None
                            if has_s:
                                STO(D["rets_o"][:, hh, :, :].rearrange("j d e -> d j e"), Usf[:, :, 0:128], [BUs])
                            if last:
                                STO(D["retp"][hh], Up[:, 4 + hh, 0:128], [BU[4 + hh]])
                        for _ in front_r(0):
                            pass
                        for hh in range(4):
                            STEP[0] = front_r(hh + 1) if hh + 1 < 4 else None
                            back_r(hh)
                            while STEP[0] is not None:
                                step()
                        resid_proj(D["w_out_ab"][0] if False else D["w_out_ab"], NT, mix, Bmix)
                        ck(4)
                    else:
                        W = D["w_in_cd"]
                        qf, ff, eb, enb, tmp = Fs[0], Fs[1], Fs[2], Fs[3], Fs[4]
                        qT, kT, gT = Hs[0], Hs[1], Hs[2]
                        bqT, bkT, bgT = BH[0], BH[1], BH[2]
                        def front_h(hh):
                            qT, kT, gT, vt_, bqT, bkT, bgT, bv_ = SETS[hh % 2]
                            slot, bs = wload([(D["w_cd_h"][:, hh * 512:(hh + 1) * 512], 0)], 8)

                            def evq(ps, bp, bi, o, n):
                                A(lambda e: e.activation(qf[:, o:o + n], ps[:, 0:n], AF.Copy, scale=128.0 ** -0.5), [bp], [BF[0]])
                            proj_fm(slot, bs, 0, NT, evq)
                            yield

                            def evf(ps, bp, bi, o, n):
                                A(lambda e: e.activation(ff[:, o:o + n], ps[:, 0:n], AF.Sigmoid), [bp], [BF[1]])
                            proj_fm(slot, bs, 128, NT, evf)
                            yield

                            def evg(ps, bp, bi, o, n):
                                A(lambda e: e.activation(gT[:, o:o + n], ps[:, 0:n], AF.Silu), [bp], [bgT])
                            proj_fm(slot, bs, 384, NT, evg)
                            yield
                            vproj(slot, bs, 256, NT, 128, vt_, bv_)
                            yield
                            V(lambda e, hh=hh: e.tensor_scalar(ff[:, 0:NT], ff[:, 0:NT], oml[:, hh:hh + 1], lb[:, hh:hh + 1], ALU.mult, ALU.add),
                              [BF[1], Bc], [BF[1]])
                            A(lambda e: e.activation(tmp[:, 0:NT], ff[:, 0:NT], AF.Ln), [BF[1]], [BF[4]])
                            V(lambda e: e.tensor_tensor_scan(eb[:, 0:NT], segm[:, 0:NT], tmp[:, 0:NT], 0.0, ALU.mult, ALU.add), [BF[4], Bc], [BF[2]])
                            A(lambda e: e.activation(enb[:, 0:NT], eb[:, 0:NT], AF.Exp, scale=-1.0), [BF[2]], [BF[3]])
                            A(lambda e: e.activation(eb[:, 0:NT], eb[:, 0:NT], AF.Exp), [BF[2], BF[3]], [BF[2]])
                            V(lambda e, hh=hh: e.tensor_copy(hdec[:, hh % 2, 0:ntp], eb[:, 0:NP].rearrange("p (c t) -> p c t", t=128)[:, :, 127]), [BF[2]], [Bhd])
                            if has_s:
                                V(lambda e, hh=hh: e.tensor_copy(hdec[:, hh % 2, 8:24], eb[:, NP:NT].rearrange("p (j l) -> p j l", l=8)[:, :, 7]), [BF[2]], [Bhd])
                            V(lambda e: e.tensor_scalar(ff[:, 0:NT], ff[:, 0:NT], -1.0, 1.0, ALU.mult, ALU.add), [BF[1], BF[4]], [BF[1]])
                            V(lambda e: e.tensor_tensor(qT[:, 0:NT], qf[:, 0:NT], eb[:, 0:NT], ALU.mult), [BF[0], BF[2]], [bqT])
                            V(lambda e: e.tensor_tensor(kT[:, 0:NT], ff[:, 0:NT], enb[:, 0:NT], ALU.mult), [BF[1], BF[3]], [bkT])

                        def back_h(hh):
                            qT, kT, gT, vt_, bqT, bkT, bgT, bv_ = SETS[hh % 2]
                            if has_s:
                                load_sample_state(D["hgrns"], hh, 128)
                                make_qz(qT, bqT, NP)
                            for bi, (o, n) in enumerate(blks):
                                PT, bPT = PS()
                                pinned.add(pb.index(PT))
                                for c in range(n // 128):
                                    tcol = o + c * 128
                                    tix = tcol // 128
                                    smp = tcol >= NP
                                    ctx_ = CTX.pop(tcol, None) or att_pre(qT, kT, bqT, bkT, tcol, None, smp)
                                    if tcol + 128 < NT:
                                        CTX[tcol + 128] = att_pre(qT, kT, bqT, bkT, tcol + 128, None, tcol + 128 >= NP)
                                    att_tile(qT, kT, bqT, bkT, tcol, vt_[:, tix, :], 128, 8 + hh, None,
                                             (lambda j, hh=hh: hdec[:, hh % 2, 8 + j:9 + j]) if smp else hdec[:, hh % 2, tix:tix + 1],
                                             PT, bPT, c * 128, smp, usbuf=Usf, bv=bv_, ctx=ctx_)
                                    step()
                                    if PEND[0] is not None and c == 0:
                                        PEND[0]()
                                        PEND[0] = None
                                def chain_(PT=PT, bPT=bPT, o=o, n=n, bi=bi):
                                    rs = rstd_from(lambda q, PT=PT, bPT=bPT: A(lambda e: e.activation(q[:, 0:n], PT[:, 0:n], AF.Square), [bPT], [Bsq[0]]), n, None)
                                    hv = NT5[2]
                                    V(lambda e, PT=PT: e.tensor_tensor(hv[:, 0:n], PT[:, 0:n], rs[:, 0:n], ALU.mult), [bPT, BN[3]], [BN[2]])
                                    V(lambda e, hh=hh: e.scalar_tensor_tensor(mix[:, hh, o:o + n], hv[:, 0:n], pv("gnc", hh), gT[:, o:o + n], ALU.mult, ALU.mult),
                                      [BN[2], bgT, Bc], [Bmix[hh][bi]])
                                    pinned.discard(pb.index(PT))
                                    step()
                                PEND[0] = chain_
                            if PEND[0] is not None:
                                PEND[0]()
                                PEND[0] = None
                            if has_s:
                                STO(D["hgrns_o"][:, hh, :, :].rearrange("j d e -> d j e"), Usf[:, :, 0:128], [BUs])
                            if last:
                                STO(D["hgrnp"][hh], Up[:, 8 + hh, 0:128], [BU[8 + hh]])
                        for _ in front_h(0):
                            pass
                        for hh in range(4):
                            STEP[0] = front_h(hh + 1) if hh + 1 < 4 else None
                            back_h(hh)
                            while STEP[0] is not None:
                                step()
                        ck(7)
                        suf = Fs[8]; bsuf = BF[8]
                        V(lambda e: e.memset(cch[:, 5, 0:1], 0.0), (), S5FINE + [BUs, Bkz, Bqz, Bcch, Bg2, Bv2])
                        pinned.clear(); S5MODE[0] = True
                        sub = Hs[0]; bsub = BH[0]
                        if has_s:
                            LD(x0s[:, 0], D["x0re"], [Bx0]); LD(x0s[:, 1], D["x0im"], [Bx0])
                            bz = lambda v: v.unsqueeze(2).broadcast_to([128, 16, 16])
                            u1 = Fs[6][:, 0:256].rearrange("p (a b) -> p a b", a=16); u2 = Fs[7][:, 0:256].rearrange("p (a b) -> p a b", a=16)
                            u3 = Fs[6][:, 256:512].rearrange("p (a b) -> p a b", a=16); u4 = Fs[7][:, 256:512].rearrange("p (a b) -> p a b", a=16)
                            for (cr, ci) in ((izr, izi), (ar_, ai_)):
                                V(lambda e, cr=cr: e.tensor_tensor(u1, x0s[:, 0], bz(cr), ALU.mult), [Bx0, Bc], [BF[6]])
                                V(lambda e, ci=ci: e.tensor_tensor(u2, x0s[:, 1], bz(ci), ALU.mult), [Bx0, Bc], [BF[7]])
                                V(lambda e, ci=ci: e.tensor_tensor(u3, x0s[:, 0], bz(ci), ALU.mult), [Bx0, Bc], [BF[6]])
                                V(lambda e, cr=cr: e.tensor_tensor(u4, x0s[:, 1], bz(cr), ALU.mult), [Bx0, Bc], [BF[7]])
                                V(lambda e: e.tensor_tensor(x0s[:, 0], u1, u2, ALU.subtract), [BF[6], BF[7]], [Bx0])
                                V(lambda e: e.tensor_tensor(x0s[:, 1], u3, u4, ALU.add), [BF[6], BF[7]], [Bx0])
                        ntl = NT // 128
                        slot_su, bs_su = wload([(W[:, 2048:2560], 0)], 8)
                        for oc in range(4):
                            slot, bs = slot_su, bs_su

                            def evs(ps, bp, bi, o, n):
                                A(lambda e: e.copy(suf[:, o:o + n], ps[:, 0:n]), [bp], [bsuf])
                                A(lambda e: e.copy(sub[:, o:o + n], ps[:, 0:n]), [bp], [bsub])
                            proj_fm(slot, bs, oc * 128, NT, evs)
                            for part in range(2):
                                S.dma("pool", lambda e, part=part, oc=oc: e.dma_start(out=bct[:, 0, part], in_=D["BT"][part, oc * 4:(oc + 1) * 4].rearrange("i k m -> k i m")),
                                      bcsem[part * 2], (), (Bbct if part == 0 else [Bbct[part * 2]]))
                                S.dma("pool", lambda e, part=part, oc=oc: e.dma_start(out=bct[:, 1, part], in_=D["CT"][part, oc * 4:(oc + 1) * 4].rearrange("i k m -> k i m")),
                                      bcsem[part * 2 + 1], (), [Bbct[part * 2 + 1]])
                            V(lambda e: e.tensor_scalar(bct[:, 1, 1], bct[:, 1, 1], -1.0, None, ALU.mult), Bbct, [Bbct[3]])
                            LD(tbe[:], tabE[:, :, oc * 512:(oc + 1) * 512], [Btab], r=[Bscr])
                            LD(tbz[:], tabZ[:, :, 4 * oc:4 * oc + 4, :], [Btab], r=[Bscr])
                            if has_s:
                                EsT = [Hs[1][:, 0:512], Hs[2][:, 0:512]]
                                ZsT = [Hs[3][:, 0:512], Hs[4][:, 0:512]]
                                build_tables(oc, pv("s8"), jrow[:, 3, :], lambda part: EsT[part],
                                             lambda part: ZsT[part].rearrange("p (a b) -> p a b", a=4), [Bes, BH[1], BH[2]], [Bes, BH[3], BH[4]])
                                for part in range(2):
                                    V(lambda e, part=part, oc=oc: e.tensor_copy(cbs[:, part].rearrange("p a (j l) -> p a j l", l=8),
                                                                             x0s[:, part, 4 * oc:4 * oc + 4, :].unsqueeze(3).broadcast_to([128, 4, 16, 8])),
                                      [Bx0], [Bcbs])
                            i4 = slice(4 * oc, 4 * oc + 4)
                            ypsh = [None]

                            def stA(c):
                                smp = c * 128 >= NP; tc0 = c * 128; k2 = c % 2
                                wrb, wib = wtb[k2][0], wtb[k2][1]; xrb, xib = xtb[k2][0], xtb[k2][1]
                                bE = Bes if smp else Btab
                                smp = c * 128 >= NP
                                tc0 = c * 128
                                Er = EsT[0] if smp else tbe[:, 0, :]
                                Ei = EsT[1] if smp else tbe[:, 1, :]
                                bE = Bes if smp else Btab
                                k2 = c % 2
                                pr, bpr = PS(); pi_, bpi = PS()
                                P(lambda e, pr=pr: e.matmul(pr[:, 0:512], sub[:, tc0:tc0 + 128], bct[:, 0, 0].rearrange("p a b -> p (a b)"), start=True, stop=True),
                                  Bbct + [bsub], [bpr])
                                P(lambda e, pi_=pi_: e.matmul(pi_[:, 0:512], sub[:, tc0:tc0 + 128], bct[:, 0, 1].rearrange("p a b -> p (a b)"), start=True, stop=True),
                                  Bbct + [bsub], [bpi])
                                prb = qzf[:, (2 * k2) * 512:(2 * k2 + 1) * 512]; pib = qzf[:, (2 * k2 + 1) * 512:(2 * k2 + 2) * 512]
                                A(lambda e, pr=pr: e.copy(prb, pr[:, 0:512]), [bpr], [Bprb[k2]])
                                A(lambda e, pi_=pi_: e.copy(pib, pi_[:, 0:512]), [bpi], [Bprb[k2]])
                                wrb, wib = wtb[k2][0], wtb[k2][1]
                                ta, tb, tc_, td = [kzf[:, q_ * 512:(q_ + 1) * 512] for q_ in range(4)]
                                V(lambda e: e.tensor_tensor(ta, prb, Er, ALU.mult), [Bprb[k2], bE], [Bt4[0]])
                                V(lambda e: e.tensor_tensor(tb, pib, Ei, ALU.mult), [Bprb[k2], bE], [Bt4[1]])
                                V(lambda e: e.tensor_tensor(wrb[:], ta, tb, ALU.subtract), [Bt4[0], Bt4[1]], [Bwt[k2]])
                                V(lambda e: e.tensor_tensor(tc_, pib, Er, ALU.mult), [Bprb[k2], bE], [Bt4[2]])
                                V(lambda e: e.tensor_tensor(td, prb, Ei, ALU.mult), [Bprb[k2], bE], [Bt4[3]])
                                V(lambda e: e.tensor_tensor(wib[:], tc_, td, ALU.add), [Bt4[2], Bt4[3]], [Bwt[k2]])

                            TC = {}

                            def stB1(c):
                                smp = c * 128 >= NP; tc0 = c * 128; k2 = c % 2
                                wrb, wib = wtb[k2][0], wtb[k2][1]; xrb, xib = xtb[k2][0], xtb[k2][1]
                                bE = Bes if smp else Btab
                                csr, bcsr = PSS(); csi, bcsi = PSS()
                                cur = S5PAR[oc]
                                TC[c] = (csr, bcsr, csi, bcsi, cur)
                                msk = maskb if smp else maskc
                                for il in range(4):
                                    P(lambda e, il=il, csr=csr: e.matmul(csr[:, il * 128:(il + 1) * 128], wrb[:, il * 128:(il + 1) * 128], msk[:], start=True, stop=(not smp)),
                                      [Bwt[k2], Bc], [bcsr])
                                    P(lambda e, il=il, csi=csi: e.matmul(csi[:, il * 128:(il + 1) * 128], wib[:, il * 128:(il + 1) * 128], msk[:], start=True, stop=(not smp)),
                                      [Bwt[k2], Bc], [bcsi])
                                    if smp:
                                        P(lambda e, il=il, csr=csr: e.matmul(csr[:, il * 128:(il + 1) * 128], identb[:], cbs[:, 0, il, :], start=False, stop=True), [Bc, Bcbs], [bcsr])
                                        P(lambda e, il=il, csi=csi: e.matmul(csi[:, il * 128:(il + 1) * 128], identb[:], cbs[:, 1, il, :], start=False, stop=True), [Bc, Bcbs], [bcsi])
                                cs3r = csr[:, 0:512].rearrange("p (a b) -> p a b", a=4); cs3i = csi[:, 0:512].rearrange("p (a b) -> p a b", a=4)
                                if not smp:
                                    nxt = 1 - cur
                                    V(lambda e: e.tensor_tensor(cch[:, 0, :], cs3r[:, :, 127], s5d[:, cur, 0, i4], ALU.add), [bcsr, Bs5d[cur]], [Bcr])
                                    V(lambda e: e.tensor_tensor(cch[:, 1, :], cs3i[:, :, 127], s5d[:, cur, 1, i4], ALU.add), [bcsi, Bs5d[cur]], [Bcr])
                                    V(lambda e: e.tensor_tensor(cch[:, 2, :], cch[:, 0, :], a128r[:, i4], ALU.mult), [Bcr, Bc], [Bcch])
                                    V(lambda e: e.tensor_tensor(cch[:, 3, :], cch[:, 1, :], a128i[:, i4], ALU.mult), [Bcr, Bc], [Bcch])
                                    V(lambda e: e.tensor_tensor(cch[:, 4, :], cch[:, 0, :], a128i[:, i4], ALU.mult), [Bcr, Bc], [Bcch])
                                    V(lambda e: e.tensor_tensor(cch[:, 5, :], cch[:, 1, :], a128r[:, i4], ALU.mult), [Bcr, Bc], [Bcch])
                                    V(lambda e: e.tensor_tensor(s5d[:, nxt, 0, i4], cch[:, 2, :], cch[:, 3, :], ALU.subtract), [Bcch], [Bs5d[nxt]])
                                    V(lambda e: e.tensor_tensor(s5d[:, nxt, 1, i4], cch[:, 4, :], cch[:, 5, :], ALU.add), [Bcch], [Bs5d[nxt]])
                                    S5PAR[oc] = nxt

                            def stB2(c):
                                smp = c * 128 >= NP; tc0 = c * 128; k2 = c % 2
                                wrb, wib = wtb[k2][0], wtb[k2][1]; xrb, xib = xtb[k2][0], xtb[k2][1]
                                csr, bcsr, csi, bcsi, cur = TC.pop(c)
                                csbr = usbf[:, (2 * k2) * 512:(2 * k2 + 1) * 512]; csbi = usbf[:, (2 * k2 + 1) * 512:(2 * k2 + 2) * 512]
                                if not smp:
                                    for il in range(4):
                                        i = 4 * oc + il
                                        sl = slice(il * 128, (il + 1) * 128)
                                        A(lambda e, sl=sl, i=i, csr=csr: e.activation(csbr[:, sl], csr[:, sl], AF.Identity, bias=s5d[:, cur, 0, i:i + 1]), [bcsr, Bs5d[cur], Bcr], [Bcsb[k2]])
                                        A(lambda e, sl=sl, i=i, csi=csi: e.activation(csbi[:, sl], csi[:, sl], AF.Identity, bias=s5d[:, cur, 1, i:i + 1]), [bcsi, Bs5d[cur], Bcr], [Bcsb[k2]])
                                    Zr = tbz[:, 0].rearrange("p a b -> p (a b)"); Zi = tbz[:, 1].rearrange("p a b -> p (a b)"); bZ = Btab
                                else:
                                    A(lambda e, csr=csr: e.copy(csbr, csr[:, 0:512]), [bcsr], [Bcsb[k2]])
                                    A(lambda e, csi=csi: e.copy(csbi, csi[:, 0:512]), [bcsi], [Bcsb[k2]])
                                    Zr = ZsT[0]; Zi = ZsT[1]; bZ = Bes
                                p1, p2, p3, p4 = [pt4[q_] for q_ in range(4)]
                                xrb, xib = xtb[k2][0], xtb[k2][1]
                                V(lambda e: e.tensor_tensor(p1[:], csbr, Zr, ALU.mult), [Bcsb[k2], bZ], [Bp4[0]])
                                V(lambda e: e.tensor_tensor(p2[:], csbi, Zi, ALU.mult), [Bcsb[k2], bZ], [Bp4[1]])
                                V(lambda e: e.tensor_tensor(xrb[:], p1[:], p2[:], ALU.subtract), [Bp4[0], Bp4[1]], [Bxt[k2]])
                                V(lambda e: e.tensor_tensor(p3[:], csbr, Zi, ALU.mult), [Bcsb[k2], bZ], [Bp4[2]])
                                V(lambda e: e.tensor_tensor(p4[:], csbi, Zr, ALU.mult), [Bcsb[k2], bZ], [Bp4[3]])
                                V(lambda e: e.tensor_tensor(xib[:], p3[:], p4[:], ALU.add), [Bp4[2], Bp4[3]], [Bxt[k2]])
                                if last and (smp or c == NP // 128 - 1):
                                    p13 = [q_[:].rearrange("p (a b) -> p a b", a=4) for q_ in (p1, p2, p3, p4)]
                                    if not smp:
                                        V(lambda e: e.tensor_tensor(s5po[:, 0, i4], p13[0][:, :, 127], p13[1][:, :, 127], ALU.subtract), [Bp4[0], Bp4[1]], [Bs5o])
                                        V(lambda e: e.tensor_tensor(s5po[:, 1, i4], p13[2][:, :, 127], p13[3][:, :, 127], ALU.add), [Bp4[2], Bp4[3]], [Bs5o])
                                    else:
                                        l7 = lambda q3: q3.rearrange("p a (j l) -> p a j l", l=8)[:, :, :, 7]
                                        V(lambda e: e.tensor_tensor(s5so[:, 0, i4, :], l7(p13[0]), l7(p13[1]), ALU.subtract), [Bp4[0], Bp4[1]], [Bs5o])
                                        V(lambda e: e.tensor_tensor(s5so[:, 1, i4, :], l7(p13[2]), l7(p13[3]), ALU.add), [Bp4[2], Bp4[3]], [Bs5o])

                            def stC(c):
                                smp = c * 128 >= NP; tc0 = c * 128; k2 = c % 2
                                wrb, wib = wtb[k2][0], wtb[k2][1]; xrb, xib = xtb[k2][0], xtb[k2][1]
                                bE = Bes if smp else Btab
                                if c % 4 == 0:
                                    pinned.clear()
                                    ypsh[0] = PS()
                                    pinned.add(pb.index(ypsh[0][0]))
                                yp, byp = ypsh[0]
                                yc = (c % 4) * 128
                                for il in range(4):
                                    P(lambda e, il=il, yp=yp: e.matmul(yp[:, yc:yc + 128], bct[:, 1, 0, il, :], xrb[:, il * 128:(il + 1) * 128], start=(il == 0), stop=False),
                                      Bbct + [Bxt[k2]], [byp])
                                    P(lambda e, il=il, yp=yp: e.matmul(yp[:, yc:yc + 128], bct[:, 1, 1, il, :], xib[:, il * 128:(il + 1) * 128], start=False, stop=(il == 3)),
                                      Bbct + [Bxt[k2]], [byp])
                                if c % 4 == 3 or c == ntl - 1:
                                    o = (c // 4) * 512
                                    n = (c % 4 + 1) * 128
                                    bi = o // 512
                                    yv = NT5[4]
                                    V(lambda e, yp=yp: e.scalar_tensor_tensor(yv[:, 0:n], suf[:, o:o + n], pv("s5d", oc), yp[:, 0:n], ALU.mult, ALU.add),
                                      [bsuf, byp, Bc], [BN[4]])
                                    A(lambda e: e.activation(mix[:, 4 + oc, o:o + n], yv[:, 0:n], AF.Gelu), [BN[4]], [Bmix[4 + oc][bi]])
                            stA(0)
                            stB1(0)
                            for c in range(ntl):
                                if c + 1 < ntl:
                                    stA(c + 1)
                                stB2(c)
                                if c + 1 < ntl:
                                    stB1(c + 1)
                                stC(c)
                        V(lambda e: e.memset(cch[:, 5, 0:1], 0.0), (), S5FINE + [BUs, Bkz, Bqz, Bcch, Bg2, Bv2])
                        pinned.clear()
                        S5MODE[0] = False
                        if last:
                            STO(D["s5rep"], s5po[:, 0, :], [Bs5o]); STO(D["s5imp"], s5po[:, 1, :], [Bs5o])
                            STO(D["s5res"], s5so[:, 0], [Bs5o]); STO(D["s5ims"], s5so[:, 1], [Bs5o])
                        ck(8)
                        slot, bs = wload([(D["w_glu"], 0)], 4)
                        for oc in range(4):
                            for bi, (o, n) in enumerate(blks):
                                ps, bp = PS()
                                for k in range(4):
                                    P(lambda e, k=k, ps=ps, oc=oc: e.matmul(ps[:, 0:n], slot[:, k, oc * 128:(oc + 1) * 128], mix[:, 4 + k, o:o + n],
                                                                            start=(k == 0), stop=(k == 3)), bs + [Bmix[4 + k][bi] for k in range(4)], [bp])
                                A(lambda e, ps=ps, oc=oc: e.activation(hn[:, oc, o:o + n], ps[:, 0:n], AF.Sigmoid, bias=pv("bglu", oc)), [bp, Bc], [Bhn[bi]])
                        for oc in range(4):
                            for bi, (o, n) in enumerate(blks):
                                V(lambda e, oc=oc: e.tensor_tensor(mix[:, 4 + oc, o:o + n], mix[:, 4 + oc, o:o + n], hn[:, oc, o:o + n], ALU.mult),
                                  [Bhn[bi], Bmix[4 + oc][bi]], [Bmix[4 + oc][bi]])
                        resid_proj(D["w_out_cd"], NT, mix, Bmix)
                    rmsnorm("nff%d" % layer, NT)
                    for q in range(4):
                        if sbi == 0 and layer == 0:
                            build_tab_kc(q)
                        for u in range(2):
                            c0 = q * 1024 + u * 512
                            slot, bs = wload([(D["w_ff1"][layer][:, c0:c0 + 512], 0)], 8)
                            for hc in range(4):
                                c = u * 4 + hc

                                def ev(ps, bp, bi, o, n, c=c):
                                    k_ = FFK[0] % 2
                                    FFK[0] += 1
                                    t = sqb[k_]
                                    A(lambda e: e.activation(t[:, 0:n], ps[:, 0:n], AF.Relu), [bp], [Bsq[k_]])
                                    V(lambda e: e.tensor_tensor(mix[:, c, o:o + n], t[:, 0:n], t[:, 0:n], ALU.mult), [Bsq[k_]], [Bmix[c][bi]])
                                proj_fm(slot, bs, hc * 128, NT, ev)
                        for u in range(2):
                            slot, bs = wload([(D["w_ff2"][layer][q * 1024:(q + 1) * 1024, u * 512:(u + 1) * 512], 0)], 8)
                            for oc in range(4):
                                c = u * 4 + oc

                                def ev(ps, bp, bi, o, n, c=c):
                                    V(lambda e: e.tensor_tensor(h[:, c, o:o + n], h[:, c, o:o + n], ps[:, 0:n], ALU.add), [bp, Bh[c][bi]], [Bh[c][bi]])
                                proj_fm(slot, bs, oc * 128, NT, ev, rhs=mix, rbufs=lambda bi: [Bmix[k][bi] for k in range(8)])
                    ck(5)
                    rmsnorm("nple%d" % layer, NT)
                    S.dma("pool", lambda e, layer=layer: e.dma_start(out=pTb[:, :, 0:NT], in_=D["pT"][layer][:, tok0:tok0 + NT].rearrange("(k p) n -> p k n", p=128)),
                          pTsem, (), [BpT])
                    for u in range(2):
                        slot, bs = wload([(D["w_ple_gate"][layer][:, u * 512:(u + 1) * 512], 0)], 8)
                        slot2, bs2 = wload([(D["w_ple_proj"][layer][:, u * 512:(u + 1) * 512], 0)], 2)
                        for oc in range(4):
                            c = u * 4 + oc
                            for bi, (o, n) in enumerate(blks):
                                ps, bp = PS()
                                for k in range(8):
                                    P(lambda e, k=k, ps=ps: e.matmul(ps[:, 0:n], slot[:, k, oc * 128:(oc + 1) * 128], hn[:, k, o:o + n], start=(k == 0), stop=(k == 7)),
                                      bs + [Bhn[bi]], [bp])
                                k_ = FFK[0] % 2; FFK[0] += 1; gt = NT5[k_]
                                A(lambda e, ps=ps: e.activation(gt[:, 0:n], ps[:, 0:n], AF.Sigmoid), [bp], [BN[k_]])
                                ps2, bp2 = PS()
                                for k in range(2):
                                    P(lambda e, k=k, ps2=ps2: e.matmul(ps2[:, 0:n], slot2[:, k, oc * 128:(oc + 1) * 128], pTb[:, k, o:o + n], start=(k == 0), stop=(k == 1)),
                                      bs2 + [BpT], [bp2])
                                V(lambda e, ps2=ps2: e.tensor_tensor(gt[:, 0:n], gt[:, 0:n], ps2[:, 0:n], ALU.mult), [bp2, BN[k_]], [BN[k_]])
                                V(lambda e, c=c: e.tensor_tensor(h[:, c, o:o + n], h[:, c, o:o + n], gt[:, 0:n], ALU.add), [BN[k_], Bh[c][bi]], [Bh[c][bi]])
                ck(9)
                rmsnorm("nfin", NT, final_out=D["yT"][:, tok0:tok0 + NT] if True else None)
        except _Stop:
            pass
        S.final_wait("sp", OUTS)
        S.emit(st)
    return nc


_NC = [None]


def kernel(**I):
    if _NC[0] is None:
        _NC[0] = build_program()
    nc = _NC[0]
    in_maps = prep_inputs(I)
    res = run_bass_kernel_spmd(nc, in_maps, core_ids=list(range(8)))
    return assemble(res.results)


def prep_inputs(I):
    f = lambda a: np.ascontiguousarray(np.asarray(a, np.float32))
    ident = np.eye(128, dtype=np.float32)
    s_ = np.arange(128)
    maskc = (s_[:, None] <= s_[None, :]).astype(np.float32)
    maskb = maskc * (s_[:, None] // 8 == s_[None, :] // 8)
    blk3 = np.broadcast_to((np.arange(16)[:, None] == (s_[None, :] // 8)).astype(np.float32)[None], (128, 16, 128)).copy()
    rowm = (s_[:, None] // 8 == np.arange(16)[None, :]).astype(np.float32)
    segm = np.ones((3, 128, NTM), np.float32); posrow = np.zeros((3, 128, NTM), np.float32); tau = np.ones((3, 128, NTM), np.float32)
    for i, (t0, NP, hs) in enumerate(SBS):
        segm[i, :, 0:NP:128] = 0.0
        posrow[i, :, 0:NP] = np.arange(t0, t0 + NP)[None]
        tau[i, :, 0:NP] = np.arange(1, NP + 1)[None]
        if hs:
            segm[i, :, NP:NP + 128:8] = 0.0
            posrow[i, :, NP:NP + 128] = (16384 + (np.arange(128) % 8))[None]
            tau[i, :, NP:NP + 128] = (1 + (np.arange(128) % 8))[None]
    negm = np.zeros((4, 128), np.float32); negm[:, 0::8] = -1e30
    sel = np.zeros((4, 4, 128), np.float32)
    for k in range(4):
        sel[k, k, :] = 1.0
    jrow = np.zeros((128, 4, 128), np.float32); jrow[:, 0, :] = (s_ + 1)[None]; jrow[:, 1, :] = (s_ % 8 + 1)[None]; jrow[:, 2, :] = s_[None]; jrow[:, 3, :] = (s_ % 8)[None]
    pvec = np.zeros((128, NPV), np.float32)

    def put(name, arr):
        o, w = PV[name]
        pvec[:, o:o + w] = arr
    for l in range(2):
        put("nmix%d" % l, _cols(I["norm_mix"][l])); put("nff%d" % l, _cols(I["norm_ff"][l])); put("nple%d" % l, _cols(I["norm_ple"][l]))
        put("lb%d" % l, _cols(I["lb_logits"][l]))
    put("nfin", _cols(I["norm_final"]))
    for j in range(4):
        put("cw%d" % j, _cols(I["conv_w_ab"][0][j]))
    put("cb", _cols(I["conv_b_ab"][0])); put("gna", _cols(I["gn_a"][0])); put("gnc", _cols(I["gn_c"][0]))
    put("s5d", _cols(I["s5_D"][0])); put("bglu", _cols(I["b_glu"][0]))
    st = lambda a: np.ascontiguousarray(np.asarray(a, np.float32).reshape(16, 2, 64).reshape(16, 128).T)
    put("are", st(I["s5_A_re"][0])); put("aim", st(I["s5_A_im"][0]))
    put("ldt", st(np.repeat(np.asarray(I["s5_log_dt"][0], np.float32)[:, None], 64, axis=1)))
    put("invf", (10000.0 ** (-(np.arange(128) % 64) / 64.0)).astype(np.float32)[:, None])
    put("sgn", np.where(s_ < 64, -1.0, 1.0).astype(np.float32)[:, None])
    put("s0", s_.astype(np.float32)[:, None]); put("s8", (s_ % 8).astype(np.float32)[:, None]); put("pidx", (s_ + 1).astype(np.float32)[:, None]); put("pidxs", (s_ % 8 + 1).astype(np.float32)[:, None]); put("rowm", rowm)
    rw = lambda a: np.broadcast_to(np.asarray(a, np.float32).reshape(1, 2048), (128, 2048))
    rowp = np.ascontiguousarray(np.stack([rw(I["s5_A_re"][0]), rw(I["s5_A_im"][0]),
                                          rw(np.repeat(np.asarray(I["s5_log_dt"][0], np.float32)[:, None], 64, axis=1))]))
    bgv = np.asarray(I["b_gate_ab"][0], np.float32)
    bg = np.stack([bgv[:4], bgv[4:]], axis=1).copy()
    BT = np.zeros((2, 16, 128, 128), np.float32); CT = np.zeros((2, 16, 128, 128), np.float32)
    for part, (Bm, Cm) in enumerate(((I["s5_B_re"][0], I["s5_C_re"][0]), (I["s5_B_im"][0], I["s5_C_im"][0]))):
        Bm = np.asarray(Bm, np.float32); Cm = np.asarray(Cm, np.float32)
        for g in range(32):
            i, gl = g // 2, g % 2
            k0 = (g % 8) * 16
            BT[part, i, k0:k0 + 16, gl * 64:(gl + 1) * 64] = Bm[g].T
            CT[part, i, gl * 64:(gl + 1) * 64, k0:k0 + 16] = Cm[g].T
    wab = np.asarray(I["w_in_ab"][0], np.float32); wcd = np.asarray(I["w_in_cd"][0], np.float32)
    w_ab_h = np.concatenate([wab[:, b0 + sec * 512 + hh * 128:b0 + sec * 512 + (hh + 1) * 128]
                             for b0 in (0, 2056) for hh in range(4) for sec in range(4)], axis=1)
    w_cd_h = np.concatenate([wcd[:, sec * 512 + hh * 128:sec * 512 + (hh + 1) * 128] for hh in range(4) for sec in range(4)], axis=1)
    common = dict(w_ab_h=f(w_ab_h), w_cd_h=f(w_cd_h), w_in_ab=f(I["w_in_ab"][0]), wg=f(I["w_in_ab"][0][:, 2048:2056]), w_out_ab=f(I["w_out_ab"][0]),
                  w_in_cd=f(I["w_in_cd"][0]), w_glu=f(I["w_glu"][0]), w_out_cd=f(I["w_out_cd"][0]),
                  w_ff1=f(I["w_ff1"]), w_ff2=f(I["w_ff2"]), w_ple_proj=f(I["w_ple_proj"]), w_ple_gate=f(I["w_ple_gate"]),
                  pvec=pvec, bg=bg, BT=BT, CT=CT, ident=ident, maskc=maskc, maskb=maskb.astype(np.float32), blk3=blk3,
                  segm=segm, negm=negm, sel=sel, posrow=posrow, rowp=rowp, jrow=jrow)
    in_maps = []
    for c in range(8):
        sl = slice(16 * c, 16 * c + 16)
        xT = np.concatenate([np.asarray(I["x_prompt"][c]).T, np.asarray(I["x_sample"][sl]).reshape(128, 1024).T], axis=1)
        pT = np.concatenate([np.transpose(np.asarray(I["p_prompt"][:, c]), (0, 2, 1)),
                             np.transpose(np.asarray(I["p_sample"][:, sl]).reshape(2, 128, 256), (0, 2, 1))], axis=2)
        Us = np.concatenate([np.asarray(I["state_mlstm_C"][0][sl]), np.asarray(I["state_mlstm_n"][0][sl])[..., None]], axis=-1)
        x0 = lambda a: np.transpose(np.asarray(a, np.float32).reshape(16, 16, 128), (2, 1, 0))
        m = dict(common)
        m.update(xT=f(xT), pT=f(pT), convs=f(np.transpose(np.asarray(I["state_mlstm_conv"][0][sl]), (2, 0, 1))), Us=f(Us),
                 ms=f(np.asarray(I["state_mlstm_m"][0][sl]).T), rets=f(I["state_ret"][0][sl]), hgrns=f(I["state_hgrn"][0][sl]),
                 x0re=f(x0(I["state_s5_re"][0][sl])), x0im=f(x0(I["state_s5_im"][0][sl])))
        in_maps.append(m)
    return in_maps


def assemble(R):
    yp = np.zeros((8, 2048, 1024), np.float32); ys = np.zeros((128, 8, 1024), np.float32)
    convp = np.zeros((1, 8, 3, 1024), np.float32); convs = np.zeros((1, 128, 3, 1024), np.float32)
    Cp = np.zeros((1, 8, 4, 128, 128), np.float32); Cs = np.zeros((1, 128, 4, 128, 128), np.float32)
    np_ = np.zeros((1, 8, 4, 128), np.float32); ns = np.zeros((1, 128, 4, 128), np.float32)
    mp = np.zeros((1, 8, 4), np.float32); ms = np.zeros((1, 128, 4), np.float32)
    retp = np.zeros((1, 8, 4, 128, 128), np.float32); rets = np.zeros((1, 128, 4, 128, 128), np.float32)
    hgp = np.zeros((1, 8, 4, 128, 128), np.float32); hgs = np.zeros((1, 128, 4, 128, 128), np.float32)
    s5rp = np.zeros((1, 8, 32, 64), np.float32); s5ip = np.zeros((1, 8, 32, 64), np.float32)
    s5rs = np.zeros((1, 128, 32, 64), np.float32); s5is = np.zeros((1, 128, 32, 64), np.float32)
    for c in range(len(R)):
        r = R[c]
        sl = slice(16 * c, 16 * c + 16)
        yp[c] = r["yT"][:, :2048].T
        ys[sl] = r["yT"][:, 2048:].T.reshape(16, 8, 1024)
        convp[0, c] = r["convp"].T
        convs[0, sl] = np.transpose(r["convs_o"], (1, 2, 0))
        Cp[0, c] = r["Up"][:, :, :128]; np_[0, c] = r["Up"][:, :, 128]
        Cs[0, sl] = r["Us_o"][..., :128]; ns[0, sl] = r["Us_o"][..., 128]
        mp[0, c] = r["mp"][:, 0]; ms[0, sl] = r["ms_o"].T
        retp[0, c] = r["retp"]; rets[0, sl] = r["rets_o"]; hgp[0, c] = r["hgrnp"]; hgs[0, sl] = r["hgrns_o"]
        s5rp[0, c] = r["s5rep"].T.reshape(32, 64); s5ip[0, c] = r["s5imp"].T.reshape(32, 64)
        s5rs[0, sl] = np.transpose(r["s5res"], (2, 1, 0)).reshape(16, 32, 64)
        s5is[0, sl] = np.transpose(r["s5ims"], (2, 1, 0)).reshape(16, 32, 64)
    return (yp, ys, convp, convs, Cp, Cs, np_, ns, mp, ms, retp, rets, hgp, hgs, s5rp, s5rs, s5ip, s5is)
```

```python
import math, contextlib, os
import numpy as np
import concourse.bass as bass
import concourse.mybir as mybir
from concourse.bass_utils import run_bass_kernel_spmd

F32 = mybir.dt.float32
BF16 = mybir.dt.bfloat16
AF = mybir.ActivationFunctionType
ALU = mybir.AluOpType

NTM = 768
FW = 776
SBS = [(0, 768, False), (768, 768, False), (1536, 512, True)]
NTOK = 2176
EPS = 1e-6
PI = math.pi
LG = [math.log1p(-2.0 ** (-5.0 - h)) for h in range(4)]
LNK = -0.5 * math.log(128.0)


class Buf:
    __slots__ = ("w", "r")

    def __init__(self):
        self.w = None
        self.r = []


class _Rec:
    def __init__(self):
        self.call = None

    def __getattr__(self, name):
        def f(*a, **k):
            self.call = (name, a, k)
            return self
        return f


def _record(fn):
    r = _Rec()
    fn(r)
    assert r.call is not None
    return r.call


class Sched:
    ENGS = ("pe", "act", "dve", "pool", "sp")

    def __init__(self, nc):
        self.nc = nc
        self.ops = {e: [] for e in self.ENGS}
        self.cnt = {e: 0 for e in self.ENGS}
        self.seen = {e: {} for e in self.ENGS}
        self.sems = {}
        self.dma_cnt = {}

    def new_dma_sem(self):
        k = "dma%d" % len(self.dma_cnt)
        self.dma_cnt[k] = 0
        return k

    def _deps(self, eng, reads, writes, is_dma):
        waits = {}

        def add(ev, kind):
            key, val, src_eng, src_dma = ev
            if (not src_dma) and (not is_dma) and src_eng == eng and eng == "pe":
                return
            if self.seen[eng].get(key, 0) >= val:
                return
            if waits.get(key, 0) < val:
                waits[key] = val
        for b in reads:
            if b.w is not None:
                add(b.w, "raw")
        for b in writes:
            if b.w is not None:
                add(b.w, "waw")
            for r in b.r:
                add(r, "war")
        for k, v in waits.items():
            self.seen[eng][k] = v
        return list(waits.items())

    def _post(self, ev, reads, writes):
        for b in writes:
            b.w = ev
            b.r = []
        for b in reads:
            if b.w is not ev:
                b.r.append(ev)
                if len(b.r) > 24:
                    b.r = b.r[-24:] if False else b.r

    def op(self, eng, fn, reads=(), writes=()):
        waits = self._deps(eng, reads, writes, False)
        self.cnt[eng] += 1
        ev = ("e_" + eng, self.cnt[eng], eng, False)
        self.ops[eng].append((waits, _record(fn), ("e_" + eng, 1)))
        self._post(ev, reads, writes)

    def dma(self, eng, fn, sem, reads=(), writes=()):
        waits = self._deps(eng, reads, writes, True)
        prev = self.dma_cnt[sem]
        if prev > 0 and self.seen[eng].get(sem, 0) < prev:
            waits = [w_ for w_ in waits if w_[0] != sem] + [(sem, prev)]
            self.seen[eng][sem] = prev
        self.dma_cnt[sem] += 16
        ev = (sem, self.dma_cnt[sem], eng, True)
        self.ops[eng].append((waits, _record(fn), (sem, 16)))
        self._post(ev, reads, writes)

    def final_wait(self, eng, bufs):
        waits = self._deps(eng, bufs, bufs, True)
        have = dict(waits)
        for k, v in self.dma_cnt.items():
            if v > 0 and self.seen[eng].get(k, 0) < v and have.get(k, 0) < v:
                have[k] = v
        for e2 in self.ENGS:
            if e2 != eng and self.cnt[e2] > 0:
                have["e_" + e2] = self.cnt[e2]
        self.ops[eng].append((list(have.items()), None, None))

    def emit(self, stack):
        nc = self.nc
        keys = ["e_" + e for e in self.ENGS] + list(self.dma_cnt.keys())
        for k in keys:
            self.sems[k] = stack.enter_context(nc.semaphore(k))
        block = stack.enter_context(nc.Block())
        engobj = {"pe": "tensor", "act": "scalar", "dve": "vector", "pool": "gpsimd", "sp": "sync"}

        def mk(e):
            def body(engine):
                for (waits, fn, inc) in self.ops[e]:
                    for (k, v) in waits:
                        engine.wait_ge(self.sems[k], v)
                    if fn is not None:
                        name, a, k = fn
                        getattr(engine, name)(*a, **k).then_inc(self.sems[inc[0]], inc[1])
            return body
        for e in self.ENGS:
            if self.ops[e]:
                getattr(block, engobj[e])(mk(e))


PV = {}
_o = 0
for _n, _w in [("nmix0", 8), ("nmix1", 8), ("nff0", 8), ("nff1", 8), ("nple0", 8), ("nple1", 8), ("nfin", 8),
               ("cw0", 8), ("cw1", 8), ("cw2", 8), ("cw3", 8), ("cb", 8), ("gna", 4), ("gnc", 4), ("s5d", 4),
               ("bglu", 4), ("lb0", 4), ("lb1", 4), ("are", 16), ("aim", 16), ("ldt", 16), ("invf", 1),
               ("sgn", 1), ("pidx", 1), ("pidxs", 1), ("rowm", 16), ("s0", 1), ("s8", 1)]:
    PV[_n] = (_o, _w)
    _o += _w
NPV = _o


def _cols(v):
    return np.ascontiguousarray(np.asarray(v, np.float32).reshape(-1, 128).T)


def build_program():
    nc = bass.Bass("TRN2", target_bir_lowering=False)
    D = {}

    def din(name, shape):
        D[name] = nc.dram_tensor(name, list(shape), F32, kind="ExternalInput").ap()
        return D[name]

    def dout(name, shape):
        D[name] = nc.dram_tensor(name, list(shape), F32, kind="ExternalOutput").ap()
        return D[name]
    din("xT", [1024, NTOK]); din("pT", [2, 256, NTOK])
    din("convs", [1024, 16, 3]); din("Us", [16, 4, 128, 129]); din("ms", [4, 16])
    din("rets", [16, 4, 128, 128]); din("hgrns", [16, 4, 128, 128])
    din("x0re", [128, 16, 16]); din("x0im", [128, 16, 16])
    din("w_ab_h", [1024, 4096]); din("w_cd_h", [1024, 2048]); din("w_in_ab", [1024, 4104]); din("wg", [1024, 8]); din("w_out_ab", [1024, 1024])
    din("w_in_cd", [1024, 2560]); din("w_glu", [512, 512]); din("w_out_cd", [1024, 1024])
    din("w_ff1", [2, 1024, 4096]); din("w_ff2", [2, 4096, 1024])
    din("w_ple_proj", [2, 256, 1024]); din("w_ple_gate", [2, 1024, 1024])
    din("pvec", [128, NPV]); din("bg", [4, 2])
    din("BT", [2, 16, 128, 128]); din("CT", [2, 16, 128, 128])
    din("ident", [128, 128]); din("maskc", [128, 128]); din("maskb", [128, 128])
    din("blk3", [128, 16, 128]); din("segm", [3, 128, NTM]); din("negm", [4, 128]); din("sel", [4, 4, 128])
    din("posrow", [3, 128, NTM]); din("rowp", [3, 128, 2048]); din("jrow", [128, 4, 128])
    dout("yT", [1024, NTOK]); dout("convp", [1024, 3]); dout("convs_o", [1024, 16, 3])
    dout("Up", [4, 128, 129]); dout("Us_o", [16, 4, 128, 129]); dout("mp", [4, 1]); dout("ms_o", [4, 16])
    dout("retp", [4, 128, 128]); dout("rets_o", [16, 4, 128, 128])
    dout("hgrnp", [4, 128, 128]); dout("hgrns_o", [16, 4, 128, 128])
    dout("s5rep", [128, 16]); dout("s5imp", [128, 16]); dout("s5res", [128, 16, 16]); dout("s5ims", [128, 16, 16])

    st = contextlib.ExitStack()
    with st:
        S = Sched(nc)
        cnt = [0]

        def sb(shape, dt=F32):
            cnt[0] += 1
            return st.enter_context(nc.sbuf_tensor("t%d" % cnt[0], list(shape), dt))

        def psum(shape, dt=F32):
            cnt[0] += 1
            return st.enter_context(nc.psum_tensor("p%d" % cnt[0], list(shape), dt))
        V = lambda fn, r=(), w=(): S.op("dve", fn, r, w)
        A = lambda fn, r=(), w=(): S.op("act", fn, r, w)
        G = lambda fn, r=(), w=(): S.op("pool", fn, r, w)
        P = lambda fn, r=(), w=(): S.op("pe", fn, r, w)
        msems = {"sp": [S.new_dma_sem() for _ in range(24)], "pool": [S.new_dma_sem() for _ in range(8)]}
        mi = {"sp": 0, "pool": 0}

        def LD(out, in_, w, r=(), eng="sp"):
            k = msems[eng][mi[eng] % len(msems[eng])]
            mi[eng] += 1
            S.dma(eng, lambda e: e.dma_start(out=out, in_=in_), k, r, w)
        OUTS = []

        def STO(out, in_, r):
            b_ = Buf()
            OUTS.append(b_)
            LD(out, in_, [b_], r)

        h = sb([128, 8, NTM]); hn = sb([128, 8, NTM], BF16); mix = sb([128, 8, NTM], BF16)
        Bh = [[Buf() for _ in range(2)] for _ in range(8)]
        Bhn = [Buf() for _ in range(2)]
        Bmix = [[Buf() for _ in range(2)] for _ in range(8)]
        NW = 2
        wr = [sb([128, 8, 512], BF16) for _ in range(NW)]
        Bwrp = [[Buf() for _ in range(4)] for _ in range(NW)]
        wsem = [[S.new_dma_sem() for _ in range(4)] for _ in range(NW)]
        wi = [0]

        def wload(parts, nk):
            i = wi[0] % NW
            wi[0] += 1
            for pi_, (ap, co) in enumerate(parts):
                ncol = ap.shape[1]
                S.dma("pool", lambda e, ap=ap, co=co, ncol=ncol, i=i: e.dma_start(
                    out=wr[i][:, 0:nk, co:co + ncol], in_=ap.rearrange("(k p) n -> p k n", p=128)),
                    wsem[i][pi_], (), (Bwrp[i] if pi_ == 0 else [Bwrp[i][pi_]]))
            return wr[i], Bwrp[i]
        Fs = [sb([128, FW]) for _ in range(9)]
        BF = [Buf() for _ in range(9)]
        Hs = [sb([128, NTM], BF16) for _ in range(5)]
        BH = [Buf() for _ in range(5)]
        vtm = sb([128, 6, 129], BF16); Bv = Buf()
        NT5 = [sb([128, 512]) for _ in range(5)]
        BN = [Buf() for _ in range(5)]
        sqb = [sb([128, 512], BF16) for _ in range(2)]
        Bsq = [Buf() for _ in range(2)]
        pb = [psum([128, 512]) for _ in range(7)]
        Bp = [Buf() for _ in range(7)]
        ptb = psum([128, 1024], BF16); Bpt = Buf()
        pbi = [0]

        pinned = set()

        S5MODE = [False]

        def PS():
            while True:
                i = pbi[0] % (3 if S5MODE[0] else 4)
                pbi[0] += 1
                if i not in pinned:
                    return pb[i], Bp[i]
        psi = [0]

        def PSS():
            if S5MODE[0]:
                i = (4, 5, 6, 3)[psi[0] % 4]
            else:
                i = 4 + psi[0] % 3
            psi[0] += 1
            return pb[i], Bp[i]
        ident = sb([128, 128]); identb = sb([128, 128], BF16); maskc = sb([128, 128], BF16); maskb = sb([128, 128], BF16)
        onesb = sb([128, 128], BF16); blk3 = sb([128, 16, 128], BF16); segm = sb([128, NTM]); negm = sb([4, 128])
        sel = sb([4, 4, 128]); pvec = sb([128, NPV]); bg = sb([4, 2]); nbg = sb([4, 1]); jrow = sb([128, 4, 128])
        Bc = Buf()
        LD(ident[:], D["ident"], [Bc]); LD(identb[:], D["ident"], [Bc], eng="pool")
        LD(maskc[:], D["maskc"], [Bc], eng="pool"); LD(maskb[:], D["maskb"], [Bc], eng="pool")
        LD(blk3[:], D["blk3"], [Bc], eng="pool"); LD(negm[:], D["negm"], [Bc]); LD(sel[:], D["sel"], [Bc])
        LD(pvec[:], D["pvec"], [Bc]); LD(bg[:], D["bg"], [Bc]); LD(jrow[:], D["jrow"], [Bc])
        V(lambda e: e.memset(onesb[:], 1.0), (), [Bc])
        V(lambda e: e.tensor_scalar(nbg[:], bg[:, 1:2], -1.0, None, ALU.mult), [Bc], [Bc])

        cb_ = sb([128, 8])
        CBV = [EPS, LNK, 1.0, 0.0, 0.5 * PI, 0.0, 0.0, 0.0]
        for _i, _v in enumerate(CBV):
            V(lambda e, _i=_i, _v=_v: e.memset(cb_[:, _i:_i + 1], _v), (), [Bc])
        CEPS, CLNK, CONE, CZERO, CHPI = [cb_[:, i:i + 1] for i in range(5)]
        RC = 12582912.0
        I2P = 1.0 / (2 * PI)

        def sin_of(dst, src, shift, tmp, rd, wr_, btmp, npart=128):
            V(lambda e: e.tensor_scalar(tmp, src, shift, I2P, ALU.add, ALU.mult), rd, [btmp])
            V(lambda e: e.tensor_scalar(tmp, tmp, RC, None, ALU.add), [btmp], [btmp])
            V(lambda e: e.tensor_scalar(tmp, tmp, -RC, None, ALU.add), [btmp], [btmp])
            V(lambda e: e.scalar_tensor_tensor(tmp, tmp, -2 * PI, src, ALU.mult, ALU.add), [btmp] + list(rd), [btmp])
            V(lambda e: e.tensor_scalar(tmp, tmp, -PI - shift + 4e-6, PI - shift - 4e-6, ALU.max, ALU.min), [btmp], [btmp])
            A(lambda e: e.activation(dst, tmp, AF.Sin, bias=(CHPI[0:npart] if shift != 0.0 else CZERO[0:npart])), [btmp, Bc], wr_)

        def pv(name, j=0, n=1):
            o, w = PV[name]
            return pvec[:, o + j:o + j + n]
        Gq = sb([128, 2, 4, 128], BF16); gk = sb([128, 2, 4])
        for v2 in range(2):
            for hh in range(4):
                A(lambda e, v2=v2, hh=hh: e.activation(Gq[:, v2, hh, :], jrow[:, v2, :], AF.Exp, scale=LG[hh]), [Bc], [Bc])
                A(lambda e, v2=v2, hh=hh: e.activation(gk[:, v2, hh:hh + 1], pv("pidxs" if v2 else "pidx"), AF.Exp,
                                                       scale=-LG[hh], bias=CLNK), [Bc], [Bc])
        lb = sb([128, 4]); oml = sb([128, 4])
        V(lambda e: e.tensor_tensor(lb[:], pv("lb1", 0, 4), pv("lb0", 0, 4), ALU.subtract), [Bc], [Bc])
        A(lambda e: e.activation(lb[:], lb[:], AF.Sigmoid), [Bc], [Bc])
        V(lambda e: e.tensor_scalar(oml[:], lb[:], -1.0, 1.0, ALU.mult, ALU.add), [Bc], [Bc])
        s5p = sb([128, 16, 16])
        th, rr, zr, zi, rho = s5p[:, 0, :], s5p[:, 1, :], s5p[:, 2, :], s5p[:, 3, :], s5p[:, 7, :]
        t4, t5, t6 = s5p[:, 4, :], s5p[:, 5, :], s5p[:, 6, :]
        ar_, ai_, a128r, a128i, izr, izi, t7 = (s5p[:, 8, :], s5p[:, 9, :], s5p[:, 10, :], s5p[:, 11, :], s5p[:, 12, :],
                                               s5p[:, 13, :], s5p[:, 14, :])
        are, aim = pv("are", 0, 16), pv("aim", 0, 16)
        A(lambda e: e.activation(t4, pv("ldt", 0, 16), AF.Exp), [Bc], [Bc])
        V(lambda e: e.tensor_tensor(th, t4, aim, ALU.mult), [Bc], [Bc])
        V(lambda e: e.tensor_tensor(rho, t4, are, ALU.mult), [Bc], [Bc])
        A(lambda e: e.activation(rr, rho, AF.Exp), [Bc], [Bc])
        sin_of(t4, th, 0.5 * PI, t6, [Bc], [Bc], Bc)
        sin_of(t5, th, 0.0, t6, [Bc], [Bc], Bc)
        V(lambda e: e.tensor_tensor(ar_, t4, rr, ALU.mult), [Bc], [Bc])
        V(lambda e: e.tensor_tensor(ai_, t5, rr, ALU.mult), [Bc], [Bc])
        V(lambda e: e.tensor_scalar(t4, ar_, -1.0, None, ALU.add), [Bc], [Bc])
        V(lambda e: e.tensor_copy(t5, ai_), [Bc], [Bc])
        V(lambda e: e.tensor_tensor(t6, are, are, ALU.mult), [Bc], [Bc])
        V(lambda e: e.tensor_tensor(zr, aim, aim, ALU.mult), [Bc], [Bc])
        V(lambda e: e.tensor_tensor(t6, t6, zr, ALU.add), [Bc], [Bc])
        V(lambda e: e.reciprocal(t6, t6), [Bc], [Bc])
        V(lambda e: e.tensor_tensor(zr, t4, are, ALU.mult), [Bc], [Bc])
        V(lambda e: e.tensor_tensor(zi, t5, aim, ALU.mult), [Bc], [Bc])
        V(lambda e: e.tensor_tensor(zr, zr, zi, ALU.add), [Bc], [Bc])
        V(lambda e: e.tensor_tensor(zi, t5, are, ALU.mult), [Bc], [Bc])
        V(lambda e: e.tensor_tensor(t7, t4, aim, ALU.mult), [Bc], [Bc])
        V(lambda e: e.tensor_tensor(zi, zi, t7, ALU.subtract), [Bc], [Bc])
        V(lambda e: e.tensor_tensor(zr, zr, t6, ALU.mult), [Bc], [Bc])
        V(lambda e: e.tensor_tensor(zi, zi, t6, ALU.mult), [Bc], [Bc])
        V(lambda e: e.tensor_tensor(t4, zr, zr, ALU.mult), [Bc], [Bc])
        V(lambda e: e.tensor_tensor(t5, zi, zi, ALU.mult), [Bc], [Bc])
        V(lambda e: e.tensor_tensor(t4, t4, t5, ALU.add), [Bc], [Bc])
        V(lambda e: e.reciprocal(t4, t4), [Bc], [Bc])
        V(lambda e: e.tensor_tensor(izr, zr, t4, ALU.mult), [Bc], [Bc])
        V(lambda e: e.scalar_tensor_tensor(izi, zi, -1.0, t4, ALU.mult, ALU.mult), [Bc], [Bc])
        V(lambda e: e.tensor_scalar(t7, th, 128.0, None, ALU.mult), [Bc], [Bc])
        sin_of(t4, t7, 0.5 * PI, t6, [Bc], [Bc], Bc)
        sin_of(t5, t7, 0.0, t6, [Bc], [Bc], Bc)
        A(lambda e: e.activation(t6, rho, AF.Exp, scale=128.0), [Bc], [Bc])
        V(lambda e: e.tensor_tensor(a128r, t4, t6, ALU.mult), [Bc], [Bc])
        V(lambda e: e.tensor_tensor(a128i, t5, t6, ALU.mult), [Bc], [Bc])
        tabE = nc.dram_tensor("tabE", [128, 2, 2048], BF16).ap(); tabZ = nc.dram_tensor("tabZ", [128, 2, 16, 128], BF16).ap()
        tbe = sb([128, 2, 512], BF16); tbz = sb([128, 2, 4, 128], BF16); Btab = Buf(); Bscr = Buf()

        def build_tables(kc, scol, jr, outE, outZ, wE, wZ):
            f0, f1, f2, f3, f4, f5 = [Fs[k][:, 0:512] for k in range(6)]
            b0_, b1_, b2_, b3_, b4_, b5_ = BF[0:6]
            for k in range(3):
                LD(Fs[k][:, 0:512], D["rowp"][k][:, kc * 512:(kc + 1) * 512], [BF[k]])
            A(lambda e: e.activation(f2, f2, AF.Exp), [b2_], [b2_])
            V(lambda e: e.tensor_tensor(f1, f1, f2, ALU.mult), [b1_, b2_], [b1_])
            V(lambda e: e.tensor_tensor(f0, f0, f2, ALU.mult), [b0_, b2_], [b0_])
            V(lambda e: e.tensor_scalar(f1, f1, scol, None, ALU.mult), [b1_, Bc], [b1_])
            A(lambda e: e.activation(f0, f0, AF.Exp, scale=scol), [b0_, Bc], [b0_])
            V(lambda e: e.reciprocal(f0, f0), [b0_], [b0_])
            sin_of(f3, f1, 0.5 * PI, f2, [b1_], [b3_], b2_)
            sin_of(f4, f1, 0.0, f2, [b1_], [b4_], b2_)
            V(lambda e: e.tensor_tensor(outE(0), f3, f0, ALU.mult), [b3_, b0_], wE)
            V(lambda e: e.scalar_tensor_tensor(outE(1), f4, -1.0, f0, ALU.mult, ALU.mult), [b4_, b0_], wE)
            g0, g1, g2, g3, g4 = [Fs[k][:, 0:512].rearrange("p (a b) -> p a b", a=4) for k in range(5)]
            i4 = slice(4 * kc, 4 * kc + 4)
            jb = jr.unsqueeze(1).broadcast_to([128, 4, 128])
            bc4 = lambda v: v[:, i4].unsqueeze(2).broadcast_to([128, 4, 128])
            V(lambda e: e.tensor_tensor(g1, jb, bc4(th), ALU.mult), [Bc], [b1_])
            V(lambda e: e.tensor_tensor(g0, jb, bc4(rho), ALU.mult), [Bc], [b0_])
            A(lambda e: e.activation(Fs[0][:, 0:512], Fs[0][:, 0:512], AF.Exp), [b0_], [b0_])
            sin_of(f3, f1, 0.5 * PI, f2, [b1_], [b3_], b2_)
            sin_of(f4, f1, 0.0, f2, [b1_], [b4_], b2_)
            V(lambda e: e.tensor_tensor(f3, f3, f0, ALU.mult), [b3_, b0_], [b3_])
            V(lambda e: e.tensor_tensor(f4, f4, f0, ALU.mult), [b4_, b0_], [b4_])
            V(lambda e: e.tensor_tensor(g0, g3, bc4(zr), ALU.mult), [b3_, Bc], [b0_])
            V(lambda e: e.tensor_tensor(g1, g4, bc4(zi), ALU.mult), [b4_, Bc], [b1_])
            V(lambda e: e.tensor_tensor(outZ(0), g0, g1, ALU.subtract), [b0_, b1_], wZ)
            V(lambda e: e.tensor_tensor(g0, g3, bc4(zi), ALU.mult), [b3_, Bc], [b0_])
            V(lambda e: e.tensor_tensor(g1, g4, bc4(zr), ALU.mult), [b4_, Bc], [b1_])
            V(lambda e: e.tensor_tensor(outZ(1), g0, g1, ALU.add), [b0_, b1_], wZ)
        Up = sb([128, 12, 129]); Upb = sb([128, 12, 129], BF16); nbc = sb([128, 4, 128], BF16)
        BU = [Buf() for _ in range(12)]
        V(lambda e: e.memset(Up[:], 0.0), (), BU); V(lambda e: e.memset(Upb[:], 0.0), (), BU)
        V(lambda e: e.memset(nbc[:], 0.0), (), BU)
        tails = sb([128, 8, 3]); Btl = Buf()
        V(lambda e: e.memset(tails[:], 0.0), (), [Btl])
        carr = sb([4, 2]); Bcar = Buf()
        V(lambda e: e.memset(carr[:], 0.0), (), [Bcar])
        s5c = sb([128, 2, 16]); Bs5c = Buf()
        s5d = sb([128, 2, 2, 16]); Bs5d = [Buf(), Buf()]; S5PAR = [0, 0, 0, 0]; Bcr = Buf()
        V(lambda e: e.memset(s5c[:], 0.0), (), [Bs5c])
        V(lambda e: e.memset(s5d[:], 0.0), (), Bs5d)
        Usf = sb([128, 16, 129]); Usb = sb([128, 16, 129], BF16); BUs = Buf()
        qz = sb([128, 16, 128], BF16); kz = sb([128, 16, 128], BF16); nbs = kz
        Bqz = Buf(); Bkz = Buf(); Bnbs = Bkz
        stb = [sb([128, 128], BF16) for _ in range(2)]; Bst = [Buf() for _ in range(2)]
        khb = [sb([128, 128], BF16) for _ in range(2)]; Bkh = [Buf() for _ in range(2)]
        ektm = sb([128, 6, 4]); Bek = Buf()
        decbc = sb([128, 4, 24]); Bdec = Buf()
        decrow = sb([4, 24]); mxe = sb([4, 8]); ms0 = sb([4, 16]); msout = sb([4, 17]); Bsm = Buf()
        x0s = Fs[5][:, 0:512].rearrange("p (a b c) -> p a b c", a=2, b=16); Bx0 = BF[5]
        s5so = Usf[:].rearrange("p a b -> p (a b)")[:, 0:512].rearrange("p (a b c) -> p a b c", a=2, b=16); s5po = sb([128, 2, 16]); Bs5o = Buf()
        Bes = Buf()
        cbs = sb([128, 2, 4, 128], BF16); Bcbs = Buf()
        pt4all = sb([128, 2048], BF16)
        pt4 = [pt4all[:, q_ * 512:(q_ + 1) * 512] for q_ in range(4)]
        gT2 = pt4all[:, 0:NTM]; vtm2 = pt4all[:, NTM:NTM + 774].rearrange("p (a b) -> p a b", a=6); Bg2 = Buf(); Bv2 = Buf()
        hdec = sb([128, 2, 24]); Bhd = Buf()
        Bprb = [Buf() for _ in range(2)]; Bt4 = [Buf() for _ in range(4)]; Bcsb = [Buf() for _ in range(2)]; Bp4 = [Buf() for _ in range(4)]
        S5FINE = Bprb + Bt4 + Bcsb + Bp4
        qzf = qz[:].rearrange("p a b -> p (a b)"); kzf = kz[:].rearrange("p a b -> p (a b)"); usbf = Usb[:].rearrange("p a b -> p (a b)")
        wtb = [[sb([128, 512], BF16) for _ in range(2)] for _ in range(2)]; Bwt = [Buf() for _ in range(2)]
        xtb = [[sb([128, 512], BF16) for _ in range(2)] for _ in range(2)]; Bxt = [Buf() for _ in range(2)]
        cch = sb([128, 6, 4]); Bcch = Buf()
        bct = sb([128, 2, 2, 4, 128], BF16); Bbct = [Buf() for _ in range(4)]; bcsem = [S.new_dma_sem() for _ in range(4)]
        pTb = sb([128, 2, NTM], BF16); BpT = Buf(); pTsem = S.new_dma_sem()

        def blocks(ntot):
            out = []
            o = 0
            while o < ntot:
                n = min(512, ntot - o)
                out.append((o, n)); o += n
            return out

        def rmsnorm(gname, NT, final_out=None):
            for bi, (o, n) in enumerate(blocks(NT)):
                ps, bp = PS()
                for c in range(8):
                    q = sqb[c % 2]; bq = Bsq[c % 2]
                    A(lambda e, c=c, q=q: e.activation(q[:, 0:n], h[:, c, o:o + n], AF.Square), [Bh[c][bi]], [bq])
                    P(lambda e, c=c, q=q, ps=ps: e.matmul(ps[:, 0:n], onesb[:], q[:, 0:n], start=(c == 0), stop=(c == 7)),
                      [bq, Bc], [bp])
                ri = 4 if bi % 2 == 0 else 3
                rs = NT5[ri]
                A(lambda e, ps=ps: e.activation(rs[:, 0:n], ps[:, 0:n], AF.Ln, scale=1.0 / 1024, bias=CEPS), [bp], [BN[ri]])
                A(lambda e: e.activation(rs[:, 0:n], rs[:, 0:n], AF.Exp, scale=-0.5), [BN[ri]], [BN[ri]])
                for c in range(8):
                    if final_out is None:
                        V(lambda e, c=c: e.scalar_tensor_tensor(hn[:, c, o:o + n], h[:, c, o:o + n], pv(gname, c), rs[:, 0:n],
                                                                ALU.mult, ALU.mult), [Bh[c][bi], BN[ri], Bc], [Bhn[bi]])
                    else:
                        t = NT5[c % 2]
                        V(lambda e, c=c, t=t: e.scalar_tensor_tensor(t[:, 0:n], h[:, c, o:o + n], pv(gname, c), rs[:, 0:n],
                                                                     ALU.mult, ALU.mult), [Bh[c][bi], BN[ri], Bc], [BN[c % 2]])
                        STO(final_out[c * 128:(c + 1) * 128, o:o + n], t[:, 0:n], [BN[c % 2]])

        def proj_fm(slot, bs, col, NT, evac, rhs=None, nk=8, rbufs=None):
            for bi, (o, n) in enumerate(blocks(NT)):
                ps, bp = PS()
                for k in range(nk):
                    src = hn if rhs is None else rhs
                    P(lambda e, k=k, ps=ps, src=src: e.matmul(ps[:, 0:n], slot[:, k, col:col + 128], src[:, k, o:o + n],
                                                             start=(k == 0), stop=(k == nk - 1)),
                      bs + ([Bhn[bi]] if rbufs is None else rbufs(bi)), [bp])
                evac(ps, bp, bi, o, n)

        def resid_proj(w_ap, NT, src, srcb):
            for u in range(2):
                slot, bs = wload([(w_ap[:, u * 512:(u + 1) * 512], 0)], 8)
                for oc in range(4):
                    c = u * 4 + oc

                    def ev(ps, bp, bi, o, n, c=c):
                        V(lambda e: e.tensor_tensor(h[:, c, o:o + n], h[:, c, o:o + n], ps[:, 0:n], ALU.add),
                          [bp, Bh[c][bi]], [Bh[c][bi]])
                    proj_fm(slot, bs, oc * 128, NT, ev, rhs=src, rbufs=lambda bi: [srcb[k][bi] for k in range(8)])

        def att_pre(qT, kT, bq, bk, col, ek, sample):
            ps, bp = PSS()
            P(lambda e: e.matmul(ps[:, 0:128], kT[:, col:col + 128], qT[:, col:col + 128], start=True, stop=True),
              [bq, bk], [bp])
            i2 = att_tile.k % 2
            att_tile.k += 1
            sT = stb[i2]; bsT = Bst[i2]
            msk = maskb if sample else maskc
            if ek is not None:
                V(lambda e: e.scalar_tensor_tensor(sT[:], ps[:, 0:128], ek, msk[:], ALU.mult, ALU.mult), [bp, Bek, Bc], [bsT])
            else:
                V(lambda e: e.tensor_tensor(sT[:], ps[:, 0:128], msk[:], ALU.mult), [bp, Bc], [bsT])
            P(lambda e: e.transpose(ptb[:, 0:128], kT[:, col:col + 128], identb[:]), [bk, Bc], [Bpt])
            kh = khb[i2]; bkh = Bkh[i2]
            if ek is not None:
                A(lambda e: e.activation(kh[:], ptb[:, 0:128], AF.Copy, scale=ek), [Bpt, Bek], [bkh])
            else:
                A(lambda e: e.copy(kh[:], ptb[:, 0:128]), [Bpt], [bkh])
            return (sT, bsT, kh, bkh)

        def att_tile(qT, kT, bq, bk, col, vt, E, si, ek, dec, PT, bPT, pcol, sample, den=None, mlstm_h=None, usbuf=None, bv=None, ctx=None):
            Bv = bv
            if ctx is None:
                ctx = att_pre(qT, kT, bq, bk, col, ek, sample)
            sT, bsT, kh, bkh = ctx
            P(lambda e: e.matmul(PT[:, pcol:pcol + 128], vt[:, 0:128], sT[:], start=True, stop=False), [Bv, bsT], [bPT])
            if not sample:
                P(lambda e: e.matmul(PT[:, pcol:pcol + 128], Upb[:, si, 0:128], qT[:, col:col + 128], start=False, stop=True),
                  [BU[si], bq], [bPT])
            else:
                for j in range(16):
                    P(lambda e, j=j: e.matmul(PT[:, pcol:pcol + 128], Usb[:, j, 0:128], qz[:, j, :], start=False, stop=(j == 15)),
                      [BUs, Bqz], [bPT])
            if den is not None:
                dps, bd = den
                P(lambda e: e.matmul(dps[:, pcol:pcol + 128], onesb[:], sT[:], start=True, stop=False), [Bc, bsT], [bd])
                if not sample:
                    P(lambda e: e.matmul(dps[:, pcol:pcol + 128], nbc[:, mlstm_h, :], qT[:, col:col + 128], start=False, stop=True),
                      [BU[si], bq], [bd])
                else:
                    for j in range(16):
                        P(lambda e, j=j: e.matmul(dps[:, pcol:pcol + 128], nbs[:, j, :], qz[:, j, :], start=False, stop=(j == 15)),
                          [Bnbs, Bqz], [bd])
            if not sample:
                ps2, bp2 = PSS()
                P(lambda e: e.matmul(ps2[:, 0:E], ident[:], Up[:, si, 0:E], start=True, stop=False), [Bc, BU[si]], [bp2])
                P(lambda e: e.matmul(ps2[:, 0:E], kh[:], vt[:, 0:E], start=False, stop=True), [bkh, Bv], [bp2])
                A(lambda e: e.activation(Up[:, si, 0:E], ps2[:, 0:E], AF.Copy, scale=dec), [bp2, Bdec, Bhd], [BU[si]])
                V(lambda e: e.tensor_copy(Upb[:, si, 0:E], Up[:, si, 0:E]), [BU[si]], [BU[si]])
                if mlstm_h is not None:
                    V(lambda e: e.tensor_copy(nbc[:, mlstm_h, :], Up[:, si, 128:129].broadcast_to([128, 128])), [BU[si]], [BU[si]])
            else:
                V(lambda e: e.tensor_tensor(kz[:], kh[:].unsqueeze(1).broadcast_to([128, 16, 128]),
                                            pv("rowm", 0, 16).unsqueeze(2).broadcast_to([128, 16, 128]), ALU.mult),
                  [bkh, Bc], [Bkz])
                for j in range(16):
                    ps2, bp2 = PSS()
                    P(lambda e, j=j, ps2=ps2: e.matmul(ps2[:, 0:E], ident[:], Usf[:, j, 0:E], start=True, stop=False), [Bc, BUs], [bp2])
                    P(lambda e, j=j, ps2=ps2: e.matmul(ps2[:, 0:E], kz[:, j, :], vt[:, 0:E], start=False, stop=True), [Bkz, Bv], [bp2])
                    A(lambda e, j=j, ps2=ps2: e.activation(usbuf[:, j, 0:E], ps2[:, 0:E], AF.Copy, scale=dec(j)),
                      [bp2, Bdec, Bhd], [BUs])
        att_tile.k = 0
        CTX = {}
        FFK = [0]
        PEND = [None]

        def load_sample_state(src, hh, E):
            LD(Usf[:, :, 0:E], src[:, hh, :, :].rearrange("j d e -> d j e"), [BUs])
            V(lambda e: e.tensor_copy(Usb[:, :, 0:E], Usf[:, :, 0:E]), [BUs], [BUs])

        def make_qz(qT, bq, col):
            V(lambda e: e.tensor_tensor(qz[:], qT[:, col:col + 128].unsqueeze(1).broadcast_to([128, 16, 128]), blk3[:], ALU.mult),
              [bq, Bc], [Bqz])

        def vproj(slot, bs, col, NT, E, vt_, bv_):
            nt = NT // 128
            for c in range(nt):
                ps, bp = PS()
                for k in range(8):
                    P(lambda e, k=k, ps=ps: e.matmul(ps[:, 0:128], hn[:, k, c * 128:(c + 1) * 128], slot[:, k, col:col + 128],
                                                     start=(k == 0), stop=(k == 7)), bs + [Bhn[(c * 128) // 512]], [bp])
                A(lambda e, ps=ps: e.copy(vt_[:, c, 0:128], ps[:, 0:128]), [bp], [bv_])

        def rstd_from(sq_src_fn, n, srcb):
            q = sqb[0]
            sq_src_fn(q)
            ps, bp = PSS()
            P(lambda e: e.matmul(ps[:, 0:n], onesb[:], q[:, 0:n], start=True, stop=True), [Bsq[0], Bc], [bp])
            rs = NT5[3]
            A(lambda e: e.activation(rs[:, 0:n], ps[:, 0:n], AF.Ln, scale=1.0 / 128, bias=CEPS), [bp], [BN[3]])
            A(lambda e: e.activation(rs[:, 0:n], rs[:, 0:n], AF.Exp, scale=-0.5), [BN[3]], [BN[3]])
            return rs

        def build_tab_kc(kc_):
            build_tables(kc_, pv("s0"), jrow[:, 2, :], lambda part: tbe[:, part, :], lambda part: tbz[:, part, :, :], [Btab], [Btab])
            LD(tabE[:, :, kc_ * 512:(kc_ + 1) * 512], tbe[:], [Bscr], r=[Btab])
            LD(tabZ[:, :, 4 * kc_:4 * kc_ + 4, :], tbz[:], [Bscr], r=[Btab])
        STEP = [None]

        def step():
            g = STEP[0]
            if g is not None:
                try:
                    next(g)
                except StopIteration:
                    STEP[0] = None
        CUT = int(os.environ.get("KCUT", "0"))

        class _Stop(Exception):
            pass

        def ck(k):
            if CUT == k:
                raise _Stop()
        try:
            for sbi, (tok0, NP, has_s) in enumerate(SBS):
                NT = NP + (128 if has_s else 0)
                ntp = NP // 128
                blks = blocks(NT)
                last = (sbi == len(SBS) - 1)
                LD(segm[:, 0:NTM], D["segm"][sbi], [Bc], r=[Bc])
                for c in range(8):
                    for bi, (o, n) in enumerate(blks):
                        LD(h[:, c, o:o + n], D["xT"][c * 128:(c + 1) * 128, tok0 + o:tok0 + o + n], [Bh[c][bi]])
                for layer in range(2):
                    rmsnorm("nmix%d" % layer, NT)
                    ck(1)
                    if layer == 0:
                        wgs, bwg = wload([(D["wg"], 0)], 8)
                        A1, A2, A3, A4 = Fs[2], Fs[3], Fs[5], Fs[4]
                        b1, b2, b3, b4 = BF[2], BF[3], BF[5], BF[4]
                        for bi, (o, n) in enumerate(blks):
                            ps, bp = PS()
                            for k in range(8):
                                P(lambda e, k=k, ps=ps: e.matmul(ps[0:4, 0:n], wgs[:, k, 0:4], hn[:, k, o:o + n], start=(k == 0), stop=(k == 7)),
                                  bwg + [Bhn[bi]], [bp])
                            A(lambda e, ps=ps: e.activation(A1[0:4, o:o + n], ps[0:4, 0:n], AF.Identity, bias=bg[:, 0:1]), [bp, Bc], [b1])
                            ps, bp = PS()
                            for k in range(8):
                                P(lambda e, k=k, ps=ps: e.matmul(ps[0:4, 0:n], wgs[:, k, 4:8], hn[:, k, o:o + n], start=(k == 0), stop=(k == 7)),
                                  bwg + [Bhn[bi]], [bp])
                            A(lambda e, ps=ps: e.activation(A2[0:4, o:o + n], ps[0:4, 0:n], AF.Exp, scale=-1.0, bias=nbg[:, 0:1]), [bp, Bc], [b2])
                        A(lambda e: e.activation(A2[0:4, 0:NT], A2[0:4, 0:NT], AF.Ln, bias=CONE[0:4]), [b2], [b2])
                        V(lambda e: e.memset(A4[0:4, 0:NT], 1.0), (), [b4])
                        V(lambda e: e.tensor_tensor_scan(A3[0:4, 0:NP], A4[0:4, 0:NP], A2[0:4, 0:NP], carr[:, 0:1], ALU.mult, ALU.add),
                          [b2, b4, Bcar], [b3])
                        if has_s:
                            V(lambda e: e.tensor_tensor_scan(A3[0:4, NP:NT], segm[0:4, NP:NT], A2[0:4, NP:NT], 0.0, ALU.mult, ALU.add),
                              [b2, Bc], [b3])
                        V(lambda e: e.tensor_tensor(A1[0:4, 0:NT], A1[0:4, 0:NT], A3[0:4, 0:NT], ALU.add), [b1, b3], [b1])
                        V(lambda e: e.memset(A4[0:4, 0:NT], 0.0), (), [b4])
                        V(lambda e: e.tensor_tensor_scan(A2[0:4, 0:NP], A4[0:4, 0:NP], A1[0:4, 0:NP], carr[:, 1:2], ALU.add, ALU.max),
                          [b1, b4, Bcar], [b2])
                        V(lambda e: e.tensor_copy(mxe[:, 0:1], carr[:, 1:2]), [Bcar], [Bsm])
                        V(lambda e: e.tensor_copy(mxe[:, 1:1 + ntp], A2[0:4, 0:NP].rearrange("p (c t) -> p c t", t=128)[:, :, 127]), [b2], [Bsm])
                        if has_s:
                            LD(ms0[:], D["ms"], [Bsm])
                            V(lambda e: e.tensor_copy(A4[0:4, NP:NT], A1[0:4, NP:NT]), [b1], [b4])
                            g3 = A4[0:4, NP:NT].rearrange("p (j l) -> p j l", l=8)
                            V(lambda e: e.tensor_tensor(g3[:, :, 0], g3[:, :, 0], ms0[:], ALU.max), [b4, Bsm], [b4])
                            V(lambda e: e.tensor_tensor_scan(A2[0:4, NP:NT], negm[:], A4[0:4, NP:NT], 0.0, ALU.add, ALU.max), [b4, Bc], [b2])
                        V(lambda e: e.tensor_copy(A4[0:4, 0:NP].rearrange("p (c t) -> p c t", t=128),
                                                  mxe[:, 0:ntp].unsqueeze(2).broadcast_to([4, ntp, 128])), [Bsm], [b4])
                        if has_s:
                            V(lambda e: e.tensor_copy(A4[0:4, NP:NT].rearrange("p (j l) -> p j l", l=8),
                                                      ms0[:].unsqueeze(2).broadcast_to([4, 16, 8])), [Bsm], [b4])
                        V(lambda e: e.tensor_tensor(decrow[:, 0:ntp], mxe[:, 0:ntp], mxe[:, 1:1 + ntp], ALU.subtract), [Bsm], [Bsm])
                        if has_s:
                            V(lambda e: e.tensor_tensor(decrow[:, 8:24], ms0[:], A2[0:4, NP:NT].rearrange("p (j l) -> p j l", l=8)[:, :, 7],
                                                        ALU.subtract), [Bsm, b2], [Bsm])
                        else:
                            V(lambda e: e.memset(decrow[:, 8:24], 0.0), (), [Bsm])
                        if ntp < 8:
                            V(lambda e: e.memset(decrow[:, ntp:8], 0.0), (), [Bsm])
                        A(lambda e: e.activation(decrow[:], decrow[:], AF.Exp), [Bsm], [Bsm])
                        ps, bp = PSS()
                        for hh in range(4):
                            P(lambda e, hh=hh, ps=ps: e.matmul(ps[:, hh * 24:(hh + 1) * 24], sel[:, hh, :], decrow[:], start=True, stop=True),
                              [Bc, Bsm], [bp])
                        V(lambda e, ps=ps: e.tensor_copy(decbc[:].rearrange("p a b -> p (a b)"), ps[:, 0:96]), [bp], [Bdec])
                        if last:
                            V(lambda e: e.tensor_tensor(msout[:, 16:17], A2[0:4, NP - 1:NP], A3[0:4, NP - 1:NP], ALU.subtract), [b2, b3], [Bsm])
                            V(lambda e: e.tensor_tensor(msout[:, 0:16], A2[0:4, NP:NT].rearrange("p (j l) -> p j l", l=8)[:, :, 7],
                                                        A3[0:4, NP:NT].rearrange("p (j l) -> p j l", l=8)[:, :, 7], ALU.subtract), [b2, b3], [Bsm])
                            STO(D["mp"], msout[:, 16:17], [Bsm]); STO(D["ms_o"], msout[:, 0:16], [Bsm])
                        V(lambda e: e.tensor_copy(carr[:, 0:1], A3[0:4, NP - 1:NP]), [b3], [Bcar])
                        V(lambda e: e.tensor_copy(carr[:, 1:2], A2[0:4, NP - 1:NP]), [b2], [Bcar])
                        V(lambda e: e.tensor_tensor(A3[0:4, 0:NT], A4[0:4, 0:NT], A3[0:4, 0:NT], ALU.subtract), [b3, b4], [b3])
                        V(lambda e: e.tensor_tensor(A1[0:4, 0:NT], A1[0:4, 0:NT], A4[0:4, 0:NT], ALU.subtract), [b1, b4], [b1])
                        A(lambda e: e.activation(A1[0:4, 0:NT], A1[0:4, 0:NT], AF.Exp, bias=CLNK[0:4]), [b1], [b1])
                        ps, bp = PSS()
                        for c in range(NT // 128):
                            P(lambda e, c=c, ps=ps: e.matmul(ps[:, c * 4:(c + 1) * 4], A1[0:4, c * 128:(c + 1) * 128], ident[0:4, 0:4],
                                                             start=True, stop=True), [b1, Bc], [bp])
                        V(lambda e, ps=ps: e.tensor_copy(ektm[:, 0:NT // 128, :].rearrange("p a b -> p (a b)"), ps[:, 0:4 * (NT // 128)]),
                          [bp], [Bek])
                        ck(2)
                        C0, S0 = Fs[6], Fs[7]
                        LD(Fs[8][:, 0:NT], D["posrow"][sbi][:, 0:NT], [BF[8]])
                        V(lambda e: e.tensor_scalar(Fs[8][:, 0:NT], Fs[8][:, 0:NT], pv("invf"), None, ALU.mult), [BF[8], Bc], [BF[8]])
                        sin_of(C0[:, 0:NT], Fs[8][:, 0:NT], 0.5 * PI, Fs[2][:, 0:NT], [BF[8]], [BF[6]], BF[2])
                        sin_of(S0[:, 0:NT], Fs[8][:, 0:NT], 0.0, Fs[2][:, 0:NT], [BF[8]], [BF[7]], BF[2])
                        V(lambda e: e.tensor_scalar(S0[:, 0:NT], S0[:, 0:NT], pv("sgn"), None, ALU.mult), [BF[7], Bc], [BF[7]])
                        V(lambda e: e.memset(vtm[:, :, 128:129], 1.0), (), [Bv])
                        V(lambda e: e.memset(vtm2[:, :, 128:129], 1.0), (), [Bv2])
                        SETS = [(Hs[0], Hs[1], Hs[2], vtm, BH[0], BH[1], BH[2], Bv), (Hs[3], Hs[4], gT2, vtm2, BH[3], BH[4], Bg2, Bv2)]
                        xq, xk, qT, kT, gT = Fs[0], Fs[1], Hs[0], Hs[1], Hs[2]
                        bxq, bxk, bqT, bkT, bgT = BF[0], BF[1], BH[0], BH[1], BH[2]
                        W = D["w_in_ab"]
                        def front_m(hh):
                            qT, kT, gT, vt_, bqT, bkT, bgT, bv_ = SETS[hh % 2]
                            slot, bs = wload([(D["w_ab_h"][:, hh * 512:(hh + 1) * 512], 0)], 8)
                            for (xx, bx, cc) in ((xq, bxq, 0), (xk, bxk, 128)):
                                def ev(ps, bp, bi, o, n, xx=xx, bx=bx):
                                    if o < NP:
                                        A(lambda e: e.copy(xx[:, 3 + o:3 + o + n], ps[:, 0:n]), [bp], [bx])
                                    else:
                                        A(lambda e: e.copy(xx[:, NP + 3:NP + 3 + 176].rearrange("p (j l) -> p j l", l=11)[:, :, 3:11],
                                                           ps[:, 0:128].rearrange("p (j l) -> p j l", l=8)), [bp], [bx])
                                proj_fm(slot, bs, cc, NT, ev)
                                yield

                            def evg(ps, bp, bi, o, n):
                                A(lambda e: e.activation(gT[:, o:o + n], ps[:, 0:n], AF.Sigmoid), [bp], [bgT])
                            proj_fm(slot, bs, 384, NT, evg)
                            yield
                            vproj(slot, bs, 256, NT, 129, vt_, bv_)
                            yield
                            for (xx, bx, ch, oT, boT) in ((xq, bxq, hh, qT, bqT), (xk, bxk, 4 + hh, kT, bkT)):
                                V(lambda e, xx=xx, ch=ch: e.tensor_copy(xx[:, 0:3], tails[:, ch, :]), [Btl], [bx])
                                acc = Fs[2]
                                V(lambda e, xx=xx, ch=ch: e.tensor_scalar(acc[:, 0:NP], xx[:, 0:NP], pv("cw0", ch), pv("cb", ch), ALU.mult, ALU.add),
                                  [bx, Bc], [BF[2]])
                                for j in range(1, 4):
                                    V(lambda e, xx=xx, ch=ch, j=j: e.scalar_tensor_tensor(acc[:, 0:NP], xx[:, j:j + NP], pv("cw%d" % j, ch), acc[:, 0:NP],
                                                                                         ALU.mult, ALU.add), [bx, Bc, BF[2]], [BF[2]])
                                if has_s:
                                    LD(xx[:, NP + 3:NP + 3 + 176].rearrange("p (j l) -> p j l", l=11)[:, :, 0:3],
                                       D["convs"][ch * 128:(ch + 1) * 128], [bx])
                                    xs3 = xx[:, NP + 3:NP + 3 + 176].rearrange("p (j l) -> p j l", l=11)
                                    a3 = acc[:, NP:NT].rearrange("p (j l) -> p j l", l=8)
                                    V(lambda e, xs3=xs3, a3=a3, ch=ch: e.tensor_scalar(a3, xs3[:, :, 0:8], pv("cw0", ch), pv("cb", ch), ALU.mult, ALU.add),
                                      [bx, Bc], [BF[2]])
                                    for j in range(1, 4):
                                        V(lambda e, xs3=xs3, a3=a3, ch=ch, j=j: e.scalar_tensor_tensor(a3, xs3[:, :, j:j + 8], pv("cw%d" % j, ch), a3,
                                                                                                      ALU.mult, ALU.add), [bx, Bc, BF[2]], [BF[2]])
                                    STO(D["convs_o"][ch * 128:(ch + 1) * 128], xs3[:, :, 8:11], [bx])
                                    STO(D["convp"][ch * 128:(ch + 1) * 128], xx[:, NP:NP + 3], [bx])
                                A(lambda e, oT=oT: e.activation(oT[:, 0:NT], acc[:, 0:NT], AF.Silu), [BF[2]], [boT])
                                V(lambda e, xx=xx, ch=ch: e.tensor_copy(tails[:, ch, :], xx[:, NP:NP + 3]), [bx], [Btl])
                                yield

                        def back_m(hh):
                            qT, kT, gT, vt_, bqT, bkT, bgT, bv_ = SETS[hh % 2]
                            if has_s:
                                load_sample_state(D["Us"], hh, 129)
                                make_qz(qT, bqT, NP)
                                V(lambda e: e.tensor_copy(nbs[:], Usf[:, :, 128:129].broadcast_to([128, 16, 128])), [BUs], [Bnbs])
                            for bi, (o, n) in enumerate(blks):
                                PT, bPT = PS()
                                dps, bd = PS()
                                pinned.update((pb.index(PT), pb.index(dps)))
                                for c in range(n // 128):
                                    tcol = o + c * 128
                                    tix = tcol // 128
                                    smp = tcol >= NP
                                    ekf = lambda t_: ektm[:, t_ // 128, hh:hh + 1]
                                    ctx_ = CTX.pop(tcol, None) or att_pre(qT, kT, bqT, bkT, tcol, ekf(tcol), smp)
                                    if tcol + 128 < NT:
                                        CTX[tcol + 128] = att_pre(qT, kT, bqT, bkT, tcol + 128, ekf(tcol + 128), tcol + 128 >= NP)
                                    att_tile(qT, kT, bqT, bkT, tcol, vt_[:, tix, :], 129, hh, ektm[:, tix, hh:hh + 1],
                                             (lambda j, hh=hh: decbc[:, hh, 8 + j:9 + j]) if smp else decbc[:, hh, tix:tix + 1],
                                             PT, bPT, c * 128, smp, den=(dps, bd), mlstm_h=hh, usbuf=Usf, bv=bv_, ctx=ctx_)
                                    step()
                                ps, bp = PSS()
                                P(lambda e, ps=ps, hh=hh: e.matmul(ps[:, 0:n], sel[:, hh, :], A3[0:4, o:o + n], start=True, stop=True), [Bc, b3], [bp])
                                dn = NT5[0]
                                A(lambda e, ps=ps: e.activation(dn[:, 0:n], ps[:, 0:n], AF.Exp, scale=-1.0), [bp], [BN[0]])
                                ab = NT5[1]
                                A(lambda e, dps=dps: e.activation(ab[:, 0:n], dps[:, 0:n], AF.Abs), [bd], [BN[1]])
                                V(lambda e: e.tensor_tensor(ab[:, 0:n], ab[:, 0:n], dn[:, 0:n], ALU.max), [BN[0], BN[1]], [BN[1]])
                                V(lambda e: e.reciprocal(ab[:, 0:n], ab[:, 0:n]), [BN[1]], [BN[1]])
                                hv = NT5[2]
                                V(lambda e, PT=PT: e.tensor_tensor(hv[:, 0:n], PT[:, 0:n], ab[:, 0:n], ALU.mult), [bPT, BN[1]], [BN[2]])
                                V(lambda e: e.tensor_tensor(hv[:, 0:n], hv[:, 0:n], gT[:, o:o + n], ALU.mult), [BN[2], bgT], [BN[2]])
                                rs = rstd_from(lambda q: A(lambda e: e.activation(q[:, 0:n], hv[:, 0:n], AF.Square), [BN[2]], [Bsq[0]]), n, None)
                                V(lambda e, hh=hh: e.scalar_tensor_tensor(mix[:, hh, o:o + n], hv[:, 0:n], pv("gna", hh), rs[:, 0:n], ALU.mult, ALU.mult),
                                  [BN[2], BN[3], Bc], [Bmix[hh][bi]])
                                pinned.clear()
                                step()
                            if has_s:
                                STO(D["Us_o"][:, hh, :, :].rearrange("j d e -> d j e"), Usf[:, :, :], [BUs])
                            if last:
                                STO(D["Up"][hh], Up[:, hh, :], [BU[hh]])
                        for _ in front_m(0):
                            pass
                        for hh in range(4):
                            STEP[0] = front_m(hh + 1) if hh + 1 < 4 else None
                            back_m(hh)
                            while STEP[0] is not None:
                                step()
                        ck(3)
                        def front_r(hh):
                            qT, kT, gT, vt_, bqT, bkT, bgT, bv_ = SETS[hh % 2]
                            b0 = 2056
                            slot, bs = wload([(D["w_ab_h"][:, (4 + hh) * 512:(5 + hh) * 512], 0)], 8)
                            for (xx, bx, cc, oT, boT) in ((xq, bxq, 0, qT, bqT), (xk, bxk, 128, kT, bkT)):
                                def ev(ps, bp, bi, o, n, xx=xx, bx=bx):
                                    A(lambda e: e.copy(xx[:, o:o + n], ps[:, 0:n]), [bp], [bx])
                                proj_fm(slot, bs, cc, NT, ev)
                                yield
                                xsw, t1, t2 = Fs[2], Fs[3], Fs[4]
                                A(lambda e, xx=xx: e.copy(xsw[0:64, 0:NT], xx[64:128, 0:NT]), [bx], [BF[2]])
                                A(lambda e, xx=xx: e.copy(xsw[64:128, 0:NT], xx[0:64, 0:NT]), [bx], [BF[2]])
                                V(lambda e, xx=xx: e.tensor_tensor(t1[:, 0:NT], xx[:, 0:NT], C0[:, 0:NT], ALU.mult), [bx, BF[6]], [BF[3]])
                                V(lambda e: e.tensor_tensor(t2[:, 0:NT], xsw[:, 0:NT], S0[:, 0:NT], ALU.mult), [BF[2], BF[7]], [BF[4]])
                                V(lambda e, oT=oT: e.tensor_tensor(oT[:, 0:NT], t1[:, 0:NT], t2[:, 0:NT], ALU.add), [BF[3], BF[4]], [boT])

                            def evg(ps, bp, bi, o, n):
                                A(lambda e: e.activation(gT[:, o:o + n], ps[:, 0:n], AF.Silu), [bp], [bgT])
                            proj_fm(slot, bs, 384, NT, evg)
                            yield
                            vproj(slot, bs, 256, NT, 128, vt_, bv_)
                            yield

                        def back_r(hh):
                            qT, kT, gT, vt_, bqT, bkT, bgT, bv_ = SETS[hh % 2]
                            if has_s:
                                load_sample_state(D["rets"], hh, 128)
                                make_qz(qT, bqT, NP)
                            g128 = math.exp(128 * LG[hh]); g8 = math.exp(8 * LG[hh])
                            for bi, (o, n) in enumerate(blks):
                                PT, bPT = PS()
                                pinned.add(pb.index(PT))
                                for c in range(n // 128):
                                    tcol = o + c * 128
                                    tix = tcol // 128
                                    smp = tcol >= NP
                                    ekf = lambda t_: gk[:, 1 if t_ >= NP else 0, hh:hh + 1]
                                    ctx_ = CTX.pop(tcol, None) or att_pre(qT, kT, bqT, bkT, tcol, ekf(tcol), smp)
                                    if tcol + 128 < NT:
                                        CTX[tcol + 128] = att_pre(qT, kT, bqT, bkT, tcol + 128, ekf(tcol + 128), tcol + 128 >= NP)
                                    att_tile(qT, kT, bqT, bkT, tcol, vt_[:, tix, :], 128, 4 + hh, gk[:, 1 if smp else 0, hh:hh + 1],
                                             (lambda j, g8=g8: g8) if smp else g128, PT, bPT, c * 128, smp, usbuf=Usf, bv=bv_, ctx=ctx_)
                                    step()
                                    if PEND[0] is not None and c == 0:
                                        PEND[0]()
                                        PEND[0] = None
                                def chain_(PT=PT, bPT=bPT, o=o, n=n, bi=bi):
                                    hv = NT5[2]
                                    if o < NP:
                                        V(lambda e, PT=PT, hh=hh: e.tensor_tensor(hv[:, 0:n].rearrange("p (c t) -> p c t", t=128),
                                                                                 PT[:, 0:n].rearrange("p (c t) -> p c t", t=128),
                                                                                 Gq[:, 0, hh, :].unsqueeze(1).broadcast_to([128, n // 128, 128]), ALU.mult),
                                          [bPT, Bc], [BN[2]])
                                    else:
                                        V(lambda e, PT=PT, hh=hh: e.tensor_tensor(hv[:, 0:n], PT[:, 0:n], Gq[:, 1, hh, :], ALU.mult), [bPT, Bc], [BN[2]])
                                    rs = rstd_from(lambda q: A(lambda e: e.activation(q[:, 0:n], hv[:, 0:n], AF.Square), [BN[2]], [Bsq[0]]), n, None)
                                    V(lambda e: e.tensor_tensor(hv[:, 0:n], hv[:, 0:n], rs[:, 0:n], ALU.mult), [BN[2], BN[3]], [BN[2]])
                                    V(lambda e, hh=hh: e.tensor_tensor(mix[:, 4 + hh, o:o + n], hv[:, 0:n], gT[:, o:o + n], ALU.mult),
                                      [BN[2], bgT], [Bmix[4 + hh][bi]])
                                    pinned.discard(pb.index(PT))
                                    step()
                                PEND[0] = chain_
                            if PEND[0] is not None:
                                PEND[0]()
                                PEND[0] = None
                            if has_s:
                                STO(D["rets_o"][:, hh, :, :].rearrange("j d e -> d j e"), Usf[:, :, 0:128], [BUs])
                            if last:
                                STO(D["retp"][hh], Up[:, 4 + hh, 0:128], [BU[4 + hh]])
                        for _ in front_r(0):
                            pass
                        for hh in range(4):
                            STEP[0] = front_r(hh + 1) if hh + 1 < 4 else None
                            back_r(hh)
                            while STEP[0] is not None:
                                step()
                        resid_proj(D["w_out_ab"][0] if False else D["w_out_ab"], NT, mix, Bmix)
                        ck(4)
                    else:
                        W = D["w_in_cd"]
                        qf, ff, eb, enb, tmp = Fs[0], Fs[1], Fs[2], Fs[3], Fs[4]
                        qT, kT, gT = Hs[0], Hs[1], Hs[2]
                        bqT, bkT, bgT = BH[0], BH[1], BH[2]
                        def front_h(hh):
                            qT, kT, gT, vt_, bqT, bkT, bgT, bv_ = SETS[hh % 2]
                            slot, bs = wload([(D["w_cd_h"][:, hh * 512:(hh + 1) * 512], 0)], 8)

                            def evq(ps, bp, bi, o, n):
                                A(lambda e: e.activation(qf[:, o:o + n], ps[:, 0:n], AF.Copy, scale=128.0 ** -0.5), [bp], [BF[0]])
                            proj_fm(slot, bs, 0, NT, evq)
                            yield

                            def evf(ps, bp, bi, o, n):
                                A(lambda e: e.activation(ff[:, o:o + n], ps[:, 0:n], AF.Sigmoid), [bp], [BF[1]])
                            proj_fm(slot, bs, 128, NT, evf)
                            yield

                            def evg(ps, bp, bi, o, n):
                                A(lambda e: e.activation(gT[:, o:o + n], ps[:, 0:n], AF.Silu), [bp], [bgT])
                            proj_fm(slot, bs, 384, NT, evg)
                            yield
                            vproj(slot, bs, 256, NT, 128, vt_, bv_)
                            yield
                            V(lambda e, hh=hh: e.tensor_scalar(ff[:, 0:NT], ff[:, 0:NT], oml[:, hh:hh + 1], lb[:, hh:hh + 1], ALU.mult, ALU.add),
                              [BF[1], Bc], [BF[1]])
                            A(lambda e: e.activation(tmp[:, 0:NT], ff[:, 0:NT], AF.Ln), [BF[1]], [BF[4]])
                            V(lambda e: e.tensor_tensor_scan(eb[:, 0:NT], segm[:, 0:NT], tmp[:, 0:NT], 0.0, ALU.mult, ALU.add), [BF[4], Bc], [BF[2]])
                            A(lambda e: e.activation(enb[:, 0:NT], eb[:, 0:NT], AF.Exp, scale=-1.0), [BF[2]], [BF[3]])
                            A(lambda e: e.activation(eb[:, 0:NT], eb[:, 0:NT], AF.Exp), [BF[2], BF[3]], [BF[2]])
                            V(lambda e, hh=hh: e.tensor_copy(hdec[:, hh % 2, 0:ntp], eb[:, 0:NP].rearrange("p (c t) -> p c t", t=128)[:, :, 127]), [BF[2]], [Bhd])
                            if has_s:
                                V(lambda e, hh=hh: e.tensor_copy(hdec[:, hh % 2, 8:24], eb[:, NP:NT].rearrange("p (j l) -> p j l", l=8)[:, :, 7]), [BF[2]], [Bhd])
                            V(lambda e: e.tensor_scalar(ff[:, 0:NT], ff[:, 0:NT], -1.0, 1.0, ALU.mult, ALU.add), [BF[1], BF[4]], [BF[1]])
                            V(lambda e: e.tensor_tensor(qT[:, 0:NT], qf[:, 0:NT], eb[:, 0:NT], ALU.mult), [BF[0], BF[2]], [bqT])
                            V(lambda e: e.tensor_tensor(kT[:, 0:NT], ff[:, 0:NT], enb[:, 0:NT], ALU.mult), [BF[1], BF[3]], [bkT])

                        def back_h(hh):
                            qT, kT, gT, vt_, bqT, bkT, bgT, bv_ = SETS[hh % 2]
                            if has_s:
                                load_sample_state(D["hgrns"], hh, 128)
                                make_qz(qT, bqT, NP)
                            for bi, (o, n) in enumerate(blks):
                                PT, bPT = PS()
                                pinned.add(pb.index(PT))
                                for c in range(n // 128):
                                    tcol = o + c * 128
                                    tix = tcol // 128
                                    smp = tcol >= NP
                                    ctx_ = CTX.pop(tcol, None) or att_pre(qT, kT, bqT, bkT, tcol, None, smp)
                                    if tcol + 128 < NT:
                                        CTX[tcol + 128] = att_pre(qT, kT, bqT, bkT, tcol + 128, None, tcol + 128 >= NP)
                                    att_tile(qT, kT, bqT, bkT, tcol, vt_[:, tix, :], 128, 8 + hh, None,
                                             (lambda j, hh=hh: hdec[:, hh % 2, 8 + j:9 + j]) if smp else hdec[:, hh % 2, tix:tix + 1],
                                             PT, bPT, c * 128, smp, usbuf=Usf, bv=bv_, ctx=ctx_)
                                    step()
                                    if PEND[0] is not None and c == 0:
                                        PEND[0]()
                                        PEND[0] = None
                                def chain_(PT=PT, bPT=bPT, o=o, n=n, bi=bi):
                                    rs = rstd_from(lambda q, PT=PT, bPT=bPT: A(lambda e: e.activation(q[:, 0:n], PT[:, 0:n], AF.Square), [bPT], [Bsq[0]]), n, None)
                                    hv = NT5[2]
                                    V(lambda e, PT=PT: e.tensor_tensor(hv[:, 0:n], PT[:, 0:n], rs[:, 0:n], ALU.mult), [bPT, BN[3]], [BN[2]])
                                    V(lambda e, hh=hh: e.scalar_tensor_tensor(mix[:, hh, o:o + n], hv[:, 0:n], pv("gnc", hh), gT[:, o:o + n], ALU.mult, ALU.mult),
                                      [BN[2], bgT, Bc], [Bmix[hh][bi]])
                                    pinned.discard(pb.index(PT))
                                    step()
                                PEND[0] = chain_
                            if PEND[0] is not None:
                                PEND[0]()
                                PEND[0] = None
                            if has_s:
                                STO(D["hgrns_o"][:, hh, :, :].rearrange("j d e -> d j e"), Usf[:, :, 0:128], [BUs])
                            if last:
                                STO(D["hgrnp"][hh], Up[:, 8 + hh, 0:128], [BU[8 + hh]])
                        for _ in front_h(0):
                            pass
                        for hh in range(4):
                            STEP[0] = front_h(hh + 1) if hh + 1 < 4 else None
                            back_h(hh)
                            while STEP[0] is not None:
                                step()
                        ck(7)
                        suf = Fs[8]; bsuf = BF[8]
                        V(lambda e: e.memset(cch[:, 5, 0:1], 0.0), (), S5FINE + [BUs, Bkz, Bqz, Bcch, Bg2, Bv2])
                        pinned.clear(); S5MODE[0] = True
                        sub = Hs[0]; bsub = BH[0]
                        if has_s:
                            LD(x0s[:, 0], D["x0re"], [Bx0]); LD(x0s[:, 1], D["x0im"], [Bx0])
                            bz = lambda v: v.unsqueeze(2).broadcast_to([128, 16, 16])
                            u1 = Fs[6][:, 0:256].rearrange("p (a b) -> p a b", a=16); u2 = Fs[7][:, 0:256].rearrange("p (a b) -> p a b", a=16)
                            u3 = Fs[6][:, 256:512].rearrange("p (a b) -> p a b", a=16); u4 = Fs[7][:, 256:512].rearrange("p (a b) -> p a b", a=16)
                            for (cr, ci) in ((izr, izi), (ar_, ai_)):
                                V(lambda e, cr=cr: e.tensor_tensor(u1, x0s[:, 0], bz(cr), ALU.mult), [Bx0, Bc], [BF[6]])
                                V(lambda e, ci=ci: e.tensor_tensor(u2, x0s[:, 1], bz(ci), ALU.mult), [Bx0, Bc], [BF[7]])
                                V(lambda e, ci=ci: e.tensor_tensor(u3, x0s[:, 0], bz(ci), ALU.mult), [Bx0, Bc], [BF[6]])
                                V(lambda e, cr=cr: e.tensor_tensor(u4, x0s[:, 1], bz(cr), ALU.mult), [Bx0, Bc], [BF[7]])
                                V(lambda e: e.tensor_tensor(x0s[:, 0], u1, u2, ALU.subtract), [BF[6], BF[7]], [Bx0])
                                V(lambda e: e.tensor_tensor(x0s[:, 1], u3, u4, ALU.add), [BF[6], BF[7]], [Bx0])
                        ntl = NT // 128
                        slot_su, bs_su = wload([(W[:, 2048:2560], 0)], 8)
                        for oc in range(4):
                            slot, bs = slot_su, bs_su

                            def evs(ps, bp, bi, o, n):
                                A(lambda e: e.copy(suf[:, o:o + n], ps[:, 0:n]), [bp], [bsuf])
                                A(lambda e: e.copy(sub[:, o:o + n], ps[:, 0:n]), [bp], [bsub])
                            proj_fm(slot, bs, oc * 128, NT, evs)
                            for part in range(2):
                                S.dma("pool", lambda e, part=part, oc=oc: e.dma_start(out=bct[:, 0, part], in_=D["BT"][part, oc * 4:(oc + 1) * 4].rearrange("i k m -> k i m")),
                                      bcsem[part * 2], (), (Bbct if part == 0 else [Bbct[part * 2]]))
                                S.dma("pool", lambda e, part=part, oc=oc: e.dma_start(out=bct[:, 1, part], in_=D["CT"][part, oc * 4:(oc + 1) * 4].rearrange("i k m -> k i m")),
                                      bcsem[part * 2 + 1], (), [Bbct[part * 2 + 1]])
                            V(lambda e: e.tensor_scalar(bct[:, 1, 1], bct[:, 1, 1], -1.0, None, ALU.mult), Bbct, [Bbct[3]])
                            LD(tbe[:], tabE[:, :, oc * 512:(oc + 1) * 512], [Btab], r=[Bscr])
                            LD(tbz[:], tabZ[:, :, 4 * oc:4 * oc + 4, :], [Btab], r=[Bscr])
                            if has_s:
                                EsT = [Hs[1][:, 0:512], Hs[2][:, 0:512]]
                                ZsT = [Hs[3][:, 0:512], Hs[4][:, 0:512]]
                                build_tables(oc, pv("s8"), jrow[:, 3, :], lambda part: EsT[part],
                                             lambda part: ZsT[part].rearrange("p (a b) -> p a b", a=4), [Bes, BH[1], BH[2]], [Bes, BH[3], BH[4]])
                                for part in range(2):
                                    V(lambda e, part=part, oc=oc: e.tensor_copy(cbs[:, part].rearrange("p a (j l) -> p a j l", l=8),
                                                                             x0s[:, part, 4 * oc:4 * oc + 4, :].unsqueeze(3).broadcast_to([128, 4, 16, 8])),
                                      [Bx0], [Bcbs])
                            i4 = slice(4 * oc, 4 * oc + 4)
                            ypsh = [None]

                            def stA(c):
                                smp = c * 128 >= NP; tc0 = c * 128; k2 = c % 2
                                wrb, wib = wtb[k2][0], wtb[k2][1]; xrb, xib = xtb[k2][0], xtb[k2][1]
                                bE = Bes if smp else Btab
                                smp = c * 128 >= NP
                                tc0 = c * 128
                                Er = EsT[0] if smp else tbe[:, 0, :]
                                Ei = EsT[1] if smp else tbe[:, 1, :]
                                bE = Bes if smp else Btab
                                k2 = c % 2
                                pr, bpr = PS(); pi_, bpi = PS()
                                P(lambda e, pr=pr: e.matmul(pr[:, 0:512], sub[:, tc0:tc0 + 128], bct[:, 0, 0].rearrange("p a b -> p (a b)"), start=True, stop=True),
                                  Bbct + [bsub], [bpr])
                                P(lambda e, pi_=pi_: e.matmul(pi_[:, 0:512], sub[:, tc0:tc0 + 128], bct[:, 0, 1].rearrange("p a b -> p (a b)"), start=True, stop=True),
                                  Bbct + [bsub], [bpi])
                                prb = qzf[:, (2 * k2) * 512:(2 * k2 + 1) * 512]; pib = qzf[:, (2 * k2 + 1) * 512:(2 * k2 + 2) * 512]
                                A(lambda e, pr=pr: e.copy(prb, pr[:, 0:512]), [bpr], [Bprb[k2]])
                                A(lambda e, pi_=pi_: e.copy(pib, pi_[:, 0:512]), [bpi], [Bprb[k2]])
                                wrb, wib = wtb[k2][0], wtb[k2][1]
                                ta, tb, tc_, td = [kzf[:, q_ * 512:(q_ + 1) * 512] for q_ in range(4)]
                                V(lambda e: e.tensor_tensor(ta, prb, Er, ALU.mult), [Bprb[k2], bE], [Bt4[0]])
                                V(lambda e: e.tensor_tensor(tb, pib, Ei, ALU.mult), [Bprb[k2], bE], [Bt4[1]])
                                V(lambda e: e.tensor_tensor(wrb[:], ta, tb, ALU.subtract), [Bt4[0], Bt4[1]], [Bwt[k2]])
                                V(lambda e: e.tensor_tensor(tc_, pib, Er, ALU.mult), [Bprb[k2], bE], [Bt4[2]])
                                V(lambda e: e.tensor_tensor(td, prb, Ei, ALU.mult), [Bprb[k2], bE], [Bt4[3]])
                                V(lambda e: e.tensor_tensor(wib[:], tc_, td, ALU.add), [Bt4[2], Bt4[3]], [Bwt[k2]])

                            TC = {}

                            def stB1(c):
                                smp = c * 128 >= NP; tc0 = c * 128; k2 = c % 2
                                wrb, wib = wtb[k2][0], wtb[k2][1]; xrb, xib = xtb[k2][0], xtb[k2][1]
                                bE = Bes if smp else Btab
                                csr, bcsr = PSS(); csi, bcsi = PSS()
                                cur = S5PAR[oc]
                                TC[c] = (csr, bcsr, csi, bcsi, cur)
                                msk = maskb if smp else maskc
                                for il in range(4):
                                    P(lambda e, il=il, csr=csr: e.matmul(csr[:, il * 128:(il + 1) * 128], wrb[:, il * 128:(il + 1) * 128], msk[:], start=True, stop=(not smp)),
                                      [Bwt[k2], Bc], [bcsr])
                                    P(lambda e, il=il, csi=csi: e.matmul(csi[:, il * 128:(il + 1) * 128], wib[:, il * 128:(il + 1) * 128], msk[:], start=True, stop=(not smp)),
                                      [Bwt[k2], Bc], [bcsi])
                                    if smp:
                                        P(lambda e, il=il, csr=csr: e.matmul(csr[:, il * 128:(il + 1) * 128], identb[:], cbs[:, 0, il, :], start=False, stop=True), [Bc, Bcbs], [bcsr])
                                        P(lambda e, il=il, csi=csi: e.matmul(csi[:, il * 128:(il + 1) * 128], identb[:], cbs[:, 1, il, :], start=False, stop=True), [Bc, Bcbs], [bcsi])
                                cs3r = csr[:, 0:512].rearrange("p (a b) -> p a b", a=4); cs3i = csi[:, 0:512].rearrange("p (a b) -> p a b", a=4)
                                if not smp:
                                    nxt = 1 - cur
                                    V(lambda e: e.tensor_tensor(cch[:, 0, :], cs3r[:, :, 127], s5d[:, cur, 0, i4], ALU.add), [bcsr, Bs5d[cur]], [Bcr])
                                    V(lambda e: e.tensor_tensor(cch[:, 1, :], cs3i[:, :, 127], s5d[:, cur, 1, i4], ALU.add), [bcsi, Bs5d[cur]], [Bcr])
                                    V(lambda e: e.tensor_tensor(cch[:, 2, :], cch[:, 0, :], a128r[:, i4], ALU.mult), [Bcr, Bc], [Bcch])
                                    V(lambda e: e.tensor_tensor(cch[:, 3, :], cch[:, 1, :], a128i[:, i4], ALU.mult), [Bcr, Bc], [Bcch])
                                    V(lambda e: e.tensor_tensor(cch[:, 4, :], cch[:, 0, :], a128i[:, i4], ALU.mult), [Bcr, Bc], [Bcch])
                                    V(lambda e: e.tensor_tensor(cch[:, 5, :], cch[:, 1, :], a128r[:, i4], ALU.mult), [Bcr, Bc], [Bcch])
                                    V(lambda e: e.tensor_tensor(s5d[:, nxt, 0, i4], cch[:, 2, :], cch[:, 3, :], ALU.subtract), [Bcch], [Bs5d[nxt]])
                                    V(lambda e: e.tensor_tensor(s5d[:, nxt, 1, i4], cch[:, 4, :], cch[:, 5, :], ALU.add), [Bcch], [Bs5d[nxt]])
                                    S5PAR[oc] = nxt

                            def stB2(c):
                                smp = c * 128 >= NP; tc0 = c * 128; k2 = c % 2
                                wrb, wib = wtb[k2][0], wtb[k2][1]; xrb, xib = xtb[k2][0], xtb[k2][1]
                                csr, bcsr, csi, bcsi, cur = TC.pop(c)
                                csbr = usbf[:, (2 * k2) * 512:(2 * k2 + 1) * 512]; csbi = usbf[:, (2 * k2 + 1) * 512:(2 * k2 + 2) * 512]
                                if not smp:
                                    for il in range(4):
                                        i = 4 * oc + il
                                        sl = slice(il * 128, (il + 1) * 128)
                                        A(lambda e, sl=sl, i=i, csr=csr: e.activation(csbr[:, sl], csr[:, sl], AF.Identity, bias=s5d[:, cur, 0, i:i + 1]), [bcsr, Bs5d[cur], Bcr], [Bcsb[k2]])
                                        A(lambda e, sl=sl, i=i, csi=csi: e.activation(csbi[:, sl], csi[:, sl], AF.Identity, bias=s5d[:, cur, 1, i:i + 1]), [bcsi, Bs5d[cur], Bcr], [Bcsb[k2]])
                                    Zr = tbz[:, 0].rearrange("p a b -> p (a b)"); Zi = tbz[:, 1].rearrange("p a b -> p (a b)"); bZ = Btab
                                else:
                                    A(lambda e, csr=csr: e.copy(csbr, csr[:, 0:512]), [bcsr], [Bcsb[k2]])
                                    A(lambda e, csi=csi: e.copy(csbi, csi[:, 0:512]), [bcsi], [Bcsb[k2]])
                                    Zr = ZsT[0]; Zi = ZsT[1]; bZ = Bes
                                p1, p2, p3, p4 = [pt4[q_] for q_ in range(4)]
                                xrb, xib = xtb[k2][0], xtb[k2][1]
                                V(lambda e: e.tensor_tensor(p1[:], csbr, Zr, ALU.mult), [Bcsb[k2], bZ], [Bp4[0]])
                                V(lambda e: e.tensor_tensor(p2[:], csbi, Zi, ALU.mult), [Bcsb[k2], bZ], [Bp4[1]])
                                V(lambda e: e.tensor_tensor(xrb[:], p1[:], p2[:], ALU.subtract), [Bp4[0], Bp4[1]], [Bxt[k2]])
                                V(lambda e: e.tensor_tensor(p3[:], csbr, Zi, ALU.mult), [Bcsb[k2], bZ], [Bp4[2]])
                                V(lambda e: e.tensor_tensor(p4[:], csbi, Zr, ALU.mult), [Bcsb[k2], bZ], [Bp4[3]])
                                V(lambda e: e.tensor_tensor(xib[:], p3[:], p4[:], ALU.add), [Bp4[2], Bp4[3]], [Bxt[k2]])
                                if last and (smp or c == NP // 128 - 1):
                                    p13 = [q_[:].rearrange("p (a b) -> p a b", a=4) for q_ in (p1, p2, p3, p4)]
                                    if not smp:
                                        V(lambda e: e.tensor_tensor(s5po[:, 0, i4], p13[0][:, :, 127], p13[1][:, :, 127], ALU.subtract), [Bp4[0], Bp4[1]], [Bs5o])
                                        V(lambda e: e.tensor_tensor(s5po[:, 1, i4], p13[2][:, :, 127], p13[3][:, :, 127], ALU.add), [Bp4[2], Bp4[3]], [Bs5o])
                                    else:
                                        l7 = lambda q3: q3.rearrange("p a (j l) -> p a j l", l=8)[:, :, :, 7]
                                        V(lambda e: e.tensor_tensor(s5so[:, 0, i4, :], l7(p13[0]), l7(p13[1]), ALU.subtract), [Bp4[0], Bp4[1]], [Bs5o])
                                        V(lambda e: e.tensor_tensor(s5so[:, 1, i4, :], l7(p13[2]), l7(p13[3]), ALU.add), [Bp4[2], Bp4[3]], [Bs5o])

                            def stC(c):
                                smp = c * 128 >= NP; tc0 = c * 128; k2 = c % 2
                                wrb, wib = wtb[k2][0], wtb[k2][1]; xrb, xib = xtb[k2][0], xtb[k2][1]
                                bE = Bes if smp else Btab
                                if c % 4 == 0:
                                    pinned.clear()
                                    ypsh[0] = PS()
                                    pinned.add(pb.index(ypsh[0][0]))
                                yp, byp = ypsh[0]
                                yc = (c % 4) * 128
                                for il in range(4):
                                    P(lambda e, il=il, yp=yp: e.matmul(yp[:, yc:yc + 128], bct[:, 1, 0, il, :], xrb[:, il * 128:(il + 1) * 128], start=(il == 0), stop=False),
                                      Bbct + [Bxt[k2]], [byp])
                                    P(lambda e, il=il, yp=yp: e.matmul(yp[:, yc:yc + 128], bct[:, 1, 1, il, :], xib[:, il * 128:(il + 1) * 128], start=False, stop=(il == 3)),
                                      Bbct + [Bxt[k2]], [byp])
                                if c % 4 == 3 or c == ntl - 1:
                                    o = (c // 4) * 512
                                    n = (c % 4 + 1) * 128
                                    bi = o // 512
                                    yv = NT5[4]
                                    V(lambda e, yp=yp: e.scalar_tensor_tensor(yv[:, 0:n], suf[:, o:o + n], pv("s5d", oc), yp[:, 0:n], ALU.mult, ALU.add),
                                      [bsuf, byp, Bc], [BN[4]])
                                    A(lambda e: e.activation(mix[:, 4 + oc, o:o + n], yv[:, 0:n], AF.Gelu), [BN[4]], [Bmix[4 + oc][bi]])
                            stA(0)
                            stB1(0)
                            for c in range(ntl):
                                if c + 1 < ntl:
                                    stA(c + 1)
                                stB2(c)
                                if c + 1 < ntl:
                                    stB1(c + 1)
                                stC(c)
                        V(lambda e: e.memset(cch[:, 5, 0:1], 0.0), (), S5FINE + [BUs, Bkz, Bqz, Bcch, Bg2, Bv2])
                        pinned.clear()
                        S5MODE[0] = False
                        if last:
                            STO(D["s5rep"], s5po[:, 0, :], [Bs5o]); STO(D["s5imp"], s5po[:, 1, :], [Bs5o])
                            STO(D["s5res"], s5so[:, 0], [Bs5o]); STO(D["s5ims"], s5so[:, 1], [Bs5o])
                        ck(8)
                        slot, bs = wload([(D["w_glu"], 0)], 4)
                        for oc in range(4):
                            for bi, (o, n) in enumerate(blks):
                                ps, bp = PS()
                                for k in range(4):
                                    P(lambda e, k=k, ps=ps, oc=oc: e.matmul(ps[:, 0:n], slot[:, k, oc * 128:(oc + 1) * 128], mix[:, 4 + k, o:o + n],
                                                                            start=(k == 0), stop=(k == 3)), bs + [Bmix[4 + k][bi] for k in range(4)], [bp])
                                A(lambda e, ps=ps, oc=oc: e.activation(hn[:, oc, o:o + n], ps[:, 0:n], AF.Sigmoid, bias=pv("bglu", oc)), [bp, Bc], [Bhn[bi]])
                        for oc in range(4):
                            for bi, (o, n) in enumerate(blks):
                                V(lambda e, oc=oc: e.tensor_tensor(mix[:, 4 + oc, o:o + n], mix[:, 4 + oc, o:o + n], hn[:, oc, o:o + n], ALU.mult),
                                  [Bhn[bi], Bmix[4 + oc][bi]], [Bmix[4 + oc][bi]])
                        resid_proj(D["w_out_cd"], NT, mix, Bmix)
                    rmsnorm("nff%d" % layer, NT)
                    for q in range(4):
                        if sbi == 0 and layer == 0:
                            build_tab_kc(q)
                        for u in range(2):
                            c0 = q * 1024 + u * 512
                            slot, bs = wload([(D["w_ff1"][layer][:, c0:c0 + 512], 0)], 8)
                            for hc in range(4):
                                c = u * 4 + hc

                                def ev(ps, bp, bi, o, n, c=c):
                                    k_ = FFK[0] % 2
                                    FFK[0] += 1
                                    t = sqb[k_]
                                    A(lambda e: e.activation(t[:, 0:n], ps[:, 0:n], AF.Relu), [bp], [Bsq[k_]])
                                    V(lambda e: e.tensor_tensor(mix[:, c, o:o + n], t[:, 0:n], t[:, 0:n], ALU.mult), [Bsq[k_]], [Bmix[c][bi]])
                                proj_fm(slot, bs, hc * 128, NT, ev)
                        for u in range(2):
                            slot, bs = wload([(D["w_ff2"][layer][q * 1024:(q + 1) * 1024, u * 512:(u + 1) * 512], 0)], 8)
                            for oc in range(4):
                                c = u * 4 + oc

                                def ev(ps, bp, bi, o, n, c=c):
                                    V(lambda e: e.tensor_tensor(h[:, c, o:o + n], h[:, c, o:o + n], ps[:, 0:n], ALU.add), [bp, Bh[c][bi]], [Bh[c][bi]])
                                proj_fm(slot, bs, oc * 128, NT, ev, rhs=mix, rbufs=lambda bi: [Bmix[k][bi] for k in range(8)])
                    ck(5)
                    rmsnorm("nple%d" % layer, NT)
                    S.dma("pool", lambda e, layer=layer: e.dma_start(out=pTb[:, :, 0:NT], in_=D["pT"][layer][:, tok0:tok0 + NT].rearrange("(k p) n -> p k n", p=128)),
                          pTsem, (), [BpT])
                    for u in range(2):
                        slot, bs = wload([(D["w_ple_gate"][layer][:, u * 512:(u + 1) * 512], 0)], 8)
                        slot2, bs2 = wload([(D["w_ple_proj"][layer][:, u * 512:(u + 1) * 512], 0)], 2)
                        for oc in range(4):
                            c = u * 4 + oc
                            for bi, (o, n) in enumerate(blks):
                                ps, bp = PS()
                                for k in range(8):
                                    P(lambda e, k=k, ps=ps: e.matmul(ps[:, 0:n], slot[:, k, oc * 128:(oc + 1) * 128], hn[:, k, o:o + n], start=(k == 0), stop=(k == 7)),
                                      bs + [Bhn[bi]], [bp])
                                k_ = FFK[0] % 2; FFK[0] += 1; gt = NT5[k_]
                                A(lambda e, ps=ps: e.activation(gt[:, 0:n], ps[:, 0:n], AF.Sigmoid), [bp], [BN[k_]])
                                ps2, bp2 = PS()
                                for k in range(2):
                                    P(lambda e, k=k, ps2=ps2: e.matmul(ps2[:, 0:n], slot2[:, k, oc * 128:(oc + 1) * 128], pTb[:, k, o:o + n], start=(k == 0), stop=(k == 1)),
                                      bs2 + [BpT], [bp2])
                                V(lambda e, ps2=ps2: e.tensor_tensor(gt[:, 0:n], gt[:, 0:n], ps2[:, 0:n], ALU.mult), [bp2, BN[k_]], [BN[k_]])
                                V(lambda e, c=c: e.tensor_tensor(h[:, c, o:o + n], h[:, c, o:o + n], gt[:, 0:n], ALU.add), [BN[k_], Bh[c][bi]], [Bh[c][bi]])
                ck(9)
                rmsnorm("nfin", NT, final_out=D["yT"][:, tok0:tok0 + NT] if True else None)
        except _Stop:
            pass
        S.final_wait("sp", OUTS)
        S.emit(st)
    return nc


_NC = [None]


def kernel(**I):
    if _NC[0] is None:
        _NC[0] = build_program()
    nc = _NC[0]
    in_maps = prep_inputs(I)
    res = run_bass_kernel_spmd(nc, in_maps, core_ids=list(range(8)))
    return assemble(res.results)


def prep_inputs(I):
    f = lambda a: np.ascontiguousarray(np.asarray(a, np.float32))
    ident = np.eye(128, dtype=np.float32)
    s_ = np.arange(128)
    maskc = (s_[:, None] <= s_[None, :]).astype(np.float32)
    maskb = maskc * (s_[:, None] // 8 == s_[None, :] // 8)
    blk3 = np.broadcast_to((np.arange(16)[:, None] == (s_[None, :] // 8)).astype(np.float32)[None], (128, 16, 128)).copy()
    rowm = (s_[:, None] // 8 == np.arange(16)[None, :]).astype(np.float32)
    segm = np.ones((3, 128, NTM), np.float32); posrow = np.zeros((3, 128, NTM), np.float32); tau = np.ones((3, 128, NTM), np.float32)
    for i, (t0, NP, hs) in enumerate(SBS):
        segm[i, :, 0:NP:128] = 0.0
        posrow[i, :, 0:NP] = np.arange(t0, t0 + NP)[None]
        tau[i, :, 0:NP] = np.arange(1, NP + 1)[None]
        if hs:
            segm[i, :, NP:NP + 128:8] = 0.0
            posrow[i, :, NP:NP + 128] = (16384 + (np.arange(128) % 8))[None]
            tau[i, :, NP:NP + 128] = (1 + (np.arange(128) % 8))[None]
    negm = np.zeros((4, 128), np.float32); negm[:, 0::8] = -1e30
    sel = np.zeros((4, 4, 128), np.float32)
    for k in range(4):
        sel[k, k, :] = 1.0
    jrow = np.zeros((128, 4, 128), np.float32); jrow[:, 0, :] = (s_ + 1)[None]; jrow[:, 1, :] = (s_ % 8 + 1)[None]; jrow[:, 2, :] = s_[None]; jrow[:, 3, :] = (s_ % 8)[None]
    pvec = np.zeros((128, NPV), np.float32)

    def put(name, arr):
        o, w = PV[name]
        pvec[:, o:o + w] = arr
    for l in range(2):
        put("nmix%d" % l, _cols(I["norm_mix"][l])); put("nff%d" % l, _cols(I["norm_ff"][l])); put("nple%d" % l, _cols(I["norm_ple"][l]))
        put("lb%d" % l, _cols(I["lb_logits"][l]))
    put("nfin", _cols(I["norm_final"]))
    for j in range(4):
        put("cw%d" % j, _cols(I["conv_w_ab"][0][j]))
    put("cb", _cols(I["conv_b_ab"][0])); put("gna", _cols(I["gn_a"][0])); put("gnc", _cols(I["gn_c"][0]))
    put("s5d", _cols(I["s5_D"][0])); put("bglu", _cols(I["b_glu"][0]))
    st = lambda a: np.ascontiguousarray(np.asarray(a, np.float32).reshape(16, 2, 64).reshape(16, 128).T)
    put("are", st(I["s5_A_re"][0])); put("aim", st(I["s5_A_im"][0]))
    put("ldt", st(np.repeat(np.asarray(I["s5_log_dt"][0], np.float32)[:, None], 64, axis=1)))
    put("invf", (10000.0 ** (-(np.arange(128) % 64) / 64.0)).astype(np.float32)[:, None])
    put("sgn", np.where(s_ < 64, -1.0, 1.0).astype(np.float32)[:, None])
    put("s0", s_.astype(np.float32)[:, None]); put("s8", (s_ % 8).astype(np.float32)[:, None]); put("pidx", (s_ + 1).astype(np.float32)[:, None]); put("pidxs", (s_ % 8 + 1).astype(np.float32)[:, None]); put("rowm", rowm)
    rw = lambda a: np.broadcast_to(np.asarray(a, np.float32).reshape(1, 2048), (128, 2048))
    rowp = np.ascontiguousarray(np.stack([rw(I["s5_A_re"][0]), rw(I["s5_A_im"][0]),
                                          rw(np.repeat(np.asarray(I["s5_log_dt"][0], np.float32)[:, None], 64, axis=1))]))
    bgv = np.asarray(I["b_gate_ab"][0], np.float32)
    bg = np.stack([bgv[:4], bgv[4:]], axis=1).copy()
    BT = np.zeros((2, 16, 128, 128), np.float32); CT = np.zeros((2, 16, 128, 128), np.float32)
    for part, (Bm, Cm) in enumerate(((I["s5_B_re"][0], I["s5_C_re"][0]), (I["s5_B_im"][0], I["s5_C_im"][0]))):
        Bm = np.asarray(Bm, np.float32); Cm = np.asarray(Cm, np.float32)
        for g in range(32):
            i, gl = g // 2, g % 2
            k0 = (g % 8) * 16
            BT[part, i, k0:k0 + 16, gl * 64:(gl + 1) * 64] = Bm[g].T
            CT[part, i, gl * 64:(gl + 1) * 64, k0:k0 + 16] = Cm[g].T
    wab = np.asarray(I["w_in_ab"][0], np.float32); wcd = np.asarray(I["w_in_cd"][0], np.float32)
    w_ab_h = np.concatenate([wab[:, b0 + sec * 512 + hh * 128:b0 + sec * 512 + (hh + 1) * 128]
                             for b0 in (0, 2056) for hh in range(4) for sec in range(4)], axis=1)
    w_cd_h = np.concatenate([wcd[:, sec * 512 + hh * 128:sec * 512 + (hh + 1) * 128] for hh in range(4) for sec in range(4)], axis=1)
    common = dict(w_ab_h=f(w_ab_h), w_cd_h=f(w_cd_h), w_in_ab=f(I["w_in_ab"][0]), wg=f(I["w_in_ab"][0][:, 2048:2056]), w_out_ab=f(I["w_out_ab"][0]),
                  w_in_cd=f(I["w_in_cd"][0]), w_glu=f(I["w_glu"][0]), w_out_cd=f(I["w_out_cd"][0]),
                  w_ff1=f(I["w_ff1"]), w_ff2=f(I["w_ff2"]), w_ple_proj=f(I["w_ple_proj"]), w_ple_gate=f(I["w_ple_gate"]),
                  pvec=pvec, bg=bg, BT=BT, CT=CT, ident=ident, maskc=maskc, maskb=maskb.astype(np.float32), blk3=blk3,
                  segm=segm, negm=negm, sel=sel, posrow=posrow, rowp=rowp, jrow=jrow)
    in_maps = []
    for c in range(8):
        sl = slice(16 * c, 16 * c + 16)
        xT = np.concatenate([np.asarray(I["x_prompt"][c]).T, np.asarray(I["x_sample"][sl]).reshape(128, 1024).T], axis=1)
        pT = np.concatenate([np.transpose(np.asarray(I["p_prompt"][:, c]), (0, 2, 1)),
                             np.transpose(np.asarray(I["p_sample"][:, sl]).reshape(2, 128, 256), (0, 2, 1))], axis=2)
        Us = np.concatenate([np.asarray(I["state_mlstm_C"][0][sl]), np.asarray(I["state_mlstm_n"][0][sl])[..., None]], axis=-1)
        x0 = lambda a: np.transpose(np.asarray(a, np.float32).reshape(16, 16, 128), (2, 1, 0))
        m = dict(common)
        m.update(xT=f(xT), pT=f(pT), convs=f(np.transpose(np.asarray(I["state_mlstm_conv"][0][sl]), (2, 0, 1))), Us=f(Us),
                 ms=f(np.asarray(I["state_mlstm_m"][0][sl]).T), rets=f(I["state_ret"][0][sl]), hgrns=f(I["state_hgrn"][0][sl]),
                 x0re=f(x0(I["state_s5_re"][0][sl])), x0im=f(x0(I["state_s5_im"][0][sl])))
        in_maps.append(m)
    return in_maps


def assemble(R):
    yp = np.zeros((8, 2048, 1024), np.float32); ys = np.zeros((128, 8, 1024), np.float32)
    convp = np.zeros((1, 8, 3, 1024), np.float32); convs = np.zeros((1, 128, 3, 1024), np.float32)
    Cp = np.zeros((1, 8, 4, 128, 128), np.float32); Cs = np.zeros((1, 128, 4, 128, 128), np.float32)
    np_ = np.zeros((1, 8, 4, 128), np.float32); ns = np.zeros((1, 128, 4, 128), np.float32)
    mp = np.zeros((1, 8, 4), np.float32); ms = np.zeros((1, 128, 4), np.float32)
    retp = np.zeros((1, 8, 4, 128, 128), np.float32); rets = np.zeros((1, 128, 4, 128, 128), np.float32)
    hgp = np.zeros((1, 8, 4, 128, 128), np.float32); hgs = np.zeros((1, 128, 4, 128, 128), np.float32)
    s5rp = np.zeros((1, 8, 32, 64), np.float32); s5ip = np.zeros((1, 8, 32, 64), np.float32)
    s5rs = np.zeros((1, 128, 32, 64), np.float32); s5is = np.zeros((1, 128, 32, 64), np.float32)
    for c in range(len(R)):
        r = R[c]
        sl = slice(16 * c, 16 * c + 16)
        yp[c] = r["yT"][:, :2048].T
        ys[sl] = r["yT"][:, 2048:].T.reshape(16, 8, 1024)
        convp[0, c] = r["convp"].T
        convs[0, sl] = np.transpose(r["convs_o"], (1, 2, 0))
        Cp[0, c] = r["Up"][:, :, :128]; np_[0, c] = r["Up"][:, :, 128]
        Cs[0, sl] = r["Us_o"][..., :128]; ns[0, sl] = r["Us_o"][..., 128]
        mp[0, c] = r["mp"][:, 0]; ms[0, sl] = r["ms_o"].T
        retp[0, c] = r["retp"]; rets[0, sl] = r["rets_o"]; hgp[0, c] = r["hgrnp"]; hgs[0, sl] = r["hgrns_o"]
        s5rp[0, c] = r["s5rep"].T.reshape(32, 64); s5ip[0, c] = r["s5imp"].T.reshape(32, 64)
        s5rs[0, sl] = np.transpose(r["s5res"], (2, 1, 0)).reshape(16, 32, 64)
        s5is[0, sl] = np.transpose(r["s5ims"], (2, 1, 0)).reshape(16, 32, 64)
    return (yp, ys, convp, convs, Cp, Cs, np_, ns, mp, ms, retp, rets, hgp, hgs, s5rp, s5rs, s5ip, s5is)
```

```python
import math, contextlib, os
import numpy as np
import concourse.bass as bass
import concourse.mybir as mybir
from concourse.bass_utils import run_bass_kernel_spmd

F32 = mybir.dt.float32
BF16 = mybir.dt.bfloat16
AF = mybir.ActivationFunctionType
ALU = mybir.AluOpType

NTM = 768
FW = 776
SBS = [(0, 768, False), (768, 768, False), (1536, 512, True)]
NTOK = 2176
EPS = 1e-6
PI = math.pi
LG = [math.log1p(-2.0 ** (-5.0 - h)) for h in range(4)]
LNK = -0.5 * math.log(128.0)


class Buf:
    __slots__ = ("w", "r")

    def __init__(self):
        self.w = None
        self.r = []


class _Rec:
    def __init__(self):
        self.call = None

    def __getattr__(self, name):
        def f(*a, **k):
            self.call = (name, a, k)
            return self
        return f


def _record(fn):
    r = _Rec()
    fn(r)
    assert r.call is not None
    return r.call


class Sched:
    ENGS = ("pe", "act", "dve", "pool", "sp")

    def __init__(self, nc):
        self.nc = nc
        self.ops = {e: [] for e in self.ENGS}
        self.cnt = {e: 0 for e in self.ENGS}
        self.seen = {e: {} for e in self.ENGS}
        self.sems = {}
        self.dma_cnt = {}

    def new_dma_sem(self):
        k = "dma%d" % len(self.dma_cnt)
        self.dma_cnt[k] = 0
        return k

    def _deps(self, eng, reads, writes, is_dma):
        waits = {}

        def add(ev, kind):
            key, val, src_eng, src_dma = ev
            if (not src_dma) and (not is_dma) and src_eng == eng and eng == "pe":
                return
            if self.seen[eng].get(key, 0) >= val:
                return
            if waits.get(key, 0) < val:
                waits[key] = val
        for b in reads:
            if b.w is not None:
                add(b.w, "raw")
        for b in writes:
            if b.w is not None:
                add(b.w, "waw")
            for r in b.r:
                add(r, "war")
        for k, v in waits.items():
            self.seen[eng][k] = v
        return list(waits.items())

    def _post(self, ev, reads, writes):
        for b in writes:
            b.w = ev
            b.r = []
        for b in reads:
            if b.w is not ev:
                b.r.append(ev)
                if len(b.r) > 24:
                    b.r = b.r[-24:] if False else b.r

    def op(self, eng, fn, reads=(), writes=()):
        waits = self._deps(eng, reads, writes, False)
        self.cnt[eng] += 1
        ev = ("e_" + eng, self.cnt[eng], eng, False)
        self.ops[eng].append((waits, _record(fn), ("e_" + eng, 1)))
        self._post(ev, reads, writes)

    def dma(self, eng, fn, sem, reads=(), writes=()):
        waits = self._deps(eng, reads, writes, True)
        prev = self.dma_cnt[sem]
        if prev > 0 and self.seen[eng].get(sem, 0) < prev:
            waits = [w_ for w_ in waits if w_[0] != sem] + [(sem, prev)]
            self.seen[eng][sem] = prev
        self.dma_cnt[sem] += 16
        ev = (sem, self.dma_cnt[sem], eng, True)
        self.ops[eng].append((waits, _record(fn), (sem, 16)))
        self._post(ev, reads, writes)

    def final_wait(self, eng, bufs):
        waits = self._deps(eng, bufs, bufs, True)
        have = dict(waits)
        for k, v in self.dma_cnt.items():
            if v > 0 and self.seen[eng].get(k, 0) < v and have.get(k, 0) < v:
                have[k] = v
        for e2 in self.ENGS:
            if e2 != eng and self.cnt[e2] > 0:
                have["e_" + e2] = self.cnt[e2]
        self.ops[eng].append((list(have.items()), None, None))

    def emit(self, stack):
        nc = self.nc
        keys = ["e_" + e for e in self.ENGS] + list(self.dma_cnt.keys())
        for k in keys:
            self.sems[k] = stack.enter_context(nc.semaphore(k))
        block = stack.enter_context(nc.Block())
        engobj = {"pe": "tensor", "act": "scalar", "dve": "vector", "pool": "gpsimd", "sp": "sync"}

        def mk(e):
            def body(engine):
                for (waits, fn, inc) in self.ops[e]:
                    for (k, v) in waits:
                        engine.wait_ge(self.sems[k], v)
                    if fn is not None:
                        name, a, k = fn
                        getattr(engine, name)(*a, **k).then_inc(self.sems[inc[0]], inc[1])
            return body
        for e in self.ENGS:
            if self.ops[e]:
                getattr(block, engobj[e])(mk(e))


PV = {}
_o = 0
for _n, _w in [("nmix0", 8), ("nmix1", 8), ("nff0", 8), ("nff1", 8), ("nple0", 8), ("nple1", 8), ("nfin", 8),
               ("cw0", 8), ("cw1", 8), ("cw2", 8), ("cw3", 8), ("cb", 8), ("gna", 4), ("gnc", 4), ("s5d", 4),
               ("bglu", 4), ("lb0", 4), ("lb1", 4), ("are", 16), ("aim", 16), ("ldt", 16), ("invf", 1),
               ("sgn", 1), ("pidx", 1), ("pidxs", 1), ("rowm", 16), ("s0", 1), ("s8", 1)]:
    PV[_n] = (_o, _w)
    _o += _w
NPV = _o


def _cols(v):
    return np.ascontiguousarray(np.asarray(v, np.float32).reshape(-1, 128).T)


def build_program():
    nc = bass.Bass("TRN2", target_bir_lowering=False)
    D = {}

    def din(name, shape):
        D[name] = nc.dram_tensor(name, list(shape), F32, kind="ExternalInput").ap()
        return D[name]

    def dout(name, shape):
        D[name] = nc.dram_tensor(name, list(shape), F32, kind="ExternalOutput").ap()
        return D[name]
    din("xT", [1024, NTOK]); din("pT", [2, 256, NTOK])
    din("convs", [1024, 16, 3]); din("Us", [16, 4, 128, 129]); din("ms", [4, 16])
    din("rets", [16, 4, 128, 128]); din("hgrns", [16, 4, 128, 128])
    din("x0re", [128, 16, 16]); din("x0im", [128, 16, 16])
    din("w_ab_h", [1024, 4096]); din("w_cd_h", [1024, 2048]); din("w_in_ab", [1024, 4104]); din("wg", [1024, 8]); din("w_out_ab", [1024, 1024])
    din("w_in_cd", [1024, 2560]); din("w_glu", [512, 512]); din("w_out_cd", [1024, 1024])
    din("w_ff1", [2, 1024, 4096]); din("w_ff2", [2, 4096, 1024])
    din("w_ple_proj", [2, 256, 1024]); din("w_ple_gate", [2, 1024, 1024])
    din("pvec", [128, NPV]); din("bg", [4, 2])
    din("BT", [2, 16, 128, 128]); din("CT", [2, 16, 128, 128])
    din("ident", [128, 128]); din("maskc", [128, 128]); din("maskb", [128, 128])
    din("blk3", [128, 16, 128]); din("segm", [3, 128, NTM]); din("negm", [4, 128]); din("sel", [4, 4, 128])
    din("posrow", [3, 128, NTM]); din("rowp", [3, 128, 2048]); din("jrow", [128, 4, 128])
    dout("yT", [1024, NTOK]); dout("convp", [1024, 3]); dout("convs_o", [1024, 16, 3])
    dout("Up", [4, 128, 129]); dout("Us_o", [16, 4, 128, 129]); dout("mp", [4, 1]); dout("ms_o", [4, 16])
    dout("retp", [4, 128, 128]); dout("rets_o", [16, 4, 128, 128])
    dout("hgrnp", [4, 128, 128]); dout("hgrns_o", [16, 4, 128, 128])
    dout("s5rep", [128, 16]); dout("s5imp", [128, 16]); dout("s5res", [128, 16, 16]); dout("s5ims", [128, 16, 16])

    st = contextlib.ExitStack()
    with st:
        S = Sched(nc)
        cnt = [0]

        def sb(shape, dt=F32):
            cnt[0] += 1
            return st.enter_context(nc.sbuf_tensor("t%d" % cnt[0], list(shape), dt))

        def psum(shape, dt=F32):
            cnt[0] += 1
            return st.enter_context(nc.psum_tensor("p%d" % cnt[0], list(shape), dt))
        V = lambda fn, r=(), w=(): S.op("dve", fn, r, w)
        A = lambda fn, r=(), w=(): S.op("act", fn, r, w)
        G = lambda fn, r=(), w=(): S.op("pool", fn, r, w)
        P = lambda fn, r=(), w=(): S.op("pe", fn, r, w)
        msems = {"sp": [S.new_dma_sem() for _ in range(24)], "pool": [S.new_dma_sem() for _ in range(8)]}
        mi = {"sp": 0, "pool": 0}

        def LD(out, in_, w, r=(), eng="sp"):
            k = msems[eng][mi[eng] % len(msems[eng])]
            mi[eng] += 1
            S.dma(eng, lambda e: e.dma_start(out=out, in_=in_), k, r, w)
        OUTS = []

        def STO(out, in_, r):
            b_ = Buf()
            OUTS.append(b_)
            LD(out, in_, [b_], r)

        h = sb([128, 8, NTM]); hn = sb([128, 8, NTM], BF16); mix = sb([128, 8, NTM], BF16)
        Bh = [[Buf() for _ in range(2)] for _ in range(8)]
        Bhn = [Buf() for _ in range(2)]
        Bmix = [[Buf() for _ in range(2)] for _ in range(8)]
        NW = 2
        wr = [sb([128, 8, 512], BF16) for _ in range(NW)]
        Bwrp = [[Buf() for _ in range(4)] for _ in range(NW)]
        wsem = [[S.new_dma_sem() for _ in range(4)] for _ in range(NW)]
        wi = [0]

        def wload(parts, nk):
            i = wi[0] % NW
            wi[0] += 1
            for pi_, (ap, co) in enumerate(parts):
                ncol = ap.shape[1]
                S.dma("pool", lambda e, ap=ap, co=co, ncol=ncol, i=i: e.dma_start(
                    out=wr[i][:, 0:nk, co:co + ncol], in_=ap.rearrange("(k p) n -> p k n", p=128)),
                    wsem[i][pi_], (), (Bwrp[i] if pi_ == 0 else [Bwrp[i][pi_]]))
            return wr[i], Bwrp[i]
        Fs = [sb([128, FW]) for _ in range(9)]
        BF = [Buf() for _ in range(9)]
        Hs = [sb([128, NTM], BF16) for _ in range(5)]
        BH = [Buf() for _ in range(5)]
        vtm = sb([128, 6, 129], BF16); Bv = Buf()
        NT5 = [sb([128, 512]) for _ in range(5)]
        BN = [Buf() for _ in range(5)]
        sqb = [sb([128, 512], BF16) for _ in range(2)]
        Bsq = [Buf() for _ in range(2)]
        pb = [psum([128, 512]) for _ in range(7)]
        Bp = [Buf() for _ in range(7)]
        ptb = psum([128, 1024], BF16); Bpt = Buf()
        pbi = [0]

        pinned = set()

        S5MODE = [False]

        def PS():
            while True:
                i = pbi[0] % (3 if S5MODE[0] else 4)
                pbi[0] += 1
                if i not in pinned:
                    return pb[i], Bp[i]
        psi = [0]

        def PSS():
            if S5MODE[0]:
                i = (4, 5, 6, 3)[psi[0] % 4]
            else:
                i = 4 + psi[0] % 3
            psi[0] += 1
            return pb[i], Bp[i]
        ident = sb([128, 128]); identb = sb([128, 128], BF16); maskc = sb([128, 128], BF16); maskb = sb([128, 128], BF16)
        onesb = sb([128, 128], BF16); blk3 = sb([128, 16, 128], BF16); segm = sb([128, NTM]); negm = sb([4, 128])
        sel = sb([4, 4, 128]); pvec = sb([128, NPV]); bg = sb([4, 2]); nbg = sb([4, 1]); jrow = sb([128, 4, 128])
        Bc = Buf()
        LD(ident[:], D["ident"], [Bc]); LD(identb[:], D["ident"], [Bc], eng="pool")
        LD(maskc[:], D["maskc"], [Bc], eng="pool"); LD(maskb[:], D["maskb"], [Bc], eng="pool")
        LD(blk3[:], D["blk3"], [Bc], eng="pool"); LD(negm[:], D["negm"], [Bc]); LD(sel[:], D["sel"], [Bc])
        LD(pvec[:], D["pvec"], [Bc]); LD(bg[:], D["bg"], [Bc]); LD(jrow[:], D["jrow"], [Bc])
        V(lambda e: e.memset(onesb[:], 1.0), (), [Bc])
        V(lambda e: e.tensor_scalar(nbg[:], bg[:, 1:2], -1.0, None, ALU.mult), [Bc], [Bc])

        cb_ = sb([128, 8])
        CBV = [EPS, LNK, 1.0, 0.0, 0.5 * PI, 0.0, 0.0, 0.0]
        for _i, _v in enumerate(CBV):
            V(lambda e, _i=_i, _v=_v: e.memset(cb_[:, _i:_i + 1], _v), (), [Bc])
        CEPS, CLNK, CONE, CZERO, CHPI = [cb_[:, i:i + 1] for i in range(5)]
        RC = 12582912.0
        I2P = 1.0 / (2 * PI)

        def sin_of(dst, src, shift, tmp, rd, wr_, btmp, npart=128):
            V(lambda e: e.tensor_scalar(tmp, src, shift, I2P, ALU.add, ALU.mult), rd, [btmp])
            V(lambda e: e.tensor_scalar(tmp, tmp, RC, None, ALU.add), [btmp], [btmp])
            V(lambda e: e.tensor_scalar(tmp, tmp, -RC, None, ALU.add), [btmp], [btmp])
            V(lambda e: e.scalar_tensor_tensor(tmp, tmp, -2 * PI, src, ALU.mult, ALU.add), [btmp] + list(rd), [btmp])
            V(lambda e: e.tensor_scalar(tmp, tmp, -PI - shift + 4e-6, PI - shift - 4e-6, ALU.max, ALU.min), [btmp], [btmp])
            A(lambda e: e.activation(dst, tmp, AF.Sin, bias=(CHPI[0:npart] if shift != 0.0 else CZERO[0:npart])), [btmp, Bc], wr_)

        def pv(name, j=0, n=1):
            o, w = PV[name]
            return pvec[:, o + j:o + j + n]
        Gq = sb([128, 2, 4, 128], BF16); gk = sb([128, 2, 4])
        for v2 in range(2):
            for hh in range(4):
                A(lambda e, v2=v2, hh=hh: e.activation(Gq[:, v2, hh, :], jrow[:, v2, :], AF.Exp, scale=LG[hh]), [Bc], [Bc])
                A(lambda e, v2=v2, hh=hh: e.activation(gk[:, v2, hh:hh + 1], pv("pidxs" if v2 else "pidx"), AF.Exp,
                                                       scale=-LG[hh], bias=CLNK), [Bc], [Bc])
        lb = sb([128, 4]); oml = sb([128, 4])
        V(lambda e: e.tensor_tensor(lb[:], pv("lb1", 0, 4), pv("lb0", 0, 4), ALU.subtract), [Bc], [Bc])
        A(lambda e: e.activation(lb[:], lb[:], AF.Sigmoid), [Bc], [Bc])
        V(lambda e: e.tensor_scalar(oml[:], lb[:], -1.0, 1.0, ALU.mult, ALU.add), [Bc], [Bc])
        s5p = sb([128, 16, 16])
        th, rr, zr, zi, rho = s5p[:, 0, :], s5p[:, 1, :], s5p[:, 2, :], s5p[:, 3, :], s5p[:, 7, :]
        t4, t5, t6 = s5p[:, 4, :], s5p[:, 5, :], s5p[:, 6, :]
        ar_, ai_, a128r, a128i, izr, izi, t7 = (s5p[:, 8, :], s5p[:, 9, :], s5p[:, 10, :], s5p[:, 11, :], s5p[:, 12, :],
                                               s5p[:, 13, :], s5p[:, 14, :])
        are, aim = pv("are", 0, 16), pv("aim", 0, 16)
        A(lambda e: e.activation(t4, pv("ldt", 0, 16), AF.Exp), [Bc], [Bc])
        V(lambda e: e.tensor_tensor(th, t4, aim, ALU.mult), [Bc], [Bc])
        V(lambda e: e.tensor_tensor(rho, t4, are, ALU.mult), [Bc], [Bc])
        A(lambda e: e.activation(rr, rho, AF.Exp), [Bc], [Bc])
        sin_of(t4, th, 0.5 * PI, t6, [Bc], [Bc], Bc)
        sin_of(t5, th, 0.0, t6, [Bc], [Bc], Bc)
        V(lambda e: e.tensor_tensor(ar_, t4, rr, ALU.mult), [Bc], [Bc])
        V(lambda e: e.tensor_tensor(ai_, t5, rr, ALU.mult), [Bc], [Bc])
        V(lambda e: e.tensor_scalar(t4, ar_, -1.0, None, ALU.add), [Bc], [Bc])
        V(lambda e: e.tensor_copy(t5, ai_), [Bc], [Bc])
        V(lambda e: e.tensor_tensor(t6, are, are, ALU.mult), [Bc], [Bc])
        V(lambda e: e.tensor_tensor(zr, aim, aim, ALU.mult), [Bc], [Bc])
        V(lambda e: e.tensor_tensor(t6, t6, zr, ALU.add), [Bc], [Bc])
        V(lambda e: e.reciprocal(t6, t6), [Bc], [Bc])
        V(lambda e: e.tensor_tensor(zr, t4, are, ALU.mult), [Bc], [Bc])
        V(lambda e: e.tensor_tensor(zi, t5, aim, ALU.mult), [Bc], [Bc])
        V(lambda e: e.tensor_tensor(zr, zr, zi, ALU.add), [Bc], [Bc])
        V(lambda e: e.tensor_tensor(zi, t5, are, ALU.mult), [Bc], [Bc])
        V(lambda e: e.tensor_tensor(t7, t4, aim, ALU.mult), [Bc], [Bc])
        V(lambda e: e.tensor_tensor(zi, zi, t7, ALU.subtract), [Bc], [Bc])
        V(lambda e: e.tensor_tensor(zr, zr, t6, ALU.mult), [Bc], [Bc])
        V(lambda e: e.tensor_tensor(zi, zi, t6, ALU.mult), [Bc], [Bc])
        V(lambda e: e.tensor_tensor(t4, zr, zr, ALU.mult), [Bc], [Bc])
        V(lambda e: e.tensor_tensor(t5, zi, zi, ALU.mult), [Bc], [Bc])
        V(lambda e: e.tensor_tensor(t4, t4, t5, ALU.add), [Bc], [Bc])
        V(lambda e: e.reciprocal(t4, t4), [Bc], [Bc])
        V(lambda e: e.tensor_tensor(izr, zr, t4, ALU.mult), [Bc], [Bc])
        V(lambda e: e.scalar_tensor_tensor(izi, zi, -1.0, t4, ALU.mult, ALU.mult), [Bc], [Bc])
        V(lambda e: e.tensor_scalar(t7, th, 128.0, None, ALU.mult), [Bc], [Bc])
        sin_of(t4, t7, 0.5 * PI, t6, [Bc], [Bc], Bc)
        sin_of(t5, t7, 0.0, t6, [Bc], [Bc], Bc)
        A(lambda e: e.activation(t6, rho, AF.Exp, scale=128.0), [Bc], [Bc])
        V(lambda e: e.tensor_tensor(a128r, t4, t6, ALU.mult), [Bc], [Bc])
        V(lambda e: e.tensor_tensor(a128i, t5, t6, ALU.mult), [Bc], [Bc])
        tabE = nc.dram_tensor("tabE", [128, 2, 2048], BF16).ap(); tabZ = nc.dram_tensor("tabZ", [128, 2, 16, 128], BF16).ap()
        tbe = sb([128, 2, 512], BF16); tbz = sb([128, 2, 4, 128], BF16); Btab = Buf(); Bscr = Buf()

        def build_tables(kc, scol, jr, outE, outZ, wE, wZ):
            f0, f1, f2, f3, f4, f5 = [Fs[k][:, 0:512] for k in range(6)]
            b0_, b1_, b2_, b3_, b4_, b5_ = BF[0:6]
            for k in range(3):
                LD(Fs[k][:, 0:512], D["rowp"][k][:, kc * 512:(kc + 1) * 512], [BF[k]])
            A(lambda e: e.activation(f2, f2, AF.Exp), [b2_], [b2_])
            V(lambda e: e.tensor_tensor(f1, f1, f2, ALU.mult), [b1_, b2_], [b1_])
            V(lambda e: e.tensor_tensor(f0, f0, f2, ALU.mult), [b0_, b2_], [b0_])
            V(lambda e: e.tensor_scalar(f1, f1, scol, None, ALU.mult), [b1_, Bc], [b1_])
            A(lambda e: e.activation(f0, f0, AF.Exp, scale=scol), [b0_, Bc], [b0_])
            V(lambda e: e.reciprocal(f0, f0), [b0_], [b0_])
            sin_of(f3, f1, 0.5 * PI, f2, [b1_], [b3_], b2_)
            sin_of(f4, f1, 0.0, f2, [b1_], [b4_], b2_)
            V(lambda e: e.tensor_tensor(outE(0), f3, f0, ALU.mult), [b3_, b0_], wE)
            V(lambda e: e.scalar_tensor_tensor(outE(1), f4, -1.0, f0, ALU.mult, ALU.mult), [b4_, b0_], wE)
            g0, g1, g2, g3, g4 = [Fs[k][:, 0:512].rearrange("p (a b) -> p a b", a=4) for k in range(5)]
            i4 = slice(4 * kc, 4 * kc + 4)
            jb = jr.unsqueeze(1).broadcast_to([128, 4, 128])
            bc4 = lambda v: v[:, i4].unsqueeze(2).broadcast_to([128, 4, 128])
            V(lambda e: e.tensor_tensor(g1, jb, bc4(th), ALU.mult), [Bc], [b1_])
            V(lambda e: e.tensor_tensor(g0, jb, bc4(rho), ALU.mult), [Bc], [b0_])
            A(lambda e: e.activation(Fs[0][:, 0:512], Fs[0][:, 0:512], AF.Exp), [b0_], [b0_])
            sin_of(f3, f1, 0.5 * PI, f2, [b1_], [b3_], b2_)
            sin_of(f4, f1, 0.0, f2, [b1_], [b4_], b2_)
            V(lambda e: e.tensor_tensor(f3, f3, f0, ALU.mult), [b3_, b0_], [b3_])
            V(lambda e: e.tensor_tensor(f4, f4, f0, ALU.mult), [b4_, b0_], [b4_])
            V(lambda e: e.tensor_tensor(g0, g3, bc4(zr), ALU.mult), [b3_, Bc], [b0_])
            V(lambda e: e.tensor_tensor(g1, g4, bc4(zi), ALU.mult), [b4_, Bc], [b1_])
            V(lambda e: e.tensor_tensor(outZ(0), g0, g1, ALU.subtract), [b0_, b1_], wZ)
            V(lambda e: e.tensor_tensor(g0, g3, bc4(zi), ALU.mult), [b3_, Bc], [b0_])
            V(lambda e: e.tensor_tensor(g1, g4, bc4(zr), ALU.mult), [b4_, Bc], [b1_])
            V(lambda e: e.tensor_tensor(outZ(1), g0, g1, ALU.add), [b0_, b1_], wZ)
        Up = sb([128, 12, 129]); Upb = sb([128, 12, 129], BF16); nbc = sb([128, 4, 128], BF16)
        BU = [Buf() for _ in range(12)]
        V(lambda e: e.memset(Up[:], 0.0), (), BU); V(lambda e: e.memset(Upb[:], 0.0), (), BU)
        V(lambda e: e.memset(nbc[:], 0.0), (), BU)
        tails = sb([128, 8, 3]); Btl = Buf()
        V(lambda e: e.memset(tails[:], 0.0), (), [Btl])
        carr = sb([4, 2]); Bcar = Buf()
        V(lambda e: e.memset(carr[:], 0.0), (), [Bcar])
        s5c = sb([128, 2, 16]); Bs5c = Buf()
        s5d = sb([128, 2, 2, 16]); Bs5d = [Buf(), Buf()]; S5PAR = [0, 0, 0, 0]; Bcr = Buf()
        V(lambda e: e.memset(s5c[:], 0.0), (), [Bs5c])
        V(lambda e: e.memset(s5d[:], 0.0), (), Bs5d)
        Usf = sb([128, 16, 129]); Usb = sb([128, 16, 129], BF16); BUs = Buf()
        qz = sb([128, 16, 128], BF16); kz = sb([128, 16, 128], BF16); nbs = kz
        Bqz = Buf(); Bkz = Buf(); Bnbs = Bkz
        stb = [sb([128, 128], BF16) for _ in range(2)]; Bst = [Buf() for _ in range(2)]
        khb = [sb([128, 128], BF16) for _ in range(2)]; Bkh = [Buf() for _ in range(2)]
        ektm = sb([128, 6, 4]); Bek = Buf()
        decbc = sb([128, 4, 24]); Bdec = Buf()
        decrow = sb([4, 24]); mxe = sb([4, 8]); ms0 = sb([4, 16]); msout = sb([4, 17]); Bsm = Buf()
        x0s = Fs[5][:, 0:512].rearrange("p (a b c) -> p a b c", a=2, b=16); Bx0 = BF[5]
        s5so = Usf[:].rearrange("p a b -> p (a b)")[:, 0:512].rearrange("p (a b c) -> p a b c", a=2, b=16); s5po = sb([128, 2, 16]); Bs5o = Buf()
        Bes = Buf()
        cbs = sb([128, 2, 4, 128], BF16); Bcbs = Buf()
        pt4all = sb([128, 2048], BF16)
        pt4 = [pt4all[:, q_ * 512:(q_ + 1) * 512] for q_ in range(4)]
        gT2 = pt4all[:, 0:NTM]; vtm2 = pt4all[:, NTM:NTM + 774].rearrange("p (a b) -> p a b", a=6); Bg2 = Buf(); Bv2 = Buf()
        hdec = sb([128, 2, 24]); Bhd = Buf()
        Bprb = [Buf() for _ in range(2)]; Bt4 = [Buf() for _ in range(4)]; Bcsb = [Buf() for _ in range(2)]; Bp4 = [Buf() for _ in range(4)]
        S5FINE = Bprb + Bt4 + Bcsb + Bp4
        qzf = qz[:].rearrange("p a b -> p (a b)"); kzf = kz[:].rearrange("p a b -> p (a b)"); usbf = Usb[:].rearrange("p a b -> p (a b)")
        wtb = [[sb([128, 512], BF16) for _ in range(2)] for _ in range(2)]; Bwt = [Buf() for _ in range(2)]
        xtb = [[sb([128, 512], BF16) for _ in range(2)] for _ in range(2)]; Bxt = [Buf() for _ in range(2)]
        cch = sb([128, 6, 4]); Bcch = Buf()
        bct = sb([128, 2, 2, 4, 128], BF16); Bbct = [Buf() for _ in range(4)]; bcsem = [S.new_dma_sem() for _ in range(4)]
        pTb = sb([128, 2, NTM], BF16); BpT = Buf(); pTsem = S.new_dma_sem()

        def blocks(ntot):
            out = []
            o = 0
            while o < ntot:
                n = min(512, ntot - o)
                out.append((o, n)); o += n
            return out

        def rmsnorm(gname, NT, final_out=None):
            for bi, (o, n) in enumerate(blocks(NT)):
                ps, bp = PS()
                for c in range(8):
                    q = sqb[c % 2]; bq = Bsq[c % 2]
                    A(lambda e, c=c, q=q: e.activation(q[:, 0:n], h[:, c, o:o + n], AF.Square), [Bh[c][bi]], [bq])
                    P(lambda e, c=c, q=q, ps=ps: e.matmul(ps[:, 0:n], onesb[:], q[:, 0:n], start=(c == 0), stop=(c == 7)),
                      [bq, Bc], [bp])
                rs = NT5[4]
                A(lambda e, ps=ps: e.activation(rs[:, 0:n], ps[:, 0:n], AF.Ln, scale=1.0 / 1024, bias=CEPS), [bp], [BN[4]])
                A(lambda e: e.activation(rs[:, 0:n], rs[:, 0:n], AF.Exp, scale=-0.5), [BN[4]], [BN[4]])
                for c in range(8):
                    if final_out is None:
                        V(lambda e, c=c: e.scalar_tensor_tensor(hn[:, c, o:o + n], h[:, c, o:o + n], pv(gname, c), rs[:, 0:n],
                                                                ALU.mult, ALU.mult), [Bh[c][bi], BN[4], Bc], [Bhn[bi]])
                    else:
                        t = NT5[c % 2]
                        V(lambda e, c=c, t=t: e.scalar_tensor_tensor(t[:, 0:n], h[:, c, o:o + n], pv(gname, c), rs[:, 0:n],
                                                                     ALU.mult, ALU.mult), [Bh[c][bi], BN[4], Bc], [BN[c % 2]])
                        STO(final_out[c * 128:(c + 1) * 128, o:o + n], t[:, 0:n], [BN[c % 2]])

        def proj_fm(slot, bs, col, NT, evac, rhs=None, nk=8, rbufs=None):
            for bi, (o, n) in enumerate(blocks(NT)):
                ps, bp = PS()
                for k in range(nk):
                    src = hn if rhs is None else rhs
                    P(lambda e, k=k, ps=ps, src=src: e.matmul(ps[:, 0:n], slot[:, k, col:col + 128], src[:, k, o:o + n],
                                                             start=(k == 0), stop=(k == nk - 1)),
                      bs + ([Bhn[bi]] if rbufs is None else rbufs(bi)), [bp])
                evac(ps, bp, bi, o, n)

        def resid_proj(w_ap, NT, src, srcb):
            for u in range(2):
                slot, bs = wload([(w_ap[:, u * 512:(u + 1) * 512], 0)], 8)
                for oc in range(4):
                    c = u * 4 + oc

                    def ev(ps, bp, bi, o, n, c=c):
                        V(lambda e: e.tensor_tensor(h[:, c, o:o + n], h[:, c, o:o + n], ps[:, 0:n], ALU.add),
                          [bp, Bh[c][bi]], [Bh[c][bi]])
                    proj_fm(slot, bs, oc * 128, NT, ev, rhs=src, rbufs=lambda bi: [srcb[k][bi] for k in range(8)])

        def att_pre(qT, kT, bq, bk, col, ek, sample):
            ps, bp = PSS()
            P(lambda e: e.matmul(ps[:, 0:128], kT[:, col:col + 128], qT[:, col:col + 128], start=True, stop=True),
              [bq, bk], [bp])
            i2 = att_tile.k % 2
            att_tile.k += 1
            sT = stb[i2]; bsT = Bst[i2]
            msk = maskb if sample else maskc
            if ek is not None:
                V(lambda e: e.scalar_tensor_tensor(sT[:], ps[:, 0:128], ek, msk[:], ALU.mult, ALU.mult), [bp, Bek, Bc], [bsT])
            else:
                V(lambda e: e.tensor_tensor(sT[:], ps[:, 0:128], msk[:], ALU.mult), [bp, Bc], [bsT])
            P(lambda e: e.transpose(ptb[:, 0:128], kT[:, col:col + 128], identb[:]), [bk, Bc], [Bpt])
            kh = khb[i2]; bkh = Bkh[i2]
            if ek is not None:
                A(lambda e: e.activation(kh[:], ptb[:, 0:128], AF.Copy, scale=ek), [Bpt, Bek], [bkh])
            else:
                A(lambda e: e.copy(kh[:], ptb[:, 0:128]), [Bpt], [bkh])
            return (sT, bsT, kh, bkh)

        def att_tile(qT, kT, bq, bk, col, vt, E, si, ek, dec, PT, bPT, pcol, sample, den=None, mlstm_h=None, usbuf=None, bv=None, ctx=None):
            Bv = bv
            if ctx is None:
                ctx = att_pre(qT, kT, bq, bk, col, ek, sample)
            sT, bsT, kh, bkh = ctx
            P(lambda e: e.matmul(PT[:, pcol:pcol + 128], vt[:, 0:128], sT[:], start=True, stop=False), [Bv, bsT], [bPT])
            if not sample:
                P(lambda e: e.matmul(PT[:, pcol:pcol + 128], Upb[:, si, 0:128], qT[:, col:col + 128], start=False, stop=True),
                  [BU[si], bq], [bPT])
            else:
                for j in range(16):
                    P(lambda e, j=j: e.matmul(PT[:, pcol:pcol + 128], Usb[:, j, 0:128], qz[:, j, :], start=False, stop=(j == 15)),
                      [BUs, Bqz], [bPT])
            if den is not None:
                dps, bd = den
                P(lambda e: e.matmul(dps[:, pcol:pcol + 128], onesb[:], sT[:], start=True, stop=False), [Bc, bsT], [bd])
                if not sample:
                    P(lambda e: e.matmul(dps[:, pcol:pcol + 128], nbc[:, mlstm_h, :], qT[:, col:col + 128], start=False, stop=True),
                      [BU[si], bq], [bd])
                else:
                    for j in range(16):
                        P(lambda e, j=j: e.matmul(dps[:, pcol:pcol + 128], nbs[:, j, :], qz[:, j, :], start=False, stop=(j == 15)),
                          [Bnbs, Bqz], [bd])
            if not sample:
                ps2, bp2 = PSS()
                P(lambda e: e.matmul(ps2[:, 0:E], ident[:], Up[:, si, 0:E], start=True, stop=False), [Bc, BU[si]], [bp2])
                P(lambda e: e.matmul(ps2[:, 0:E], kh[:], vt[:, 0:E], start=False, stop=True), [bkh, Bv], [bp2])
                A(lambda e: e.activation(Up[:, si, 0:E], ps2[:, 0:E], AF.Copy, scale=dec), [bp2, Bdec, Bhd], [BU[si]])
                V(lambda e: e.tensor_copy(Upb[:, si, 0:E], Up[:, si, 0:E]), [BU[si]], [BU[si]])
                if mlstm_h is not None:
                    V(lambda e: e.tensor_copy(nbc[:, mlstm_h, :], Up[:, si, 128:129].broadcast_to([128, 128])), [BU[si]], [BU[si]])
            else:
                V(lambda e: e.tensor_tensor(kz[:], kh[:].unsqueeze(1).broadcast_to([128, 16, 128]),
                                            pv("rowm", 0, 16).unsqueeze(2).broadcast_to([128, 16, 128]), ALU.mult),
                  [bkh, Bc], [Bkz])
                for j in range(16):
                    ps2, bp2 = PSS()
                    P(lambda e, j=j, ps2=ps2: e.matmul(ps2[:, 0:E], ident[:], Usf[:, j, 0:E], start=True, stop=False), [Bc, BUs], [bp2])
                    P(lambda e, j=j, ps2=ps2: e.matmul(ps2[:, 0:E], kz[:, j, :], vt[:, 0:E], start=False, stop=True), [Bkz, Bv], [bp2])
                    A(lambda e, j=j, ps2=ps2: e.activation(usbuf[:, j, 0:E], ps2[:, 0:E], AF.Copy, scale=dec(j)),
                      [bp2, Bdec, Bhd], [BUs])
        att_tile.k = 0
        CTX = {}
        FFK = [0]
        PEND = [None]

        def load_sample_state(src, hh, E):
            LD(Usf[:, :, 0:E], src[:, hh, :, :].rearrange("j d e -> d j e"), [BUs])
            V(lambda e: e.tensor_copy(Usb[:, :, 0:E], Usf[:, :, 0:E]), [BUs], [BUs])

        def make_qz(qT, bq, col):
            V(lambda e: e.tensor_tensor(qz[:], qT[:, col:col + 128].unsqueeze(1).broadcast_to([128, 16, 128]), blk3[:], ALU.mult),
              [bq, Bc], [Bqz])

        def vproj(slot, bs, col, NT, E, vt_, bv_):
            nt = NT // 128
            for c in range(nt):
                ps, bp = PS()
                for k in range(8):
                    P(lambda e, k=k, ps=ps: e.matmul(ps[:, 0:128], hn[:, k, c * 128:(c + 1) * 128], slot[:, k, col:col + 128],
                                                     start=(k == 0), stop=(k == 7)), bs + [Bhn[(c * 128) // 512]], [bp])
                A(lambda e, ps=ps: e.copy(vt_[:, c, 0:128], ps[:, 0:128]), [bp], [bv_])

        def rstd_from(sq_src_fn, n, srcb):
            q = sqb[0]
            sq_src_fn(q)
            ps, bp = PSS()
            P(lambda e: e.matmul(ps[:, 0:n], onesb[:], q[:, 0:n], start=True, stop=True), [Bsq[0], Bc], [bp])
            rs = NT5[3]
            A(lambda e: e.activation(rs[:, 0:n], ps[:, 0:n], AF.Ln, scale=1.0 / 128, bias=CEPS), [bp], [BN[3]])
            A(lambda e: e.activation(rs[:, 0:n], rs[:, 0:n], AF.Exp, scale=-0.5), [BN[3]], [BN[3]])
            return rs

        def build_tab_kc(kc_):
            build_tables(kc_, pv("s0"), jrow[:, 2, :], lambda part: tbe[:, part, :], lambda part: tbz[:, part, :, :], [Btab], [Btab])
            LD(tabE[:, :, kc_ * 512:(kc_ + 1) * 512], tbe[:], [Bscr], r=[Btab])
            LD(tabZ[:, :, 4 * kc_:4 * kc_ + 4, :], tbz[:], [Bscr], r=[Btab])
        STEP = [None]

        def step():
            g = STEP[0]
            if g is not None:
                try:
                    next(g)
                except StopIteration:
                    STEP[0] = None
        CUT = int(os.environ.get("KCUT", "0"))

        class _Stop(Exception):
            pass

        def ck(k):
            if CUT == k:
                raise _Stop()
        try:
            for sbi, (tok0, NP, has_s) in enumerate(SBS):
                NT = NP + (128 if has_s else 0)
                ntp = NP // 128
                blks = blocks(NT)
                last = (sbi == len(SBS) - 1)
                LD(segm[:, 0:NTM], D["segm"][sbi], [Bc], r=[Bc])
                for c in range(8):
                    for bi, (o, n) in enumerate(blks):
                        LD(h[:, c, o:o + n], D["xT"][c * 128:(c + 1) * 128, tok0 + o:tok0 + o + n], [Bh[c][bi]])
                for layer in range(2):
                    rmsnorm("nmix%d" % layer, NT)
                    ck(1)
                    if layer == 0:
                        wgs, bwg = wload([(D["wg"], 0)], 8)
                        A1, A2, A3, A4 = Fs[2], Fs[3], Fs[5], Fs[4]
                        b1, b2, b3, b4 = BF[2], BF[3], BF[5], BF[4]
                        for bi, (o, n) in enumerate(blks):
                            ps, bp = PS()
                            for k in range(8):
                                P(lambda e, k=k, ps=ps: e.matmul(ps[0:4, 0:n], wgs[:, k, 0:4], hn[:, k, o:o + n], start=(k == 0), stop=(k == 7)),
                                  bwg + [Bhn[bi]], [bp])
                            A(lambda e, ps=ps: e.activation(A1[0:4, o:o + n], ps[0:4, 0:n], AF.Identity, bias=bg[:, 0:1]), [bp, Bc], [b1])
                            ps, bp = PS()
                            for k in range(8):
                                P(lambda e, k=k, ps=ps: e.matmul(ps[0:4, 0:n], wgs[:, k, 4:8], hn[:, k, o:o + n], start=(k == 0), stop=(k == 7)),
                                  bwg + [Bhn[bi]], [bp])
                            A(lambda e, ps=ps: e.activation(A2[0:4, o:o + n], ps[0:4, 0:n], AF.Exp, scale=-1.0, bias=nbg[:, 0:1]), [bp, Bc], [b2])
                        A(lambda e: e.activation(A2[0:4, 0:NT], A2[0:4, 0:NT], AF.Ln, bias=CONE[0:4]), [b2], [b2])
                        V(lambda e: e.memset(A4[0:4, 0:NT], 1.0), (), [b4])
                        V(lambda e: e.tensor_tensor_scan(A3[0:4, 0:NP], A4[0:4, 0:NP], A2[0:4, 0:NP], carr[:, 0:1], ALU.mult, ALU.add),
                          [b2, b4, Bcar], [b3])
                        if has_s:
                            V(lambda e: e.tensor_tensor_scan(A3[0:4, NP:NT], segm[0:4, NP:NT], A2[0:4, NP:NT], 0.0, ALU.mult, ALU.add),
                              [b2, Bc], [b3])
                        V(lambda e: e.tensor_tensor(A1[0:4, 0:NT], A1[0:4, 0:NT], A3[0:4, 0:NT], ALU.add), [b1, b3], [b1])
                        V(lambda e: e.memset(A4[0:4, 0:NT], 0.0), (), [b4])
                        V(lambda e: e.tensor_tensor_scan(A2[0:4, 0:NP], A4[0:4, 0:NP], A1[0:4, 0:NP], carr[:, 1:2], ALU.add, ALU.max),
                          [b1, b4, Bcar], [b2])
                        V(lambda e: e.tensor_copy(mxe[:, 0:1], carr[:, 1:2]), [Bcar], [Bsm])
                        V(lambda e: e.tensor_copy(mxe[:, 1:1 + ntp], A2[0:4, 0:NP].rearrange("p (c t) -> p c t", t=128)[:, :, 127]), [b2], [Bsm])
                        if has_s:
                            LD(ms0[:], D["ms"], [Bsm])
                            V(lambda e: e.tensor_copy(A4[0:4, NP:NT], A1[0:4, NP:NT]), [b1], [b4])
                            g3 = A4[0:4, NP:NT].rearrange("p (j l) -> p j l", l=8)
                            V(lambda e: e.tensor_tensor(g3[:, :, 0], g3[:, :, 0], ms0[:], ALU.max), [b4, Bsm], [b4])
                            V(lambda e: e.tensor_tensor_scan(A2[0:4, NP:NT], negm[:], A4[0:4, NP:NT], 0.0, ALU.add, ALU.max), [b4, Bc], [b2])
                        V(lambda e: e.tensor_copy(A4[0:4, 0:NP].rearrange("p (c t) -> p c t", t=128),
                                                  mxe[:, 0:ntp].unsqueeze(2).broadcast_to([4, ntp, 128])), [Bsm], [b4])
                        if has_s:
                            V(lambda e: e.tensor_copy(A4[0:4, NP:NT].rearrange("p (j l) -> p j l", l=8),
                                                      ms0[:].unsqueeze(2).broadcast_to([4, 16, 8])), [Bsm], [b4])
                        V(lambda e: e.tensor_tensor(decrow[:, 0:ntp], mxe[:, 0:ntp], mxe[:, 1:1 + ntp], ALU.subtract), [Bsm], [Bsm])
                        if has_s:
                            V(lambda e: e.tensor_tensor(decrow[:, 8:24], ms0[:], A2[0:4, NP:NT].rearrange("p (j l) -> p j l", l=8)[:, :, 7],
                                                        ALU.subtract), [Bsm, b2], [Bsm])
                        else:
                            V(lambda e: e.memset(decrow[:, 8:24], 0.0), (), [Bsm])
                        if ntp < 8:
                            V(lambda e: e.memset(decrow[:, ntp:8], 0.0), (), [Bsm])
                        A(lambda e: e.activation(decrow[:], decrow[:], AF.Exp), [Bsm], [Bsm])
                        ps, bp = PSS()
                        for hh in range(4):
                            P(lambda e, hh=hh, ps=ps: e.matmul(ps[:, hh * 24:(hh + 1) * 24], sel[:, hh, :], decrow[:], start=True, stop=True),
                              [Bc, Bsm], [bp])
                        V(lambda e, ps=ps: e.tensor_copy(decbc[:].rearrange("p a b -> p (a b)"), ps[:, 0:96]), [bp], [Bdec])
                        if last:
                            V(lambda e: e.tensor_tensor(msout[:, 16:17], A2[0:4, NP - 1:NP], A3[0:4, NP - 1:NP], ALU.subtract), [b2, b3], [Bsm])
                            V(lambda e: e.tensor_tensor(msout[:, 0:16], A2[0:4, NP:NT].rearrange("p (j l) -> p j l", l=8)[:, :, 7],
                                                        A3[0:4, NP:NT].rearrange("p (j l) -> p j l", l=8)[:, :, 7], ALU.subtract), [b2, b3], [Bsm])
                            STO(D["mp"], msout[:, 16:17], [Bsm]); STO(D["ms_o"], msout[:, 0:16], [Bsm])
                        V(lambda e: e.tensor_copy(carr[:, 0:1], A3[0:4, NP - 1:NP]), [b3], [Bcar])
                        V(lambda e: e.tensor_copy(carr[:, 1:2], A2[0:4, NP - 1:NP]), [b2], [Bcar])
                        V(lambda e: e.tensor_tensor(A3[0:4, 0:NT], A4[0:4, 0:NT], A3[0:4, 0:NT], ALU.subtract), [b3, b4], [b3])
                        V(lambda e: e.tensor_tensor(A1[0:4, 0:NT], A1[0:4, 0:NT], A4[0:4, 0:NT], ALU.subtract), [b1, b4], [b1])
                        A(lambda e: e.activation(A1[0:4, 0:NT], A1[0:4, 0:NT], AF.Exp, bias=CLNK[0:4]), [b1], [b1])
                        ps, bp = PSS()
                        for c in range(NT // 128):
                            P(lambda e, c=c, ps=ps: e.matmul(ps[:, c * 4:(c + 1) * 4], A1[0:4, c * 128:(c + 1) * 128], ident[0:4, 0:4],
                                                             start=True, stop=True), [b1, Bc], [bp])
                        V(lambda e, ps=ps: e.tensor_copy(ektm[:, 0:NT // 128, :].rearrange("p a b -> p (a b)"), ps[:, 0:4 * (NT // 128)]),
                          [bp], [Bek])
                        ck(2)
                        C0, S0 = Fs[6], Fs[7]
                        LD(Fs[8][:, 0:NT], D["posrow"][sbi][:, 0:NT], [BF[8]])
                        V(lambda e: e.tensor_scalar(Fs[8][:, 0:NT], Fs[8][:, 0:NT], pv("invf"), None, ALU.mult), [BF[8], Bc], [BF[8]])
                        sin_of(C0[:, 0:NT], Fs[8][:, 0:NT], 0.5 * PI, Fs[2][:, 0:NT], [BF[8]], [BF[6]], BF[2])
                        sin_of(S0[:, 0:NT], Fs[8][:, 0:NT], 0.0, Fs[2][:, 0:NT], [BF[8]], [BF[7]], BF[2])
                        V(lambda e: e.tensor_scalar(S0[:, 0:NT], S0[:, 0:NT], pv("sgn"), None, ALU.mult), [BF[7], Bc], [BF[7]])
                        V(lambda e: e.memset(vtm[:, :, 128:129], 1.0), (), [Bv])
                        V(lambda e: e.memset(vtm2[:, :, 128:129], 1.0), (), [Bv2])
                        SETS = [(Hs[0], Hs[1], Hs[2], vtm, BH[0], BH[1], BH[2], Bv), (Hs[3], Hs[4], gT2, vtm2, BH[3], BH[4], Bg2, Bv2)]
                        xq, xk, qT, kT, gT = Fs[0], Fs[1], Hs[0], Hs[1], Hs[2]
                        bxq, bxk, bqT, bkT, bgT = BF[0], BF[1], BH[0], BH[1], BH[2]
                        W = D["w_in_ab"]
                        def front_m(hh):
                            qT, kT, gT, vt_, bqT, bkT, bgT, bv_ = SETS[hh % 2]
                            slot, bs = wload([(D["w_ab_h"][:, hh * 512:(hh + 1) * 512], 0)], 8)
                            for (xx, bx, cc) in ((xq, bxq, 0), (xk, bxk, 128)):
                                def ev(ps, bp, bi, o, n, xx=xx, bx=bx):
                                    if o < NP:
                                        A(lambda e: e.copy(xx[:, 3 + o:3 + o + n], ps[:, 0:n]), [bp], [bx])
                                    else:
                                        A(lambda e: e.copy(xx[:, NP + 3:NP + 3 + 176].rearrange("p (j l) -> p j l", l=11)[:, :, 3:11],
                                                           ps[:, 0:128].rearrange("p (j l) -> p j l", l=8)), [bp], [bx])
                                proj_fm(slot, bs, cc, NT, ev)
                                yield

                            def evg(ps, bp, bi, o, n):
                                A(lambda e: e.activation(gT[:, o:o + n], ps[:, 0:n], AF.Sigmoid), [bp], [bgT])
                            proj_fm(slot, bs, 384, NT, evg)
                            yield
                            vproj(slot, bs, 256, NT, 129, vt_, bv_)
                            yield
                            for (xx, bx, ch, oT, boT) in ((xq, bxq, hh, qT, bqT), (xk, bxk, 4 + hh, kT, bkT)):
                                V(lambda e, xx=xx, ch=ch: e.tensor_copy(xx[:, 0:3], tails[:, ch, :]), [Btl], [bx])
                                acc, bacc = (Fs[2], BF[2]) if ch < 4 else (Fs[3], BF[3])
                                V(lambda e, xx=xx, ch=ch: e.tensor_scalar(acc[:, 0:NP], xx[:, 0:NP], pv("cw0", ch), pv("cb", ch), ALU.mult, ALU.add),
                                  [bx, Bc], [bacc])
                                for j in range(1, 4):
                                    V(lambda e, xx=xx, ch=ch, j=j: e.scalar_tensor_tensor(acc[:, 0:NP], xx[:, j:j + NP], pv("cw%d" % j, ch), acc[:, 0:NP],
                                                                                         ALU.mult, ALU.add), [bx, Bc, bacc], [bacc])
                                if has_s:
                                    LD(xx[:, NP + 3:NP + 3 + 176].rearrange("p (j l) -> p j l", l=11)[:, :, 0:3],
                                       D["convs"][ch * 128:(ch + 1) * 128], [bx])
                                    xs3 = xx[:, NP + 3:NP + 3 + 176].rearrange("p (j l) -> p j l", l=11)
                                    a3 = acc[:, NP:NT].rearrange("p (j l) -> p j l", l=8)
                                    V(lambda e, xs3=xs3, a3=a3, ch=ch: e.tensor_scalar(a3, xs3[:, :, 0:8], pv("cw0", ch), pv("cb", ch), ALU.mult, ALU.add),
                                      [bx, Bc], [bacc])
                                    for j in range(1, 4):
                                        V(lambda e, xs3=xs3, a3=a3, ch=ch, j=j: e.scalar_tensor_tensor(a3, xs3[:, :, j:j + 8], pv("cw%d" % j, ch), a3,
                                                                                                      ALU.mult, ALU.add), [bx, Bc, bacc], [bacc])
                                    STO(D["convs_o"][ch * 128:(ch + 1) * 128], xs3[:, :, 8:11], [bx])
                                    STO(D["convp"][ch * 128:(ch + 1) * 128], xx[:, NP:NP + 3], [bx])
                                A(lambda e, oT=oT: e.activation(oT[:, 0:NT], acc[:, 0:NT], AF.Silu), [bacc], [boT])
                                V(lambda e, xx=xx, ch=ch: e.tensor_copy(tails[:, ch, :], xx[:, NP:NP + 3]), [bx], [Btl])
                                yield

                        def back_m(hh):
                            qT, kT, gT, vt_, bqT, bkT, bgT, bv_ = SETS[hh % 2]
                            if has_s:
                                load_sample_state(D["Us"], hh, 129)
                                make_qz(qT, bqT, NP)
                                V(lambda e: e.tensor_copy(nbs[:], Usf[:, :, 128:129].broadcast_to([128, 16, 128])), [BUs], [Bnbs])
                            for bi, (o, n) in enumerate(blks):
                                PT, bPT = PS()
                                dps, bd = PS()
                                pinned.update((pb.index(PT), pb.index(dps)))
                                for c in range(n // 128):
                                    tcol = o + c * 128
                                    tix = tcol // 128
                                    smp = tcol >= NP
                                    ekf = lambda t_: ektm[:, t_ // 128, hh:hh + 1]
                                    ctx_ = CTX.pop(tcol, None) or att_pre(qT, kT, bqT, bkT, tcol, ekf(tcol), smp)
                                    if tcol + 128 < NT:
                                        CTX[tcol + 128] = att_pre(qT, kT, bqT, bkT, tcol + 128, ekf(tcol + 128), tcol + 128 >= NP)
                                    att_tile(qT, kT, bqT, bkT, tcol, vt_[:, tix, :], 129, hh, ektm[:, tix, hh:hh + 1],
                                             (lambda j, hh=hh: decbc[:, hh, 8 + j:9 + j]) if smp else decbc[:, hh, tix:tix + 1],
                                             PT, bPT, c * 128, smp, den=(dps, bd), mlstm_h=hh, usbuf=Usf, bv=bv_, ctx=ctx_)
                                    step()
                                ps, bp = PSS()
                                P(lambda e, ps=ps, hh=hh: e.matmul(ps[:, 0:n], sel[:, hh, :], A3[0:4, o:o + n], start=True, stop=True), [Bc, b3], [bp])
                                dn = NT5[0]
                                A(lambda e, ps=ps: e.activation(dn[:, 0:n], ps[:, 0:n], AF.Exp, scale=-1.0), [bp], [BN[0]])
                                ab = NT5[1]
                                A(lambda e, dps=dps: e.activation(ab[:, 0:n], dps[:, 0:n], AF.Abs), [bd], [BN[1]])
                                V(lambda e: e.tensor_tensor(ab[:, 0:n], ab[:, 0:n], dn[:, 0:n], ALU.max), [BN[0], BN[1]], [BN[1]])
                                V(lambda e: e.reciprocal(ab[:, 0:n], ab[:, 0:n]), [BN[1]], [BN[1]])
                                hv = NT5[2]
                                V(lambda e, PT=PT: e.tensor_tensor(hv[:, 0:n], PT[:, 0:n], ab[:, 0:n], ALU.mult), [bPT, BN[1]], [BN[2]])
                                V(lambda e: e.tensor_tensor(hv[:, 0:n], hv[:, 0:n], gT[:, o:o + n], ALU.mult), [BN[2], bgT], [BN[2]])
                                rs = rstd_from(lambda q: A(lambda e: e.activation(q[:, 0:n], hv[:, 0:n], AF.Square), [BN[2]], [Bsq[0]]), n, None)
                                V(lambda e, hh=hh: e.scalar_tensor_tensor(mix[:, hh, o:o + n], hv[:, 0:n], pv("gna", hh), rs[:, 0:n], ALU.mult, ALU.mult),
                                  [BN[2], BN[3], Bc], [Bmix[hh][bi]])
                                pinned.clear()
                                step()
                            if has_s:
                                STO(D["Us_o"][:, hh, :, :].rearrange("j d e -> d j e"), Usf[:, :, :], [BUs])
                            if last:
                                STO(D["Up"][hh], Up[:, hh, :], [BU[hh]])
                        for _ in front_m(0):
                            pass
                        for hh in range(4):
                            STEP[0] = front_m(hh + 1) if hh + 1 < 4 else None
                            back_m(hh)
                            while STEP[0] is not None:
                                step()
                        ck(3)
                        def front_r(hh):
                            qT, kT, gT, vt_, bqT, bkT, bgT, bv_ = SETS[hh % 2]
                            b0 = 2056
                            slot, bs = wload([(D["w_ab_h"][:, (4 + hh) * 512:(5 + hh) * 512], 0)], 8)
                            for (xx, bx, cc, oT, boT) in ((xq, bxq, 0, qT, bqT), (xk, bxk, 128, kT, bkT)):
                                def ev(ps, bp, bi, o, n, xx=xx, bx=bx):
                                    A(lambda e: e.copy(xx[:, o:o + n], ps[:, 0:n]), [bp], [bx])
                                proj_fm(slot, bs, cc, NT, ev)
                                yield
                                xsw, t1, t2, bxs, bt1 = (Fs[2], Fs[3], Fs[4], BF[2], BF[3]) if cc == 0 else (Fs[5], Fs[8], Fs[4], BF[5], BF[8])
                                A(lambda e, xx=xx: e.copy(xsw[0:64, 0:NT], xx[64:128, 0:NT]), [bx], [bxs])
                                A(lambda e, xx=xx: e.copy(xsw[64:128, 0:NT], xx[0:64, 0:NT]), [bx], [bxs])
                                V(lambda e, xx=xx: e.tensor_tensor(t1[:, 0:NT], xx[:, 0:NT], C0[:, 0:NT], ALU.mult), [bx, BF[6]], [bt1])
                                V(lambda e: e.tensor_tensor(t2[:, 0:NT], xsw[:, 0:NT], S0[:, 0:NT], ALU.mult), [bxs, BF[7]], [BF[4]])
                                V(lambda e, oT=oT: e.tensor_tensor(oT[:, 0:NT], t1[:, 0:NT], t2[:, 0:NT], ALU.add), [bt1, BF[4]], [boT])

                            def evg(ps, bp, bi, o, n):
                                A(lambda e: e.activation(gT[:, o:o + n], ps[:, 0:n], AF.Silu), [bp], [bgT])
                            proj_fm(slot, bs, 384, NT, evg)
                            yield
                            vproj(slot, bs, 256, NT, 128, vt_, bv_)
                            yield

                        def back_r(hh):
                            qT, kT, gT, vt_, bqT, bkT, bgT, bv_ = SETS[hh % 2]
                            if has_s:
                                load_sample_state(D["rets"], hh, 128)
                                make_qz(qT, bqT, NP)
                            g128 = math.exp(128 * LG[hh]); g8 = math.exp(8 * LG[hh])
                            for bi, (o, n) in enumerate(blks):
                                PT, bPT = PS()
                                pinned.add(pb.index(PT))
                                for c in range(n // 128):
                                    tcol = o + c * 128
                                    tix = tcol // 128
                                    smp = tcol >= NP
                                    ekf = lambda t_: gk[:, 1 if t_ >= NP else 0, hh:hh + 1]
                                    ctx_ = CTX.pop(tcol, None) or att_pre(qT, kT, bqT, bkT, tcol, ekf(tcol), smp)
                                    if tcol + 128 < NT:
                                        CTX[tcol + 128] = att_pre(qT, kT, bqT, bkT, tcol + 128, ekf(tcol + 128), tcol + 128 >= NP)
                                    att_tile(qT, kT, bqT, bkT, tcol, vt_[:, tix, :], 128, 4 + hh, gk[:, 1 if smp else 0, hh:hh + 1],
                                             (lambda j, g8=g8: g8) if smp else g128, PT, bPT, c * 128, smp, usbuf=Usf, bv=bv_, ctx=ctx_)
                                    step()
                                    if PEND[0] is not None and c == 0:
                                        PEND[0]()
                                        PEND[0] = None
                                def chain_(PT=PT, bPT=bPT, o=o, n=n, bi=bi):
                                    hv = NT5[2]
                                    if o < NP:
                                        V(lambda e, PT=PT, hh=hh: e.tensor_tensor(hv[:, 0:n].rearrange("p (c t) -> p c t", t=128),
                                                                                 PT[:, 0:n].rearrange("p (c t) -> p c t", t=128),
                                                                                 Gq[:, 0, hh, :].unsqueeze(1).broadcast_to([128, n // 128, 128]), ALU.mult),
                                          [bPT, Bc], [BN[2]])
                                    else:
                                        V(lambda e, PT=PT, hh=hh: e.tensor_tensor(hv[:, 0:n], PT[:, 0:n], Gq[:, 1, hh, :], ALU.mult), [bPT, Bc], [BN[2]])
                                    rs = rstd_from(lambda q: A(lambda e: e.activation(q[:, 0:n], hv[:, 0:n], AF.Square), [BN[2]], [Bsq[0]]), n, None)
                                    V(lambda e: e.tensor_tensor(hv[:, 0:n], hv[:, 0:n], rs[:, 0:n], ALU.mult), [BN[2], BN[3]], [BN[2]])
                                    V(lambda e, hh=hh: e.tensor_tensor(mix[:, 4 + hh, o:o + n], hv[:, 0:n], gT[:, o:o + n], ALU.mult),
                                      [BN[2], bgT], [Bmix[4 + hh][bi]])
                                    pinned.discard(pb.index(PT))
                                    step()
                                PEND[0] = chain_
                            if PEND[0] is not None:
                                PEND[0]()
                                PEND[0] = None
                            if has_s:
                                STO(D["rets_o"][:, hh, :, :].rearrange("j d e -> d j e"), Usf[:, :, 0:128], [BUs])
                            if last:
                                STO(D["retp"][hh], Up[:, 4 + hh, 0:128], [BU[4 + hh]])
                        for _ in front_r(0):
                            pass
                        for hh in range(4):
                            STEP[0] = front_r(hh + 1) if hh + 1 < 4 else None
                            back_r(hh)
                            while STEP[0] is not None:
                                step()
                        resid_proj(D["w_out_ab"][0] if False else D["w_out_ab"], NT, mix, Bmix)
                        ck(4)
                    else:
                        W = D["w_in_cd"]
                        qf, ff, eb, enb, tmp = Fs[0], Fs[1], Fs[2], Fs[3], Fs[4]
                        qT, kT, gT = Hs[0], Hs[1], Hs[2]
                        bqT, bkT, bgT = BH[0], BH[1], BH[2]
                        def front_h(hh):
                            qT, kT, gT, vt_, bqT, bkT, bgT, bv_ = SETS[hh % 2]
                            slot, bs = wload([(D["w_cd_h"][:, hh * 512:(hh + 1) * 512], 0)], 8)

                            def evq(ps, bp, bi, o, n):
                                A(lambda e: e.activation(qf[:, o:o + n], ps[:, 0:n], AF.Copy, scale=128.0 ** -0.5), [bp], [BF[0]])
                            proj_fm(slot, bs, 0, NT, evq)
                            yield

                            def evf(ps, bp, bi, o, n):
                                A(lambda e: e.activation(ff[:, o:o + n], ps[:, 0:n], AF.Sigmoid), [bp], [BF[1]])
                            proj_fm(slot, bs, 128, NT, evf)
                            yield

                            def evg(ps, bp, bi, o, n):
                                A(lambda e: e.activation(gT[:, o:o + n], ps[:, 0:n], AF.Silu), [bp], [bgT])
                            proj_fm(slot, bs, 384, NT, evg)
                            yield
                            vproj(slot, bs, 256, NT, 128, vt_, bv_)
                            yield
                            V(lambda e, hh=hh: e.tensor_scalar(ff[:, 0:NT], ff[:, 0:NT], oml[:, hh:hh + 1], lb[:, hh:hh + 1], ALU.mult, ALU.add),
                              [BF[1], Bc], [BF[1]])
                            A(lambda e: e.activation(tmp[:, 0:NT], ff[:, 0:NT], AF.Ln), [BF[1]], [BF[4]])
                            V(lambda e: e.tensor_tensor_scan(eb[:, 0:NT], segm[:, 0:NT], tmp[:, 0:NT], 0.0, ALU.mult, ALU.add), [BF[4], Bc], [BF[2]])
                            A(lambda e: e.activation(enb[:, 0:NT], eb[:, 0:NT], AF.Exp, scale=-1.0), [BF[2]], [BF[3]])
                            A(lambda e: e.activation(eb[:, 0:NT], eb[:, 0:NT], AF.Exp), [BF[2], BF[3]], [BF[2]])
                            V(lambda e, hh=hh: e.tensor_copy(hdec[:, hh % 2, 0:ntp], eb[:, 0:NP].rearrange("p (c t) -> p c t", t=128)[:, :, 127]), [BF[2]], [Bhd])
                            if has_s:
                                V(lambda e, hh=hh: e.tensor_copy(hdec[:, hh % 2, 8:24], eb[:, NP:NT].rearrange("p (j l) -> p j l", l=8)[:, :, 7]), [BF[2]], [Bhd])
                            V(lambda e: e.tensor_scalar(ff[:, 0:NT], ff[:, 0:NT], -1.0, 1.0, ALU.mult, ALU.add), [BF[1], BF[4]], [BF[1]])
                            V(lambda e: e.tensor_tensor(qT[:, 0:NT], qf[:, 0:NT], eb[:, 0:NT], ALU.mult), [BF[0], BF[2]], [bqT])
                            V(lambda e: e.tensor_tensor(kT[:, 0:NT], ff[:, 0:NT], enb[:, 0:NT], ALU.mult), [BF[1], BF[3]], [bkT])

                        def back_h(hh):
                            qT, kT, gT, vt_, bqT, bkT, bgT, bv_ = SETS[hh % 2]
                            if has_s:
                                load_sample_state(D["hgrns"], hh, 128)
                                make_qz(qT, bqT, NP)
                            for bi, (o, n) in enumerate(blks):
                                PT, bPT = PS()
                                pinned.add(pb.index(PT))
                                for c in range(n // 128):
                                    tcol = o + c * 128
                                    tix = tcol // 128
                                    smp = tcol >= NP
                                    ctx_ = CTX.pop(tcol, None) or att_pre(qT, kT, bqT, bkT, tcol, None, smp)
                                    if tcol + 128 < NT:
                                        CTX[tcol + 128] = att_pre(qT, kT, bqT, bkT, tcol + 128, None, tcol + 128 >= NP)
                                    att_tile(qT, kT, bqT, bkT, tcol, vt_[:, tix, :], 128, 8 + hh, None,
                                             (lambda j, hh=hh: hdec[:, hh % 2, 8 + j:9 + j]) if smp else hdec[:, hh % 2, tix:tix + 1],
                                             PT, bPT, c * 128, smp, usbuf=Usf, bv=bv_, ctx=ctx_)
                                    step()
                                    if PEND[0] is not None and c == 0:
                                        PEND[0]()
                                        PEND[0] = None
                                def chain_(PT=PT, bPT=bPT, o=o, n=n, bi=bi):
                                    rs = rstd_from(lambda q, PT=PT, bPT=bPT: A(lambda e: e.activation(q[:, 0:n], PT[:, 0:n], AF.Square), [bPT], [Bsq[0]]), n, None)
                                    hv = NT5[2]
                                    V(lambda e, PT=PT: e.tensor_tensor(hv[:, 0:n], PT[:, 0:n], rs[:, 0:n], ALU.mult), [bPT, BN[3]], [BN[2]])
                                    V(lambda e, hh=hh: e.scalar_tensor_tensor(mix[:, hh, o:o + n], hv[:, 0:n], pv("gnc", hh), gT[:, o:o + n], ALU.mult, ALU.mult),
                                      [BN[2], bgT, Bc], [Bmix[hh][bi]])
                                    pinned.discard(pb.index(PT))
                                    step()
                                PEND[0] = chain_
                            if PEND[0] is not None:
                                PEND[0]()
                                PEND[0] = None
                            if has_s:
                                STO(D["hgrns_o"][:, hh, :, :].rearrange("j d e -> d j e"), Usf[:, :, 0:128], [BUs])
                            if last:
                                STO(D["hgrnp"][hh], Up[:, 8 + hh, 0:128], [BU[8 + hh]])
                        for _ in front_h(0):
                            pass
                        for hh in range(4):
                            STEP[0] = front_h(hh + 1) if hh + 1 < 4 else None
                            back_h(hh)
                            while STEP[0] is not None:
                                step()
                        ck(7)
                        suf = Fs[8]; bsuf = BF[8]
                        V(lambda e: e.memset(cch[:, 5, 0:1], 0.0), (), S5FINE + [BUs, Bkz, Bqz, Bcch, Bg2, Bv2])
                        pinned.clear(); S5MODE[0] = True
                        sub = Hs[0]; bsub = BH[0]
                        if has_s:
                            LD(x0s[:, 0], D["x0re"], [Bx0]); LD(x0s[:, 1], D["x0im"], [Bx0])
                            bz = lambda v: v.unsqueeze(2).broadcast_to([128, 16, 16])
                            u1 = Fs[6][:, 0:256].rearrange("p (a b) -> p a b", a=16); u2 = Fs[7][:, 0:256].rearrange("p (a b) -> p a b", a=16)
                            u3 = Fs[6][:, 256:512].rearrange("p (a b) -> p a b", a=16); u4 = Fs[7][:, 256:512].rearrange("p (a b) -> p a b", a=16)
                            for (cr, ci) in ((izr, izi), (ar_, ai_)):
                                V(lambda e, cr=cr: e.tensor_tensor(u1, x0s[:, 0], bz(cr), ALU.mult), [Bx0, Bc], [BF[6]])
                                V(lambda e, ci=ci: e.tensor_tensor(u2, x0s[:, 1], bz(ci), ALU.mult), [Bx0, Bc], [BF[7]])
                                V(lambda e, ci=ci: e.tensor_tensor(u3, x0s[:, 0], bz(ci), ALU.mult), [Bx0, Bc], [BF[6]])
                                V(lambda e, cr=cr: e.tensor_tensor(u4, x0s[:, 1], bz(cr), ALU.mult), [Bx0, Bc], [BF[7]])
                                V(lambda e: e.tensor_tensor(x0s[:, 0], u1, u2, ALU.subtract), [BF[6], BF[7]], [Bx0])
                                V(lambda e: e.tensor_tensor(x0s[:, 1], u3, u4, ALU.add), [BF[6], BF[7]], [Bx0])
                        ntl = NT // 128
                        slot_su, bs_su = wload([(W[:, 2048:2560], 0)], 8)
                        for oc in range(4):
                            slot, bs = slot_su, bs_su

                            def evs(ps, bp, bi, o, n):
                                A(lambda e: e.copy(suf[:, o:o + n], ps[:, 0:n]), [bp], [bsuf])
                                A(lambda e: e.copy(sub[:, o:o + n], ps[:, 0:n]), [bp], [bsub])
                            proj_fm(slot, bs, oc * 128, NT, evs)
                            for part in range(2):
                                S.dma("pool", lambda e, part=part, oc=oc: e.dma_start(out=bct[:, 0, part], in_=D["BT"][part, oc * 4:(oc + 1) * 4].rearrange("i k m -> k i m")),
                                      bcsem[part * 2], (), (Bbct if part == 0 else [Bbct[part * 2]]))
                                S.dma("pool", lambda e, part=part, oc=oc: e.dma_start(out=bct[:, 1, part], in_=D["CT"][part, oc * 4:(oc + 1) * 4].rearrange("i k m -> k i m")),
                                      bcsem[part * 2 + 1], (), [Bbct[part * 2 + 1]])
                            V(lambda e: e.tensor_scalar(bct[:, 1, 1], bct[:, 1, 1], -1.0, None, ALU.mult), Bbct, [Bbct[3]])
                            LD(tbe[:], tabE[:, :, oc * 512:(oc + 1) * 512], [Btab], r=[Bscr])
                            LD(tbz[:], tabZ[:, :, 4 * oc:4 * oc + 4, :], [Btab], r=[Bscr])
                            if has_s:
                                EsT = [Hs[1][:, 0:512], Hs[2][:, 0:512]]
                                ZsT = [Hs[3][:, 0:512], Hs[4][:, 0:512]]
                                build_tables(oc, pv("s8"), jrow[:, 3, :], lambda part: EsT[part],
                                             lambda part: ZsT[part].rearrange("p (a b) -> p a b", a=4), [Bes, BH[1], BH[2]], [Bes, BH[3], BH[4]])
                                for part in range(2):
                                    V(lambda e, part=part, oc=oc: e.tensor_copy(cbs[:, part].rearrange("p a (j l) -> p a j l", l=8),
                                                                             x0s[:, part, 4 * oc:4 * oc + 4, :].unsqueeze(3).broadcast_to([128, 4, 16, 8])),
                                      [Bx0], [Bcbs])
                            i4 = slice(4 * oc, 4 * oc + 4)
                            ypsh = [None]

                            def stA(c):
                                smp = c * 128 >= NP; tc0 = c * 128; k2 = c % 2
                                wrb, wib = wtb[k2][0], wtb[k2][1]; xrb, xib = xtb[k2][0], xtb[k2][1]
                                bE = Bes if smp else Btab
                                smp = c * 128 >= NP
                                tc0 = c * 128
                                Er = EsT[0] if smp else tbe[:, 0, :]
                                Ei = EsT[1] if smp else tbe[:, 1, :]
                                bE = Bes if smp else Btab
                                k2 = c % 2
                                pr, bpr = PS(); pi_, bpi = PS()
                                P(lambda e, pr=pr: e.matmul(pr[:, 0:512], sub[:, tc0:tc0 + 128], bct[:, 0, 0].rearrange("p a b -> p (a b)"), start=True, stop=True),
                                  Bbct + [bsub], [bpr])
                                P(lambda e, pi_=pi_: e.matmul(pi_[:, 0:512], sub[:, tc0:tc0 + 128], bct[:, 0, 1].rearrange("p a b -> p (a b)"), start=True, stop=True),
                                  Bbct + [bsub], [bpi])
                                prb = qzf[:, (2 * k2) * 512:(2 * k2 + 1) * 512]; pib = qzf[:, (2 * k2 + 1) * 512:(2 * k2 + 2) * 512]
                                A(lambda e, pr=pr: e.copy(prb, pr[:, 0:512]), [bpr], [Bprb[k2]])
                                A(lambda e, pi_=pi_: e.copy(pib, pi_[:, 0:512]), [bpi], [Bprb[k2]])
                                wrb, wib = wtb[k2][0], wtb[k2][1]
                                ta, tb, tc_, td = [kzf[:, q_ * 512:(q_ + 1) * 512] for q_ in range(4)]
                                V(lambda e: e.tensor_tensor(ta, prb, Er, ALU.mult), [Bprb[k2], bE], [Bt4[0]])
                                V(lambda e: e.tensor_tensor(tb, pib, Ei, ALU.mult), [Bprb[k2], bE], [Bt4[1]])
                                V(lambda e: e.tensor_tensor(wrb[:], ta, tb, ALU.subtract), [Bt4[0], Bt4[1]], [Bwt[k2]])
                                V(lambda e: e.tensor_tensor(tc_, pib, Er, ALU.mult), [Bprb[k2], bE], [Bt4[2]])
                                V(lambda e: e.tensor_tensor(td, prb, Ei, ALU.mult), [Bprb[k2], bE], [Bt4[3]])
                                V(lambda e: e.tensor_tensor(wib[:], tc_, td, ALU.add), [Bt4[2], Bt4[3]], [Bwt[k2]])

                            TC = {}

                            def stB1(c):
                                smp = c * 128 >= NP; tc0 = c * 128; k2 = c % 2
                                wrb, wib = wtb[k2][0], wtb[k2][1]; xrb, xib = xtb[k2][0], xtb[k2][1]
                                bE = Bes if smp else Btab
                                csr, bcsr = PSS(); csi, bcsi = PSS()
                                cur = S5PAR[oc]
                                TC[c] = (csr, bcsr, csi, bcsi, cur)
                                msk = maskb if smp else maskc
                                for il in range(4):
                                    P(lambda e, il=il, csr=csr: e.matmul(csr[:, il * 128:(il + 1) * 128], wrb[:, il * 128:(il + 1) * 128], msk[:], start=True, stop=(not smp)),
                                      [Bwt[k2], Bc], [bcsr])
                                    P(lambda e, il=il, csi=csi: e.matmul(csi[:, il * 128:(il + 1) * 128], wib[:, il * 128:(il + 1) * 128], msk[:], start=True, stop=(not smp)),
                                      [Bwt[k2], Bc], [bcsi])
                                    if smp:
                                        P(lambda e, il=il, csr=csr: e.matmul(csr[:, il * 128:(il + 1) * 128], identb[:], cbs[:, 0, il, :], start=False, stop=True), [Bc, Bcbs], [bcsr])
                                        P(lambda e, il=il, csi=csi: e.matmul(csi[:, il * 128:(il + 1) * 128], identb[:], cbs[:, 1, il, :], start=False, stop=True), [Bc, Bcbs], [bcsi])
                                cs3r = csr[:, 0:512].rearrange("p (a b) -> p a b", a=4); cs3i = csi[:, 0:512].rearrange("p (a b) -> p a b", a=4)
                                if not smp:
                                    nxt = 1 - cur
                                    V(lambda e: e.tensor_tensor(cch[:, 0, :], cs3r[:, :, 127], s5d[:, cur, 0, i4], ALU.add), [bcsr, Bs5d[cur]], [Bcr])
                                    V(lambda e: e.tensor_tensor(cch[:, 1, :], cs3i[:, :, 127], s5d[:, cur, 1, i4], ALU.add), [bcsi, Bs5d[cur]], [Bcr])
                                    V(lambda e: e.tensor_tensor(cch[:, 2, :], cch[:, 0, :], a128r[:, i4], ALU.mult), [Bcr, Bc], [Bcch])
                                    V(lambda e: e.tensor_tensor(cch[:, 3, :], cch[:, 1, :], a128i[:, i4], ALU.mult), [Bcr, Bc], [Bcch])
                                    V(lambda e: e.tensor_tensor(cch[:, 4, :], cch[:, 0, :], a128i[:, i4], ALU.mult), [Bcr, Bc], [Bcch])
                                    V(lambda e: e.tensor_tensor(cch[:, 5, :], cch[:, 1, :], a128r[:, i4], ALU.mult), [Bcr, Bc], [Bcch])
                                    V(lambda e: e.tensor_tensor(s5d[:, nxt, 0, i4], cch[:, 2, :], cch[:, 3, :], ALU.subtract), [Bcch], [Bs5d[nxt]])
                                    V(lambda e: e.tensor_tensor(s5d[:, nxt, 1, i4], cch[:, 4, :], cch[:, 5, :], ALU.add), [Bcch], [Bs5d[nxt]])
                                    S5PAR[oc] = nxt

                            def stB2(c):
                                smp = c * 128 >= NP; tc0 = c * 128; k2 = c % 2
                                wrb, wib = wtb[k2][0], wtb[k2][1]; xrb, xib = xtb[k2][0], xtb[k2][1]
                                csr, bcsr, csi, bcsi, cur = TC.pop(c)
                                csbr = usbf[:, (2 * k2) * 512:(2 * k2 + 1) * 512]; csbi = usbf[:, (2 * k2 + 1) * 512:(2 * k2 + 2) * 512]
                                if not smp:
                                    for il in range(4):
                                        i = 4 * oc + il
                                        sl = slice(il * 128, (il + 1) * 128)
                                        A(lambda e, sl=sl, i=i, csr=csr: e.activation(csbr[:, sl], csr[:, sl], AF.Identity, bias=s5d[:, cur, 0, i:i + 1]), [bcsr, Bs5d[cur], Bcr], [Bcsb[k2]])
                                        A(lambda e, sl=sl, i=i, csi=csi: e.activation(csbi[:, sl], csi[:, sl], AF.Identity, bias=s5d[:, cur, 1, i:i + 1]), [bcsi, Bs5d[cur], Bcr], [Bcsb[k2]])
                                    Zr = tbz[:, 0].rearrange("p a b -> p (a b)"); Zi = tbz[:, 1].rearrange("p a b -> p (a b)"); bZ = Btab
                                else:
                                    A(lambda e, csr=csr: e.copy(csbr, csr[:, 0:512]), [bcsr], [Bcsb[k2]])
                                    A(lambda e, csi=csi: e.copy(csbi, csi[:, 0:512]), [bcsi], [Bcsb[k2]])
                                    Zr = ZsT[0]; Zi = ZsT[1]; bZ = Bes
                                p1, p2, p3, p4 = [pt4[q_] for q_ in range(4)]
                                xrb, xib = xtb[k2][0], xtb[k2][1]
                                V(lambda e: e.tensor_tensor(p1[:], csbr, Zr, ALU.mult), [Bcsb[k2], bZ], [Bp4[0]])
                                V(lambda e: e.tensor_tensor(p2[:], csbi, Zi, ALU.mult), [Bcsb[k2], bZ], [Bp4[1]])
                                V(lambda e: e.tensor_tensor(xrb[:], p1[:], p2[:], ALU.subtract), [Bp4[0], Bp4[1]], [Bxt[k2]])
                                V(lambda e: e.tensor_tensor(p3[:], csbr, Zi, ALU.mult), [Bcsb[k2], bZ], [Bp4[2]])
                                V(lambda e: e.tensor_tensor(p4[:], csbi, Zr, ALU.mult), [Bcsb[k2], bZ], [Bp4[3]])
                                V(lambda e: e.tensor_tensor(xib[:], p3[:], p4[:], ALU.add), [Bp4[2], Bp4[3]], [Bxt[k2]])
                                if last and (smp or c == NP // 128 - 1):
                                    p13 = [q_[:].rearrange("p (a b) -> p a b", a=4) for q_ in (p1, p2, p3, p4)]
                                    if not smp:
                                        V(lambda e: e.tensor_tensor(s5po[:, 0, i4], p13[0][:, :, 127], p13[1][:, :, 127], ALU.subtract), [Bp4[0], Bp4[1]], [Bs5o])
                                        V(lambda e: e.tensor_tensor(s5po[:, 1, i4], p13[2][:, :, 127], p13[3][:, :, 127], ALU.add), [Bp4[2], Bp4[3]], [Bs5o])
                                    else:
                                        l7 = lambda q3: q3.rearrange("p a (j l) -> p a j l", l=8)[:, :, :, 7]
                                        V(lambda e: e.tensor_tensor(s5so[:, 0, i4, :], l7(p13[0]), l7(p13[1]), ALU.subtract), [Bp4[0], Bp4[1]], [Bs5o])
                                        V(lambda e: e.tensor_tensor(s5so[:, 1, i4, :], l7(p13[2]), l7(p13[3]), ALU.add), [Bp4[2], Bp4[3]], [Bs5o])

                            def stC(c):
                                smp = c * 128 >= NP; tc0 = c * 128; k2 = c % 2
                                wrb, wib = wtb[k2][0], wtb[k2][1]; xrb, xib = xtb[k2][0], xtb[k2][1]
                                bE = Bes if smp else Btab
                                if c % 4 == 0:
                                    pinned.clear()
                                    ypsh[0] = PS()
                                    pinned.add(pb.index(ypsh[0][0]))
                                yp, byp = ypsh[0]
                                yc = (c % 4) * 128
                                for il in range(4):
                                    P(lambda e, il=il, yp=yp: e.matmul(yp[:, yc:yc + 128], bct[:, 1, 0, il, :], xrb[:, il * 128:(il + 1) * 128], start=(il == 0), stop=False),
                                      Bbct + [Bxt[k2]], [byp])
                                    P(lambda e, il=il, yp=yp: e.matmul(yp[:, yc:yc + 128], bct[:, 1, 1, il, :], xib[:, il * 128:(il + 1) * 128], start=False, stop=(il == 3)),
                                      Bbct + [Bxt[k2]], [byp])
                                if c % 4 == 3 or c == ntl - 1:
                                    o = (c // 4) * 512
                                    n = (c % 4 + 1) * 128
                                    bi = o // 512
                                    yv = NT5[4]
                                    V(lambda e, yp=yp: e.scalar_tensor_tensor(yv[:, 0:n], suf[:, o:o + n], pv("s5d", oc), yp[:, 0:n], ALU.mult, ALU.add),
                                      [bsuf, byp, Bc], [BN[4]])
                                    A(lambda e: e.activation(mix[:, 4 + oc, o:o + n], yv[:, 0:n], AF.Gelu), [BN[4]], [Bmix[4 + oc][bi]])
                            stA(0)
                            stB1(0)
                            for c in range(ntl):
                                if c + 1 < ntl:
                                    stA(c + 1)
                                stB2(c)
                                if c + 1 < ntl:
                                    stB1(c + 1)
                                stC(c)
                        V(lambda e: e.memset(cch[:, 5, 0:1], 0.0), (), S5FINE + [BUs, Bkz, Bqz, Bcch, Bg2, Bv2])
                        pinned.clear()
                        S5MODE[0] = False
                        if last:
                            STO(D["s5rep"], s5po[:, 0, :], [Bs5o]); STO(D["s5imp"], s5po[:, 1, :], [Bs5o])
                            STO(D["s5res"], s5so[:, 0], [Bs5o]); STO(D["s5ims"], s5so[:, 1], [Bs5o])
                        ck(8)
                        slot, bs = wload([(D["w_glu"], 0)], 4)
                        for oc in range(4):
                            for bi, (o, n) in enumerate(blks):
                                ps, bp = PS()
                                for k in range(4):
                                    P(lambda e, k=k, ps=ps, oc=oc: e.matmul(ps[:, 0:n], slot[:, k, oc * 128:(oc + 1) * 128], mix[:, 4 + k, o:o + n],
                                                                            start=(k == 0), stop=(k == 3)), bs + [Bmix[4 + k][bi] for k in range(4)], [bp])
                                A(lambda e, ps=ps, oc=oc: e.activation(hn[:, oc, o:o + n], ps[:, 0:n], AF.Sigmoid, bias=pv("bglu", oc)), [bp, Bc], [Bhn[bi]])
                        for oc in range(4):
                            for bi, (o, n) in enumerate(blks):
                                V(lambda e, oc=oc: e.tensor_tensor(mix[:, 4 + oc, o:o + n], mix[:, 4 + oc, o:o + n], hn[:, oc, o:o + n], ALU.mult),
                                  [Bhn[bi], Bmix[4 + oc][bi]], [Bmix[4 + oc][bi]])
                        resid_proj(D["w_out_cd"], NT, mix, Bmix)
                    rmsnorm("nff%d" % layer, NT)
                    for q in range(4):
                        if sbi == 0 and layer == 0:
                            build_tab_kc(q)
                        for u in range(2):
                            c0 = q * 1024 + u * 512
                            slot, bs = wload([(D["w_ff1"][layer][:, c0:c0 + 512], 0)], 8)
                            for hc in range(4):
                                c = u * 4 + hc

                                def ev(ps, bp, bi, o, n, c=c):
                                    k_ = FFK[0] % 2
                                    FFK[0] += 1
                                    t = sqb[k_]
                                    A(lambda e: e.activation(t[:, 0:n], ps[:, 0:n], AF.Relu), [bp], [Bsq[k_]])
                                    V(lambda e: e.tensor_tensor(mix[:, c, o:o + n], t[:, 0:n], t[:, 0:n], ALU.mult), [Bsq[k_]], [Bmix[c][bi]])
                                proj_fm(slot, bs, hc * 128, NT, ev)
                        for u in range(2):
                            slot, bs = wload([(D["w_ff2"][layer][q * 1024:(q + 1) * 1024, u * 512:(u + 1) * 512], 0)], 8)
                            for oc in range(4):
                                c = u * 4 + oc

                                def ev(ps, bp, bi, o, n, c=c):
                                    V(lambda e: e.tensor_tensor(h[:, c, o:o + n], h[:, c, o:o + n], ps[:, 0:n], ALU.add), [bp, Bh[c][bi]], [Bh[c][bi]])
                                proj_fm(slot, bs, oc * 128, NT, ev, rhs=mix, rbufs=lambda bi: [Bmix[k][bi] for k in range(8)])
                    ck(5)
                    rmsnorm("nple%d" % layer, NT)
                    S.dma("pool", lambda e, layer=layer: e.dma_start(out=pTb[:, :, 0:NT], in_=D["pT"][layer][:, tok0:tok0 + NT].rearrange("(k p) n -> p k n", p=128)),
                          pTsem, (), [BpT])
                    for u in range(2):
                        slot, bs = wload([(D["w_ple_gate"][layer][:, u * 512:(u + 1) * 512], 0)], 8)
                        slot2, bs2 = wload([(D["w_ple_proj"][layer][:, u * 512:(u + 1) * 512], 0)], 2)
                        for oc in range(4):
                            c = u * 4 + oc
                            for bi, (o, n) in enumerate(blks):
                                ps, bp = PS()
                                for k in range(8):
                                    P(lambda e, k=k, ps=ps: e.matmul(ps[:, 0:n], slot[:, k, oc * 128:(oc + 1) * 128], hn[:, k, o:o + n], start=(k == 0), stop=(k == 7)),
                                      bs + [Bhn[bi]], [bp])
                                k_ = FFK[0] % 2; FFK[0] += 1; gt = NT5[k_]
                                A(lambda e, ps=ps: e.activation(gt[:, 0:n], ps[:, 0:n], AF.Sigmoid), [bp], [BN[k_]])
                                ps2, bp2 = PS()
                                for k in range(2):
                                    P(lambda e, k=k, ps2=ps2: e.matmul(ps2[:, 0:n], slot2[:, k, oc * 128:(oc + 1) * 128], pTb[:, k, o:o + n], start=(k == 0), stop=(k == 1)),
                                      bs2 + [BpT], [bp2])
                                V(lambda e, ps2=ps2: e.tensor_tensor(gt[:, 0:n], gt[:, 0:n], ps2[:, 0:n], ALU.mult), [bp2, BN[k_]], [BN[k_]])
                                V(lambda e, c=c: e.tensor_tensor(h[:, c, o:o + n], h[:, c, o:o + n], gt[:, 0:n], ALU.add), [BN[k_], Bh[c][bi]], [Bh[c][bi]])
                ck(9)
                rmsnorm("nfin", NT, final_out=D["yT"][:, tok0:tok0 + NT] if True else None)
        except _Stop:
            pass
        S.final_wait("sp", OUTS)
        S.emit(st)
    return nc


_NC = [None]


def kernel(**I):
    if _NC[0] is None:
        _NC[0] = build_program()
    nc = _NC[0]
    in_maps = prep_inputs(I)
    res = run_bass_kernel_spmd(nc, in_maps, core_ids=list(range(8)))
    return assemble(res.results)


def prep_inputs(I):
    f = lambda a: np.ascontiguousarray(np.asarray(a, np.float32))
    ident = np.eye(128, dtype=np.float32)
    s_ = np.arange(128)
    maskc = (s_[:, None] <= s_[None, :]).astype(np.float32)
    maskb = maskc * (s_[:, None] // 8 == s_[None, :] // 8)
    blk3 = np.broadcast_to((np.arange(16)[:, None] == (s_[None, :] // 8)).astype(np.float32)[None], (128, 16, 128)).copy()
    rowm = (s_[:, None] // 8 == np.arange(16)[None, :]).astype(np.float32)
    segm = np.ones((3, 128, NTM), np.float32); posrow = np.zeros((3, 128, NTM), np.float32); tau = np.ones((3, 128, NTM), np.float32)
    for i, (t0, NP, hs) in enumerate(SBS):
        segm[i, :, 0:NP:128] = 0.0
        posrow[i, :, 0:NP] = np.arange(t0, t0 + NP)[None]
        tau[i, :, 0:NP] = np.arange(1, NP + 1)[None]
        if hs:
            segm[i, :, NP:NP + 128:8] = 0.0
            posrow[i, :, NP:NP + 128] = (16384 + (np.arange(128) % 8))[None]
            tau[i, :, NP:NP + 128] = (1 + (np.arange(128) % 8))[None]
    negm = np.zeros((4, 128), np.float32); negm[:, 0::8] = -1e30
    sel = np.zeros((4, 4, 128), np.float32)
    for k in range(4):
        sel[k, k, :] = 1.0
    jrow = np.zeros((128, 4, 128), np.float32); jrow[:, 0, :] = (s_ + 1)[None]; jrow[:, 1, :] = (s_ % 8 + 1)[None]; jrow[:, 2, :] = s_[None]; jrow[:, 3, :] = (s_ % 8)[None]
    pvec = np.zeros((128, NPV), np.float32)

    def put(name, arr):
        o, w = PV[name]
        pvec[:, o:o + w] = arr
    for l in range(2):
        put("nmix%d" % l, _cols(I["norm_mix"][l])); put("nff%d" % l, _cols(I["norm_ff"][l])); put("nple%d" % l, _cols(I["norm_ple"][l]))
        put("lb%d" % l, _cols(I["lb_logits"][l]))
    put("nfin", _cols(I["norm_final"]))
    for j in range(4):
        put("cw%d" % j, _cols(I["conv_w_ab"][0][j]))
    put("cb", _cols(I["conv_b_ab"][0])); put("gna", _cols(I["gn_a"][0])); put("gnc", _cols(I["gn_c"][0]))
    put("s5d", _cols(I["s5_D"][0])); put("bglu", _cols(I["b_glu"][0]))
    st = lambda a: np.ascontiguousarray(np.asarray(a, np.float32).reshape(16, 2, 64).reshape(16, 128).T)
    put("are", st(I["s5_A_re"][0])); put("aim", st(I["s5_A_im"][0]))
    put("ldt", st(np.repeat(np.asarray(I["s5_log_dt"][0], np.float32)[:, None], 64, axis=1)))
    put("invf", (10000.0 ** (-(np.arange(128) % 64) / 64.0)).astype(np.float32)[:, None])
    put("sgn", np.where(s_ < 64, -1.0, 1.0).astype(np.float32)[:, None])
    put("s0", s_.astype(np.float32)[:, None]); put("s8", (s_ % 8).astype(np.float32)[:, None]); put("pidx", (s_ + 1).astype(np.float32)[:, None]); put("pidxs", (s_ % 8 + 1).astype(np.float32)[:, None]); put("rowm", rowm)
    rw = lambda a: np.broadcast_to(np.asarray(a, np.float32).reshape(1, 2048), (128, 2048))
    rowp = np.ascontiguousarray(np.stack([rw(I["s5_A_re"][0]), rw(I["s5_A_im"][0]),
                                          rw(np.repeat(np.asarray(I["s5_log_dt"][0], np.float32)[:, None], 64, axis=1))]))
    bgv = np.asarray(I["b_gate_ab"][0], np.float32)
    bg = np.stack([bgv[:4], bgv[4:]], axis=1).copy()
    BT = np.zeros((2, 16, 128, 128), np.float32); CT = np.zeros((2, 16, 128, 128), np.float32)
    for part, (Bm, Cm) in enumerate(((I["s5_B_re"][0], I["s5_C_re"][0]), (I["s5_B_im"][0], I["s5_C_im"][0]))):
        Bm = np.asarray(Bm, np.float32); Cm = np.asarray(Cm, np.float32)
        for g in range(32):
            i, gl = g // 2, g % 2
            k0 = (g % 8) * 16
            BT[part, i, k0:k0 + 16, gl * 64:(gl + 1) * 64] = Bm[g].T
            CT[part, i, gl * 64:(gl + 1) * 64, k0:k0 + 16] = Cm[g].T
    wab = np.asarray(I["w_in_ab"][0], np.float32); wcd = np.asarray(I["w_in_cd"][0], np.float32)
    w_ab_h = np.concatenate([wab[:, b0 + sec * 512 + hh * 128:b0 + sec * 512 + (hh + 1) * 128]
                             for b0 in (0, 2056) for hh in range(4) for sec in range(4)], axis=1)
    w_cd_h = np.concatenate([wcd[:, sec * 512 + hh * 128:sec * 512 + (hh + 1) * 128] for hh in range(4) for sec in range(4)], axis=1)
    common = dict(w_ab_h=f(w_ab_h), w_cd_h=f(w_cd_h), w_in_ab=f(I["w_in_ab"][0]), wg=f(I["w_in_ab"][0][:, 2048:2056]), w_out_ab=f(I["w_out_ab"][0]),
                  w_in_cd=f(I["w_in_cd"][0]), w_glu=f(I["w_glu"][0]), w_out_cd=f(I["w_out_cd"][0]),
                  w_ff1=f(I["w_ff1"]), w_ff2=f(I["w_ff2"]), w_ple_proj=f(I["w_ple_proj"]), w_ple_gate=f(I["w_ple_gate"]),
                  pvec=pvec, bg=bg, BT=BT, CT=CT, ident=ident, maskc=maskc, maskb=maskb.astype(np.float32), blk3=blk3,
                  segm=segm, negm=negm, sel=sel, posrow=posrow, rowp=rowp, jrow=jrow)
    in_maps = []
    for c in range(8):
        sl = slice(16 * c, 16 * c + 16)
        xT = np.concatenate([np.asarray(I["x_prompt"][c]).T, np.asarray(I["x_sample"][sl]).reshape(128, 1024).T], axis=1)
        pT = np.concatenate([np.transpose(np.asarray(I["p_prompt"][:, c]), (0, 2, 1)),
                             np.transpose(np.asarray(I["p_sample"][:, sl]).reshape(2, 128, 256), (0, 2, 1))], axis=2)
        Us = np.concatenate([np.asarray(I["state_mlstm_C"][0][sl]), np.asarray(I["state_mlstm_n"][0][sl])[..., None]], axis=-1)
        x0 = lambda a: np.transpose(np.asarray(a, np.float32).reshape(16, 16, 128), (2, 1, 0))
        m = dict(common)
        m.update(xT=f(xT), pT=f(pT), convs=f(np.transpose(np.asarray(I["state_mlstm_conv"][0][sl]), (2, 0, 1))), Us=f(Us),
                 ms=f(np.asarray(I["state_mlstm_m"][0][sl]).T), rets=f(I["state_ret"][0][sl]), hgrns=f(I["state_hgrn"][0][sl]),
                 x0re=f(x0(I["state_s5_re"][0][sl])), x0im=f(x0(I["state_s5_im"][0][sl])))
        in_maps.append(m)
    return in_maps


def assemble(R):
    yp = np.zeros((8, 2048, 1024), np.float32); ys = np.zeros((128, 8, 1024), np.float32)
    convp = np.zeros((1, 8, 3, 1024), np.float32); convs = np.zeros((1, 128, 3, 1024), np.float32)
    Cp = np.zeros((1, 8, 4, 128, 128), np.float32); Cs = np.zeros((1, 128, 4, 128, 128), np.float32)
    np_ = np.zeros((1, 8, 4, 128), np.float32); ns = np.zeros((1, 128, 4, 128), np.float32)
    mp = np.zeros((1, 8, 4), np.float32); ms = np.zeros((1, 128, 4), np.float32)
    retp = np.zeros((1, 8, 4, 128, 128), np.float32); rets = np.zeros((1, 128, 4, 128, 128), np.float32)
    hgp = np.zeros((1, 8, 4, 128, 128), np.float32); hgs = np.zeros((1, 128, 4, 128, 128), np.float32)
    s5rp = np.zeros((1, 8, 32, 64), np.float32); s5ip = np.zeros((1, 8, 32, 64), np.float32)
    s5rs = np.zeros((1, 128, 32, 64), np.float32); s5is = np.zeros((1, 128, 32, 64), np.float32)
    for c in range(len(R)):
        r = R[c]
        sl = slice(16 * c, 16 * c + 16)
        yp[c] = r["yT"][:, :2048].T
        ys[sl] = r["yT"][:, 2048:].T.reshape(16, 8, 1024)
        convp[0, c] = r["convp"].T
        convs[0, sl] = np.transpose(r["convs_o"], (1, 2, 0))
        Cp[0, c] = r["Up"][:, :, :128]; np_[0, c] = r["Up"][:, :, 128]
        Cs[0, sl] = r["Us_o"][..., :128]; ns[0, sl] = r["Us_o"][..., 128]
        mp[0, c] = r["mp"][:, 0]; ms[0, sl] = r["ms_o"].T
        retp[0, c] = r["retp"]; rets[0, sl] = r["rets_o"]; hgp[0, c] = r["hgrnp"]; hgs[0, sl] = r["hgrns_o"]
        s5rp[0, c] = r["s5rep"].T.reshape(32, 64); s5ip[0, c] = r["s5imp"].T.reshape(32, 64)
        s5rs[0, sl] = np.transpose(r["s5res"], (2, 1, 0)).reshape(16, 32, 64)
        s5is[0, sl] = np.transpose(r["s5ims"], (2, 1, 0)).reshape(16, 32, 64)
    return (yp, ys, convp, convs, Cp, Cs, np_, ns, mp, ms, retp, rets, hgp, hgs, s5rp, s5rs, s5ip, s5is)
```
